# Optimizing a Trainium2 kernel written in Bass

```python
import math
import jax, jax.numpy as jnp
from jax import lax
import numpy as np

D_MODEL = 1024
BATCH = 16
SEQ = 2048
DEPTH = 1
DEC_BATCH = 8
DEC_SEQ = 64
PAST_LEN = 2048

CHUNK = 64
D_MIX = D_MODEL
C_CONV = D_MIX // 2
CONV_WIDTH = 31
N_HEADS = 4
HEAD_DIM = (D_MIX - C_CONV) // (2 * N_HEADS)
V_DIM = 2 * HEAD_DIM
QK_WIDTH = N_HEADS * 2 * HEAD_DIM
ATTN_WIDTH = N_HEADS * V_DIM
D_IN = 2 * C_CONV + 2 * QK_WIDTH + ATTN_WIDTH
N_BUCKETS = 32
MAX_DISTANCE = 128
Q_BLOCK = 128
N_KEYS = 128
N_EXPERTS = N_KEYS * N_KEYS
R_HEADS = 8
TOPK = 16
D_QUERY = 256
D_HALF = D_QUERY // 2
PEER_BLOCK = 128
EPS = 1e-6
NEG = -1e30

kernel_name = "hymba_conformer_diffattn_peer_stream"


def _lambda_init(layer):
    return 0.8 - 0.6 * math.exp(-0.3 * layer)


def _rmsnorm(x, g):
    xf = x.astype(jnp.float32)
    y = xf * lax.rsqrt(jnp.mean(xf * xf, axis=-1, keepdims=True) + EPS)
    return (y * g.astype(jnp.float32)).astype(x.dtype)


def _rel_bucket(rel):
    nb = N_BUCKETS // 2
    max_exact = nb // 2
    ret = jnp.where(rel > 0, nb, 0)
    n = jnp.abs(rel)
    nf = jnp.maximum(n, 1).astype(jnp.float32)
    large = max_exact + (jnp.log(nf / max_exact) / math.log(MAX_DISTANCE / max_exact)
                         * (nb - max_exact)).astype(jnp.int32)
    large = jnp.minimum(large, nb - 1)
    return ret + jnp.where(n < max_exact, n, large)


def _project(x, g_mix, w_in):
    B, S, _ = x.shape
    z = _rmsnorm(x, g_mix) @ w_in
    glu_in, q, k, v = jnp.split(z, [2 * C_CONV, 2 * C_CONV + QK_WIDTH, 2 * C_CONV + 2 * QK_WIDTH], axis=-1)
    a = glu_in[..., :C_CONV] * jax.nn.sigmoid(glu_in[..., C_CONV:])
    q = q.reshape(B, S, N_HEADS, 2, HEAD_DIM)
    k = k.reshape(B, S, N_HEADS, 2, HEAD_DIM)
    v = v.reshape(B, S, N_HEADS, V_DIM)
    return a, q, k, v


def _conv_branch(a, left, conv_w, conv_b, ln_g, ln_b):
    padded = jnp.concatenate([left, a], axis=1)
    y = lax.conv_general_dilated(padded, conv_w[:, None, :], window_strides=(1,), padding='VALID',
                                 dimension_numbers=('NWC', 'WIO', 'NWC'), feature_group_count=C_CONV)
    yf = (y + conv_b).astype(jnp.float32)
    mu = jnp.mean(yf, axis=-1, keepdims=True)
    var = jnp.mean((yf - mu) ** 2, axis=-1, keepdims=True)
    yn = (yf - mu) * lax.rsqrt(var + EPS) * ln_g.astype(jnp.float32) + ln_b.astype(jnp.float32)
    return jax.nn.silu(yn).astype(a.dtype), padded[:, -(CONV_WIDTH - 1):]


def _diff_attn(q, k, v, q_pos, k_pos, rel_bias, lam):
    logits = jnp.einsum('bqhmd,bkhmd->bhmqk', q, k).astype(jnp.float32) * (HEAD_DIM ** -0.5)
    bias = jnp.moveaxis(rel_bias[_rel_bucket(k_pos[None, :] - q_pos[:, None])].astype(jnp.float32), -1, 0)
    logits = logits + bias[None, :, None]
    mask = (k_pos[None, :] // CHUNK) <= (q_pos[:, None] // CHUNK)
    p = jax.nn.softmax(jnp.where(mask, logits, NEG), axis=-1)
    attn = p[:, :, 0] - lam * p[:, :, 1]
    return jnp.einsum('bhqk,bkhe->bqhe', attn.astype(v.dtype), v)


def _subln(o, g, lambda_init):
    of = o.astype(jnp.float32)
    y = of * lax.rsqrt(jnp.mean(of * of, axis=-1, keepdims=True) + EPS) * g.astype(jnp.float32)
    y = y * (1.0 - lambda_init)
    B, S = o.shape[:2]
    return y.reshape(B, S, ATTN_WIDTH).astype(o.dtype)


def _peer(h, w_query, sub_keys, peer_u, peer_v):
    B, S, D = h.shape
    T = B * S
    n_blk = -(-T // PEER_BLOCK)
    flat = jnp.pad(h.reshape(T, D), ((0, n_blk * PEER_BLOCK - T), (0, 0))).reshape(n_blk, PEER_BLOCK, D)

    def one_block(xb):
        q = (xb @ w_query).reshape(PEER_BLOCK, R_HEADS, 2, D_HALF)
        s = jnp.einsum('trpd,rpnd->trpn', q, sub_keys).astype(jnp.float32)
        s_top, i_top = lax.top_k(s, TOPK)
        cand = s_top[:, :, 0, :, None] + s_top[:, :, 1, None, :]
        cand_idx = i_top[:, :, 0, :, None] * N_KEYS + i_top[:, :, 1, None, :]
        best, pos = lax.top_k(cand.reshape(PEER_BLOCK, R_HEADS, TOPK * TOPK), TOPK)
        idx = jnp.take_along_axis(cand_idx.reshape(PEER_BLOCK, R_HEADS, TOPK * TOPK), pos, axis=-1)
        gate = jax.nn.softmax(best, axis=-1)
        act = jax.nn.gelu(jnp.einsum('trkd,td->trk', peer_u[idx], xb).astype(jnp.float32), approximate=False)
        coef = (gate * act).astype(xb.dtype)
        return jnp.einsum('trk,trkd->td', coef, peer_v[idx])

    out = lax.map(one_block, flat)
    return out.reshape(n_blk * PEER_BLOCK, D)[:T].reshape(B, S, D)


def setup_inputs(seed: int = 0) -> dict:
    key = jax.random.key(seed)
    ks = jax.random.split(key, 24)
    nrm = lambda k, shape, s: jax.random.normal(k, shape, jnp.float32) * s
    return {
        "x_prompt": nrm(ks[0], (BATCH, SEQ, D_MODEL), 1.0),
        "x_sample": nrm(ks[1], (DEC_BATCH, DEC_SEQ, D_MODEL), 1.0),
        "cache_k": nrm(ks[2], (DEPTH, DEC_BATCH, PAST_LEN, N_HEADS, 2, HEAD_DIM), 1.0),
        "cache_v": nrm(ks[3], (DEPTH, DEC_BATCH, PAST_LEN, N_HEADS, V_DIM), 1.0),
        "state_conv": nrm(ks[4], (DEPTH, DEC_BATCH, CONV_WIDTH - 1, C_CONV), 0.5),
        "g_mix": 1.0 + nrm(ks[5], (DEPTH, D_MODEL), 0.02),
        "w_in": nrm(ks[6], (DEPTH, D_MODEL, D_IN), D_MODEL ** -0.5),
        "conv_w": nrm(ks[7], (DEPTH, CONV_WIDTH, C_CONV), CONV_WIDTH ** -0.5),
        "conv_b": nrm(ks[8], (DEPTH, C_CONV), 0.02),
        "conv_ln_g": 1.0 + nrm(ks[9], (DEPTH, C_CONV), 0.02),
        "conv_ln_b": nrm(ks[10], (DEPTH, C_CONV), 0.02),
        "lambda_q1": nrm(ks[11], (DEPTH, HEAD_DIM), 0.1),
        "lambda_k1": nrm(ks[12], (DEPTH, HEAD_DIM), 0.1),
        "lambda_q2": nrm(ks[13], (DEPTH, HEAD_DIM), 0.1),
        "lambda_k2": nrm(ks[14], (DEPTH, HEAD_DIM), 0.1),
        "subln_g": 1.0 + nrm(ks[15], (DEPTH, V_DIM), 0.02),
        "rel_bias": nrm(ks[16], (N_BUCKETS, N_HEADS), 0.5),
        "w_out": nrm(ks[17], (DEPTH, D_MIX, D_MODEL), D_MIX ** -0.5),
        "g_ffn": 1.0 + nrm(ks[18], (DEPTH, D_MODEL), 0.02),
        "w_query": nrm(ks[19], (DEPTH, D_MODEL, R_HEADS * D_QUERY), D_MODEL ** -0.5),
        "sub_keys": nrm(ks[20], (DEPTH, R_HEADS, 2, N_KEYS, D_HALF), D_HALF ** -0.5),
        "peer_u": nrm(ks[21], (DEPTH, N_EXPERTS, D_MODEL), D_MODEL ** -0.5),
        "peer_v": nrm(ks[22], (DEPTH, N_EXPERTS, D_MODEL), 0.25),
        "g_final": 1.0 + nrm(ks[23], (D_MODEL,), 0.02),
    }


def reference(x_prompt, x_sample, cache_k, cache_v, state_conv, g_mix, w_in, conv_w, conv_b,
              conv_ln_g, conv_ln_b, lambda_q1, lambda_k1, lambda_q2, lambda_k2, subln_g, rel_bias,
              w_out, g_ffn, w_query, sub_keys, peer_u, peer_v, g_final):
    B, S, _ = x_prompt.shape
    Bd, Sd, _ = x_sample.shape
    past = cache_k.shape[2]
    pos_p = jnp.arange(S, dtype=jnp.int32)
    pos_s = past + jnp.arange(Sd, dtype=jnp.int32)
    pos_all = jnp.arange(past + Sd, dtype=jnp.int32)

    xp, xs = x_prompt, x_sample
    kp_l, vp_l, cp_l, ks_l, vs_l, cs_l = [], [], [], [], [], []
    for l in range(DEPTH):
        lam_init = _lambda_init(l)
        lam = (jnp.exp(jnp.sum(lambda_q1[l].astype(jnp.float32) * lambda_k1[l].astype(jnp.float32)))
               - jnp.exp(jnp.sum(lambda_q2[l].astype(jnp.float32) * lambda_k2[l].astype(jnp.float32)))
               + lam_init)

        a_p, q_p, k_p, v_p = _project(xp, g_mix[l], w_in[l])
        zero_left = jnp.zeros((B, CONV_WIDTH - 1, C_CONV), a_p.dtype)
        conv_p, tail_p = _conv_branch(a_p, zero_left, conv_w[l], conv_b[l], conv_ln_g[l], conv_ln_b[l])
        outs = []
        for start in range(0, S, Q_BLOCK):
            stop = start + Q_BLOCK
            outs.append(_diff_attn(q_p[:, start:stop], k_p[:, :stop], v_p[:, :stop],
                                   pos_p[start:stop], pos_p[:stop], rel_bias, lam))
        att_p = _subln(jnp.concatenate(outs, axis=1), subln_g[l], lam_init)
        xp = xp + jnp.concatenate([conv_p, att_p], axis=-1) @ w_out[l]
        xp = xp + _peer(_rmsnorm(xp, g_ffn[l]), w_query[l], sub_keys[l], peer_u[l], peer_v[l])

        a_s, q_s, k_s, v_s = _project(xs, g_mix[l], w_in[l])
        conv_s, tail_s = _conv_branch(a_s, state_conv[l], conv_w[l], conv_b[l], conv_ln_g[l], conv_ln_b[l])
        keys = jnp.concatenate([cache_k[l], k_s], axis=1)
        vals = jnp.concatenate([cache_v[l], v_s], axis=1)
        att_s = _subln(_diff_attn(q_s, keys, vals, pos_s, pos_all, rel_bias, lam), subln_g[l], lam_init)
        xs = xs + jnp.concatenate([conv_s, att_s], axis=-1) @ w_out[l]
        xs = xs + _peer(_rmsnorm(xs, g_ffn[l]), w_query[l], sub_keys[l], peer_u[l], peer_v[l])

        kp_l.append(k_p); vp_l.append(v_p); cp_l.append(tail_p)
        ks_l.append(k_s); vs_l.append(v_s); cs_l.append(tail_s)

    y_prompt = _rmsnorm(xp, g_final)
    y_sample = _rmsnorm(xs, g_final)
    return (y_prompt, y_sample, jnp.stack(kp_l), jnp.stack(vp_l), jnp.stack(cp_l),
            jnp.stack(ks_l), jnp.stack(vs_l), jnp.stack(cs_l))
```

```python
import math
import os
from contextlib import ExitStack

import numpy as np
import concourse.bass as bass
import concourse.mybir as mybir
from concourse.bass_utils import run_bass_kernel_spmd

F32 = mybir.dt.float32
BF16 = mybir.dt.bfloat16
U32 = mybir.dt.uint32
AF = mybir.ActivationFunctionType
ALU = mybir.AluOpType
AX = mybir.AxisListType

EPS = 1e-6
LAM_INIT = 0.8 - 0.6 * math.exp(-0.3 * 0)
NCORES = 8
SEQ = 2048
DM = 1024
NEXP_SIDE = 128

SEM_LIMIT = 30000


class Counter:
    def __init__(self, S, name):
        self.S = S
        self.name = name
        self.epoch = 0
        self.val = 0
        self.sem = S.new_sem(f"{name}_e0")

    def bump(self, inc):
        if self.val + inc > SEM_LIMIT:
            self.epoch += 1
            self.val = 0
            self.sem = self.S.new_sem(f"{self.name}_e{self.epoch}")
        self.val += inc
        return (self.sem, self.val, self.name, self.epoch)


class Tile:
    __slots__ = ("name", "w", "r", "dmac")

    def __init__(self, name):
        self.name = name
        self.w = None
        self.r = []
        self.dmac = None


class Sched:
    def __init__(self, nc, stack):
        self.nc = nc
        self.stack = stack
        self.nsem = 0
        self.engs = {"pe": nc.tensor, "act": nc.scalar, "dve": nc.vector,
                     "pool": nc.gpsimd, "sp": nc.sync}
        self.cnt = {k: Counter(self, k) for k in self.engs}
        self.known = {k: {} for k in self.engs}
        self.nops = {k: 0 for k in self.engs}
        self.tiles = []

    def new_sem(self, name):
        self.nsem += 1
        return self.stack.enter_context(self.nc.semaphore(f"s{self.nsem}_{name}"))

    def tile(self, name):
        t = Tile(name)
        self.tiles.append(t)
        return t

    def _wait(self, e, ev):
        sem, val, name, epoch = ev
        key = (name, epoch)
        if self.known[e].get(key, 0) >= val:
            return
        self.known[e][key] = val
        self.engs[e].wait_ge(sem, val)

    def _deps(self, reads, writes):
        evs = []
        for t in reads:
            if t.w is not None:
                evs.append(t.w)
        for t in writes:
            if t.w is not None:
                evs.append(t.w)
            evs.extend(t.r)
        return evs

    def op(self, e, reads, writes, fn):
        for ev in self._deps(reads, writes):
            if ev[2] == e:
                if e == "pe":
                    continue
                if ev[3] == self.cnt[e].epoch and self.cnt[e].val - ev[1] >= 2:
                    continue
            self._wait(e, ev)
        ins = fn(self.engs[e])
        ev = self.cnt[e].bump(1)
        ins.then_inc(ev[0], 1)
        self.nops[e] += 1
        self._mark(ev, reads, writes)
        return ev

    def _mark(self, ev, reads, writes):
        k = (ev[2], ev[3])
        for t in reads:
            t.r = [x for x in t.r if (x[2], x[3]) != k]
            t.r.append(ev)
        for t in writes:
            t.w = ev
            t.r = []

    def dma(self, q, reads, writes, fn, key=None):
        kt = key or (writes[0] if writes else reads[0])
        if kt.dmac is None:
            kt.dmac = Counter(self, "d_" + kt.name)
        for ev in self._deps(reads, writes):
            self._wait(q, ev)
        ins = fn(self.engs[q])
        ev = kt.dmac.bump(16)
        ins.then_inc(ev[0], 16)
        self.nops[q] += 1
        self._mark(ev, reads, writes)
        return ev

    def _all_events(self):
        evs = {}
        for t in self.tiles:
            for ev in ([t.w] if t.w else []) + t.r:
                k = (ev[2], ev[3])
                if k not in evs or evs[k][1] < ev[1]:
                    evs[k] = ev
        return evs

    def barrier(self):
        evs = self._all_events()
        for e in self.engs:
            for ev in evs.values():
                self._wait(e, ev)
        for t in self.tiles:
            t.w = None
            t.r = []

    def finish(self, e="sp"):
        for ev in self._all_events().values():
            self._wait(e, ev)


def _bucket_np(rel):
    nb = 16
    max_exact = 8
    ret = np.where(rel > 0, nb, 0)
    n = np.abs(rel)
    nf = np.maximum(n, 1).astype(np.float32)
    large = max_exact + (np.log(nf / max_exact) / math.log(128 / max_exact) * (nb - max_exact)).astype(np.int32)
    large = np.minimum(large, nb - 1)
    return ret + np.where(n < max_exact, n, large)


def _bucket_tiles():
    k = np.arange(128)[:, None]
    q = np.arange(128)[None, :]
    b0 = _bucket_np(k - q).astype(np.float32)
    masked = (k // 64) > (q // 64)
    b0 = np.where(masked, 32.0, b0)
    b1 = _bucket_np(k - q - 128).astype(np.float32)
    return np.stack([b0, b1], axis=1).astype(np.float32)


def build_program(n_pseq=2, with_peer=True, dbg=False):
    nc = bass.Bass("TRN2", target_bir_lowering=False)
    NT = n_pseq * SEQ + 64

    def din(name, shape, dt=F32):
        return nc.dram_tensor(name, list(shape), dt, kind="ExternalInput").ap()

    def dout(name, shape, dt=F32):
        return nc.dram_tensor(name, list(shape), dt, kind="ExternalOutput").ap()

    xp = din("xp", [n_pseq, SEQ, DM])
    xs = din("xs", [64, DM])
    ckT = din("ckT", [4, 128, SEQ])
    cv = din("cv", [SEQ, 512])
    scT = din("scT", [128, 4, 30])
    w_in = din("w_in", [DM, 2560])
    gmix = din("gmix", [128, 8])
    convw = din("convw", [128, 4, 31])
    cvec = din("cvec", [128, 12])
    lam = din("lam", [1, 256])
    subg = din("subg", [1, 128])
    relb = din("relb", [1, 128])
    w_out = din("w_out", [DM, DM])
    gffn = din("gffn", [128, 8])
    wq = din("wq", [DM, 2048])
    keysT = din("keysT", [16, 128, 128])
    uT = din("uT", [DM, 16384])
    pv = din("pv", [16384, DM])
    gfin = din("gfin", [1, DM])
    bkc = din("bkc", [128, 2, 128])

    yp = dout("yp", [n_pseq, SEQ, DM])
    ys = dout("ys", [64, DM])
    kp = dout("kp", [n_pseq, SEQ, 512])
    vp = dout("vp", [n_pseq, SEQ, 512])
    cp = dout("cp", [n_pseq, 30, 512])
    ks = dout("ks", [64, 512])
    vs = dout("vs", [64, 512])
    cs = dout("cs", [30, 512])

    kind_scr = "ExternalOutput" if dbg else "Internal"
    x1d = nc.dram_tensor("x1d", [NT, DM], F32, kind=kind_scr).ap()
    h2Td = nc.dram_tensor("h2Td", [8, 128, NT], BF16, kind="Internal").ap()

    with ExitStack() as st:
        S = Sched(nc, st)

        cur = [st]

        def sb(name, shape, dt):
            return cur[0].enter_context(nc.sbuf_tensor(name, list(shape), dt)), S.tile(name)

        def ps(name, shape, dt):
            return cur[0].enter_context(nc.psum_tensor(name, list(shape), dt)), S.tile(name)

        ident_f, T_identf = sb("ident_f", [128, 128], F32)
        ident_b, T_identb = sb("ident_b", [128, 128], BF16)
        ones_b, T_ones = sb("ones_b", [128, 128], BF16)
        iota_t, T_iota = sb("iota_t", [128, 128], F32)
        T_const = S.tile("consts")

        S.op("pool", [], [T_iota], lambda e: e.iota(iota_t[:], pattern=[[1, 128]], base=0, channel_multiplier=-1,
                                                    allow_small_or_imprecise_dtypes=True))
        S.op("dve", [T_iota], [T_identf], lambda e: e.tensor_scalar(out=ident_f[:], in0=iota_t[:], scalar1=0.0,
                                                                     scalar2=None, op0=ALU.is_equal))
        S.op("dve", [T_identf], [T_identb], lambda e: e.tensor_copy(out=ident_b[:], in_=ident_f[:]))
        S.op("pool", [], [T_ones], lambda e: e.memset(ones_b[:], 1.0))

        eps_t, T_eps = sb("eps_t", [128, 1], F32)
        S.op("pool", [], [T_eps], lambda e: e.memset(eps_t[:], EPS))
        EPS_AP = eps_t
        iota_r, T_iotar = sb("iota_r", [128, 128], F32)
        S.op("pool", [], [T_iotar], lambda e: e.iota(iota_r[:], pattern=[[1, 128]], base=0, channel_multiplier=0,
                                                     allow_small_or_imprecise_dtypes=True))
        st1 = ExitStack()
        cur[0] = st1
        w_in_b, T_win = sb("w_in_b", [128, 8, 2560], BF16)
        w_out_b, T_wout = sb("w_out_b", [128, 8, DM], BF16)
        diag, T_diag = sb("diag", [128, 124, 128], BF16)
        gmix_s, _ = sb("gmix_s", [128, 8], F32)
        convw_s, _ = sb("convw_s", [128, 4, 31], F32)
        cvec_s, _ = sb("cvec_s", [128, 12], F32)
        lam_s, _ = sb("lam_s", [128, 256], F32)
        gsub_s, _ = sb("gsub_s", [128, 128], F32)
        relb_s, _ = sb("relb_s", [128, 128], F32)
        bk_s, _ = sb("bk_s", [128, 2, 128], F32)
        Tb, T_Tb = sb("Tb", [128, 4, 2, 128], F32)
        eqm, T_eqm = sb("eqm", [128, 2, 128], F32)
        small, T_small = sb("small", [128, 16], F32)
        stage0, T_st0 = sb("stage0", [128, 1024], F32)
        stage1, T_st1 = sb("stage1", [128, 1024], F32)

        for dst, src in ((gmix_s, gmix), (convw_s, convw), (cvec_s, cvec), (bk_s, bkc)):
            S.dma("sp", [], [T_const], lambda e, d=dst, s_=src: e.dma_start(out=d[:], in_=s_))
        for dst, src, n in ((lam_s, lam, 256), (gsub_s, subg, 128), (relb_s, relb, 128)):
            S.dma("sp", [], [T_const], lambda e, d=dst, s_=src, n=n: e.dma_start(out=d[:], in_=s_.to_broadcast([128, n])))

        stg = [(stage0, T_st0), (stage1, T_st1)]
        i = 0
        for kc in range(8):
            for (a0, a1) in ((0, 1024), (1024, 2048), (2048, 2560)):
                stt, T_s = stg[i % 2]
                i += 1
                S.dma("sp", [], [T_s], lambda e, stt=stt, kc=kc, a0=a0, a1=a1: e.dma_start(
                    out=stt[:, 0:a1 - a0], in_=w_in[kc * 128:(kc + 1) * 128, a0:a1]))
                S.op("dve", [T_s, T_const], [T_win], lambda e, stt=stt, kc=kc, a0=a0, a1=a1: e.tensor_scalar(
                    out=w_in_b[:, kc, a0:a1], in0=stt[:, 0:a1 - a0], scalar1=gmix_s[:, kc:kc + 1],
                    scalar2=None, op0=ALU.mult))
        for kc in range(8):
            stt, T_s = stg[i % 2]
            i += 1
            S.dma("sp", [], [T_s], lambda e, stt=stt, kc=kc: e.dma_start(
                out=stt[:, 0:DM], in_=w_out[kc * 128:(kc + 1) * 128, :]))
            S.op("act", [T_s], [T_wout], lambda e, stt=stt, kc=kc: e.copy(out=w_out_b[:, kc, :], in_=stt[:, 0:DM]))
        for w in range(31):
            for cb in range(4):
                S.op("dve", [T_const, T_identf], [T_diag], lambda e, w=w, cb=cb: e.tensor_scalar(
                    out=diag[:, w * 4 + cb, :], in0=ident_f[:], scalar1=convw_s[:, cb, w:w + 1], scalar2=None,
                    op0=ALU.mult))
        S.op("dve", [T_const], [T_eqm], lambda e: e.tensor_scalar(
            out=eqm[:], in0=bk_s[:], scalar1=32.0, scalar2=-30000.0, op0=ALU.is_equal, op1=ALU.mult))
        for h in range(4):
            S.op("dve", [T_eqm], [T_Tb], lambda e, h=h: e.tensor_copy(out=Tb[:, h, :, :], in_=eqm[:]))
        for b in range(32):
            S.op("dve", [T_const], [T_eqm], lambda e, b=b: e.tensor_scalar(
                out=eqm[:], in0=bk_s[:], scalar1=float(b), scalar2=None, op0=ALU.is_equal))
            for h in range(4):
                S.op("dve", [T_eqm, T_const, T_Tb], [T_Tb], lambda e, b=b, h=h: e.scalar_tensor_tensor(
                    out=Tb[:, h, :, :], in0=eqm[:], scalar=relb_s[:, b * 4 + h:b * 4 + h + 1], in1=Tb[:, h, :, :],
                    op0=ALU.mult, op1=ALU.add))
        S.op("dve", [T_const], [T_eqm], lambda e: e.tensor_tensor(
            out=eqm[:, 0, :].rearrange("p (a b) -> p a b", a=2), in0=lam_s[:].rearrange("p (a b c) -> p a b c", a=2, b=2)[:, :, 0, :],
            in1=lam_s[:].rearrange("p (a b c) -> p a b c", a=2, b=2)[:, :, 1, :], op=ALU.mult))
        S.op("dve", [T_eqm], [T_small], lambda e: e.reduce_sum(
            out=small[:, 0:2], in_=eqm[:, 0, :].rearrange("p (a b) -> p a b", a=2), axis=AX.X))
        S.op("act", [T_small], [T_small], lambda e: e.activation(out=small[:, 2:4], in_=small[:, 0:2], func=AF.Exp))
        S.op("dve", [T_small], [T_small], lambda e: e.tensor_tensor(
            out=small[:, 4:5], in0=small[:, 3:4], in1=small[:, 2:3], op=ALU.subtract))
        S.op("dve", [T_small], [T_small], lambda e: e.tensor_scalar(
            out=small[:, 4:5], in0=small[:, 4:5], scalar1=-LAM_INIT, scalar2=None, op0=ALU.add))
        S.op("dve", [T_const], [T_const], lambda e: e.tensor_scalar(
            out=gsub_s[:], in0=gsub_s[:], scalar1=1.0 - LAM_INIT, scalar2=None, op0=ALU.mult))
        neg_lam = small[:, 4:5]

        fT, T_fT = sb("fT", [128, 8, 512], BF16)
        xt, T_xt = sb("xt", [128, DM], F32)
        xr, T_xr = xt, T_xt
        junk, T_junk = sb("junk", [128, DM], BF16)
        hb, T_hb = sb("hb", [128, DM], BF16)
        stat, T_stat = sb("stat", [128, 8], F32)
        aT, T_aT = sb("aT", [128, 4, 30 + 512], BF16)
        sig, T_sig = sb("sig", [128, 512], F32)
        a32, T_a32 = sb("a32", [128, 512], F32)
        qT, T_qT = sb("qT", [128, 4, 512], BF16)
        kT, T_kT = sb("kT", [128, 4, SEQ + 64], BF16)
        vaug, T_v = sb("vaug", [128, 17, 4, 130], BF16)
        catT, T_cat = sb("catT", [128, 8, 512], BF16)
        zq, T_zq = sb("zq", [128, 512], BF16)
        zk32, T_zk32 = sb("zk32", [128, 512], F32)
        zkb, T_zkb = sb("zkb", [128, 512], BF16)
        zv32, T_zv32 = sb("zv32", [128, 512], F32)
        PT, T_PT = sb("PT", [128, 512], BF16)
        PT2, T_PT2 = sb("PT2", [128, 128], BF16)
        tmpb, T_tmpb = sb("tmpb", [128, 128], F32)
        att, T_att = sb("att", [128, 128], F32)
        attb, T_attb = sb("attb", [128, 512], BF16)
        astat, T_astat = sb("astat", [128, 8], F32)
        y32, T_y32 = sb("y32", [128, 4, 512], F32)
        ybf, T_ybf = sb("ybf", [128, 4, 512], BF16)
        ysq, T_ysq = sb("ysq", [128, 4, 512], BF16)
        mu, T_mu = sig, T_sig
        rs, T_rs = a32, T_a32
        ctail, T_ctail = zk32, T_zk32
        cst32, T_cst32 = sb("cst32", [128, 4, 30], F32)

        pT, T_pT = ps("pT", [128, 1024], BF16)
        pM = [ps(f"pM{i}", [128, 512], F32) for i in range(2)]
        pS, T_pS = ps("pS", [128, 512], F32)
        pN, T_pN = ps("pN", [128, 512], F32)
        pO, T_pO = ps("pO", [128, 2, 256], F32)
        pX = [ps(f"pX{i}", [128, 512], F32) for i in range(2)]
        pm_i = [0]

        def next_pM():
            pm_i[0] += 1
            return pM[pm_i[0] % 2]

        S.op("pool", [], [T_v], lambda e: e.memset(vaug[:], 1.0))

        def rms_to_bf16(n, src, T_src, dst_b, T_dst, col):
            S.op("act", [T_src], [T_junk, T_stat], lambda e: e.activation(
                out=junk[0:n, :], in_=src[0:n, :], func=AF.Square, accum_out=stat[0:n, col:col + 1]))
            S.op("act", [T_stat], [T_stat], lambda e: e.activation(
                out=stat[0:n, col + 1:col + 2], in_=stat[0:n, col:col + 1], func=AF.Sqrt, scale=1.0 / DM, bias=EPS_AP[0:n, :]))
            S.op("dve", [T_stat], [T_stat], lambda e: e.reciprocal(
                out=stat[0:n, col + 1:col + 2], in_=stat[0:n, col + 1:col + 2]))
            S.op("dve", [T_src, T_stat], [T_dst], lambda e: e.tensor_scalar(
                out=dst_b[0:n, :], in0=src[0:n, :], scalar1=stat[0:n, col + 1:col + 2], scalar2=None, op0=ALU.mult))

        def transpose_to_fT(n, src_b, T_src, c0):
            for kc in range(8):
                S.op("pe", [T_src, T_identb], [T_pT], lambda e, kc=kc: e.transpose(
                    out=pT[:, kc * 128:kc * 128 + n], in_=src_b[0:n, kc * 128:(kc + 1) * 128], identity=ident_b[0:n, 0:n]))
            S.op("act", [T_pT], [T_fT], lambda e: e.copy(
                out=fT[:, :, c0:c0 + n], in_=pT[:].rearrange("p (k c) -> p k c", k=8)[:, :, 0:n]))

        seqs = []
        for s_ in range(n_pseq):
            seqs.append(("p", xp[s_], SEQ, kp[s_], vp[s_], cp[s_], s_ * SEQ))
        seqs.append(("s", xs, 64, ks, vs, cs, n_pseq * SEQ))

        S.barrier()
        import os
        STOP = int(os.environ.get("KSTOP", "99"))
        if STOP <= 0:
            seqs = []

        for (kind, xd, ntok, kd, vd, cd, tok0) in seqs:
            past = SEQ if kind == "s" else 0
            if kind == "p":
                S.op("pool", [], [T_aT], lambda e: e.memset(aT[:, :, 0:30], 0.0))
            else:
                S.dma("sp", [], [T_cst32], lambda e: e.dma_start(out=cst32[:], in_=scT))
                S.op("dve", [T_cst32], [T_aT], lambda e: e.tensor_copy(out=aT[:, :, 0:30], in_=cst32[:]))
                for h in range(4):
                    for hf in range(2):
                        stt, T_s = stg[(h * 2 + hf) % 2]
                        S.dma("sp", [], [T_s], lambda e, stt=stt, h=h, hf=hf: e.dma_start(
                            out=stt[:, 0:1024], in_=ckT[h, :, hf * 1024:(hf + 1) * 1024]))
                        S.op("act", [T_s], [T_kT], lambda e, stt=stt, h=h, hf=hf: e.copy(
                            out=kT[:, h, hf * 1024:(hf + 1) * 1024], in_=stt[:, 0:1024]))
                for blk in range(16):
                    stt, T_s = stg[blk % 2]
                    S.dma("sp", [], [T_s], lambda e, stt=stt, blk=blk: e.dma_start(
                        out=stt[:, 0:512], in_=cv[blk * 128:(blk + 1) * 128, :]))
                    S.op("dve", [T_s], [T_v], lambda e, stt=stt, blk=blk: e.tensor_copy(
                        out=vaug[:, blk, :, 0:128], in_=stt[:, 0:512].rearrange("p (h e) -> p h e", h=4)))

            ngroups = (ntok + 511) // 512
            for g in range(ngroups):
                g0 = g * 512
                N = min(512, ntok - g0)
                tiles = [(c0, min(128, N - c0)) for c0 in range(0, N, 128)]
                last_group = (g == ngroups - 1)

                for (c0, n) in tiles:
                    S.dma("sp", [], [T_xt], lambda e, c0=c0, n=n: e.dma_start(out=xt[0:n, :], in_=xd[g0 + c0:g0 + c0 + n, :]))
                    rms_to_bf16(n, xt, T_xt, hb, T_hb, 0)
                    transpose_to_fT(n, hb, T_hb, c0)

                if STOP <= 1:
                    continue
                for (c0, n) in tiles:
                    blk = (past + g0 + c0) // 128
                    kcol = past + g0 + c0
                    for j in range(int(os.environ.get('KJ', '3'))):
                        pm, T_pm = next_pM()
                        for kc in range(8):
                            S.op("pe", [T_fT, T_win], [T_pm], lambda e, pm=pm, kc=kc, j=j, c0=c0, n=n: e.matmul(
                                pm[0:n, :], lhsT=fT[:, kc, c0:c0 + n], rhs=w_in_b[:, kc, 1024 + j * 512:1024 + (j + 1) * 512],
                                start=(kc == 0), stop=(kc == 7)))
                        if j == 0:
                            S.op("act", [T_pm], [T_zq], lambda e, pm=pm, n=n: e.activation(
                                out=zq[0:n, :], in_=pm[0:n, :], func=AF.Copy, scale=0.125))
                            for h in range(4):
                                S.op("pe", [T_zq, T_identb], [T_pT], lambda e, h=h, n=n: e.transpose(
                                    out=pT[:, h * 128:h * 128 + n], in_=zq[0:n, h * 128:(h + 1) * 128], identity=ident_b[0:n, 0:n]))
                            S.op("dve", [T_pT], [T_qT], lambda e, c0=c0, n=n: e.tensor_copy(
                                out=qT[:, :, c0:c0 + n], in_=pT[:, 0:512].rearrange("p (k c) -> p k c", k=4)[:, :, 0:n]))
                        elif j == 1:
                            if not os.environ.get("K1A"):
                                S.op("dve", [T_pm], [T_zk32], lambda e, pm=pm, n=n: e.tensor_copy(out=zk32[0:n, :], in_=pm[0:n, :]))
                            S.op("act", [T_zk32], [T_zkb], lambda e, pm=pm, n=n: e.copy(out=zkb[0:n, :], in_=zk32[0:n, :]))
                            if not os.environ.get("NOKD"):
                                S.dma("sp", [T_zk32], [], lambda e, c0=c0, n=n: e.dma_start(
                                    out=kd[g0 + c0:g0 + c0 + n, :], in_=zk32[0:n, :]))
                            for h in range(0 if os.environ.get("K1B") else 4):
                                S.op("pe", [T_zkb, T_identb], [T_pT], lambda e, h=h, n=n: e.transpose(
                                    out=pT[:, 512 + h * 128:512 + h * 128 + n], in_=zkb[0:n, h * 128:(h + 1) * 128],
                                    identity=ident_b[0:n, 0:n]))
                            if not os.environ.get("K1C"):
                              S.op("dve", [T_pT], [T_kT], lambda e, kcol=kcol, n=n: e.tensor_copy(
                                out=kT[:, :, kcol:kcol + n], in_=pT[:, 512:1024].rearrange("p (k c) -> p k c", k=4)[:, :, 0:n]))
                        else:
                            S.op("dve", [T_pm], [T_zv32], lambda e, pm=pm, n=n: e.tensor_copy(out=zv32[0:n, :], in_=pm[0:n, :]))
                            S.op("act", [T_zv32], [T_v], lambda e, pm=pm, n=n, blk=blk: e.copy(
                                out=vaug[0:n, blk, :, 0:128], in_=zv32[0:n, :].rearrange("p (h e) -> p h e", h=4)))
                            S.dma("sp", [T_zv32], [], lambda e, c0=c0, n=n: e.dma_start(
                                out=vd[g0 + c0:g0 + c0 + n, :], in_=zv32[0:n, :]))

                if STOP <= 2:
                    continue
                for cb in range(4):
                    pa, T_pa = next_pM()
                    pg, T_pg = next_pM()
                    for kc in range(8):
                        S.op("pe", [T_fT, T_win], [T_pa], lambda e, pa=pa, kc=kc, cb=cb: e.matmul(
                            pa[:, 0:N], lhsT=w_in_b[:, kc, cb * 128:(cb + 1) * 128], rhs=fT[:, kc, 0:N],
                            start=(kc == 0), stop=(kc == 7)))
                    for kc in range(8):
                        S.op("pe", [T_fT, T_win], [T_pg], lambda e, pg=pg, kc=kc, cb=cb: e.matmul(
                            pg[:, 0:N], lhsT=w_in_b[:, kc, 512 + cb * 128:512 + (cb + 1) * 128], rhs=fT[:, kc, 0:N],
                            start=(kc == 0), stop=(kc == 7)))
                    S.op("act", [T_pg], [T_sig], lambda e, pg=pg: e.activation(out=sig[:, 0:N], in_=pg[:, 0:N], func=AF.Sigmoid))
                    S.op("dve", [T_pa, T_sig], [T_a32], lambda e, pa=pa: e.tensor_tensor(
                        out=a32[:, 0:N], in0=pa[:, 0:N], in1=sig[:, 0:N], op=ALU.mult))
                    S.op("pool", [T_a32], [T_aT], lambda e, cb=cb: e.tensor_copy(out=aT[:, cb, 30:30 + N], in_=a32[:, 0:N]))
                    if last_group:
                        pm, T_pm = pX[0]
                        S.op("pe", [T_a32, T_identf], [T_pm], lambda e, pm=pm, cb=cb: e.transpose(
                            out=pm[0:30, cb * 128:(cb + 1) * 128], in_=a32[:, N - 30:N], identity=ident_f[:]))
                if last_group:
                    pm, T_pm = pX[0]
                    S.op("act", [T_pm], [T_ctail], lambda e, pm=pm: e.copy(out=ctail[0:30, :], in_=pm[0:30, :]))
                    S.dma("sp", [T_ctail], [], lambda e: e.dma_start(out=cd, in_=ctail[0:30, :]))

                if STOP <= 3:
                    continue
                for (c0, n) in tiles:
                    qi = (g0 + c0) // 128
                    if kind == "p":
                        far = list(range(0, max(qi - 1, 0)))
                        near = ([(qi - 1, 128, 1)] if qi >= 1 else []) + [(qi, 128, 0)]
                    else:
                        far = list(range(0, 15))
                        near = [(15, 128, 1), (16, 64, 0)]
                    nblk = len(far) + len(near)
                    for h in range(4):
                        for m in range(2):
                            done = 0
                            mrow = slice(m * 64, (m + 1) * 64)
                            for f0 in range(0, len(far), 4):
                                chunk = far[f0:f0 + 4]
                                for j, blk in enumerate(chunk):
                                    S.op("pe", [T_kT, T_qT], [T_pS], lambda e, j=j, blk=blk, h=h, mrow=mrow, c0=c0, n=n: e.matmul(
                                        pS[:, j * n:(j + 1) * n], lhsT=kT[mrow, h, blk * 128:(blk + 1) * 128],
                                        rhs=qT[mrow, h, c0:c0 + n], start=True, stop=True))
                                cn = len(chunk) * n
                                S.op("act", [T_pS, T_const], [T_PT], lambda e, cn=cn, h=h: e.activation(
                                    out=PT[:, 0:cn], in_=pS[:, 0:cn], func=AF.Exp, bias=relb_s[:, 60 + h:61 + h]))
                                for j, blk in enumerate(chunk):
                                    S.op("pe", [T_PT, T_v], [T_pO], lambda e, j=j, blk=blk, h=h, m=m, n=n, done=done: e.matmul(
                                        pO[0:n, m, 0:129], lhsT=PT[:, j * n:(j + 1) * n], rhs=vaug[:, blk, h, 0:129],
                                        start=(done == 0), stop=(done == nblk - 1)))
                                    done += 1
                            for (blk, nk, bkind) in near:
                                S.op("pe", [T_kT, T_qT], [T_pN], lambda e, blk=blk, nk=nk, h=h, mrow=mrow, c0=c0, n=n: e.matmul(
                                    pN[0:nk, 0:n], lhsT=kT[mrow, h, blk * 128:blk * 128 + nk],
                                    rhs=qT[mrow, h, c0:c0 + n], start=True, stop=True))
                                S.op("dve", [T_pN, T_Tb], [T_tmpb], lambda e, nk=nk, n=n, h=h, bkind=bkind: e.tensor_tensor(
                                    out=tmpb[0:nk, 0:n], in0=pN[0:nk, 0:n], in1=Tb[0:nk, h, bkind, 0:n], op=ALU.add))
                                S.op("act", [T_tmpb], [T_PT2], lambda e, nk=nk, n=n: e.activation(
                                    out=PT2[0:nk, 0:n], in_=tmpb[0:nk, 0:n], func=AF.Exp))
                                S.op("pe", [T_PT2, T_v], [T_pO], lambda e, blk=blk, nk=nk, h=h, m=m, n=n, done=done: e.matmul(
                                    pO[0:n, m, 0:129], lhsT=PT2[0:nk, 0:n], rhs=vaug[0:nk, blk, h, 0:129],
                                    start=(done == 0), stop=(done == nblk - 1)))
                                done += 1
                        S.op("dve", [T_pO], [T_astat], lambda e, n=n: e.reciprocal(
                            out=astat[0:n, 0:2], in_=pO[0:n, :, 128:129].rearrange("p a b -> p (a b)")))
                        S.op("dve", [T_astat, T_small], [T_astat], lambda e, n=n: e.tensor_tensor(
                            out=astat[0:n, 2:3], in0=astat[0:n, 1:2], in1=neg_lam[0:n, :], op=ALU.mult))
                        S.op("dve", [T_pO, T_astat], [T_att], lambda e, n=n: e.tensor_scalar(
                            out=att[0:n, :], in0=pO[0:n, 0, 0:128], scalar1=astat[0:n, 0:1], scalar2=None, op0=ALU.mult))
                        S.op("dve", [T_pO, T_astat, T_att], [T_att], lambda e, n=n: e.scalar_tensor_tensor(
                            out=att[0:n, :], in0=pO[0:n, 1, 0:128], scalar=astat[0:n, 2:3], in1=att[0:n, :],
                            op0=ALU.mult, op1=ALU.add))
                        S.op("act", [T_att], [T_junk, T_astat], lambda e, n=n: e.activation(
                            out=junk[0:n, 0:128], in_=att[0:n, :], func=AF.Square, accum_out=astat[0:n, 3:4]))
                        S.op("act", [T_astat, T_eps], [T_astat], lambda e, n=n: e.activation(
                            out=astat[0:n, 4:5], in_=astat[0:n, 3:4], func=AF.Sqrt, scale=1.0 / 128, bias=eps_t[0:n, :]))
                        S.op("dve", [T_astat], [T_astat], lambda e, n=n: e.reciprocal(out=astat[0:n, 4:5], in_=astat[0:n, 4:5]))
                        S.op("dve", [T_att, T_astat, T_const], [T_attb], lambda e, n=n, h=h: e.scalar_tensor_tensor(
                            out=attb[0:n, h * 128:(h + 1) * 128], in0=att[0:n, :], scalar=astat[0:n, 4:5], in1=gsub_s[0:n, :],
                            op0=ALU.mult, op1=ALU.mult))
                    for h in range(4):
                        S.op("pe", [T_attb, T_identb], [T_pT], lambda e, h=h, n=n: e.transpose(
                            out=pT[:, h * 128:h * 128 + n], in_=attb[0:n, h * 128:(h + 1) * 128], identity=ident_b[0:n, 0:n]))
                    S.op("act", [T_pT], [T_cat], lambda e, c0=c0, n=n: e.copy(
                        out=catT[:, 4:8, c0:c0 + n], in_=pT[:, 0:512].rearrange("p (k c) -> p k c", k=4)[:, :, 0:n]))

                if STOP <= 4:
                    continue
                for cb in range(4):
                    pm, T_pm = next_pM()
                    for w in range(31):
                        S.op("pe", [T_aT, T_diag], [T_pm], lambda e, pm=pm, w=w, cb=cb: e.matmul(
                            pm[:, 0:N], lhsT=diag[:, w * 4 + cb, :], rhs=aT[:, cb, w:w + N], start=(w == 0), stop=(w == 30)))
                    S.op("act", [T_pm, T_const], [T_y32], lambda e, pm=pm, cb=cb: e.activation(
                        out=y32[:, cb, 0:N], in_=pm[:, 0:N], func=AF.Identity, bias=cvec_s[:, cb:cb + 1]))
                    S.op("act", [T_pm, T_const], [T_ysq], lambda e, pm=pm, cb=cb: e.activation(
                        out=ysq[:, cb, 0:N], in_=pm[:, 0:N], func=AF.Square, bias=cvec_s[:, cb:cb + 1]))
                    S.op("pool", [T_y32], [T_ybf], lambda e, cb=cb: e.tensor_copy(out=ybf[:, cb, 0:N], in_=y32[:, cb, 0:N]))
                p1, T_p1 = pX[0]
                p2, T_p2 = pX[1]
                for cb in range(4):
                    S.op("pe", [T_ybf, T_ones], [T_p1], lambda e, cb=cb: e.matmul(
                        p1[:, 0:N], lhsT=ones_b[:], rhs=ybf[:, cb, 0:N], start=(cb == 0), stop=(cb == 3)))
                for cb in range(4):
                    S.op("pe", [T_ysq, T_ones], [T_p2], lambda e, cb=cb: e.matmul(
                        p2[:, 0:N], lhsT=ones_b[:], rhs=ysq[:, cb, 0:N], start=(cb == 0), stop=(cb == 3)))
                S.op("dve", [T_p1], [T_mu], lambda e: e.tensor_scalar(
                    out=mu[:, 0:N], in0=p1[:, 0:N], scalar1=1.0 / 512, scalar2=None, op0=ALU.mult))
                S.op("dve", [T_mu], [T_rs], lambda e: e.tensor_tensor(out=rs[:, 0:N], in0=mu[:, 0:N], in1=mu[:, 0:N], op=ALU.mult))
                S.op("dve", [T_p2, T_rs], [T_rs], lambda e: e.scalar_tensor_tensor(
                    out=rs[:, 0:N], in0=p2[:, 0:N], scalar=1.0 / 512, in1=rs[:, 0:N], op0=ALU.mult, op1=ALU.subtract))
                S.op("act", [T_rs, T_eps], [T_rs], lambda e: e.activation(
                    out=rs[:, 0:N], in_=rs[:, 0:N], func=AF.Sqrt, bias=eps_t[:, :]))
                S.op("dve", [T_rs], [T_rs], lambda e: e.reciprocal(out=rs[:, 0:N], in_=rs[:, 0:N]))
                for cb in range(4):
                    S.op("dve", [T_y32, T_mu], [T_y32], lambda e, cb=cb: e.tensor_tensor(
                        out=y32[:, cb, 0:N], in0=y32[:, cb, 0:N], in1=mu[:, 0:N], op=ALU.subtract))
                    S.op("pool", [T_y32, T_rs], [T_y32], lambda e, cb=cb: e.tensor_tensor(
                        out=y32[:, cb, 0:N], in0=y32[:, cb, 0:N], in1=rs[:, 0:N], op=ALU.mult))
                    S.op("act", [T_y32, T_const], [T_cat], lambda e, cb=cb: e.activation(
                        out=catT[:, cb, 0:N], in_=y32[:, cb, 0:N], func=AF.Silu,
                        scale=cvec_s[:, 4 + cb:5 + cb], bias=cvec_s[:, 8 + cb:9 + cb]))
                if not last_group:
                    S.op("pool", [T_aT], [T_aT], lambda e: e.tensor_copy(out=aT[:, :, 0:30], in_=aT[:, :, N:N + 30]))

                if STOP <= 5:
                    continue
                for (c0, n) in tiles:
                    S.dma("sp", [], [T_xr], lambda e, c0=c0, n=n: e.dma_start(out=xr[0:n, :], in_=xd[g0 + c0:g0 + c0 + n, :]))
                    for hf in range(2):
                        po, T_po = pX[hf]
                        for kc in range(8):
                            S.op("pe", [T_cat, T_wout], [T_po], lambda e, po=po, kc=kc, hf=hf, c0=c0, n=n: e.matmul(
                                po[0:n, :], lhsT=catT[:, kc, c0:c0 + n], rhs=w_out_b[:, kc, hf * 512:(hf + 1) * 512],
                                start=(kc == 0), stop=(kc == 7)))
                        S.op("dve", [T_po, T_xr], [T_xr], lambda e, po=po, hf=hf, n=n: e.tensor_tensor(
                            out=xr[0:n, hf * 512:(hf + 1) * 512], in0=po[0:n, :], in1=xr[0:n, hf * 512:(hf + 1) * 512], op=ALU.add))
                    S.dma("sp", [T_xr], [], lambda e, c0=c0, n=n: e.dma_start(
                        out=x1d[tok0 + g0 + c0:tok0 + g0 + c0 + n, :], in_=xr[0:n, :]))
                    rms_to_bf16(n, xr, T_xr, hb, T_hb, 2)
                    transpose_to_fT(n, hb, T_hb, c0)
                for kc in range(8):
                    S.dma("sp", [T_fT], [], lambda e, kc=kc: e.dma_start(
                        out=h2Td[kc, :, tok0 + g0:tok0 + g0 + N], in_=fT[:, kc, 0:N]))

        S.barrier()
        st1.close()
        st2 = ExitStack()
        st.enter_context(st2)
        cur[0] = st2
        TG = 256
        if with_peer:
            wq_b, T_wq = sb("wq_b", [128, 8, 2048], BF16)
            keys_b, T_keys = sb("keys_b", [128, 16, 128], BF16)
            gfin_s, T_gfin = sb("gfin_s", [128, DM], F32)
            gffn_s, T_gffn = sb("gffn_s", [128, 8], F32)
            sg0, T_sg0 = sb("sg0", [128, 1024], F32)
            sg1, T_sg1 = sb("sg1", [128, 1024], F32)
            h2g, T_h2g = sb("h2g", [128, 8, TG], BF16)
            qryT, T_qry = sb("qryT", [128, 16, TG], BF16)
            s_sb, T_ssb = sb("s_sb", [128, 16, 128], F32)
            wk, T_wk = sb("wk", [128, 256], F32)
            A_, T_A = sb("A_", [128, 16, 16], F32)
            Iu, T_Iu = sb("Iu", [128, 16, 16], U32)
            If, T_If = sb("If", [128, 16, 16], F32)
            cand, T_cand = sb("cand", [128, 8, 256], F32)
            C_, T_C = sb("C_", [128, 8, 16], F32)
            pos, T_pos = sb("pos", [128, 8, 16], U32)
            ku, T_ku = sb("ku", [128, 2, 128], U32)
            kf, T_kf = sb("kf", [128, 2, 128], F32)
            E_, T_E = sb("E_", [128, 8, 16], F32)
            gst, T_gst = sb("gst", [128, 32], F32)
            oh, T_oh = sb("oh", [128, 8, 16, 16], F32)
            ijw, T_ijw = sb("ijw", [128, 3, 128], F32)
            ITJW, T_ITJW = sb("ITJW", [128, 3, TG], F32)
            P4 = [sb(f"P4_{i}", [128, 4, 128], BF16) for i in range(2)]
            Qe = [sb(f"Qe_{i}", [128, 4, 128], F32) for i in range(2)]
            Q4 = [sb(f"Q4_{i}", [128, 4, 128], BF16) for i in range(2)]
            Gall, T_Gall = sb("Gall", [128, 128, TG], BF16)
            ubuf = [sb(f"ubuf{i}", [128, 8, 512], BF16) for i in range(2)]
            vbuf = [sb(f"vbuf{i}", [128, DM], BF16) for i in range(3)]
            gbuf = [sb(f"gbuf{i}", [128, TG], F32) for i in range(2)]
            cbuf = [sb(f"cbuf{i}", [128, TG], BF16) for i in range(2)]
            x2, T_x2 = sb("x2", [128, DM], F32)
            yt, T_yt = sb("yt", [128, DM], F32)
            junk2, T_junk2 = sb("junk2", [128, DM], BF16)
            st2s, T_st2s = sb("st2s", [128, 4], F32)

            pY = [[ps(f"pY{t}{h}", [128, 512], F32) for h in range(2)] for t in range(2)]
            pA = [ps(f"pA{i}", [128, 512], F32) for i in range(2)]
            pG = [ps(f"pG{i}", [128, 512], F32) for i in range(2)]
            pg_i = [0]

            def next_pG():
                pg_i[0] += 1
                return pG[pg_i[0] % 2]

            T_c2 = S.tile("consts2")
            S.dma("sp", [], [T_c2], lambda e: e.dma_start(out=gffn_s[:], in_=gffn))
            S.dma("sp", [], [T_c2], lambda e: e.dma_start(out=gfin_s[:], in_=gfin.to_broadcast([128, DM])))
            S.dma("pool", [], [T_keys], lambda e: e.dma_start(out=keys_b[:], in_=keysT.rearrange("r d n -> d r n")))
            sgs = [(sg0, T_sg0), (sg1, T_sg1)]
            ii = 0
            for kc in range(8):
                for hf in range(2):
                    stt, T_s = sgs[ii % 2]
                    ii += 1
                    S.dma("sp", [], [T_s], lambda e, stt=stt, kc=kc, hf=hf: e.dma_start(
                        out=stt[:], in_=wq[kc * 128:(kc + 1) * 128, hf * 1024:(hf + 1) * 1024]))
                    S.op("dve", [T_s, T_c2], [T_wq], lambda e, stt=stt, kc=kc, hf=hf: e.tensor_scalar(
                        out=wq_b[:, kc, hf * 1024:(hf + 1) * 1024], in0=stt[:], scalar1=gffn_s[:, kc:kc + 1],
                        scalar2=None, op0=ALU.mult))

            groups = [(t0, min(TG, NT - t0)) for t0 in range(0, NT, TG)]
            MAXG = int(os.environ.get("KGROUPS", "999"))
            vb_i = 0
            ub_i = 0
            for gi, (t0, N) in enumerate(groups[:MAXG]):
                tiles = [(c0, min(128, N - c0)) for c0 in range(0, N, 128)]
                S.dma("sp", [], [T_h2g], lambda e: e.dma_start(
                    out=h2g[:, :, 0:N], in_=h2Td[:, :, t0:t0 + N].rearrange("k p t -> p k t")))
                for blk in range(16):
                    pg, T_pg = next_pG()
                    for kc in range(8):
                        S.op("pe", [T_wq, T_h2g], [T_pg], lambda e, pg=pg, kc=kc, blk=blk: e.matmul(
                            pg[:, 0:N], lhsT=wq_b[:, kc, blk * 128:(blk + 1) * 128], rhs=h2g[:, kc, 0:N],
                            start=(kc == 0), stop=(kc == 7)))
                    S.op("act", [T_pg], [T_qry], lambda e, pg=pg, blk=blk: e.copy(out=qryT[:, blk, 0:N], in_=pg[:, 0:N]))
                for ti, (c0, n) in enumerate(tiles):
                    for q4 in range(4):
                        pg, T_pg = next_pG()
                        for j in range(4):
                            rp = q4 * 4 + j
                            S.op("pe", [T_qry, T_keys], [T_pg], lambda e, pg=pg, j=j, rp=rp: e.matmul(
                                pg[0:n, j * 128:(j + 1) * 128], lhsT=qryT[:, rp, c0:c0 + n], rhs=keys_b[:, rp, :],
                                start=True, stop=True))
                        S.op("act", [T_pg], [T_ssb], lambda e, pg=pg, q4=q4: e.copy(
                            out=s_sb[0:n, q4 * 4:(q4 + 1) * 4, :], in_=pg[0:n, :].rearrange("p (a b) -> p a b", a=4)))
                    for rp in range(16):
                        S.op("dve", [T_ssb], [T_A], lambda e, rp=rp: e.max(out=A_[0:n, rp, 0:8], in_=s_sb[0:n, rp, :]))
                        S.op("dve", [T_ssb, T_A], [T_Iu], lambda e, rp=rp: e.max_index(
                            out=Iu[0:n, rp, 0:8], in_max=A_[0:n, rp, 0:8], in_values=s_sb[0:n, rp, :]))
                        S.op("dve", [T_ssb, T_A], [T_wk], lambda e, rp=rp: e.match_replace(
                            out=wk[0:n, 0:128], in_to_replace=A_[0:n, rp, 0:8], in_values=s_sb[0:n, rp, :], imm_value=-1e30))
                        S.op("dve", [T_wk], [T_A], lambda e, rp=rp: e.max(out=A_[0:n, rp, 8:16], in_=wk[0:n, 0:128]))
                        S.op("dve", [T_wk, T_A], [T_Iu], lambda e, rp=rp: e.max_index(
                            out=Iu[0:n, rp, 8:16], in_max=A_[0:n, rp, 8:16], in_values=wk[0:n, 0:128]))
                    S.op("dve", [T_Iu], [T_If], lambda e: e.tensor_copy(out=If[0:n], in_=Iu[0:n]))
                    A4 = A_[0:n].rearrange("p (r a) k -> p r a k", a=2)
                    I4 = If[0:n].rearrange("p (r a) k -> p r a k", a=2)
                    S.op("dve", [T_A], [T_cand], lambda e: e.tensor_tensor(
                        out=cand[0:n].rearrange("p r (a b) -> p r a b", a=16),
                        in0=A4[:, :, 0, :].unsqueeze(3).to_broadcast([n, 8, 16, 16]),
                        in1=A4[:, :, 1, :].unsqueeze(2).to_broadcast([n, 8, 16, 16]), op=ALU.add))
                    for r in range(8):
                        S.op("dve", [T_cand], [T_C], lambda e, r=r: e.max(out=C_[0:n, r, 0:8], in_=cand[0:n, r, :]))
                        S.op("dve", [T_cand, T_C], [T_pos], lambda e, r=r: e.max_index(
                            out=pos[0:n, r, 0:8], in_max=C_[0:n, r, 0:8], in_values=cand[0:n, r, :]))
                        S.op("dve", [T_cand, T_C], [T_wk], lambda e, r=r: e.match_replace(
                            out=wk[0:n, :], in_to_replace=C_[0:n, r, 0:8], in_values=cand[0:n, r, :], imm_value=-1e30))
                        S.op("dve", [T_wk], [T_C], lambda e, r=r: e.max(out=C_[0:n, r, 8:16], in_=wk[0:n, :]))
                        S.op("dve", [T_wk, T_C], [T_pos], lambda e, r=r: e.max_index(
                            out=pos[0:n, r, 8:16], in_max=C_[0:n, r, 8:16], in_values=wk[0:n, :]))
                    S.op("dve", [T_C], [T_gst], lambda e: e.tensor_scalar(
                        out=gst[0:n, 0:8], in0=C_[0:n, :, 0], scalar1=-1.0, scalar2=None, op0=ALU.mult))
                    for r in range(8):
                        S.op("act", [T_C, T_gst], [T_E, T_gst], lambda e, r=r: e.activation(
                            out=E_[0:n, r, :], in_=C_[0:n, r, :], func=AF.Exp, bias=gst[0:n, r:r + 1],
                            accum_out=gst[0:n, 8 + r:9 + r]))
                    S.op("dve", [T_gst], [T_gst], lambda e: e.reciprocal(out=gst[0:n, 16:24], in_=gst[0:n, 8:16]))
                    S.op("dve", [T_E, T_gst], [T_ijw], lambda e: e.tensor_tensor(
                        out=ijw[0:n, 2, :].rearrange("p (r k) -> p r k", r=8), in0=E_[0:n],
                        in1=gst[0:n, 16:24].unsqueeze(2).to_broadcast([n, 8, 16]), op=ALU.mult))
                    S.op("dve", [T_pos], [T_ku], lambda e: e.tensor_single_scalar(
                        out=ku[0:n, 0, :], in_=pos[0:n].rearrange("p r k -> p (r k)"), scalar=4, op=ALU.logical_shift_right))
                    S.op("dve", [T_pos], [T_ku], lambda e: e.tensor_single_scalar(
                        out=ku[0:n, 1, :], in_=pos[0:n].rearrange("p r k -> p (r k)"), scalar=15, op=ALU.bitwise_and))
                    S.op("dve", [T_ku], [T_kf], lambda e: e.tensor_copy(out=kf[0:n], in_=ku[0:n]))
                    for a in range(2):
                        S.op("dve", [T_kf, T_iotar], [T_oh], lambda e, a=a: e.tensor_tensor(
                            out=oh[0:n],
                            in0=kf[0:n, a, :].rearrange("p (r k) -> p r k", r=8).unsqueeze(3).to_broadcast([n, 8, 16, 16]),
                            in1=iota_r[0:n, 0:16].unsqueeze(1).unsqueeze(1).to_broadcast([n, 8, 16, 16]), op=ALU.is_equal))
                        S.op("dve", [T_oh, T_If], [T_oh], lambda e, a=a: e.tensor_tensor(
                            out=oh[0:n], in0=oh[0:n],
                            in1=I4[:, :, a, :].unsqueeze(2).to_broadcast([n, 8, 16, 16]), op=ALU.mult))
                        S.op("dve", [T_oh], [T_ijw], lambda e, a=a: e.reduce_sum(
                            out=ijw[0:n, a, :].rearrange("p (r k) -> p r k", r=8), in_=oh[0:n], axis=AX.X))
                    pg, T_pg = next_pG()
                    for a in range(3):
                        S.op("pe", [T_ijw, T_identf], [T_pg], lambda e, pg=pg, a=a: e.transpose(
                            out=pg[:, a * 128:a * 128 + n], in_=ijw[0:n, a, :], identity=ident_f[0:n, 0:n]))
                    S.op("act", [T_pg], [T_ITJW], lambda e, pg=pg: e.copy(
                        out=ITJW[:, :, c0:c0 + n], in_=pg[:, 0:384].rearrange("p (a t) -> p a t", a=3)[:, :, 0:n]))
                    for q in range(n // 4):
                        tl = c0 + q * 4
                        (p4, T_p4), (qe, T_qe), (q4_, T_q4) = P4[q % 2], Qe[q % 2], Q4[q % 2]
                        S.op("dve", [T_ITJW, T_iotar], [T_p4], lambda e, p4=p4, tl=tl: e.tensor_tensor(
                            out=p4[:], in0=iota_r[:, :].unsqueeze(1).to_broadcast([128, 4, 128]),
                            in1=ITJW[:, 0, tl:tl + 4].unsqueeze(2).to_broadcast([128, 4, 128]), op=ALU.is_equal))
                        S.op("dve", [T_ITJW, T_iotar], [T_qe], lambda e, qe=qe, tl=tl: e.tensor_tensor(
                            out=qe[:], in0=iota_r[:, :].unsqueeze(1).to_broadcast([128, 4, 128]),
                            in1=ITJW[:, 1, tl:tl + 4].unsqueeze(2).to_broadcast([128, 4, 128]), op=ALU.is_equal))
                        S.op("dve", [T_ITJW, T_qe], [T_q4], lambda e, qe=qe, q4_=q4_, tl=tl: e.tensor_tensor(
                            out=q4_[:], in0=qe[:],
                            in1=ITJW[:, 2, tl:tl + 4].unsqueeze(2).to_broadcast([128, 4, 128]), op=ALU.mult))
                        pg, T_pg = next_pG()
                        for u in range(4):
                            S.op("pe", [T_p4, T_q4], [T_pg], lambda e, pg=pg, u=u, p4=p4, q4_=q4_: e.matmul(
                                pg[:, u * 128:(u + 1) * 128], lhsT=q4_[:, u, :], rhs=p4[:, u, :], start=True, stop=True))
                        S.op("act", [T_pg], [T_Gall], lambda e, pg=pg, tl=tl: e.copy(
                            out=Gall[:, :, tl:tl + 4], in_=pg[:, :].rearrange("p (t i) -> p i t", t=4)))
                for ic in range(32):
                    ub, T_ub = ubuf[ub_i % 2]
                    ub_i += 1
                    for k4 in range(2):
                        S.dma("pool", [], [T_ub], lambda e, ub=ub, ic=ic, k4=k4: e.dma_start(
                            out=ub[:, k4 * 4:(k4 + 1) * 4, :],
                            in_=uT[k4 * 512:(k4 + 1) * 512, ic * 512:(ic + 1) * 512].rearrange("(k p) e -> p k e", p=128)))
                    for ib in range(4):
                        i = ic * 4 + ib
                        vb, T_vb = vbuf[vb_i % 3]
                        vb_i += 1
                        S.dma("pool", [], [T_vb], lambda e, vb=vb, i=i: e.dma_start(out=vb[:], in_=pv[i * 128:(i + 1) * 128, :]))
                        pa, T_pa = pA[i % 2]
                        gb, T_gb = gbuf[i % 2]
                        cb_, T_cb = cbuf[i % 2]
                        for kc in range(8):
                            S.op("pe", [T_ub, T_h2g], [T_pa], lambda e, pa=pa, ub=ub, kc=kc, ib=ib: e.matmul(
                                pa[:, 0:N], lhsT=ub[:, kc, ib * 128:(ib + 1) * 128], rhs=h2g[:, kc, 0:N],
                                start=(kc == 0), stop=(kc == 7)))
                        S.op("act", [T_pa], [T_gb], lambda e, pa=pa, gb=gb: e.activation(out=gb[:, 0:N], in_=pa[:, 0:N], func=AF.Gelu))
                        S.op("dve", [T_gb, T_Gall], [T_cb], lambda e, gb=gb, cb_=cb_, i=i: e.tensor_tensor(
                            out=cb_[:, 0:N], in0=gb[:, 0:N], in1=Gall[:, i, 0:N], op=ALU.mult))
                        for ti, (c0, n) in enumerate(tiles):
                            for hf in range(2):
                                py, T_py = pY[ti][hf]
                                S.op("pe", [T_cb, T_vb], [T_py], lambda e, py=py, cb_=cb_, vb=vb, c0=c0, n=n, hf=hf, i=i: e.matmul(
                                    py[0:n, :], lhsT=cb_[:, c0:c0 + n], rhs=vb[:, hf * 512:(hf + 1) * 512],
                                    start=(i == 0), stop=(i == 127)))
                for ti, (c0, n) in enumerate(tiles):
                    tg = t0 + c0
                    S.dma("sp", [], [T_x2], lambda e, tg=tg, n=n: e.dma_start(out=x2[0:n, :], in_=x1d[tg:tg + n, :]))
                    for hf in range(2):
                        py, T_py = pY[ti][hf]
                        S.op("dve", [T_py, T_x2], [T_x2], lambda e, py=py, hf=hf, n=n: e.tensor_tensor(
                            out=x2[0:n, hf * 512:(hf + 1) * 512], in0=py[0:n, :], in1=x2[0:n, hf * 512:(hf + 1) * 512], op=ALU.add))
                    S.op("act", [T_x2], [T_junk2, T_st2s], lambda e, n=n: e.activation(
                        out=junk2[0:n, :], in_=x2[0:n, :], func=AF.Square, accum_out=st2s[0:n, 0:1]))
                    S.op("act", [T_st2s, T_eps], [T_st2s], lambda e, n=n: e.activation(
                        out=st2s[0:n, 1:2], in_=st2s[0:n, 0:1], func=AF.Sqrt, scale=1.0 / DM, bias=eps_t[0:n, :]))
                    S.op("dve", [T_st2s], [T_st2s], lambda e, n=n: e.reciprocal(out=st2s[0:n, 1:2], in_=st2s[0:n, 1:2]))
                    S.op("dve", [T_x2, T_st2s, T_c2], [T_yt], lambda e, n=n: e.scalar_tensor_tensor(
                        out=yt[0:n, :], in0=x2[0:n, :], scalar=st2s[0:n, 1:2], in1=gfin_s[0:n, :], op0=ALU.mult, op1=ALU.mult))
                    if tg < n_pseq * SEQ:
                        dst = yp[tg // SEQ][tg % SEQ:tg % SEQ + n, :]
                    else:
                        dst = ys[0:n, :]
                    S.dma("sp", [T_yt], [], lambda e, dst=dst, n=n: e.dma_start(out=dst, in_=yt[0:n, :]))

        S.barrier()
        S.finish("sp")
        print("ops per engine:", S.nops, "sems:", S.nsem)
    return nc


def _prep_shared(inp):
    f = lambda a: np.ascontiguousarray(np.asarray(a, dtype=np.float32))
    sh = {}
    sh["w_in"] = f(inp["w_in"][0])
    sh["gmix"] = f(inp["g_mix"][0].reshape(8, 128).T)
    sh["convw"] = f(inp["conv_w"][0].reshape(31, 4, 128).transpose(2, 1, 0))
    sh["cvec"] = f(np.concatenate([inp["conv_b"][0].reshape(4, 128).T, inp["conv_ln_g"][0].reshape(4, 128).T,
                                   inp["conv_ln_b"][0].reshape(4, 128).T], axis=1))
    sh["lam"] = f(np.stack([inp["lambda_q1"][0], inp["lambda_k1"][0], inp["lambda_q2"][0], inp["lambda_k2"][0]]).reshape(1, 256))
    sh["subg"] = f(inp["subln_g"][0].reshape(1, 128))
    sh["relb"] = f(inp["rel_bias"].reshape(1, 128))
    sh["w_out"] = f(inp["w_out"][0])
    sh["gffn"] = f(inp["g_ffn"][0].reshape(8, 128).T)
    sh["wq"] = f(inp["w_query"][0])
    sh["keysT"] = f(inp["sub_keys"][0].reshape(16, 128, 128).transpose(0, 2, 1))
    sh["uT"] = f(inp["peer_u"][0].T)
    sh["pv"] = f(inp["peer_v"][0])
    sh["gfin"] = f(inp["g_final"].reshape(1, DM))
    sh["bkc"] = _bucket_tiles()
    return sh


def kernel(**inp):
    f = lambda a: np.ascontiguousarray(np.asarray(a, dtype=np.float32))
    sh = _prep_shared(inp)
    nc = build_program(2)
    in_maps = []
    for c in range(NCORES):
        m = dict(sh)
        m["xp"] = f(inp["x_prompt"][2 * c:2 * c + 2])
        m["xs"] = f(inp["x_sample"][c])
        m["ckT"] = f(np.asarray(inp["cache_k"][0, c]).reshape(SEQ, 4, 128).transpose(1, 2, 0))
        m["cv"] = f(np.asarray(inp["cache_v"][0, c]).reshape(SEQ, 512))
        m["scT"] = f(np.asarray(inp["state_conv"][0, c]).reshape(30, 4, 128).transpose(2, 1, 0))
        in_maps.append(m)
    res = run_bass_kernel_spmd(nc, in_maps, core_ids=list(range(NCORES)))
    R = res.results
    y_prompt = np.concatenate([r["yp"] for r in R], axis=0)
    y_sample = np.stack([r["ys"] for r in R], axis=0)
    k_prompt = np.concatenate([r["kp"] for r in R], axis=0).reshape(1, 16, SEQ, 4, 2, 64)
    v_prompt = np.concatenate([r["vp"] for r in R], axis=0).reshape(1, 16, SEQ, 4, 128)
    c_prompt = np.concatenate([r["cp"] for r in R], axis=0).reshape(1, 16, 30, 512)
    k_sample = np.stack([r["ks"] for r in R], axis=0).reshape(1, 8, 64, 4, 2, 64)
    v_sample = np.stack([r["vs"] for r in R], axis=0).reshape(1, 8, 64, 4, 128)
    c_sample = np.stack([r["cs"] for r in R], axis=0).reshape(1, 8, 30, 512)
    return (y_prompt, y_sample, k_prompt, v_prompt, c_prompt, k_sample, v_sample, c_sample)
```

```python
import math
import os
from contextlib import ExitStack

import numpy as np
import concourse.bass as bass
import concourse.mybir as mybir
from concourse.bass_utils import run_bass_kernel_spmd

F32 = mybir.dt.float32
BF16 = mybir.dt.bfloat16
U32 = mybir.dt.uint32
AF = mybir.ActivationFunctionType
ALU = mybir.AluOpType
AX = mybir.AxisListType

EPS = 1e-6
LAM_INIT = 0.8 - 0.6 * math.exp(-0.3 * 0)
NCORES = 8
SEQ = 2048
DM = 1024
NEXP_SIDE = 128

SEM_LIMIT = 30000


class Counter:
    def __init__(self, S, name):
        self.S = S
        self.name = name
        self.epoch = 0
        self.val = 0
        self.sem = S.new_sem(f"{name}_e0")

    def bump(self, inc):
        if self.val + inc > SEM_LIMIT:
            self.epoch += 1
            self.val = 0
            self.sem = self.S.new_sem(f"{self.name}_e{self.epoch}")
        self.val += inc
        return (self.sem, self.val, self.name, self.epoch)


class Tile:
    __slots__ = ("name", "w", "r", "dmac")

    def __init__(self, name):
        self.name = name
        self.w = None
        self.r = []
        self.dmac = None


class Sched:
    def __init__(self, nc, stack):
        self.nc = nc
        self.stack = stack
        self.nsem = 0
        self.engs = {"pe": nc.tensor, "act": nc.scalar, "dve": nc.vector,
                     "pool": nc.gpsimd, "sp": nc.sync}
        self.cnt = {k: Counter(self, k) for k in self.engs}
        self.known = {k: {} for k in self.engs}
        self.nops = {k: 0 for k in self.engs}
        self.tiles = []

    def new_sem(self, name):
        self.nsem += 1
        return self.stack.enter_context(self.nc.semaphore(f"s{self.nsem}_{name}"))

    def tile(self, name):
        t = Tile(name)
        self.tiles.append(t)
        return t

    def _wait(self, e, ev):
        sem, val, name, epoch = ev
        key = (name, epoch)
        if self.known[e].get(key, 0) >= val:
            return
        self.known[e][key] = val
        self.engs[e].wait_ge(sem, val)

    def _deps(self, reads, writes):
        evs = []
        for t in reads:
            if t.w is not None:
                evs.append(t.w)
        for t in writes:
            if t.w is not None:
                evs.append(t.w)
            evs.extend(t.r)
        return evs

    def op(self, e, reads, writes, fn):
        for ev in self._deps(reads, writes):
            if ev[2] == e:
                if e == "pe":
                    continue
                if ev[3] == self.cnt[e].epoch and self.cnt[e].val - ev[1] >= 2:
                    continue
            self._wait(e, ev)
        ins = fn(self.engs[e])
        ev = self.cnt[e].bump(1)
        ins.then_inc(ev[0], 1)
        self.nops[e] += 1
        self._mark(ev, reads, writes)
        return ev

    def _mark(self, ev, reads, writes):
        k = (ev[2], ev[3])
        for t in reads:
            t.r = [x for x in t.r if (x[2], x[3]) != k]
            t.r.append(ev)
        for t in writes:
            t.w = ev
            t.r = []

    def dma(self, q, reads, writes, fn, key=None):
        kt = key or (writes[0] if writes else reads[0])
        if kt.dmac is None:
            kt.dmac = Counter(self, "d_" + kt.name)
        for ev in self._deps(reads, writes):
            self._wait(q, ev)
        ins = fn(self.engs[q])
        ev = kt.dmac.bump(16)
        ins.then_inc(ev[0], 16)
        self.nops[q] += 1
        self._mark(ev, reads, writes)
        return ev

    def _all_events(self):
        evs = {}
        for t in self.tiles:
            for ev in ([t.w] if t.w else []) + t.r:
                k = (ev[2], ev[3])
                if k not in evs or evs[k][1] < ev[1]:
                    evs[k] = ev
        return evs

    def barrier(self):
        evs = self._all_events()
        for e in self.engs:
            for ev in evs.values():
                self._wait(e, ev)
        for t in self.tiles:
            t.w = None
            t.r = []

    def finish(self, e="sp"):
        for ev in self._all_events().values():
            self._wait(e, ev)


def _bucket_np(rel):
    nb = 16
    max_exact = 8
    ret = np.where(rel > 0, nb, 0)
    n = np.abs(rel)
    nf = np.maximum(n, 1).astype(np.float32)
    large = max_exact + (np.log(nf / max_exact) / math.log(128 / max_exact) * (nb - max_exact)).astype(np.int32)
    large = np.minimum(large, nb - 1)
    return ret + np.where(n < max_exact, n, large)


def _bucket_tiles():
    k = np.arange(128)[:, None]
    q = np.arange(128)[None, :]
    b0 = _bucket_np(k - q).astype(np.float32)
    masked = (k // 64) > (q // 64)
    b0 = np.where(masked, 32.0, b0)
    b1 = _bucket_np(k - q - 128).astype(np.float32)
    return np.stack([b0, b1], axis=1).astype(np.float32)


def build_program(n_pseq=2, with_peer=True, dbg=False):
    nc = bass.Bass("TRN2", target_bir_lowering=False)
    NT = n_pseq * SEQ + 64

    def din(name, shape, dt=F32):
        return nc.dram_tensor(name, list(shape), dt, kind="ExternalInput").ap()

    def dout(name, shape, dt=F32):
        return nc.dram_tensor(name, list(shape), dt, kind="ExternalOutput").ap()

    xp = din("xp", [n_pseq, SEQ, DM])
    xs = din("xs", [64, DM])
    ckT = din("ckT", [4, 128, SEQ])
    cv = din("cv", [SEQ, 512])
    scT = din("scT", [128, 4, 30])
    w_in = din("w_in", [DM, 2560])
    gmix = din("gmix", [128, 8])
    convw = din("convw", [128, 4, 31])
    cvec = din("cvec", [128, 12])
    lam = din("lam", [1, 256])
    subg = din("subg", [1, 128])
    relb = din("relb", [1, 128])
    w_out = din("w_out", [DM, DM])
    gffn = din("gffn", [128, 8])
    wq = din("wq", [DM, 2048])
    keysT = din("keysT", [16, 128, 128])
    uT = din("uT", [DM, 16384])
    pv = din("pv", [16384, DM])
    gfin = din("gfin", [1, DM])
    bkc = din("bkc", [128, 2, 128])

    yp = dout("yp", [n_pseq, SEQ, DM])
    ys = dout("ys", [64, DM])
    kp = dout("kp", [n_pseq, SEQ, 512])
    vp = dout("vp", [n_pseq, SEQ, 512])
    cp = dout("cp", [n_pseq, 30, 512])
    ks = dout("ks", [64, 512])
    vs = dout("vs", [64, 512])
    cs = dout("cs", [30, 512])

    kind_scr = "ExternalOutput" if dbg else "Internal"
    x1d = nc.dram_tensor("x1d", [NT, DM], F32, kind=kind_scr).ap()
    h2Td = nc.dram_tensor("h2Td", [8, 128, NT], BF16, kind="Internal").ap()

    with ExitStack() as st:
        S = Sched(nc, st)

        cur = [st]

        def sb(name, shape, dt):
            return cur[0].enter_context(nc.sbuf_tensor(name, list(shape), dt)), S.tile(name)

        def ps(name, shape, dt):
            return cur[0].enter_context(nc.psum_tensor(name, list(shape), dt)), S.tile(name)

        ident_f, T_identf = sb("ident_f", [128, 128], F32)
        ident_b, T_identb = sb("ident_b", [128, 128], BF16)
        ones_b, T_ones = sb("ones_b", [128, 128], BF16)
        iota_t, T_iota = sb("iota_t", [128, 128], F32)
        T_const = S.tile("consts")

        S.op("pool", [], [T_iota], lambda e: e.iota(iota_t[:], pattern=[[1, 128]], base=0, channel_multiplier=-1,
                                                    allow_small_or_imprecise_dtypes=True))
        S.op("dve", [T_iota], [T_identf], lambda e: e.tensor_scalar(out=ident_f[:], in0=iota_t[:], scalar1=0.0,
                                                                     scalar2=None, op0=ALU.is_equal))
        S.op("dve", [T_identf], [T_identb], lambda e: e.tensor_copy(out=ident_b[:], in_=ident_f[:]))
        S.op("pool", [], [T_ones], lambda e: e.memset(ones_b[:], 1.0))

        eps_t, T_eps = sb("eps_t", [128, 1], F32)
        S.op("pool", [], [T_eps], lambda e: e.memset(eps_t[:], EPS))
        EPS_AP = eps_t
        iota_r, T_iotar = sb("iota_r", [128, 128], F32)
        S.op("pool", [], [T_iotar], lambda e: e.iota(iota_r[:], pattern=[[1, 128]], base=0, channel_multiplier=0,
                                                     allow_small_or_imprecise_dtypes=True))
        st1 = ExitStack()
        cur[0] = st1
        w_in_b, T_win = sb("w_in_b", [128, 8, 2560], BF16)
        w_out_b, T_wout = sb("w_out_b", [128, 8, DM], BF16)
        diag, T_diag = sb("diag", [128, 124, 128], BF16)
        gmix_s, _ = sb("gmix_s", [128, 8], F32)
        convw_s, _ = sb("convw_s", [128, 4, 31], F32)
        cvec_s, _ = sb("cvec_s", [128, 12], F32)
        lam_s, _ = sb("lam_s", [128, 256], F32)
        gsub_s, _ = sb("gsub_s", [128, 128], F32)
        relb_s, _ = sb("relb_s", [128, 128], F32)
        bk_s, _ = sb("bk_s", [128, 2, 128], F32)
        Tb, T_Tb = sb("Tb", [128, 4, 2, 128], F32)
        eqm, T_eqm = sb("eqm", [128, 2, 128], F32)
        small, T_small = sb("small", [128, 16], F32)
        stage0, T_st0 = sb("stage0", [128, 1024], F32)
        stage1, T_st1 = sb("stage1", [128, 1024], F32)

        for dst, src in ((gmix_s, gmix), (convw_s, convw), (cvec_s, cvec), (bk_s, bkc)):
            S.dma("sp", [], [T_const], lambda e, d=dst, s_=src: e.dma_start(out=d[:], in_=s_))
        for dst, src, n in ((lam_s, lam, 256), (gsub_s, subg, 128), (relb_s, relb, 128)):
            S.dma("sp", [], [T_const], lambda e, d=dst, s_=src, n=n: e.dma_start(out=d[:], in_=s_.to_broadcast([128, n])))

        stg = [(stage0, T_st0), (stage1, T_st1)]
        i = 0
        for kc in range(8):
            for (a0, a1) in ((0, 1024), (1024, 2048), (2048, 2560)):
                stt, T_s = stg[i % 2]
                i += 1
                S.dma("sp", [], [T_s], lambda e, stt=stt, kc=kc, a0=a0, a1=a1: e.dma_start(
                    out=stt[:, 0:a1 - a0], in_=w_in[kc * 128:(kc + 1) * 128, a0:a1]))
                S.op("dve", [T_s, T_const], [T_win], lambda e, stt=stt, kc=kc, a0=a0, a1=a1: e.tensor_scalar(
                    out=w_in_b[:, kc, a0:a1], in0=stt[:, 0:a1 - a0], scalar1=gmix_s[:, kc:kc + 1],
                    scalar2=None, op0=ALU.mult))
        for kc in range(8):
            stt, T_s = stg[i % 2]
            i += 1
            S.dma("sp", [], [T_s], lambda e, stt=stt, kc=kc: e.dma_start(
                out=stt[:, 0:DM], in_=w_out[kc * 128:(kc + 1) * 128, :]))
            S.op("act", [T_s], [T_wout], lambda e, stt=stt, kc=kc: e.copy(out=w_out_b[:, kc, :], in_=stt[:, 0:DM]))
        for w in range(31):
            for cb in range(4):
                S.op("dve", [T_const, T_identf], [T_diag], lambda e, w=w, cb=cb: e.tensor_scalar(
                    out=diag[:, w * 4 + cb, :], in0=ident_f[:], scalar1=convw_s[:, cb, w:w + 1], scalar2=None,
                    op0=ALU.mult))
        S.op("dve", [T_const], [T_eqm], lambda e: e.tensor_scalar(
            out=eqm[:], in0=bk_s[:], scalar1=32.0, scalar2=-30000.0, op0=ALU.is_equal, op1=ALU.mult))
        for h in range(4):
            S.op("dve", [T_eqm], [T_Tb], lambda e, h=h: e.tensor_copy(out=Tb[:, h, :, :], in_=eqm[:]))
        for b in range(32):
            S.op("dve", [T_const], [T_eqm], lambda e, b=b: e.tensor_scalar(
                out=eqm[:], in0=bk_s[:], scalar1=float(b), scalar2=None, op0=ALU.is_equal))
            for h in range(4):
                S.op("dve", [T_eqm, T_const, T_Tb], [T_Tb], lambda e, b=b, h=h: e.scalar_tensor_tensor(
                    out=Tb[:, h, :, :], in0=eqm[:], scalar=relb_s[:, b * 4 + h:b * 4 + h + 1], in1=Tb[:, h, :, :],
                    op0=ALU.mult, op1=ALU.add))
        S.op("dve", [T_const], [T_eqm], lambda e: e.tensor_tensor(
            out=eqm[:, 0, :].rearrange("p (a b) -> p a b", a=2), in0=lam_s[:].rearrange("p (a b c) -> p a b c", a=2, b=2)[:, :, 0, :],
            in1=lam_s[:].rearrange("p (a b c) -> p a b c", a=2, b=2)[:, :, 1, :], op=ALU.mult))
        S.op("dve", [T_eqm], [T_small], lambda e: e.reduce_sum(
            out=small[:, 0:2], in_=eqm[:, 0, :].rearrange("p (a b) -> p a b", a=2), axis=AX.X))
        S.op("act", [T_small], [T_small], lambda e: e.activation(out=small[:, 2:4], in_=small[:, 0:2], func=AF.Exp))
        S.op("dve", [T_small], [T_small], lambda e: e.tensor_tensor(
            out=small[:, 4:5], in0=small[:, 3:4], in1=small[:, 2:3], op=ALU.subtract))
        S.op("dve", [T_small], [T_small], lambda e: e.tensor_scalar(
            out=small[:, 4:5], in0=small[:, 4:5], scalar1=-LAM_INIT, scalar2=None, op0=ALU.add))
        S.op("dve", [T_const], [T_const], lambda e: e.tensor_scalar(
            out=gsub_s[:], in0=gsub_s[:], scalar1=1.0 - LAM_INIT, scalar2=None, op0=ALU.mult))
        neg_lam = small[:, 4:5]

        fT, T_fT = sb("fT", [128, 8, 512], BF16)
        xt, T_xt = sb("xt", [128, DM], F32)
        xr, T_xr = xt, T_xt
        junk, T_junk = sb("junk", [128, DM], BF16)
        hb, T_hb = sb("hb", [128, DM], BF16)
        stat, T_stat = sb("stat", [128, 8], F32)
        aT, T_aT = sb("aT", [128, 4, 30 + 512], BF16)
        sig, T_sig = sb("sig", [128, 512], F32)
        a32, T_a32 = sb("a32", [128, 512], F32)
        qT, T_qT = sb("qT", [128, 4, 512], BF16)
        kT, T_kT = sb("kT", [128, 4, SEQ + 64], BF16)
        vaug, T_v = sb("vaug", [128, 17, 4, 130], BF16)
        catT, T_cat = sb("catT", [128, 8, 512], BF16)
        zq, T_zq = sb("zq", [128, 512], BF16)
        zk32, T_zk32 = sb("zk32", [128, 512], F32)
        zkb, T_zkb = sb("zkb", [128, 512], BF16)
        zv32, T_zv32 = sb("zv32", [128, 512], F32)
        PT, T_PT = sb("PT", [128, 512], BF16)
        PT2, T_PT2 = sb("PT2", [128, 128], BF16)
        tmpb, T_tmpb = sb("tmpb", [128, 128], F32)
        att, T_att = sb("att", [128, 128], F32)
        attb, T_attb = sb("attb", [128, 512], BF16)
        astat, T_astat = sb("astat", [128, 8], F32)
        y32, T_y32 = sb("y32", [128, 4, 512], F32)
        ybf, T_ybf = sb("ybf", [128, 4, 512], BF16)
        ysq, T_ysq = sb("ysq", [128, 4, 512], BF16)
        mu, T_mu = sig, T_sig
        rs, T_rs = a32, T_a32
        ctail, T_ctail = zk32, T_zk32
        cst32, T_cst32 = sb("cst32", [128, 4, 30], F32)

        pT, T_pT = ps("pT", [128, 1024], BF16)
        pM = [ps(f"pM{i}", [128, 512], F32) for i in range(2)]
        pS, T_pS = ps("pS", [128, 512], F32)
        pN, T_pN = ps("pN", [128, 512], F32)
        pO, T_pO = ps("pO", [128, 2, 256], F32)
        pX = [ps(f"pX{i}", [128, 512], F32) for i in range(2)]
        pm_i = [0]

        def next_pM():
            pm_i[0] += 1
            return pM[pm_i[0] % 2]

        S.op("pool", [], [T_v], lambda e: e.memset(vaug[:], 1.0))

        def rms_to_bf16(n, src, T_src, dst_b, T_dst, col):
            S.op("act", [T_src], [T_junk, T_stat], lambda e: e.activation(
                out=junk[0:n, :], in_=src[0:n, :], func=AF.Square, accum_out=stat[0:n, col:col + 1]))
            S.op("act", [T_stat], [T_stat], lambda e: e.activation(
                out=stat[0:n, col + 1:col + 2], in_=stat[0:n, col:col + 1], func=AF.Sqrt, scale=1.0 / DM, bias=EPS_AP[0:n, :]))
            S.op("dve", [T_stat], [T_stat], lambda e: e.reciprocal(
                out=stat[0:n, col + 1:col + 2], in_=stat[0:n, col + 1:col + 2]))
            S.op("dve", [T_src, T_stat], [T_dst], lambda e: e.tensor_scalar(
                out=dst_b[0:n, :], in0=src[0:n, :], scalar1=stat[0:n, col + 1:col + 2], scalar2=None, op0=ALU.mult))

        def transpose_to_fT(n, src_b, T_src, c0):
            for kc in range(8):
                S.op("pe", [T_src, T_identb], [T_pT], lambda e, kc=kc: e.transpose(
                    out=pT[:, kc * 128:kc * 128 + n], in_=src_b[0:n, kc * 128:(kc + 1) * 128], identity=ident_b[0:n, 0:n]))
            S.op("act", [T_pT], [T_fT], lambda e: e.copy(
                out=fT[:, :, c0:c0 + n], in_=pT[:].rearrange("p (k c) -> p k c", k=8)[:, :, 0:n]))

        seqs = []
        for s_ in range(n_pseq):
            seqs.append(("p", xp[s_], SEQ, kp[s_], vp[s_], cp[s_], s_ * SEQ))
        seqs.append(("s", xs, 64, ks, vs, cs, n_pseq * SEQ))

        S.barrier()
        import os
        STOP = int(os.environ.get("KSTOP", "99"))
        if STOP <= 0:
            seqs = []

        for (kind, xd, ntok, kd, vd, cd, tok0) in seqs:
            past = SEQ if kind == "s" else 0
            if kind == "p":
                S.op("pool", [], [T_aT], lambda e: e.memset(aT[:, :, 0:30], 0.0))
            else:
                S.dma("sp", [], [T_cst32], lambda e: e.dma_start(out=cst32[:], in_=scT))
                S.op("dve", [T_cst32], [T_aT], lambda e: e.tensor_copy(out=aT[:, :, 0:30], in_=cst32[:]))
                for h in range(4):
                    for hf in range(2):
                        stt, T_s = stg[(h * 2 + hf) % 2]
                        S.dma("sp", [], [T_s], lambda e, stt=stt, h=h, hf=hf: e.dma_start(
                            out=stt[:, 0:1024], in_=ckT[h, :, hf * 1024:(hf + 1) * 1024]))
                        S.op("act", [T_s], [T_kT], lambda e, stt=stt, h=h, hf=hf: e.copy(
                            out=kT[:, h, hf * 1024:(hf + 1) * 1024], in_=stt[:, 0:1024]))
                for blk in range(16):
                    stt, T_s = stg[blk % 2]
                    S.dma("sp", [], [T_s], lambda e, stt=stt, blk=blk: e.dma_start(
                        out=stt[:, 0:512], in_=cv[blk * 128:(blk + 1) * 128, :]))
                    S.op("dve", [T_s], [T_v], lambda e, stt=stt, blk=blk: e.tensor_copy(
                        out=vaug[:, blk, :, 0:128], in_=stt[:, 0:512].rearrange("p (h e) -> p h e", h=4)))

            ngroups = (ntok + 511) // 512
            for g in range(ngroups):
                g0 = g * 512
                N = min(512, ntok - g0)
                tiles = [(c0, min(128, N - c0)) for c0 in range(0, N, 128)]
                last_group = (g == ngroups - 1)

                for (c0, n) in tiles:
                    S.dma("sp", [], [T_xt], lambda e, c0=c0, n=n: e.dma_start(out=xt[0:n, :], in_=xd[g0 + c0:g0 + c0 + n, :]))
                    rms_to_bf16(n, xt, T_xt, hb, T_hb, 0)
                    transpose_to_fT(n, hb, T_hb, c0)

                if STOP <= 1:
                    continue
                for (c0, n) in tiles:
                    blk = (past + g0 + c0) // 128
                    kcol = past + g0 + c0
                    for j in range(int(os.environ.get('KJ', '3'))):
                        pm, T_pm = next_pM()
                        for kc in range(8):
                            S.op("pe", [T_fT, T_win], [T_pm], lambda e, pm=pm, kc=kc, j=j, c0=c0, n=n: e.matmul(
                                pm[0:n, :], lhsT=fT[:, kc, c0:c0 + n], rhs=w_in_b[:, kc, 1024 + j * 512:1024 + (j + 1) * 512],
                                start=(kc == 0), stop=(kc == 7)))
                        if j == 0:
                            S.op("act", [T_pm], [T_zq], lambda e, pm=pm, n=n: e.activation(
                                out=zq[0:n, :], in_=pm[0:n, :], func=AF.Copy, scale=0.125))
                            for h in range(4):
                                S.op("pe", [T_zq, T_identb], [T_pT], lambda e, h=h, n=n: e.transpose(
                                    out=pT[:, h * 128:h * 128 + n], in_=zq[0:n, h * 128:(h + 1) * 128], identity=ident_b[0:n, 0:n]))
                            S.op("dve", [T_pT], [T_qT], lambda e, c0=c0, n=n: e.tensor_copy(
                                out=qT[:, :, c0:c0 + n], in_=pT[:, 0:512].rearrange("p (k c) -> p k c", k=4)[:, :, 0:n]))
                        elif j == 1:
                            if not os.environ.get("K1A"):
                                S.op("dve", [T_pm], [T_zk32], lambda e, pm=pm, n=n: e.tensor_copy(out=zk32[0:n, :], in_=pm[0:n, :]))
                            S.op("act", [T_zk32], [T_zkb], lambda e, pm=pm, n=n: e.copy(out=zkb[0:n, :], in_=zk32[0:n, :]))
                            if not os.environ.get("NOKD"):
                                S.dma("sp", [T_zk32], [], lambda e, c0=c0, n=n: e.dma_start(
                                    out=kd[g0 + c0:g0 + c0 + n, :], in_=zk32[0:n, :]))
                            for h in range(0 if os.environ.get("K1B") else 4):
                                S.op("pe", [T_zkb, T_identb], [T_pT], lambda e, h=h, n=n: e.transpose(
                                    out=pT[:, 512 + h * 128:512 + h * 128 + n], in_=zkb[0:n, h * 128:(h + 1) * 128],
                                    identity=ident_b[0:n, 0:n]))
                            if not os.environ.get("K1C"):
                              S.op("dve", [T_pT], [T_kT], lambda e, kcol=kcol, n=n: e.tensor_copy(
                                out=kT[:, :, kcol:kcol + n], in_=pT[:, 512:1024].rearrange("p (k c) -> p k c", k=4)[:, :, 0:n]))
                        else:
                            S.op("dve", [T_pm], [T_zv32], lambda e, pm=pm, n=n: e.tensor_copy(out=zv32[0:n, :], in_=pm[0:n, :]))
                            S.op("act", [T_zv32], [T_v], lambda e, pm=pm, n=n, blk=blk: e.copy(
                                out=vaug[0:n, blk, :, 0:128], in_=zv32[0:n, :].rearrange("p (h e) -> p h e", h=4)))
                            S.dma("sp", [T_zv32], [], lambda e, c0=c0, n=n: e.dma_start(
                                out=vd[g0 + c0:g0 + c0 + n, :], in_=zv32[0:n, :]))

                if STOP <= 2:
                    continue
                for cb in range(4):
                    pa, T_pa = next_pM()
                    pg, T_pg = next_pM()
                    for kc in range(8):
                        S.op("pe", [T_fT, T_win], [T_pa], lambda e, pa=pa, kc=kc, cb=cb: e.matmul(
                            pa[:, 0:N], lhsT=w_in_b[:, kc, cb * 128:(cb + 1) * 128], rhs=fT[:, kc, 0:N],
                            start=(kc == 0), stop=(kc == 7)))
                    for kc in range(8):
                        S.op("pe", [T_fT, T_win], [T_pg], lambda e, pg=pg, kc=kc, cb=cb: e.matmul(
                            pg[:, 0:N], lhsT=w_in_b[:, kc, 512 + cb * 128:512 + (cb + 1) * 128], rhs=fT[:, kc, 0:N],
                            start=(kc == 0), stop=(kc == 7)))
                    S.op("act", [T_pg], [T_sig], lambda e, pg=pg: e.activation(out=sig[:, 0:N], in_=pg[:, 0:N], func=AF.Sigmoid))
                    S.op("dve", [T_pa, T_sig], [T_a32], lambda e, pa=pa: e.tensor_tensor(
                        out=a32[:, 0:N], in0=pa[:, 0:N], in1=sig[:, 0:N], op=ALU.mult))
                    S.op("pool", [T_a32], [T_aT], lambda e, cb=cb: e.tensor_copy(out=aT[:, cb, 30:30 + N], in_=a32[:, 0:N]))
                    if last_group:
                        pm, T_pm = pX[0]
                        S.op("pe", [T_a32, T_identf], [T_pm], lambda e, pm=pm, cb=cb: e.transpose(
                            out=pm[0:30, cb * 128:(cb + 1) * 128], in_=a32[:, N - 30:N], identity=ident_f[:]))
                if last_group:
                    pm, T_pm = pX[0]
                    S.op("act", [T_pm], [T_ctail], lambda e, pm=pm: e.copy(out=ctail[0:30, :], in_=pm[0:30, :]))
                    S.dma("sp", [T_ctail], [], lambda e: e.dma_start(out=cd, in_=ctail[0:30, :]))

                if STOP <= 3:
                    continue
                for (c0, n) in tiles:
                    qi = (g0 + c0) // 128
                    if kind == "p":
                        far = list(range(0, max(qi - 1, 0)))
                        near = ([(qi - 1, 128, 1)] if qi >= 1 else []) + [(qi, 128, 0)]
                    else:
                        far = list(range(0, 15))
                        near = [(15, 128, 1), (16, 64, 0)]
                    nblk = len(far) + len(near)
                    for h in range(4):
                        for m in range(2):
                            done = 0
                            mrow = slice(m * 64, (m + 1) * 64)
                            for f0 in range(0, len(far), 4):
                                chunk = far[f0:f0 + 4]
                                for j, blk in enumerate(chunk):
                                    S.op("pe", [T_kT, T_qT], [T_pS], lambda e, j=j, blk=blk, h=h, mrow=mrow, c0=c0, n=n: e.matmul(
                                        pS[:, j * n:(j + 1) * n], lhsT=kT[mrow, h, blk * 128:(blk + 1) * 128],
                                        rhs=qT[mrow, h, c0:c0 + n], start=True, stop=True))
                                cn = len(chunk) * n
                                S.op("act", [T_pS, T_const], [T_PT], lambda e, cn=cn, h=h: e.activation(
                                    out=PT[:, 0:cn], in_=pS[:, 0:cn], func=AF.Exp, bias=relb_s[:, 60 + h:61 + h]))
                                for j, blk in enumerate(chunk):
                                    S.op("pe", [T_PT, T_v], [T_pO], lambda e, j=j, blk=blk, h=h, m=m, n=n, done=done: e.matmul(
                                        pO[0:n, m, 0:129], lhsT=PT[:, j * n:(j + 1) * n], rhs=vaug[:, blk, h, 0:129],
                                        start=(done == 0), stop=(done == nblk - 1)))
                                    done += 1
                            for (blk, nk, bkind) in near:
                                S.op("pe", [T_kT, T_qT], [T_pN], lambda e, blk=blk, nk=nk, h=h, mrow=mrow, c0=c0, n=n: e.matmul(
                                    pN[0:nk, 0:n], lhsT=kT[mrow, h, blk * 128:blk * 128 + nk],
                                    rhs=qT[mrow, h, c0:c0 + n], start=True, stop=True))
                                S.op("dve", [T_pN, T_Tb], [T_tmpb], lambda e, nk=nk, n=n, h=h, bkind=bkind: e.tensor_tensor(
                                    out=tmpb[0:nk, 0:n], in0=pN[0:nk, 0:n], in1=Tb[0:nk, h, bkind, 0:n], op=ALU.add))
                                S.op("act", [T_tmpb], [T_PT2], lambda e, nk=nk, n=n: e.activation(
                                    out=PT2[0:nk, 0:n], in_=tmpb[0:nk, 0:n], func=AF.Exp))
                                S.op("pe", [T_PT2, T_v], [T_pO], lambda e, blk=blk, nk=nk, h=h, m=m, n=n, done=done: e.matmul(
                                    pO[0:n, m, 0:129], lhsT=PT2[0:nk, 0:n], rhs=vaug[0:nk, blk, h, 0:129],
                                    start=(done == 0), stop=(done == nblk - 1)))
                                done += 1
                        S.op("dve", [T_pO], [T_astat], lambda e, n=n: e.reciprocal(
                            out=astat[0:n, 0:2], in_=pO[0:n, :, 128:129].rearrange("p a b -> p (a b)")))
                        S.op("dve", [T_astat, T_small], [T_astat], lambda e, n=n: e.tensor_tensor(
                            out=astat[0:n, 2:3], in0=astat[0:n, 1:2], in1=neg_lam[0:n, :], op=ALU.mult))
                        S.op("dve", [T_pO, T_astat], [T_att], lambda e, n=n: e.tensor_scalar(
                            out=att[0:n, :], in0=pO[0:n, 0, 0:128], scalar1=astat[0:n, 0:1], scalar2=None, op0=ALU.mult))
                        S.op("dve", [T_pO, T_astat, T_att], [T_att], lambda e, n=n: e.scalar_tensor_tensor(
                            out=att[0:n, :], in0=pO[0:n, 1, 0:128], scalar=astat[0:n, 2:3], in1=att[0:n, :],
                            op0=ALU.mult, op1=ALU.add))
                        S.op("act", [T_att], [T_junk, T_astat], lambda e, n=n: e.activation(
                            out=junk[0:n, 0:128], in_=att[0:n, :], func=AF.Square, accum_out=astat[0:n, 3:4]))
                        S.op("act", [T_astat, T_eps], [T_astat], lambda e, n=n: e.activation(
                            out=astat[0:n, 4:5], in_=astat[0:n, 3:4], func=AF.Sqrt, scale=1.0 / 128, bias=eps_t[0:n, :]))
                        S.op("dve", [T_astat], [T_astat], lambda e, n=n: e.reciprocal(out=astat[0:n, 4:5], in_=astat[0:n, 4:5]))
                        S.op("dve", [T_att, T_astat, T_const], [T_attb], lambda e, n=n, h=h: e.scalar_tensor_tensor(
                            out=attb[0:n, h * 128:(h + 1) * 128], in0=att[0:n, :], scalar=astat[0:n, 4:5], in1=gsub_s[0:n, :],
                            op0=ALU.mult, op1=ALU.mult))
                    for h in range(4):
                        S.op("pe", [T_attb, T_identb], [T_pT], lambda e, h=h, n=n: e.transpose(
                            out=pT[:, h * 128:h * 128 + n], in_=attb[0:n, h * 128:(h + 1) * 128], identity=ident_b[0:n, 0:n]))
                    S.op("act", [T_pT], [T_cat], lambda e, c0=c0, n=n: e.copy(
                        out=catT[:, 4:8, c0:c0 + n], in_=pT[:, 0:512].rearrange("p (k c) -> p k c", k=4)[:, :, 0:n]))

                if STOP <= 4:
                    continue
                for cb in range(4):
                    pm, T_pm = next_pM()
                    for w in range(31):
                        S.op("pe", [T_aT, T_diag], [T_pm], lambda e, pm=pm, w=w, cb=cb: e.matmul(
                            pm[:, 0:N], lhsT=diag[:, w * 4 + cb, :], rhs=aT[:, cb, w:w + N], start=(w == 0), stop=(w == 30)))
                    S.op("act", [T_pm, T_const], [T_y32], lambda e, pm=pm, cb=cb: e.activation(
                        out=y32[:, cb, 0:N], in_=pm[:, 0:N], func=AF.Identity, bias=cvec_s[:, cb:cb + 1]))
                    S.op("act", [T_pm, T_const], [T_ysq], lambda e, pm=pm, cb=cb: e.activation(
                        out=ysq[:, cb, 0:N], in_=pm[:, 0:N], func=AF.Square, bias=cvec_s[:, cb:cb + 1]))
                    S.op("pool", [T_y32], [T_ybf], lambda e, cb=cb: e.tensor_copy(out=ybf[:, cb, 0:N], in_=y32[:, cb, 0:N]))
                p1, T_p1 = pX[0]
                p2, T_p2 = pX[1]
                for cb in range(4):
                    S.op("pe", [T_ybf, T_ones], [T_p1], lambda e, cb=cb: e.matmul(
                        p1[:, 0:N], lhsT=ones_b[:], rhs=ybf[:, cb, 0:N], start=(cb == 0), stop=(cb == 3)))
                for cb in range(4):
                    S.op("pe", [T_ysq, T_ones], [T_p2], lambda e, cb=cb: e.matmul(
                        p2[:, 0:N], lhsT=ones_b[:], rhs=ysq[:, cb, 0:N], start=(cb == 0), stop=(cb == 3)))
                S.op("dve", [T_p1], [T_mu], lambda e: e.tensor_scalar(
                    out=mu[:, 0:N], in0=p1[:, 0:N], scalar1=1.0 / 512, scalar2=None, op0=ALU.mult))
                S.op("dve", [T_mu], [T_rs], lambda e: e.tensor_tensor(out=rs[:, 0:N], in0=mu[:, 0:N], in1=mu[:, 0:N], op=ALU.mult))
                S.op("dve", [T_p2, T_rs], [T_rs], lambda e: e.scalar_tensor_tensor(
                    out=rs[:, 0:N], in0=p2[:, 0:N], scalar=1.0 / 512, in1=rs[:, 0:N], op0=ALU.mult, op1=ALU.subtract))
                S.op("act", [T_rs, T_eps], [T_rs], lambda e: e.activation(
                    out=rs[:, 0:N], in_=rs[:, 0:N], func=AF.Sqrt, bias=eps_t[:, :]))
                S.op("dve", [T_rs], [T_rs], lambda e: e.reciprocal(out=rs[:, 0:N], in_=rs[:, 0:N]))
                for cb in range(4):
                    S.op("dve", [T_y32, T_mu], [T_y32], lambda e, cb=cb: e.tensor_tensor(
                        out=y32[:, cb, 0:N], in0=y32[:, cb, 0:N], in1=mu[:, 0:N], op=ALU.subtract))
                    S.op("pool", [T_y32, T_rs], [T_y32], lambda e, cb=cb: e.tensor_tensor(
                        out=y32[:, cb, 0:N], in0=y32[:, cb, 0:N], in1=rs[:, 0:N], op=ALU.mult))
                    S.op("act", [T_y32, T_const], [T_cat], lambda e, cb=cb: e.activation(
                        out=catT[:, cb, 0:N], in_=y32[:, cb, 0:N], func=AF.Silu,
                        scale=cvec_s[:, 4 + cb:5 + cb], bias=cvec_s[:, 8 + cb:9 + cb]))
                if not last_group:
                    S.op("pool", [T_aT], [T_aT], lambda e: e.tensor_copy(out=aT[:, :, 0:30], in_=aT[:, :, N:N + 30]))

                if STOP <= 5:
                    continue
                for (c0, n) in tiles:
                    S.dma("sp", [], [T_xr], lambda e, c0=c0, n=n: e.dma_start(out=xr[0:n, :], in_=xd[g0 + c0:g0 + c0 + n, :]))
                    for hf in range(2):
                        po, T_po = pX[hf]
                        for kc in range(8):
                            S.op("pe", [T_cat, T_wout], [T_po], lambda e, po=po, kc=kc, hf=hf, c0=c0, n=n: e.matmul(
                                po[0:n, :], lhsT=catT[:, kc, c0:c0 + n], rhs=w_out_b[:, kc, hf * 512:(hf + 1) * 512],
                                start=(kc == 0), stop=(kc == 7)))
                        S.op("dve", [T_po, T_xr], [T_xr], lambda e, po=po, hf=hf, n=n: e.tensor_tensor(
                            out=xr[0:n, hf * 512:(hf + 1) * 512], in0=po[0:n, :], in1=xr[0:n, hf * 512:(hf + 1) * 512], op=ALU.add))
                    S.dma("sp", [T_xr], [], lambda e, c0=c0, n=n: e.dma_start(
                        out=x1d[tok0 + g0 + c0:tok0 + g0 + c0 + n, :], in_=xr[0:n, :]))
                    rms_to_bf16(n, xr, T_xr, hb, T_hb, 2)
                    transpose_to_fT(n, hb, T_hb, c0)
                for kc in range(8):
                    S.dma("sp", [T_fT], [], lambda e, kc=kc: e.dma_start(
                        out=h2Td[kc, :, tok0 + g0:tok0 + g0 + N], in_=fT[:, kc, 0:N]))

        S.barrier()
        st1.close()
        st2 = ExitStack()
        st.enter_context(st2)
        cur[0] = st2
        TG = 256
        if with_peer:
            wq_b, T_wq = sb("wq_b", [128, 8, 2048], BF16)
            keys_b, T_keys = sb("keys_b", [128, 16, 128], BF16)
            gfin_s, T_gfin = sb("gfin_s", [128, DM], F32)
            gffn_s, T_gffn = sb("gffn_s", [128, 8], F32)
            sg0, T_sg0 = sb("sg0", [128, 1024], F32)
            sg1, T_sg1 = sb("sg1", [128, 1024], F32)
            h2g, T_h2g = sb("h2g", [128, 8, TG], BF16)
            qryT, T_qry = sb("qryT", [128, 16, TG], BF16)
            s_sb, T_ssb = sb("s_sb", [128, 16, 128], F32)
            wk, T_wk = sb("wk", [128, 256], F32)
            A_, T_A = sb("A_", [128, 16, 16], F32)
            Iu, T_Iu = sb("Iu", [128, 16, 16], U32)
            If, T_If = sb("If", [128, 16, 16], F32)
            cand, T_cand = sb("cand", [128, 8, 256], F32)
            C_, T_C = sb("C_", [128, 8, 16], F32)
            pos, T_pos = sb("pos", [128, 8, 16], U32)
            ku, T_ku = sb("ku", [128, 2, 128], U32)
            kf, T_kf = sb("kf", [128, 2, 128], F32)
            E_, T_E = sb("E_", [128, 8, 16], F32)
            gst, T_gst = sb("gst", [128, 32], F32)
            oh, T_oh = sb("oh", [128, 8, 16, 16], F32)
            ijw, T_ijw = sb("ijw", [128, 3, 128], F32)
            ITJW, T_ITJW = sb("ITJW", [128, 3, TG], F32)
            P4 = [sb(f"P4_{i}", [128, 4, 128], BF16) for i in range(2)]
            Qe = [sb(f"Qe_{i}", [128, 4, 128], BF16) for i in range(2)]
            Q4 = [sb(f"Q4_{i}", [128, 4, 128], BF16) for i in range(2)]
            Gall, T_Gall = sb("Gall", [128, 128, TG], BF16)
            ubuf = [sb(f"ubuf{i}", [128, 8, 512], BF16) for i in range(2)]
            vbuf = [sb(f"vbuf{i}", [128, 4, DM], BF16) for i in range(2)]
            uscr_t = nc.dram_tensor("uscr", [32, 128, 8 * 512], BF16, kind="Internal").ap()
            vscr_t = nc.dram_tensor("vscr", [32, 128, 4 * DM], BF16, kind="Internal").ap()
            uscr = [uscr_t[ic].rearrange("p (k e) -> p k e", k=8) for ic in range(32)]
            vscr = [vscr_t[ic].rearrange("p (b d) -> p b d", b=4) for ic in range(32)]
            T_uscr = [S.tile(f"uscr{ic}") for ic in range(32)]
            T_vscr = [S.tile(f"vscr{ic}") for ic in range(32)]
            gbuf = [sb(f"gbuf{i}", [128, TG], F32) for i in range(2)]
            cbuf = [sb(f"cbuf{i}", [128, TG], BF16) for i in range(2)]
            x2, T_x2 = sb("x2", [128, DM], F32)
            yt, T_yt = x2, T_x2
            junk2, T_junk2 = oh[:].rearrange("p r a b -> p (r a b)"), T_oh
            st2s, T_st2s = sb("st2s", [128, 4], F32)

            pY = [[ps(f"pY{t}{h}", [128, 512], F32) for h in range(2)] for t in range(2)]
            pA = [ps(f"pA{i}", [128, 512], F32) for i in range(2)]
            pG = [ps(f"pG{i}", [128, 512], F32) for i in range(2)]
            pg_i = [0]

            def next_pG():
                pg_i[0] += 1
                return pG[pg_i[0] % 2]

            T_c2 = S.tile("consts2")
            S.dma("sp", [], [T_c2], lambda e: e.dma_start(out=gffn_s[:], in_=gffn))
            S.dma("sp", [], [T_c2], lambda e: e.dma_start(out=gfin_s[:], in_=gfin.to_broadcast([128, DM])))
            S.dma("pool", [], [T_keys], lambda e: e.dma_start(out=keys_b[:], in_=keysT.rearrange("r d n -> d r n")))
            sgs = [(sg0, T_sg0), (sg1, T_sg1)]
            ii = 0
            for kc in range(8):
                for hf in range(2):
                    stt, T_s = sgs[ii % 2]
                    ii += 1
                    S.dma("sp", [], [T_s], lambda e, stt=stt, kc=kc, hf=hf: e.dma_start(
                        out=stt[:], in_=wq[kc * 128:(kc + 1) * 128, hf * 1024:(hf + 1) * 1024]))
                    S.op("dve", [T_s, T_c2], [T_wq], lambda e, stt=stt, kc=kc, hf=hf: e.tensor_scalar(
                        out=wq_b[:, kc, hf * 1024:(hf + 1) * 1024], in0=stt[:], scalar1=gffn_s[:, kc:kc + 1],
                        scalar2=None, op0=ALU.mult))

            groups = [(t0, min(TG, NT - t0)) for t0 in range(0, NT, TG)]
            MAXG = int(os.environ.get("KGROUPS", "999"))
            vb_i = 0
            ub_i = 0
            for gi, (t0, N) in enumerate(groups[:MAXG]):
                tiles = [(c0, min(128, N - c0)) for c0 in range(0, N, 128)]
                S.dma("sp", [], [T_h2g], lambda e: e.dma_start(
                    out=h2g[:, :, 0:N], in_=h2Td[:, :, t0:t0 + N].rearrange("k p t -> p k t")))
                for blk in range(16):
                    pg, T_pg = next_pG()
                    for kc in range(8):
                        S.op("pe", [T_wq, T_h2g], [T_pg], lambda e, pg=pg, kc=kc, blk=blk: e.matmul(
                            pg[:, 0:N], lhsT=wq_b[:, kc, blk * 128:(blk + 1) * 128], rhs=h2g[:, kc, 0:N],
                            start=(kc == 0), stop=(kc == 7)))
                    S.op("act", [T_pg], [T_qry], lambda e, pg=pg, blk=blk: e.copy(out=qryT[:, blk, 0:N], in_=pg[:, 0:N]))
                for ti, (c0, n) in enumerate(tiles):
                    for q4 in range(4):
                        pg, T_pg = next_pG()
                        for j in range(4):
                            rp = q4 * 4 + j
                            S.op("pe", [T_qry, T_keys], [T_pg], lambda e, pg=pg, j=j, rp=rp: e.matmul(
                                pg[0:n, j * 128:(j + 1) * 128], lhsT=qryT[:, rp, c0:c0 + n], rhs=keys_b[:, rp, :],
                                start=True, stop=True))
                        S.op("act", [T_pg], [T_ssb], lambda e, pg=pg, q4=q4: e.copy(
                            out=s_sb[0:n, q4 * 4:(q4 + 1) * 4, :], in_=pg[0:n, :].rearrange("p (a b) -> p a b", a=4)))
                    for rp in range(16):
                        S.op("dve", [T_ssb], [T_A], lambda e, rp=rp: e.max(out=A_[0:n, rp, 0:8], in_=s_sb[0:n, rp, :]))
                        S.op("dve", [T_ssb, T_A], [T_Iu], lambda e, rp=rp: e.max_index(
                            out=Iu[0:n, rp, 0:8], in_max=A_[0:n, rp, 0:8], in_values=s_sb[0:n, rp, :]))
                        S.op("dve", [T_ssb, T_A], [T_wk], lambda e, rp=rp: e.match_replace(
                            out=wk[0:n, 0:128], in_to_replace=A_[0:n, rp, 0:8], in_values=s_sb[0:n, rp, :], imm_value=-1e30))
                        S.op("dve", [T_wk], [T_A], lambda e, rp=rp: e.max(out=A_[0:n, rp, 8:16], in_=wk[0:n, 0:128]))
                        S.op("dve", [T_wk, T_A], [T_Iu], lambda e, rp=rp: e.max_index(
                            out=Iu[0:n, rp, 8:16], in_max=A_[0:n, rp, 8:16], in_values=wk[0:n, 0:128]))
                    S.op("dve", [T_Iu], [T_If], lambda e: e.tensor_copy(out=If[0:n], in_=Iu[0:n]))
                    A4 = A_[0:n].rearrange("p (r a) k -> p r a k", a=2)
                    I4 = If[0:n].rearrange("p (r a) k -> p r a k", a=2)
                    S.op("dve", [T_A], [T_cand], lambda e: e.tensor_tensor(
                        out=cand[0:n].rearrange("p r (a b) -> p r a b", a=16),
                        in0=A4[:, :, 0, :].unsqueeze(3).to_broadcast([n, 8, 16, 16]),
                        in1=A4[:, :, 1, :].unsqueeze(2).to_broadcast([n, 8, 16, 16]), op=ALU.add))
                    for r in range(8):
                        S.op("dve", [T_cand], [T_C], lambda e, r=r: e.max(out=C_[0:n, r, 0:8], in_=cand[0:n, r, :]))
                        S.op("dve", [T_cand, T_C], [T_pos], lambda e, r=r: e.max_index(
                            out=pos[0:n, r, 0:8], in_max=C_[0:n, r, 0:8], in_values=cand[0:n, r, :]))
                        S.op("dve", [T_cand, T_C], [T_wk], lambda e, r=r: e.match_replace(
                            out=wk[0:n, :], in_to_replace=C_[0:n, r, 0:8], in_values=cand[0:n, r, :], imm_value=-1e30))
                        S.op("dve", [T_wk], [T_C], lambda e, r=r: e.max(out=C_[0:n, r, 8:16], in_=wk[0:n, :]))
                        S.op("dve", [T_wk, T_C], [T_pos], lambda e, r=r: e.max_index(
                            out=pos[0:n, r, 8:16], in_max=C_[0:n, r, 8:16], in_values=wk[0:n, :]))
                    S.op("dve", [T_C], [T_gst], lambda e: e.tensor_scalar(
                        out=gst[0:n, 0:8], in0=C_[0:n, :, 0], scalar1=-1.0, scalar2=None, op0=ALU.mult))
                    for r in range(8):
                        S.op("act", [T_C, T_gst], [T_E, T_gst], lambda e, r=r: e.activation(
                            out=E_[0:n, r, :], in_=C_[0:n, r, :], func=AF.Exp, bias=gst[0:n, r:r + 1],
                            accum_out=gst[0:n, 8 + r:9 + r]))
                    S.op("dve", [T_gst], [T_gst], lambda e: e.reciprocal(out=gst[0:n, 16:24], in_=gst[0:n, 8:16]))
                    S.op("dve", [T_E, T_gst], [T_ijw], lambda e: e.tensor_tensor(
                        out=ijw[0:n, 2, :].rearrange("p (r k) -> p r k", r=8), in0=E_[0:n],
                        in1=gst[0:n, 16:24].unsqueeze(2).to_broadcast([n, 8, 16]), op=ALU.mult))
                    S.op("dve", [T_pos], [T_ku], lambda e: e.tensor_single_scalar(
                        out=ku[0:n, 0, :], in_=pos[0:n].rearrange("p r k -> p (r k)"), scalar=4, op=ALU.logical_shift_right))
                    S.op("dve", [T_pos], [T_ku], lambda e: e.tensor_single_scalar(
                        out=ku[0:n, 1, :], in_=pos[0:n].rearrange("p r k -> p (r k)"), scalar=15, op=ALU.bitwise_and))
                    S.op("dve", [T_ku], [T_kf], lambda e: e.tensor_copy(out=kf[0:n], in_=ku[0:n]))
                    for a in range(2):
                        S.op("dve", [T_kf, T_iotar], [T_oh], lambda e, a=a: e.tensor_tensor(
                            out=oh[0:n],
                            in0=kf[0:n, a, :].rearrange("p (r k) -> p r k", r=8).unsqueeze(3).to_broadcast([n, 8, 16, 16]),
                            in1=iota_r[0:n, 0:16].unsqueeze(1).unsqueeze(1).to_broadcast([n, 8, 16, 16]), op=ALU.is_equal))
                        S.op("dve", [T_oh, T_If], [T_oh], lambda e, a=a: e.tensor_tensor(
                            out=oh[0:n], in0=oh[0:n],
                            in1=I4[:, :, a, :].unsqueeze(2).to_broadcast([n, 8, 16, 16]), op=ALU.mult))
                        S.op("dve", [T_oh], [T_ijw], lambda e, a=a: e.reduce_sum(
                            out=ijw[0:n, a, :].rearrange("p (r k) -> p r k", r=8), in_=oh[0:n], axis=AX.X))
                    pg, T_pg = next_pG()
                    for a in range(3):
                        S.op("pe", [T_ijw, T_identf], [T_pg], lambda e, pg=pg, a=a: e.transpose(
                            out=pg[:, a * 128:a * 128 + n], in_=ijw[0:n, a, :], identity=ident_f[0:n, 0:n]))
                    S.op("act", [T_pg], [T_ITJW], lambda e, pg=pg: e.copy(
                        out=ITJW[:, :, c0:c0 + n], in_=pg[:, 0:384].rearrange("p (a t) -> p a t", a=3)[:, :, 0:n]))
                    for q in range(n // 4):
                        tl = c0 + q * 4
                        (p4, T_p4), (qe, T_qe), (q4_, T_q4) = P4[q % 2], Qe[q % 2], Q4[q % 2]
                        S.op("dve", [T_ITJW, T_iotar], [T_p4], lambda e, p4=p4, tl=tl: e.tensor_tensor(
                            out=p4[:], in0=iota_r[:, :].unsqueeze(1).to_broadcast([128, 4, 128]),
                            in1=ITJW[:, 0, tl:tl + 4].unsqueeze(2).to_broadcast([128, 4, 128]), op=ALU.is_equal))
                        S.op("dve", [T_ITJW, T_iotar], [T_qe], lambda e, qe=qe, tl=tl: e.tensor_tensor(
                            out=qe[:], in0=iota_r[:, :].unsqueeze(1).to_broadcast([128, 4, 128]),
                            in1=ITJW[:, 1, tl:tl + 4].unsqueeze(2).to_broadcast([128, 4, 128]), op=ALU.is_equal))
                        S.op("dve", [T_ITJW, T_qe], [T_q4], lambda e, qe=qe, q4_=q4_, tl=tl: e.tensor_tensor(
                            out=q4_[:], in0=qe[:],
                            in1=ITJW[:, 2, tl:tl + 4].unsqueeze(2).to_broadcast([128, 4, 128]), op=ALU.mult))
                        pg, T_pg = next_pG()
                        for u in range(4):
                            S.op("pe", [T_p4, T_q4], [T_pg], lambda e, pg=pg, u=u, p4=p4, q4_=q4_: e.matmul(
                                pg[:, u * 128:(u + 1) * 128], lhsT=q4_[:, u, :], rhs=p4[:, u, :], start=True, stop=True))
                        S.op("act", [T_pg], [T_Gall], lambda e, pg=pg, tl=tl: e.copy(
                            out=Gall[:, :, tl:tl + 4], in_=pg[:, :].rearrange("p (t i) -> p i t", t=4)))
                def load_chunk(ic):
                    ub, T_ub = ubuf[ic % 2]
                    vb, T_vb = vbuf[ic % 2]
                    if gi == 0:
                        for k4 in range(2):
                            S.dma("pool", [], [T_ub], lambda e, k4=k4: e.dma_start(
                                out=ub[:, k4 * 4:(k4 + 1) * 4, :],
                                in_=uT[k4 * 512:(k4 + 1) * 512, ic * 512:(ic + 1) * 512].rearrange("(k p) e -> p k e", p=128)))
                        S.dma("pool", [], [T_vb], lambda e: e.dma_start(
                            out=vb[:], in_=pv[ic * 512:(ic + 1) * 512, :].rearrange("(b p) d -> p b d", p=128)))
                        if len(groups) > 1:
                            S.dma("sp", [T_ub], [T_uscr[ic]], lambda e: e.dma_start(out=uscr[ic], in_=ub[:]), key=T_uscr[ic])
                            S.dma("sp", [T_vb], [T_vscr[ic]], lambda e: e.dma_start(out=vscr[ic], in_=vb[:]), key=T_vscr[ic])
                    else:
                        S.dma("sp", [T_uscr[ic]], [T_ub], lambda e: e.dma_start(out=ub[:], in_=uscr[ic]))
                        S.dma("sp", [T_vscr[ic]], [T_vb], lambda e: e.dma_start(out=vb[:], in_=vscr[ic]))

                def stage_u(i):
                    ic, ib = divmod(i, 4)
                    ub, T_ub = ubuf[ic % 2]
                    pa, T_pa = pA[i % 2]
                    gb, T_gb = gbuf[i % 2]
                    cb_, T_cb = cbuf[i % 2]
                    for kc in range(8):
                        S.op("pe", [T_ub, T_h2g], [T_pa], lambda e, kc=kc: e.matmul(
                            pa[:, 0:N], lhsT=ub[:, kc, ib * 128:(ib + 1) * 128], rhs=h2g[:, kc, 0:N],
                            start=(kc == 0), stop=(kc == 7)))
                    S.op("act", [T_pa], [T_gb], lambda e: e.activation(out=gb[:, 0:N], in_=pa[:, 0:N], func=AF.Gelu))
                    S.op("dve", [T_gb, T_Gall], [T_cb], lambda e: e.tensor_tensor(
                        out=cb_[:, 0:N], in0=gb[:, 0:N], in1=Gall[:, i, 0:N], op=ALU.mult))

                def stage_v(i):
                    ic, ib = divmod(i, 4)
                    vb, T_vb = vbuf[ic % 2]
                    cb_, T_cb = cbuf[i % 2]
                    for ti, (c0, n) in enumerate(tiles):
                        for hf in range(2):
                            py, T_py = pY[ti][hf]
                            S.op("pe", [T_cb, T_vb], [T_py], lambda e, py=py, c0=c0, n=n, hf=hf: e.matmul(
                                py[0:n, :], lhsT=cb_[:, c0:c0 + n], rhs=vb[:, ib, hf * 512:(hf + 1) * 512],
                                start=(i == 0), stop=(i == 127)))

                load_chunk(0)
                for i in range(128):
                    stage_u(i)
                    if i >= 1:
                        stage_v(i - 1)
                    if i % 4 == 0 and i // 4 + 1 < 32:
                        load_chunk(i // 4 + 1)
                stage_v(127)
                for ti, (c0, n) in enumerate(tiles):
                    tg = t0 + c0
                    S.dma("sp", [], [T_x2], lambda e, tg=tg, n=n: e.dma_start(out=x2[0:n, :], in_=x1d[tg:tg + n, :]))
                    for hf in range(2):
                        py, T_py = pY[ti][hf]
                        S.op("dve", [T_py, T_x2], [T_x2], lambda e, py=py, hf=hf, n=n: e.tensor_tensor(
                            out=x2[0:n, hf * 512:(hf + 1) * 512], in0=py[0:n, :], in1=x2[0:n, hf * 512:(hf + 1) * 512], op=ALU.add))
                    S.op("act", [T_x2], [T_junk2, T_st2s], lambda e, n=n: e.activation(
                        out=junk2[0:n, 0:DM], in_=x2[0:n, :], func=AF.Square, accum_out=st2s[0:n, 0:1]))
                    S.op("act", [T_st2s, T_eps], [T_st2s], lambda e, n=n: e.activation(
                        out=st2s[0:n, 1:2], in_=st2s[0:n, 0:1], func=AF.Sqrt, scale=1.0 / DM, bias=eps_t[0:n, :]))
                    S.op("dve", [T_st2s], [T_st2s], lambda e, n=n: e.reciprocal(out=st2s[0:n, 1:2], in_=st2s[0:n, 1:2]))
                    S.op("dve", [T_x2, T_st2s, T_c2], [T_yt], lambda e, n=n: e.scalar_tensor_tensor(
                        out=yt[0:n, :], in0=x2[0:n, :], scalar=st2s[0:n, 1:2], in1=gfin_s[0:n, :], op0=ALU.mult, op1=ALU.mult))
                    if tg < n_pseq * SEQ:
                        dst = yp[tg // SEQ][tg % SEQ:tg % SEQ + n, :]
                    else:
                        dst = ys[0:n, :]
                    S.dma("sp", [T_yt], [], lambda e, dst=dst, n=n: e.dma_start(out=dst, in_=yt[0:n, :]))

        S.barrier()
        S.finish("sp")
        print("ops per engine:", S.nops, "sems:", S.nsem)
    return nc


def _prep_shared(inp):
    f = lambda a: np.ascontiguousarray(np.asarray(a, dtype=np.float32))
    sh = {}
    sh["w_in"] = f(inp["w_in"][0])
    sh["gmix"] = f(inp["g_mix"][0].reshape(8, 128).T)
    sh["convw"] = f(inp["conv_w"][0].reshape(31, 4, 128).transpose(2, 1, 0))
    sh["cvec"] = f(np.concatenate([inp["conv_b"][0].reshape(4, 128).T, inp["conv_ln_g"][0].reshape(4, 128).T,
                                   inp["conv_ln_b"][0].reshape(4, 128).T], axis=1))
    sh["lam"] = f(np.stack([inp["lambda_q1"][0], inp["lambda_k1"][0], inp["lambda_q2"][0], inp["lambda_k2"][0]]).reshape(1, 256))
    sh["subg"] = f(inp["subln_g"][0].reshape(1, 128))
    sh["relb"] = f(inp["rel_bias"].reshape(1, 128))
    sh["w_out"] = f(inp["w_out"][0])
    sh["gffn"] = f(inp["g_ffn"][0].reshape(8, 128).T)
    sh["wq"] = f(inp["w_query"][0])
    sh["keysT"] = f(inp["sub_keys"][0].reshape(16, 128, 128).transpose(0, 2, 1))
    sh["uT"] = f(inp["peer_u"][0].T)
    sh["pv"] = f(inp["peer_v"][0])
    sh["gfin"] = f(inp["g_final"].reshape(1, DM))
    sh["bkc"] = _bucket_tiles()
    return sh


def kernel(**inp):
    f = lambda a: np.ascontiguousarray(np.asarray(a, dtype=np.float32))
    sh = _prep_shared(inp)
    nc = build_program(2)
    in_maps = []
    for c in range(NCORES):
        m = dict(sh)
        m["xp"] = f(inp["x_prompt"][2 * c:2 * c + 2])
        m["xs"] = f(inp["x_sample"][c])
        m["ckT"] = f(np.asarray(inp["cache_k"][0, c]).reshape(SEQ, 4, 128).transpose(1, 2, 0))
        m["cv"] = f(np.asarray(inp["cache_v"][0, c]).reshape(SEQ, 512))
        m["scT"] = f(np.asarray(inp["state_conv"][0, c]).reshape(30, 4, 128).transpose(2, 1, 0))
        in_maps.append(m)
    res = run_bass_kernel_spmd(nc, in_maps, core_ids=list(range(NCORES)))
    R = res.results
    y_prompt = np.concatenate([r["yp"] for r in R], axis=0)
    y_sample = np.stack([r["ys"] for r in R], axis=0)
    k_prompt = np.concatenate([r["kp"] for r in R], axis=0).reshape(1, 16, SEQ, 4, 2, 64)
    v_prompt = np.concatenate([r["vp"] for r in R], axis=0).reshape(1, 16, SEQ, 4, 128)
    c_prompt = np.concatenate([r["cp"] for r in R], axis=0).reshape(1, 16, 30, 512)
    k_sample = np.stack([r["ks"] for r in R], axis=0).reshape(1, 8, 64, 4, 2, 64)
    v_sample = np.stack([r["vs"] for r in R], axis=0).reshape(1, 8, 64, 4, 128)
    c_sample = np.stack([r["cs"] for r in R], axis=0).reshape(1, 8, 30, 512)
    return (y_prompt, y_sample, k_prompt, v_prompt, c_prompt, k_sample, v_sample, c_sample)
```

```python
import math
import os
from contextlib import ExitStack

import numpy as np
import concourse.bass as bass
import concourse.mybir as mybir
from concourse.bass_utils import run_bass_kernel_spmd

F32 = mybir.dt.float32
BF16 = mybir.dt.bfloat16
U32 = mybir.dt.uint32
AF = mybir.ActivationFunctionType
ALU = mybir.AluOpType
AX = mybir.AxisListType

EPS = 1e-6
LAM_INIT = 0.8 - 0.6 * math.exp(-0.3 * 0)
NCORES = 8
SEQ = 2048
DM = 1024
NEXP_SIDE = 128

SEM_LIMIT = 30000


class Counter:
    def __init__(self, S, name):
        self.S = S
        self.name = name
        self.epoch = 0
        self.val = 0
        self.sem = S.new_sem(f"{name}_e0")

    def bump(self, inc):
        if self.val + inc > SEM_LIMIT:
            self.epoch += 1
            self.val = 0
            self.sem = self.S.new_sem(f"{self.name}_e{self.epoch}")
        self.val += inc
        return (self.sem, self.val, self.name, self.epoch)


class Tile:
    __slots__ = ("name", "w", "r", "dmac")

    def __init__(self, name):
        self.name = name
        self.w = None
        self.r = []
        self.dmac = None


class Sched:
    def __init__(self, nc, stack):
        self.nc = nc
        self.stack = stack
        self.nsem = 0
        self.engs = {"pe": nc.tensor, "act": nc.scalar, "dve": nc.vector,
                     "pool": nc.gpsimd, "sp": nc.sync}
        self.cnt = {k: Counter(self, k) for k in self.engs}
        self.known = {k: {} for k in self.engs}
        self.nops = {k: 0 for k in self.engs}
        self.tiles = []

    def new_sem(self, name):
        self.nsem += 1
        return self.stack.enter_context(self.nc.semaphore(f"s{self.nsem}_{name}"))

    def tile(self, name):
        t = Tile(name)
        self.tiles.append(t)
        return t

    def _wait(self, e, ev):
        sem, val, name, epoch = ev
        key = (name, epoch)
        if self.known[e].get(key, 0) >= val:
            return
        self.known[e][key] = val
        self.engs[e].wait_ge(sem, val)

    def _deps(self, reads, writes):
        evs = []
        for t in reads:
            if t.w is not None:
                evs.append(t.w)
        for t in writes:
            if t.w is not None:
                evs.append(t.w)
            evs.extend(t.r)
        return evs

    def op(self, e, reads, writes, fn):
        for ev in self._deps(reads, writes):
            if ev[2] == e:
                if e == "pe":
                    continue
                if ev[3] == self.cnt[e].epoch and self.cnt[e].val - ev[1] >= 2:
                    continue
            self._wait(e, ev)
        ins = fn(self.engs[e])
        ev = self.cnt[e].bump(1)
        ins.then_inc(ev[0], 1)
        self.nops[e] += 1
        self._mark(ev, reads, writes)
        return ev

    def _mark(self, ev, reads, writes):
        k = (ev[2], ev[3])
        for t in reads:
            t.r = [x for x in t.r if (x[2], x[3]) != k]
            t.r.append(ev)
        for t in writes:
            t.w = ev
            t.r = []

    def dma(self, q, reads, writes, fn, key=None):
        kt = key or (writes[0] if writes else reads[0])
        if kt.dmac is None:
            kt.dmac = Counter(self, "d_" + kt.name)
        for ev in self._deps(reads, writes):
            self._wait(q, ev)
        ins = fn(self.engs[q])
        ev = kt.dmac.bump(16)
        ins.then_inc(ev[0], 16)
        self.nops[q] += 1
        self._mark(ev, reads, writes)
        return ev

    def _all_events(self):
        evs = {}
        for t in self.tiles:
            for ev in ([t.w] if t.w else []) + t.r:
                k = (ev[2], ev[3])
                if k not in evs or evs[k][1] < ev[1]:
                    evs[k] = ev
        return evs

    def barrier(self):
        evs = self._all_events()
        for e in self.engs:
            for ev in evs.values():
                self._wait(e, ev)
        for t in self.tiles:
            t.w = None
            t.r = []

    def finish(self, e="sp"):
        for ev in self._all_events().values():
            self._wait(e, ev)


def _bucket_np(rel):
    nb = 16
    max_exact = 8
    ret = np.where(rel > 0, nb, 0)
    n = np.abs(rel)
    nf = np.maximum(n, 1).astype(np.float32)
    large = max_exact + (np.log(nf / max_exact) / math.log(128 / max_exact) * (nb - max_exact)).astype(np.int32)
    large = np.minimum(large, nb - 1)
    return ret + np.where(n < max_exact, n, large)


def _bucket_tiles():
    k = np.arange(128)[:, None]
    q = np.arange(128)[None, :]
    b0 = _bucket_np(k - q).astype(np.float32)
    masked = (k // 64) > (q // 64)
    b0 = np.where(masked, 32.0, b0)
    b1 = _bucket_np(k - q - 128).astype(np.float32)
    return np.stack([b0, b1], axis=1).astype(np.float32)


def build_program(n_pseq=2, with_peer=True, dbg=False):
    nc = bass.Bass("TRN2", target_bir_lowering=False)
    NT = n_pseq * SEQ + 64

    def din(name, shape, dt=F32):
        return nc.dram_tensor(name, list(shape), dt, kind="ExternalInput").ap()

    def dout(name, shape, dt=F32):
        return nc.dram_tensor(name, list(shape), dt, kind="ExternalOutput").ap()

    xp = din("xp", [n_pseq, SEQ, DM])
    xs = din("xs", [64, DM])
    ckT = din("ckT", [4, 128, SEQ])
    cv = din("cv", [SEQ, 512])
    scT = din("scT", [128, 4, 30])
    w_in = din("w_in", [DM, 2560])
    gmix = din("gmix", [128, 8])
    convw = din("convw", [128, 4, 31])
    cvec = din("cvec", [128, 12])
    lam = din("lam", [1, 256])
    subg = din("subg", [1, 128])
    relb = din("relb", [1, 128])
    w_out = din("w_out", [DM, DM])
    gffn = din("gffn", [128, 8])
    wq = din("wq", [DM, 2048])
    keysT = din("keysT", [16, 128, 128])
    uT = din("uT", [DM, 16384])
    pv = din("pv", [16384, DM])
    gfin = din("gfin", [1, DM])
    bkc = din("bkc", [128, 2, 128])

    yp = dout("yp", [n_pseq, SEQ, DM])
    ys = dout("ys", [64, DM])
    kp = dout("kp", [n_pseq, SEQ, 512])
    vp = dout("vp", [n_pseq, SEQ, 512])
    cp = dout("cp", [n_pseq, 30, 512])
    ks = dout("ks", [64, 512])
    vs = dout("vs", [64, 512])
    cs = dout("cs", [30, 512])

    kind_scr = "ExternalOutput" if dbg else "Internal"
    x1d = nc.dram_tensor("x1d", [NT, DM], F32, kind=kind_scr).ap()
    h2Td = nc.dram_tensor("h2Td", [8, 128, NT], BF16, kind="Internal").ap()

    with ExitStack() as st:
        S = Sched(nc, st)

        cur = [st]

        def sb(name, shape, dt):
            return cur[0].enter_context(nc.sbuf_tensor(name, list(shape), dt)), S.tile(name)

        def ps(name, shape, dt):
            return cur[0].enter_context(nc.psum_tensor(name, list(shape), dt)), S.tile(name)

        ident_f, T_identf = sb("ident_f", [128, 128], F32)
        ident_b, T_identb = sb("ident_b", [128, 128], BF16)
        ones_b, T_ones = sb("ones_b", [128, 128], BF16)
        iota_t, T_iota = sb("iota_t", [128, 128], F32)
        T_const = S.tile("consts")

        S.op("pool", [], [T_iota], lambda e: e.iota(iota_t[:], pattern=[[1, 128]], base=0, channel_multiplier=-1,
                                                    allow_small_or_imprecise_dtypes=True))
        S.op("dve", [T_iota], [T_identf], lambda e: e.tensor_scalar(out=ident_f[:], in0=iota_t[:], scalar1=0.0,
                                                                     scalar2=None, op0=ALU.is_equal))
        S.op("dve", [T_identf], [T_identb], lambda e: e.tensor_copy(out=ident_b[:], in_=ident_f[:]))
        S.op("pool", [], [T_ones], lambda e: e.memset(ones_b[:], 1.0))

        eps_t, T_eps = sb("eps_t", [128, 1], F32)
        S.op("pool", [], [T_eps], lambda e: e.memset(eps_t[:], EPS))
        EPS_AP = eps_t
        iota_r, T_iotar = sb("iota_r", [128, 128], F32)
        S.op("pool", [], [T_iotar], lambda e: e.iota(iota_r[:], pattern=[[1, 128]], base=0, channel_multiplier=0,
                                                     allow_small_or_imprecise_dtypes=True))
        st1 = ExitStack()
        cur[0] = st1
        w_in_b, T_win = sb("w_in_b", [128, 8, 2560], BF16)
        w_out_b, T_wout = sb("w_out_b", [128, 8, DM], BF16)
        diag, T_diag = sb("diag", [128, 124, 128], BF16)
        gmix_s, _ = sb("gmix_s", [128, 8], F32)
        convw_s, _ = sb("convw_s", [128, 4, 31], F32)
        cvec_s, _ = sb("cvec_s", [128, 12], F32)
        lam_s, _ = sb("lam_s", [128, 256], F32)
        gsub_s, _ = sb("gsub_s", [128, 128], F32)
        relb_s, _ = sb("relb_s", [128, 128], F32)
        bk_s, _ = sb("bk_s", [128, 2, 128], F32)
        Tb, T_Tb = sb("Tb", [128, 4, 2, 128], F32)
        eqm, T_eqm = sb("eqm", [128, 2, 128], F32)
        small, T_small = sb("small", [128, 16], F32)
        stage0, T_st0 = sb("stage0", [128, 1024], F32)
        stage1, T_st1 = sb("stage1", [128, 1024], F32)

        for dst, src in ((gmix_s, gmix), (convw_s, convw), (cvec_s, cvec), (bk_s, bkc)):
            S.dma("sp", [], [T_const], lambda e, d=dst, s_=src: e.dma_start(out=d[:], in_=s_))
        for dst, src, n in ((lam_s, lam, 256), (gsub_s, subg, 128), (relb_s, relb, 128)):
            S.dma("sp", [], [T_const], lambda e, d=dst, s_=src, n=n: e.dma_start(out=d[:], in_=s_.to_broadcast([128, n])))

        stg = [(stage0, T_st0), (stage1, T_st1)]
        i = 0
        for kc in range(8):
            for (a0, a1) in ((0, 1024), (1024, 2048), (2048, 2560)):
                stt, T_s = stg[i % 2]
                i += 1
                S.dma("sp", [], [T_s], lambda e, stt=stt, kc=kc, a0=a0, a1=a1: e.dma_start(
                    out=stt[:, 0:a1 - a0], in_=w_in[kc * 128:(kc + 1) * 128, a0:a1]))
                S.op("dve", [T_s, T_const], [T_win], lambda e, stt=stt, kc=kc, a0=a0, a1=a1: e.tensor_scalar(
                    out=w_in_b[:, kc, a0:a1], in0=stt[:, 0:a1 - a0], scalar1=gmix_s[:, kc:kc + 1],
                    scalar2=None, op0=ALU.mult))
        for kc in range(8):
            stt, T_s = stg[i % 2]
            i += 1
            S.dma("sp", [], [T_s], lambda e, stt=stt, kc=kc: e.dma_start(
                out=stt[:, 0:DM], in_=w_out[kc * 128:(kc + 1) * 128, :]))
            S.op("act", [T_s], [T_wout], lambda e, stt=stt, kc=kc: e.copy(out=w_out_b[:, kc, :], in_=stt[:, 0:DM]))
        for w in range(31):
            for cb in range(4):
                S.op("dve", [T_const, T_identf], [T_diag], lambda e, w=w, cb=cb: e.tensor_scalar(
                    out=diag[:, w * 4 + cb, :], in0=ident_f[:], scalar1=convw_s[:, cb, w:w + 1], scalar2=None,
                    op0=ALU.mult))
        S.op("dve", [T_const], [T_eqm], lambda e: e.tensor_scalar(
            out=eqm[:], in0=bk_s[:], scalar1=32.0, scalar2=-30000.0, op0=ALU.is_equal, op1=ALU.mult))
        for h in range(4):
            S.op("dve", [T_eqm], [T_Tb], lambda e, h=h: e.tensor_copy(out=Tb[:, h, :, :], in_=eqm[:]))
        for b in range(32):
            S.op("dve", [T_const], [T_eqm], lambda e, b=b: e.tensor_scalar(
                out=eqm[:], in0=bk_s[:], scalar1=float(b), scalar2=None, op0=ALU.is_equal))
            for h in range(4):
                S.op("dve", [T_eqm, T_const, T_Tb], [T_Tb], lambda e, b=b, h=h: e.scalar_tensor_tensor(
                    out=Tb[:, h, :, :], in0=eqm[:], scalar=relb_s[:, b * 4 + h:b * 4 + h + 1], in1=Tb[:, h, :, :],
                    op0=ALU.mult, op1=ALU.add))
        S.op("dve", [T_const], [T_eqm], lambda e: e.tensor_tensor(
            out=eqm[:, 0, :].rearrange("p (a b) -> p a b", a=2), in0=lam_s[:].rearrange("p (a b c) -> p a b c", a=2, b=2)[:, :, 0, :],
            in1=lam_s[:].rearrange("p (a b c) -> p a b c", a=2, b=2)[:, :, 1, :], op=ALU.mult))
        S.op("dve", [T_eqm], [T_small], lambda e: e.reduce_sum(
            out=small[:, 0:2], in_=eqm[:, 0, :].rearrange("p (a b) -> p a b", a=2), axis=AX.X))
        S.op("act", [T_small], [T_small], lambda e: e.activation(out=small[:, 2:4], in_=small[:, 0:2], func=AF.Exp))
        S.op("dve", [T_small], [T_small], lambda e: e.tensor_tensor(
            out=small[:, 4:5], in0=small[:, 3:4], in1=small[:, 2:3], op=ALU.subtract))
        S.op("dve", [T_small], [T_small], lambda e: e.tensor_scalar(
            out=small[:, 4:5], in0=small[:, 4:5], scalar1=-LAM_INIT, scalar2=None, op0=ALU.add))
        S.op("dve", [T_const], [T_const], lambda e: e.tensor_scalar(
            out=gsub_s[:], in0=gsub_s[:], scalar1=1.0 - LAM_INIT, scalar2=None, op0=ALU.mult))
        neg_lam = small[:, 4:5]

        fT, T_fT = sb("fT", [128, 8, 512], BF16)
        xt, T_xt = sb("xt", [128, DM], F32)
        xr, T_xr = xt, T_xt
        junk, T_junk = sb("junk", [128, DM], BF16)
        hb, T_hb = sb("hb", [128, DM], BF16)
        stat, T_stat = sb("stat", [128, 8], F32)
        aT, T_aT = sb("aT", [128, 4, 30 + 512], BF16)
        sig, T_sig = sb("sig", [128, 512], F32)
        a32, T_a32 = sb("a32", [128, 512], F32)
        qT, T_qT = sb("qT", [128, 4, 512], BF16)
        kT, T_kT = sb("kT", [128, 4, SEQ + 64], BF16)
        vaug, T_v = sb("vaug", [128, 17, 4, 130], BF16)
        catT, T_cat = sb("catT", [128, 8, 512], BF16)
        zq, T_zq = sb("zq", [128, 512], BF16)
        zk32, T_zk32 = sb("zk32", [128, 512], F32)
        zkb, T_zkb = sb("zkb", [128, 512], BF16)
        zv32, T_zv32 = sb("zv32", [128, 512], F32)
        PT, T_PT = sb("PT", [128, 512], BF16)
        PT2, T_PT2 = sb("PT2", [128, 128], BF16)
        tmpb, T_tmpb = sb("tmpb", [128, 128], F32)
        att, T_att = sb("att", [128, 128], F32)
        attb, T_attb = sb("attb", [128, 512], BF16)
        astat, T_astat = sb("astat", [128, 8], F32)
        y32, T_y32 = sb("y32", [128, 4, 512], F32)
        ybf, T_ybf = sb("ybf", [128, 4, 512], BF16)
        ysq, T_ysq = sb("ysq", [128, 4, 512], BF16)
        mu, T_mu = sig, T_sig
        rs, T_rs = a32, T_a32
        ctail, T_ctail = zk32, T_zk32
        cst32, T_cst32 = sb("cst32", [128, 4, 30], F32)

        pT, T_pT = ps("pT", [128, 1024], BF16)
        pM = [ps(f"pM{i}", [128, 512], F32) for i in range(2)]
        pS, T_pS = ps("pS", [128, 512], F32)
        pN, T_pN = ps("pN", [128, 512], F32)
        pO, T_pO = ps("pO", [128, 2, 256], F32)
        pX = [ps(f"pX{i}", [128, 512], F32) for i in range(2)]
        pm_i = [0]

        def next_pM():
            pm_i[0] += 1
            return pM[pm_i[0] % 2]

        S.op("pool", [], [T_v], lambda e: e.memset(vaug[:], 1.0))

        def rms_to_bf16(n, src, T_src, dst_b, T_dst, col):
            S.op("act", [T_src], [T_junk, T_stat], lambda e: e.activation(
                out=junk[0:n, :], in_=src[0:n, :], func=AF.Square, accum_out=stat[0:n, col:col + 1]))
            S.op("act", [T_stat], [T_stat], lambda e: e.activation(
                out=stat[0:n, col + 1:col + 2], in_=stat[0:n, col:col + 1], func=AF.Sqrt, scale=1.0 / DM, bias=EPS_AP[0:n, :]))
            S.op("dve", [T_stat], [T_stat], lambda e: e.reciprocal(
                out=stat[0:n, col + 1:col + 2], in_=stat[0:n, col + 1:col + 2]))
            S.op("dve", [T_src, T_stat], [T_dst], lambda e: e.tensor_scalar(
                out=dst_b[0:n, :], in0=src[0:n, :], scalar1=stat[0:n, col + 1:col + 2], scalar2=None, op0=ALU.mult))

        def transpose_to_fT(n, src_b, T_src, c0):
            for kc in range(8):
                S.op("pe", [T_src, T_identb], [T_pT], lambda e, kc=kc: e.transpose(
                    out=pT[:, kc * 128:kc * 128 + n], in_=src_b[0:n, kc * 128:(kc + 1) * 128], identity=ident_b[0:n, 0:n]))
            S.op("act", [T_pT], [T_fT], lambda e: e.copy(
                out=fT[:, :, c0:c0 + n], in_=pT[:].rearrange("p (k c) -> p k c", k=8)[:, :, 0:n]))

        seqs = []
        for s_ in range(n_pseq):
            seqs.append(("p", xp[s_], SEQ, kp[s_], vp[s_], cp[s_], s_ * SEQ))
        seqs.append(("s", xs, 64, ks, vs, cs, n_pseq * SEQ))

        S.barrier()
        import os
        STOP = int(os.environ.get("KSTOP", "99"))
        if STOP <= 0:
            seqs = []

        for (kind, xd, ntok, kd, vd, cd, tok0) in seqs:
            past = SEQ if kind == "s" else 0
            if kind == "p":
                S.op("pool", [], [T_aT], lambda e: e.memset(aT[:, :, 0:30], 0.0))
            else:
                S.dma("sp", [], [T_cst32], lambda e: e.dma_start(out=cst32[:], in_=scT))
                S.op("dve", [T_cst32], [T_aT], lambda e: e.tensor_copy(out=aT[:, :, 0:30], in_=cst32[:]))
                for h in range(4):
                    for hf in range(2):
                        stt, T_s = stg[(h * 2 + hf) % 2]
                        S.dma("sp", [], [T_s], lambda e, stt=stt, h=h, hf=hf: e.dma_start(
                            out=stt[:, 0:1024], in_=ckT[h, :, hf * 1024:(hf + 1) * 1024]))
                        S.op("act", [T_s], [T_kT], lambda e, stt=stt, h=h, hf=hf: e.copy(
                            out=kT[:, h, hf * 1024:(hf + 1) * 1024], in_=stt[:, 0:1024]))
                for blk in range(16):
                    stt, T_s = stg[blk % 2]
                    S.dma("sp", [], [T_s], lambda e, stt=stt, blk=blk: e.dma_start(
                        out=stt[:, 0:512], in_=cv[blk * 128:(blk + 1) * 128, :]))
                    S.op("dve", [T_s], [T_v], lambda e, stt=stt, blk=blk: e.tensor_copy(
                        out=vaug[:, blk, :, 0:128], in_=stt[:, 0:512].rearrange("p (h e) -> p h e", h=4)))

            ngroups = (ntok + 511) // 512
            for g in range(ngroups):
                g0 = g * 512
                N = min(512, ntok - g0)
                tiles = [(c0, min(128, N - c0)) for c0 in range(0, N, 128)]
                last_group = (g == ngroups - 1)

                for (c0, n) in tiles:
                    S.dma("sp", [], [T_xt], lambda e, c0=c0, n=n: e.dma_start(out=xt[0:n, :], in_=xd[g0 + c0:g0 + c0 + n, :]))
                    rms_to_bf16(n, xt, T_xt, hb, T_hb, 0)
                    transpose_to_fT(n, hb, T_hb, c0)

                if STOP <= 1:
                    continue
                for (c0, n) in tiles:
                    blk = (past + g0 + c0) // 128
                    kcol = past + g0 + c0
                    for j in range(int(os.environ.get('KJ', '3'))):
                        pm, T_pm = next_pM()
                        for kc in range(8):
                            S.op("pe", [T_fT, T_win], [T_pm], lambda e, pm=pm, kc=kc, j=j, c0=c0, n=n: e.matmul(
                                pm[0:n, :], lhsT=fT[:, kc, c0:c0 + n], rhs=w_in_b[:, kc, 1024 + j * 512:1024 + (j + 1) * 512],
                                start=(kc == 0), stop=(kc == 7)))
                        if j == 0:
                            S.op("act", [T_pm], [T_zq], lambda e, pm=pm, n=n: e.activation(
                                out=zq[0:n, :], in_=pm[0:n, :], func=AF.Copy, scale=0.125))
                            for h in range(4):
                                S.op("pe", [T_zq, T_identb], [T_pT], lambda e, h=h, n=n: e.transpose(
                                    out=pT[:, h * 128:h * 128 + n], in_=zq[0:n, h * 128:(h + 1) * 128], identity=ident_b[0:n, 0:n]))
                            S.op("dve", [T_pT], [T_qT], lambda e, c0=c0, n=n: e.tensor_copy(
                                out=qT[:, :, c0:c0 + n], in_=pT[:, 0:512].rearrange("p (k c) -> p k c", k=4)[:, :, 0:n]))
                        elif j == 1:
                            if not os.environ.get("K1A"):
                                S.op("dve", [T_pm], [T_zk32], lambda e, pm=pm, n=n: e.tensor_copy(out=zk32[0:n, :], in_=pm[0:n, :]))
                            S.op("act", [T_zk32], [T_zkb], lambda e, pm=pm, n=n: e.copy(out=zkb[0:n, :], in_=zk32[0:n, :]))
                            if not os.environ.get("NOKD"):
                                S.dma("sp", [T_zk32], [], lambda e, c0=c0, n=n: e.dma_start(
                                    out=kd[g0 + c0:g0 + c0 + n, :], in_=zk32[0:n, :]))
                            for h in range(0 if os.environ.get("K1B") else 4):
                                S.op("pe", [T_zkb, T_identb], [T_pT], lambda e, h=h, n=n: e.transpose(
                                    out=pT[:, 512 + h * 128:512 + h * 128 + n], in_=zkb[0:n, h * 128:(h + 1) * 128],
                                    identity=ident_b[0:n, 0:n]))
                            if not os.environ.get("K1C"):
                              S.op("dve", [T_pT], [T_kT], lambda e, kcol=kcol, n=n: e.tensor_copy(
                                out=kT[:, :, kcol:kcol + n], in_=pT[:, 512:1024].rearrange("p (k c) -> p k c", k=4)[:, :, 0:n]))
                        else:
                            S.op("dve", [T_pm], [T_zv32], lambda e, pm=pm, n=n: e.tensor_copy(out=zv32[0:n, :], in_=pm[0:n, :]))
                            S.op("act", [T_zv32], [T_v], lambda e, pm=pm, n=n, blk=blk: e.copy(
                                out=vaug[0:n, blk, :, 0:128], in_=zv32[0:n, :].rearrange("p (h e) -> p h e", h=4)))
                            S.dma("sp", [T_zv32], [], lambda e, c0=c0, n=n: e.dma_start(
                                out=vd[g0 + c0:g0 + c0 + n, :], in_=zv32[0:n, :]))

                if STOP <= 2:
                    continue
                for cb in range(4):
                    pa, T_pa = next_pM()
                    pg, T_pg = next_pM()
                    for kc in range(8):
                        S.op("pe", [T_fT, T_win], [T_pa], lambda e, pa=pa, kc=kc, cb=cb: e.matmul(
                            pa[:, 0:N], lhsT=w_in_b[:, kc, cb * 128:(cb + 1) * 128], rhs=fT[:, kc, 0:N],
                            start=(kc == 0), stop=(kc == 7)))
                    for kc in range(8):
                        S.op("pe", [T_fT, T_win], [T_pg], lambda e, pg=pg, kc=kc, cb=cb: e.matmul(
                            pg[:, 0:N], lhsT=w_in_b[:, kc, 512 + cb * 128:512 + (cb + 1) * 128], rhs=fT[:, kc, 0:N],
                            start=(kc == 0), stop=(kc == 7)))
                    S.op("act", [T_pg], [T_sig], lambda e, pg=pg: e.activation(out=sig[:, 0:N], in_=pg[:, 0:N], func=AF.Sigmoid))
                    S.op("dve", [T_pa, T_sig], [T_a32], lambda e, pa=pa: e.tensor_tensor(
                        out=a32[:, 0:N], in0=pa[:, 0:N], in1=sig[:, 0:N], op=ALU.mult))
                    S.op("pool", [T_a32], [T_aT], lambda e, cb=cb: e.tensor_copy(out=aT[:, cb, 30:30 + N], in_=a32[:, 0:N]))
                    if last_group:
                        pm, T_pm = pX[0]
                        S.op("pe", [T_a32, T_identf], [T_pm], lambda e, pm=pm, cb=cb: e.transpose(
                            out=pm[0:30, cb * 128:(cb + 1) * 128], in_=a32[:, N - 30:N], identity=ident_f[:]))
                if last_group:
                    pm, T_pm = pX[0]
                    S.op("act", [T_pm], [T_ctail], lambda e, pm=pm: e.copy(out=ctail[0:30, :], in_=pm[0:30, :]))
                    S.dma("sp", [T_ctail], [], lambda e: e.dma_start(out=cd, in_=ctail[0:30, :]))

                if STOP <= 3:
                    continue
                for (c0, n) in tiles:
                    qi = (g0 + c0) // 128
                    if kind == "p":
                        far = list(range(0, max(qi - 1, 0)))
                        near = ([(qi - 1, 128, 1)] if qi >= 1 else []) + [(qi, 128, 0)]
                    else:
                        far = list(range(0, 15))
                        near = [(15, 128, 1), (16, 64, 0)]
                    nblk = len(far) + len(near)
                    for h in range(4):
                        for m in range(2):
                            done = 0
                            mrow = slice(m * 64, (m + 1) * 64)
                            for f0 in range(0, len(far), 4):
                                chunk = far[f0:f0 + 4]
                                for j, blk in enumerate(chunk):
                                    S.op("pe", [T_kT, T_qT], [T_pS], lambda e, j=j, blk=blk, h=h, mrow=mrow, c0=c0, n=n: e.matmul(
                                        pS[:, j * n:(j + 1) * n], lhsT=kT[mrow, h, blk * 128:(blk + 1) * 128],
                                        rhs=qT[mrow, h, c0:c0 + n], start=True, stop=True))
                                cn = len(chunk) * n
                                S.op("act", [T_pS, T_const], [T_PT], lambda e, cn=cn, h=h: e.activation(
                                    out=PT[:, 0:cn], in_=pS[:, 0:cn], func=AF.Exp, bias=relb_s[:, 60 + h:61 + h]))
                                for j, blk in enumerate(chunk):
                                    S.op("pe", [T_PT, T_v], [T_pO], lambda e, j=j, blk=blk, h=h, m=m, n=n, done=done: e.matmul(
                                        pO[0:n, m, 0:129], lhsT=PT[:, j * n:(j + 1) * n], rhs=vaug[:, blk, h, 0:129],
                                        start=(done == 0), stop=(done == nblk - 1)))
                                    done += 1
                            for (blk, nk, bkind) in near:
                                S.op("pe", [T_kT, T_qT], [T_pN], lambda e, blk=blk, nk=nk, h=h, mrow=mrow, c0=c0, n=n: e.matmul(
                                    pN[0:nk, 0:n], lhsT=kT[mrow, h, blk * 128:blk * 128 + nk],
                                    rhs=qT[mrow, h, c0:c0 + n], start=True, stop=True))
                                S.op("dve", [T_pN, T_Tb], [T_tmpb], lambda e, nk=nk, n=n, h=h, bkind=bkind: e.tensor_tensor(
                                    out=tmpb[0:nk, 0:n], in0=pN[0:nk, 0:n], in1=Tb[0:nk, h, bkind, 0:n], op=ALU.add))
                                S.op("act", [T_tmpb], [T_PT2], lambda e, nk=nk, n=n: e.activation(
                                    out=PT2[0:nk, 0:n], in_=tmpb[0:nk, 0:n], func=AF.Exp))
                                S.op("pe", [T_PT2, T_v], [T_pO], lambda e, blk=blk, nk=nk, h=h, m=m, n=n, done=done: e.matmul(
                                    pO[0:n, m, 0:129], lhsT=PT2[0:nk, 0:n], rhs=vaug[0:nk, blk, h, 0:129],
                                    start=(done == 0), stop=(done == nblk - 1)))
                                done += 1
                        S.op("dve", [T_pO], [T_astat], lambda e, n=n: e.reciprocal(
                            out=astat[0:n, 0:2], in_=pO[0:n, :, 128:129].rearrange("p a b -> p (a b)")))
                        S.op("dve", [T_astat, T_small], [T_astat], lambda e, n=n: e.tensor_tensor(
                            out=astat[0:n, 2:3], in0=astat[0:n, 1:2], in1=neg_lam[0:n, :], op=ALU.mult))
                        S.op("dve", [T_pO, T_astat], [T_att], lambda e, n=n: e.tensor_scalar(
                            out=att[0:n, :], in0=pO[0:n, 0, 0:128], scalar1=astat[0:n, 0:1], scalar2=None, op0=ALU.mult))
                        S.op("dve", [T_pO, T_astat, T_att], [T_att], lambda e, n=n: e.scalar_tensor_tensor(
                            out=att[0:n, :], in0=pO[0:n, 1, 0:128], scalar=astat[0:n, 2:3], in1=att[0:n, :],
                            op0=ALU.mult, op1=ALU.add))
                        S.op("act", [T_att], [T_junk, T_astat], lambda e, n=n: e.activation(
                            out=junk[0:n, 0:128], in_=att[0:n, :], func=AF.Square, accum_out=astat[0:n, 3:4]))
                        S.op("act", [T_astat, T_eps], [T_astat], lambda e, n=n: e.activation(
                            out=astat[0:n, 4:5], in_=astat[0:n, 3:4], func=AF.Sqrt, scale=1.0 / 128, bias=eps_t[0:n, :]))
                        S.op("dve", [T_astat], [T_astat], lambda e, n=n: e.reciprocal(out=astat[0:n, 4:5], in_=astat[0:n, 4:5]))
                        S.op("dve", [T_att, T_astat, T_const], [T_attb], lambda e, n=n, h=h: e.scalar_tensor_tensor(
                            out=attb[0:n, h * 128:(h + 1) * 128], in0=att[0:n, :], scalar=astat[0:n, 4:5], in1=gsub_s[0:n, :],
                            op0=ALU.mult, op1=ALU.mult))
                    for h in range(4):
                        S.op("pe", [T_attb, T_identb], [T_pT], lambda e, h=h, n=n: e.transpose(
                            out=pT[:, h * 128:h * 128 + n], in_=attb[0:n, h * 128:(h + 1) * 128], identity=ident_b[0:n, 0:n]))
                    S.op("act", [T_pT], [T_cat], lambda e, c0=c0, n=n: e.copy(
                        out=catT[:, 4:8, c0:c0 + n], in_=pT[:, 0:512].rearrange("p (k c) -> p k c", k=4)[:, :, 0:n]))

                if STOP <= 4:
                    continue
                for cb in range(4):
                    pm, T_pm = next_pM()
                    for w in range(31):
                        S.op("pe", [T_aT, T_diag], [T_pm], lambda e, pm=pm, w=w, cb=cb: e.matmul(
                            pm[:, 0:N], lhsT=diag[:, w * 4 + cb, :], rhs=aT[:, cb, w:w + N], start=(w == 0), stop=(w == 30)))
                    S.op("act", [T_pm, T_const], [T_y32], lambda e, pm=pm, cb=cb: e.activation(
                        out=y32[:, cb, 0:N], in_=pm[:, 0:N], func=AF.Identity, bias=cvec_s[:, cb:cb + 1]))
                    S.op("act", [T_pm, T_const], [T_ysq], lambda e, pm=pm, cb=cb: e.activation(
                        out=ysq[:, cb, 0:N], in_=pm[:, 0:N], func=AF.Square, bias=cvec_s[:, cb:cb + 1]))
                    S.op("pool", [T_y32], [T_ybf], lambda e, cb=cb: e.tensor_copy(out=ybf[:, cb, 0:N], in_=y32[:, cb, 0:N]))
                p1, T_p1 = pX[0]
                p2, T_p2 = pX[1]
                for cb in range(4):
                    S.op("pe", [T_ybf, T_ones], [T_p1], lambda e, cb=cb: e.matmul(
                        p1[:, 0:N], lhsT=ones_b[:], rhs=ybf[:, cb, 0:N], start=(cb == 0), stop=(cb == 3)))
                for cb in range(4):
                    S.op("pe", [T_ysq, T_ones], [T_p2], lambda e, cb=cb: e.matmul(
                        p2[:, 0:N], lhsT=ones_b[:], rhs=ysq[:, cb, 0:N], start=(cb == 0), stop=(cb == 3)))
                S.op("dve", [T_p1], [T_mu], lambda e: e.tensor_scalar(
                    out=mu[:, 0:N], in0=p1[:, 0:N], scalar1=1.0 / 512, scalar2=None, op0=ALU.mult))
                S.op("dve", [T_mu], [T_rs], lambda e: e.tensor_tensor(out=rs[:, 0:N], in0=mu[:, 0:N], in1=mu[:, 0:N], op=ALU.mult))
                S.op("dve", [T_p2, T_rs], [T_rs], lambda e: e.scalar_tensor_tensor(
                    out=rs[:, 0:N], in0=p2[:, 0:N], scalar=1.0 / 512, in1=rs[:, 0:N], op0=ALU.mult, op1=ALU.subtract))
                S.op("act", [T_rs, T_eps], [T_rs], lambda e: e.activation(
                    out=rs[:, 0:N], in_=rs[:, 0:N], func=AF.Sqrt, bias=eps_t[:, :]))
                S.op("dve", [T_rs], [T_rs], lambda e: e.reciprocal(out=rs[:, 0:N], in_=rs[:, 0:N]))
                for cb in range(4):
                    S.op("dve", [T_y32, T_mu], [T_y32], lambda e, cb=cb: e.tensor_tensor(
                        out=y32[:, cb, 0:N], in0=y32[:, cb, 0:N], in1=mu[:, 0:N], op=ALU.subtract))
                    S.op("pool", [T_y32, T_rs], [T_y32], lambda e, cb=cb: e.tensor_tensor(
                        out=y32[:, cb, 0:N], in0=y32[:, cb, 0:N], in1=rs[:, 0:N], op=ALU.mult))
                    S.op("act", [T_y32, T_const], [T_cat], lambda e, cb=cb: e.activation(
                        out=catT[:, cb, 0:N], in_=y32[:, cb, 0:N], func=AF.Silu,
                        scale=cvec_s[:, 4 + cb:5 + cb], bias=cvec_s[:, 8 + cb:9 + cb]))
                if not last_group:
                    S.op("pool", [T_aT], [T_aT], lambda e: e.tensor_copy(out=aT[:, :, 0:30], in_=aT[:, :, N:N + 30]))

                if STOP <= 5:
                    continue
                for (c0, n) in tiles:
                    S.dma("sp", [], [T_xr], lambda e, c0=c0, n=n: e.dma_start(out=xr[0:n, :], in_=xd[g0 + c0:g0 + c0 + n, :]))
                    for hf in range(2):
                        po, T_po = pX[hf]
                        for kc in range(8):
                            S.op("pe", [T_cat, T_wout], [T_po], lambda e, po=po, kc=kc, hf=hf, c0=c0, n=n: e.matmul(
                                po[0:n, :], lhsT=catT[:, kc, c0:c0 + n], rhs=w_out_b[:, kc, hf * 512:(hf + 1) * 512],
                                start=(kc == 0), stop=(kc == 7)))
                        S.op("dve", [T_po, T_xr], [T_xr], lambda e, po=po, hf=hf, n=n: e.tensor_tensor(
                            out=xr[0:n, hf * 512:(hf + 1) * 512], in0=po[0:n, :], in1=xr[0:n, hf * 512:(hf + 1) * 512], op=ALU.add))
                    S.dma("sp", [T_xr], [], lambda e, c0=c0, n=n: e.dma_start(
                        out=x1d[tok0 + g0 + c0:tok0 + g0 + c0 + n, :], in_=xr[0:n, :]))
                    rms_to_bf16(n, xr, T_xr, hb, T_hb, 2)
                    transpose_to_fT(n, hb, T_hb, c0)
                for kc in range(8):
                    S.dma("sp", [T_fT], [], lambda e, kc=kc: e.dma_start(
                        out=h2Td[kc, :, tok0 + g0:tok0 + g0 + N], in_=fT[:, kc, 0:N]))

        S.barrier()
        st1.close()
        st2 = ExitStack()
        st.enter_context(st2)
        cur[0] = st2
        TG = 256
        NCH = 64
        if with_peer:
            ijwd = nc.dram_tensor("ijwd", [128, 3, NT], F32, kind="Internal").ap()
            uscr_t = nc.dram_tensor("uscr", [NCH, 128, 8 * 256], BF16, kind="Internal").ap()
            vscr_t = nc.dram_tensor("vscr", [NCH, 128, 2 * DM], BF16, kind="Internal").ap()
            uscr = [uscr_t[ic].rearrange("p (k e) -> p k e", k=8) for ic in range(NCH)]
            vscr = [vscr_t[ic].rearrange("p (b d) -> p b d", b=2) for ic in range(NCH)]
            T_uscr = [S.tile(f"uscr{ic}") for ic in range(NCH)]
            T_vscr = [S.tile(f"vscr{ic}") for ic in range(NCH)]
            T_ijwd = S.tile("ijwd")

            gfin_s, T_gfin = sb("gfin_s", [128, DM], F32)
            iota_b, T_iotab = sb("iota_b", [128, 128], BF16)
            T_c2 = S.tile("consts2")
            S.dma("sp", [], [T_c2], lambda e: e.dma_start(out=gfin_s[:], in_=gfin.to_broadcast([128, DM])))
            S.op("dve", [T_iotar], [T_iotab], lambda e: e.tensor_copy(out=iota_b[:], in_=iota_r[:]))

            st2a = ExitStack()
            cur[0] = st2a
            TA = 512
            wq_b, T_wq = sb("wq_b", [128, 8, 2048], BF16)
            keys_b, T_keys = sb("keys_b", [128, 16, 128], BF16)
            gffn_s, T_gffn = sb("gffn_s", [128, 8], F32)
            sg0, T_sg0 = sb("sg0", [128, 1024], F32)
            sg1, T_sg1 = sb("sg1", [128, 1024], F32)
            h2g, T_h2g = sb("h2g", [128, 8, TA], BF16)
            qryT, T_qry = sb("qryT", [128, 16, TA], BF16)
            s_sb, T_ssb = sb("s_sb", [128, 16, 128], F32)
            wk, T_wk = sb("wk", [128, 256], F32)
            A_, T_A = sb("A_", [128, 16, 16], F32)
            Iu, T_Iu = sb("Iu", [128, 16, 16], U32)
            If, T_If = sb("If", [128, 16, 16], F32)
            cand, T_cand = sb("cand", [128, 8, 256], F32)
            C_, T_C = sb("C_", [128, 8, 16], F32)
            pos, T_pos = sb("pos", [128, 8, 16], U32)
            ku, T_ku = sb("ku", [128, 2, 128], U32)
            kf, T_kf = sb("kf", [128, 2, 128], F32)
            E_, T_E = sb("E_", [128, 8, 16], F32)
            gst, T_gst = sb("gst", [128, 32], F32)
            oh, T_oh = sb("oh", [128, 8, 16, 16], F32)
            ijw, T_ijw = sb("ijw", [128, 3, 128], F32)
            ijT = [sb(f"ijT{i}", [128, 3, 128], F32) for i in range(2)]
            stu = [sb(f"stu{i}", [128, 8, 256], BF16) for i in range(2)]
            stv = [sb(f"stv{i}", [128, 2, DM], BF16) for i in range(2)]
            pGa = [ps(f"pGa{i}", [128, 512], F32) for i in range(4)]
            pga_i = [0]

            def next_pGa():
                pga_i[0] += 1
                return pGa[pga_i[0] % 4]

            S.dma("sp", [], [T_c2], lambda e: e.dma_start(out=gffn_s[:], in_=gffn))
            S.dma("pool", [], [T_keys], lambda e: e.dma_start(out=keys_b[:], in_=keysT.rearrange("r d n -> d r n")))
            sgs = [(sg0, T_sg0), (sg1, T_sg1)]
            ii = 0
            for kc in range(8):
                for hf in range(2):
                    stt, T_s = sgs[ii % 2]
                    ii += 1
                    S.dma("sp", [], [T_s], lambda e, stt=stt, kc=kc, hf=hf: e.dma_start(
                        out=stt[:], in_=wq[kc * 128:(kc + 1) * 128, hf * 1024:(hf + 1) * 1024]))
                    S.op("dve", [T_s, T_c2], [T_wq], lambda e, stt=stt, kc=kc, hf=hf: e.tensor_scalar(
                        out=wq_b[:, kc, hf * 1024:(hf + 1) * 1024], in0=stt[:], scalar1=gffn_s[:, kc:kc + 1],
                        scalar2=None, op0=ALU.mult))

            conv_i = [0]

            def convert_chunk():
                ic = conv_i[0]
                if ic >= NCH:
                    return
                conv_i[0] += 1
                (su, T_su), (sv, T_sv) = stu[ic % 2], stv[ic % 2]
                for k4 in range(2):
                    S.dma("pool", [], [T_su], lambda e, k4=k4: e.dma_start(
                        out=su[:, k4 * 4:(k4 + 1) * 4, :],
                        in_=uT[k4 * 512:(k4 + 1) * 512, ic * 256:(ic + 1) * 256].rearrange("(k p) e -> p k e", p=128)))
                S.dma("pool", [], [T_sv], lambda e: e.dma_start(
                    out=sv[:], in_=pv[ic * 256:(ic + 1) * 256, :].rearrange("(b p) d -> p b d", p=128)))
                S.dma("sp", [T_su], [T_uscr[ic]], lambda e: e.dma_start(out=uscr[ic], in_=su[:]), key=T_su)
                S.dma("sp", [T_sv], [T_vscr[ic]], lambda e: e.dma_start(out=vscr[ic], in_=sv[:]), key=T_sv)

            tile_ctr = 0
            for t0 in range(0, NT, TA):
                N = min(TA, NT - t0)
                tiles = [(c0, min(128, N - c0)) for c0 in range(0, N, 128)]
                S.dma("sp", [], [T_h2g], lambda e: e.dma_start(
                    out=h2g[:, :, 0:N], in_=h2Td[:, :, t0:t0 + N].rearrange("k p t -> p k t")))
                for blk in range(16):
                    pg, T_pg = next_pGa()
                    for kc in range(8):
                        S.op("pe", [T_wq, T_h2g], [T_pg], lambda e, pg=pg, kc=kc, blk=blk: e.matmul(
                            pg[:, 0:N], lhsT=wq_b[:, kc, blk * 128:(blk + 1) * 128], rhs=h2g[:, kc, 0:N],
                            start=(kc == 0), stop=(kc == 7)))
                    S.op("act", [T_pg], [T_qry], lambda e, pg=pg, blk=blk: e.copy(out=qryT[:, blk, 0:N], in_=pg[:, 0:N]))
                for (c0, n) in tiles:
                    convert_chunk()
                    convert_chunk()
                    for q4 in range(4):
                        pg, T_pg = next_pGa()
                        for j in range(4):
                            rp = q4 * 4 + j
                            S.op("pe", [T_qry, T_keys], [T_pg], lambda e, pg=pg, j=j, rp=rp: e.matmul(
                                pg[0:n, j * 128:(j + 1) * 128], lhsT=qryT[:, rp, c0:c0 + n], rhs=keys_b[:, rp, :],
                                start=True, stop=True))
                        S.op("act", [T_pg], [T_ssb], lambda e, pg=pg, q4=q4: e.copy(
                            out=s_sb[0:n, q4 * 4:(q4 + 1) * 4, :], in_=pg[0:n, :].rearrange("p (a b) -> p a b", a=4)))
                    for rp in range(16):
                        S.op("dve", [T_ssb], [T_A], lambda e, rp=rp: e.max(out=A_[0:n, rp, 0:8], in_=s_sb[0:n, rp, :]))
                        S.op("dve", [T_ssb, T_A], [T_Iu], lambda e, rp=rp: e.max_index(
                            out=Iu[0:n, rp, 0:8], in_max=A_[0:n, rp, 0:8], in_values=s_sb[0:n, rp, :]))
                        S.op("dve", [T_ssb, T_A], [T_wk], lambda e, rp=rp: e.match_replace(
                            out=wk[0:n, 0:128], in_to_replace=A_[0:n, rp, 0:8], in_values=s_sb[0:n, rp, :], imm_value=-1e30))
                        S.op("dve", [T_wk], [T_A], lambda e, rp=rp: e.max(out=A_[0:n, rp, 8:16], in_=wk[0:n, 0:128]))
                        S.op("dve", [T_wk, T_A], [T_Iu], lambda e, rp=rp: e.max_index(
                            out=Iu[0:n, rp, 8:16], in_max=A_[0:n, rp, 8:16], in_values=wk[0:n, 0:128]))
                    S.op("dve", [T_Iu], [T_If], lambda e: e.tensor_copy(out=If[0:n], in_=Iu[0:n]))
                    A4 = A_[0:n].rearrange("p (r a) k -> p r a k", a=2)
                    I4 = If[0:n].rearrange("p (r a) k -> p r a k", a=2)
                    S.op("dve", [T_A], [T_cand], lambda e: e.tensor_tensor(
                        out=cand[0:n].rearrange("p r (a b) -> p r a b", a=16),
                        in0=A4[:, :, 0, :].unsqueeze(3).to_broadcast([n, 8, 16, 16]),
                        in1=A4[:, :, 1, :].unsqueeze(2).to_broadcast([n, 8, 16, 16]), op=ALU.add))
                    for r in range(8):
                        S.op("dve", [T_cand], [T_C], lambda e, r=r: e.max(out=C_[0:n, r, 0:8], in_=cand[0:n, r, :]))
                        S.op("dve", [T_cand, T_C], [T_pos], lambda e, r=r: e.max_index(
                            out=pos[0:n, r, 0:8], in_max=C_[0:n, r, 0:8], in_values=cand[0:n, r, :]))
                        S.op("dve", [T_cand, T_C], [T_wk], lambda e, r=r: e.match_replace(
                            out=wk[0:n, :], in_to_replace=C_[0:n, r, 0:8], in_values=cand[0:n, r, :], imm_value=-1e30))
                        S.op("dve", [T_wk], [T_C], lambda e, r=r: e.max(out=C_[0:n, r, 8:16], in_=wk[0:n, :]))
                        S.op("dve", [T_wk, T_C], [T_pos], lambda e, r=r: e.max_index(
                            out=pos[0:n, r, 8:16], in_max=C_[0:n, r, 8:16], in_values=wk[0:n, :]))
                    S.op("dve", [T_C], [T_gst], lambda e: e.tensor_scalar(
                        out=gst[0:n, 0:8], in0=C_[0:n, :, 0], scalar1=-1.0, scalar2=None, op0=ALU.mult))
                    for r in range(8):
                        S.op("act", [T_C, T_gst], [T_E, T_gst], lambda e, r=r: e.activation(
                            out=E_[0:n, r, :], in_=C_[0:n, r, :], func=AF.Exp, bias=gst[0:n, r:r + 1],
                            accum_out=gst[0:n, 8 + r:9 + r]))
                    S.op("dve", [T_gst], [T_gst], lambda e: e.reciprocal(out=gst[0:n, 16:24], in_=gst[0:n, 8:16]))
                    S.op("dve", [T_E, T_gst], [T_ijw], lambda e: e.tensor_tensor(
                        out=ijw[0:n, 2, :].rearrange("p (r k) -> p r k", r=8), in0=E_[0:n],
                        in1=gst[0:n, 16:24].unsqueeze(2).to_broadcast([n, 8, 16]), op=ALU.mult))
                    S.op("dve", [T_pos], [T_ku], lambda e: e.tensor_single_scalar(
                        out=ku[0:n, 0, :], in_=pos[0:n].rearrange("p r k -> p (r k)"), scalar=4, op=ALU.logical_shift_right))
                    S.op("dve", [T_pos], [T_ku], lambda e: e.tensor_single_scalar(
                        out=ku[0:n, 1, :], in_=pos[0:n].rearrange("p r k -> p (r k)"), scalar=15, op=ALU.bitwise_and))
                    S.op("dve", [T_ku], [T_kf], lambda e: e.tensor_copy(out=kf[0:n], in_=ku[0:n]))
                    for a in range(2):
                        S.op("dve", [T_kf, T_iotar], [T_oh], lambda e, a=a: e.tensor_tensor(
                            out=oh[0:n],
                            in0=kf[0:n, a, :].rearrange("p (r k) -> p r k", r=8).unsqueeze(3).to_broadcast([n, 8, 16, 16]),
                            in1=iota_r[0:n, 0:16].unsqueeze(1).unsqueeze(1).to_broadcast([n, 8, 16, 16]), op=ALU.is_equal))
                        S.op("dve", [T_oh, T_If], [T_oh], lambda e, a=a: e.tensor_tensor(
                            out=oh[0:n], in0=oh[0:n],
                            in1=I4[:, :, a, :].unsqueeze(2).to_broadcast([n, 8, 16, 16]), op=ALU.mult))
                        S.op("dve", [T_oh], [T_ijw], lambda e, a=a: e.reduce_sum(
                            out=ijw[0:n, a, :].rearrange("p (r k) -> p r k", r=8), in_=oh[0:n], axis=AX.X))
                    pg, T_pg = next_pGa()
                    for a in range(3):
                        S.op("pe", [T_ijw, T_identf], [T_pg], lambda e, pg=pg, a=a: e.transpose(
                            out=pg[:, a * 128:a * 128 + n], in_=ijw[0:n, a, :], identity=ident_f[0:n, 0:n]))
                    (it_, T_it) = ijT[tile_ctr % 2]
                    tile_ctr += 1
                    S.op("act", [T_pg], [T_it], lambda e, pg=pg, it_=it_: e.copy(
                        out=it_[:, :, 0:n], in_=pg[:, 0:384].rearrange("p (a t) -> p a t", a=3)[:, :, 0:n]))
                    S.dma("sp", [T_it], [T_ijwd], lambda e, it_=it_: e.dma_start(
                        out=ijwd[:, :, t0 + c0:t0 + c0 + n], in_=it_[:, :, 0:n]), key=T_it)
            while conv_i[0] < NCH:
                convert_chunk()
            S.barrier()
            st2a.close()

            st2b = ExitStack()
            st.enter_context(st2b)
            cur[0] = st2b
            Gall = [sb(f"Gall{i}", [128, 128, TG], BF16) for i in range(2)]
            ubuf = [sb(f"ubuf{i}", [128, 8, 256], BF16) for i in range(2)]
            vbuf = [sb(f"vbuf{i}", [128, 2, DM], BF16) for i in range(2)]
            h2g2 = [sb(f"h2g2_{i}", [128, 8, TG], BF16) for i in range(2)]
            ijg = [sb(f"ijg{i}", [128, 3, TG], F32) for i in range(2)]
            P4 = [sb(f"P4_{i}", [128, 4, 128], BF16) for i in range(3)]
            Q4 = [sb(f"Q4_{i}", [128, 4, 128], BF16) for i in range(3)]
            gbuf = [sb(f"gbuf{i}", [128, TG], F32) for i in range(2)]
            cbuf = [sb(f"cbuf{i}", [128, TG], BF16) for i in range(2)]
            x2, T_x2 = sb("x2", [128, DM], F32)
            junk2, T_junk2 = sb("junk2", [128, DM], BF16)
            st2s, T_st2s = sb("st2s", [128, 4], F32)
            pY = [[ps(f"pY{t}{h}", [128, 512], F32) for h in range(2)] for t in range(2)]
            pA = [ps(f"pA{i}", [128, 512], F32) for i in range(2)]
            pG = [ps(f"pG{i}", [128, 512], F32) for i in range(2)]

            groups = [(t0, min(TG, NT - t0)) for t0 in range(0, NT, TG)]
            MAXG = int(os.environ.get("KGROUPS", "999"))
            groups = groups[:MAXG]
            NG = len(groups)
            MULT_ENG = os.environ.get("KMULT", "pool")

            def gtiles(g):
                N = groups[g][1]
                return [(c0, min(128, N - c0)) for c0 in range(0, N, 128)]

            def load_group(g):
                t0, N = groups[g]
                (hg, T_hg), (ij, T_ij) = h2g2[g % 2], ijg[g % 2]
                S.dma("sp", [], [T_hg], lambda e: e.dma_start(
                    out=hg[:, :, 0:N], in_=h2Td[:, :, t0:t0 + N].rearrange("k p t -> p k t")))
                S.dma("sp", [T_ijwd], [T_ij], lambda e: e.dma_start(out=ij[:, :, 0:N], in_=ijwd[:, :, t0:t0 + N]))

            def p10_dve(g, b):
                (ij, T_ij) = ijg[g % 2]
                (p4, T_p4), (q4_, T_q4) = P4[b % 3], Q4[b % 3]
                for u in range(4):
                    t = b * 4 + u
                    S.op("dve", [T_ij, T_iotab], [T_p4], lambda e, u=u, t=t: e.tensor_scalar(
                        out=p4[:, u, :], in0=iota_b[:], scalar1=ij[:, 0, t:t + 1], scalar2=None, op0=ALU.is_equal))
                    S.op("dve", [T_ij, T_iotab], [T_q4], lambda e, u=u, t=t: e.tensor_scalar(
                        out=q4_[:, u, :], in0=iota_b[:], scalar1=ij[:, 1, t:t + 1], scalar2=ij[:, 2, t:t + 1],
                        op0=ALU.is_equal, op1=ALU.mult))

            def p10_pe(g, b):
                (p4, T_p4), (q4_, T_q4) = P4[b % 3], Q4[b % 3]
                (ga, T_ga) = Gall[g % 2]
                pg, T_pg = pG[b % 2]
                for u in range(4):
                    S.op("pe", [T_p4, T_q4], [T_pg], lambda e, u=u: e.matmul(
                        pg[:, u * 128:(u + 1) * 128], lhsT=q4_[:, u, :], rhs=p4[:, u, :], start=True, stop=True))
                S.op("act", [T_pg], [T_ga], lambda e: e.copy(
                    out=ga[:, :, b * 4:b * 4 + 4], in_=pg[:, :].rearrange("p (t i) -> p i t", t=4)))

            class P10:
                def __init__(self, g):
                    self.g = g
                    self.nb = groups[g][1] // 4
                    self.d = 0
                    self.p = 0

                def step(self):
                    if self.d < self.nb:
                        p10_dve(self.g, self.d)
                        self.d += 1
                        if self.d - self.p >= 3:
                            p10_pe(self.g, self.p)
                            self.p += 1
                    elif self.p < self.nb:
                        p10_pe(self.g, self.p)
                        self.p += 1

                def flush(self):
                    while self.p < self.nb:
                        if self.d < self.nb and self.d - self.p < 3:
                            p10_dve(self.g, self.d)
                            self.d += 1
                        else:
                            p10_pe(self.g, self.p)
                            self.p += 1

            chunk_ctr = [0]

            def load_chunk(ic):
                c = chunk_ctr[0]
                chunk_ctr[0] += 1
                (ub, T_ub), (vb, T_vb) = ubuf[c % 2], vbuf[c % 2]
                S.dma("sp", [T_uscr[ic]], [T_ub], lambda e: e.dma_start(out=ub[:], in_=uscr[ic]))
                S.dma("sp", [T_vscr[ic]], [T_vb], lambda e: e.dma_start(out=vb[:], in_=vscr[ic]))

            def stage_u(g, i):
                N = groups[g][1]
                c = g * NCH + i // 2
                ib = i % 2
                (ub, T_ub) = ubuf[c % 2]
                (hg, T_hg) = h2g2[g % 2]
                (ga, T_ga) = Gall[g % 2]
                pa, T_pa = pA[i % 2]
                gb, T_gb = gbuf[i % 2]
                cb_, T_cb = cbuf[i % 2]
                for kc in range(8):
                    S.op("pe", [T_ub, T_hg], [T_pa], lambda e, kc=kc: e.matmul(
                        pa[:, 0:N], lhsT=ub[:, kc, ib * 128:(ib + 1) * 128], rhs=hg[:, kc, 0:N],
                        start=(kc == 0), stop=(kc == 7)))
                S.op("act", [T_pa], [T_gb], lambda e: e.activation(out=gb[:, 0:N], in_=pa[:, 0:N], func=AF.Gelu))
                S.op(MULT_ENG, [T_gb, T_ga], [T_cb], lambda e: e.tensor_tensor(
                    out=cb_[:, 0:N], in0=gb[:, 0:N], in1=ga[:, i, 0:N], op=ALU.mult))

            def stage_v(g, i):
                c = g * NCH + i // 2
                ib = i % 2
                (vb, T_vb) = vbuf[c % 2]
                cb_, T_cb = cbuf[i % 2]
                for ti, (c0, n) in enumerate(gtiles(g)):
                    for hf in range(2):
                        py, T_py = pY[ti][hf]
                        S.op("pe", [T_cb, T_vb], [T_py], lambda e, py=py, c0=c0, n=n, hf=hf: e.matmul(
                            py[0:n, :], lhsT=cb_[:, c0:c0 + n], rhs=vb[:, ib, hf * 512:(hf + 1) * 512],
                            start=(i == 0), stop=(i == 127)))

            def epilogue(g):
                t0, N = groups[g]
                for ti, (c0, n) in enumerate(gtiles(g)):
                    tg = t0 + c0
                    S.dma("sp", [], [T_x2], lambda e, tg=tg, n=n: e.dma_start(out=x2[0:n, :], in_=x1d[tg:tg + n, :]))
                    for hf in range(2):
                        py, T_py = pY[ti][hf]
                        S.op("dve", [T_py, T_x2], [T_x2], lambda e, py=py, hf=hf, n=n: e.tensor_tensor(
                            out=x2[0:n, hf * 512:(hf + 1) * 512], in0=py[0:n, :], in1=x2[0:n, hf * 512:(hf + 1) * 512], op=ALU.add))
                    S.op("act", [T_x2], [T_junk2, T_st2s], lambda e, n=n: e.activation(
                        out=junk2[0:n, :], in_=x2[0:n, :], func=AF.Square, accum_out=st2s[0:n, 0:1]))
                    S.op("act", [T_st2s, T_eps], [T_st2s], lambda e, n=n: e.activation(
                        out=st2s[0:n, 1:2], in_=st2s[0:n, 0:1], func=AF.Sqrt, scale=1.0 / DM, bias=eps_t[0:n, :]))
                    S.op("dve", [T_st2s], [T_st2s], lambda e, n=n: e.reciprocal(out=st2s[0:n, 1:2], in_=st2s[0:n, 1:2]))
                    S.op("dve", [T_x2, T_st2s, T_c2], [T_x2], lambda e, n=n: e.scalar_tensor_tensor(
                        out=x2[0:n, :], in0=x2[0:n, :], scalar=st2s[0:n, 1:2], in1=gfin_s[0:n, :], op0=ALU.mult, op1=ALU.mult))
                    if tg < n_pseq * SEQ:
                        dst = yp[tg // SEQ][tg % SEQ:tg % SEQ + n, :]
                    else:
                        dst = ys[0:n, :]
                    S.dma("sp", [T_x2], [], lambda e, dst=dst, n=n: e.dma_start(out=dst, in_=x2[0:n, :]))

            load_group(0)
            load_chunk(0)
            pz = P10(0)
            pz.flush()
            for g in range(NG):
                nxt = None
                if g + 1 < NG:
                    load_group(g + 1)
                    nxt = P10(g + 1)
                for i in range(128):
                    stage_u(g, i)
                    if i >= 1:
                        stage_v(g, i - 1)
                    elif g >= 1:
                        stage_v(g - 1, 127)
                        epilogue(g - 1)
                    if i % 2 == 0:
                        ic_next = i // 2 + 1
                        if ic_next < NCH:
                            load_chunk(ic_next)
                        elif g + 1 < NG:
                            load_chunk(0)
                    if nxt is not None and i % 2 == 1:
                        nxt.step()
                if nxt is not None:
                    nxt.flush()
            stage_v(NG - 1, 127)
            epilogue(NG - 1)

        S.barrier()
        S.finish("sp")
        print("ops per engine:", S.nops, "sems:", S.nsem)
    return nc


def _prep_shared(inp):
    f = lambda a: np.ascontiguousarray(np.asarray(a, dtype=np.float32))
    sh = {}
    sh["w_in"] = f(inp["w_in"][0])
    sh["gmix"] = f(inp["g_mix"][0].reshape(8, 128).T)
    sh["convw"] = f(inp["conv_w"][0].reshape(31, 4, 128).transpose(2, 1, 0))
    sh["cvec"] = f(np.concatenate([inp["conv_b"][0].reshape(4, 128).T, inp["conv_ln_g"][0].reshape(4, 128).T,
                                   inp["conv_ln_b"][0].reshape(4, 128).T], axis=1))
    sh["lam"] = f(np.stack([inp["lambda_q1"][0], inp["lambda_k1"][0], inp["lambda_q2"][0], inp["lambda_k2"][0]]).reshape(1, 256))
    sh["subg"] = f(inp["subln_g"][0].reshape(1, 128))
    sh["relb"] = f(inp["rel_bias"].reshape(1, 128))
    sh["w_out"] = f(inp["w_out"][0])
    sh["gffn"] = f(inp["g_ffn"][0].reshape(8, 128).T)
    sh["wq"] = f(inp["w_query"][0])
    sh["keysT"] = f(inp["sub_keys"][0].reshape(16, 128, 128).transpose(0, 2, 1))
    sh["uT"] = f(inp["peer_u"][0].T)
    sh["pv"] = f(inp["peer_v"][0])
    sh["gfin"] = f(inp["g_final"].reshape(1, DM))
    sh["bkc"] = _bucket_tiles()
    return sh


def kernel(**inp):
    f = lambda a: np.ascontiguousarray(np.asarray(a, dtype=np.float32))
    sh = _prep_shared(inp)
    nc = build_program(2)
    in_maps = []
    for c in range(NCORES):
        m = dict(sh)
        m["xp"] = f(inp["x_prompt"][2 * c:2 * c + 2])
        m["xs"] = f(inp["x_sample"][c])
        m["ckT"] = f(np.asarray(inp["cache_k"][0, c]).reshape(SEQ, 4, 128).transpose(1, 2, 0))
        m["cv"] = f(np.asarray(inp["cache_v"][0, c]).reshape(SEQ, 512))
        m["scT"] = f(np.asarray(inp["state_conv"][0, c]).reshape(30, 4, 128).transpose(2, 1, 0))
        in_maps.append(m)
    res = run_bass_kernel_spmd(nc, in_maps, core_ids=list(range(NCORES)))
    R = res.results
    y_prompt = np.concatenate([r["yp"] for r in R], axis=0)
    y_sample = np.stack([r["ys"] for r in R], axis=0)
    k_prompt = np.concatenate([r["kp"] for r in R], axis=0).reshape(1, 16, SEQ, 4, 2, 64)
    v_prompt = np.concatenate([r["vp"] for r in R], axis=0).reshape(1, 16, SEQ, 4, 128)
    c_prompt = np.concatenate([r["cp"] for r in R], axis=0).reshape(1, 16, 30, 512)
    k_sample = np.stack([r["ks"] for r in R], axis=0).reshape(1, 8, 64, 4, 2, 64)
    v_sample = np.stack([r["vs"] for r in R], axis=0).reshape(1, 8, 64, 4, 128)
    c_sample = np.stack([r["cs"] for r in R], axis=0).reshape(1, 8, 30, 512)
    return (y_prompt, y_sample, k_prompt, v_prompt, c_prompt, k_sample, v_sample, c_sample)
```

```python
import math
import os
from contextlib import ExitStack

import numpy as np
import concourse.bass as bass
import concourse.mybir as mybir
from concourse.bass_utils import run_bass_kernel_spmd

F32 = mybir.dt.float32
BF16 = mybir.dt.bfloat16
U32 = mybir.dt.uint32
AF = mybir.ActivationFunctionType
ALU = mybir.AluOpType
AX = mybir.AxisListType

EPS = 1e-6
LAM_INIT = 0.8 - 0.6 * math.exp(-0.3 * 0)
NCORES = 8
SEQ = 2048
DM = 1024
NEXP_SIDE = 128

SEM_LIMIT = 30000


class Counter:
    def __init__(self, S, name):
        self.S = S
        self.name = name
        self.epoch = 0
        self.val = 0
        self.sem = S.new_sem(f"{name}_e0")

    def bump(self, inc):
        if self.val + inc > SEM_LIMIT:
            self.epoch += 1
            self.val = 0
            self.sem = self.S.new_sem(f"{self.name}_e{self.epoch}")
        self.val += inc
        return (self.sem, self.val, self.name, self.epoch)


class Tile:
    __slots__ = ("name", "w", "r", "dmac")

    def __init__(self, name):
        self.name = name
        self.w = None
        self.r = []
        self.dmac = None


class Sched:
    def __init__(self, nc, stack):
        self.nc = nc
        self.stack = stack
        self.nsem = 0
        self.engs = {"pe": nc.tensor, "act": nc.scalar, "dve": nc.vector,
                     "pool": nc.gpsimd, "sp": nc.sync}
        self.cnt = {k: Counter(self, k) for k in self.engs}
        self.known = {k: {} for k in self.engs}
        self.nops = {k: 0 for k in self.engs}
        self.tiles = []

    def new_sem(self, name):
        self.nsem += 1
        return self.stack.enter_context(self.nc.semaphore(f"s{self.nsem}_{name}"))

    def tile(self, name):
        t = Tile(name)
        self.tiles.append(t)
        return t

    def _wait(self, e, ev):
        sem, val, name, epoch = ev
        key = (name, epoch)
        if self.known[e].get(key, 0) >= val:
            return
        self.known[e][key] = val
        self.engs[e].wait_ge(sem, val)

    def _deps(self, reads, writes):
        evs = []
        for t in reads:
            if t.w is not None:
                evs.append(t.w)
        for t in writes:
            if t.w is not None:
                evs.append(t.w)
            evs.extend(t.r)
        return evs

    def op(self, e, reads, writes, fn):
        for ev in self._deps(reads, writes):
            if ev[2] == e:
                if e == "pe":
                    continue
                if ev[3] == self.cnt[e].epoch and self.cnt[e].val - ev[1] >= 2:
                    continue
            self._wait(e, ev)
        ins = fn(self.engs[e])
        ev = self.cnt[e].bump(1)
        ins.then_inc(ev[0], 1)
        self.nops[e] += 1
        self._mark(ev, reads, writes)
        return ev

    def _mark(self, ev, reads, writes):
        k = (ev[2], ev[3])
        for t in reads:
            t.r = [x for x in t.r if (x[2], x[3]) != k]
            t.r.append(ev)
        for t in writes:
            t.w = ev
            t.r = []

    def dma(self, q, reads, writes, fn, key=None):
        kt = key or (writes[0] if writes else reads[0])
        if kt.dmac is None:
            kt.dmac = Counter(self, "d_" + kt.name)
        for ev in self._deps(reads, writes):
            self._wait(q, ev)
        ins = fn(self.engs[q])
        ev = kt.dmac.bump(16)
        ins.then_inc(ev[0], 16)
        self.nops[q] += 1
        self._mark(ev, reads, writes)
        return ev

    def _all_events(self):
        evs = {}
        for t in self.tiles:
            for ev in ([t.w] if t.w else []) + t.r:
                k = (ev[2], ev[3])
                if k not in evs or evs[k][1] < ev[1]:
                    evs[k] = ev
        return evs

    def barrier(self):
        evs = self._all_events()
        for e in self.engs:
            for ev in evs.values():
                self._wait(e, ev)
        for t in self.tiles:
            t.w = None
            t.r = []

    def finish(self, e="sp"):
        for ev in self._all_events().values():
            self._wait(e, ev)


def _bucket_np(rel):
    nb = 16
    max_exact = 8
    ret = np.where(rel > 0, nb, 0)
    n = np.abs(rel)
    nf = np.maximum(n, 1).astype(np.float32)
    large = max_exact + (np.log(nf / max_exact) / math.log(128 / max_exact) * (nb - max_exact)).astype(np.int32)
    large = np.minimum(large, nb - 1)
    return ret + np.where(n < max_exact, n, large)


def _bucket_tiles():
    k = np.arange(128)[:, None]
    q = np.arange(128)[None, :]
    b0 = _bucket_np(k - q).astype(np.float32)
    masked = (k // 64) > (q // 64)
    b0 = np.where(masked, 32.0, b0)
    b1 = _bucket_np(k - q - 128).astype(np.float32)
    return np.stack([b0, b1], axis=1).astype(np.float32)


def build_program(n_pseq=2, with_peer=True, dbg=False):
    nc = bass.Bass("TRN2", target_bir_lowering=False)
    NT = n_pseq * SEQ + 64

    def din(name, shape, dt=F32):
        return nc.dram_tensor(name, list(shape), dt, kind="ExternalInput").ap()

    def dout(name, shape, dt=F32):
        return nc.dram_tensor(name, list(shape), dt, kind="ExternalOutput").ap()

    xp = din("xp", [n_pseq, SEQ, DM])
    xs = din("xs", [64, DM])
    ckT = din("ckT", [4, 128, SEQ])
    cv = din("cv", [SEQ, 512])
    scT = din("scT", [128, 4, 30])
    w_in = din("w_in", [DM, 2560])
    gmix = din("gmix", [128, 8])
    convw = din("convw", [128, 4, 31])
    cvec = din("cvec", [128, 12])
    lam = din("lam", [1, 256])
    subg = din("subg", [1, 128])
    relb = din("relb", [1, 128])
    w_out = din("w_out", [DM, DM])
    gffn = din("gffn", [128, 8])
    wq = din("wq", [DM, 2048])
    keysT = din("keysT", [16, 128, 128])
    uT = din("uT", [DM, 16384])
    pv = din("pv", [16384, DM])
    gfin = din("gfin", [1, DM])
    bkc = din("bkc", [128, 2, 128])

    yp = dout("yp", [n_pseq, SEQ, DM])
    ys = dout("ys", [64, DM])
    kp = dout("kp", [n_pseq, SEQ, 512])
    vp = dout("vp", [n_pseq, SEQ, 512])
    cp = dout("cp", [n_pseq, 30, 512])
    ks = dout("ks", [64, 512])
    vs = dout("vs", [64, 512])
    cs = dout("cs", [30, 512])

    kind_scr = "ExternalOutput" if dbg else "Internal"
    x1d = nc.dram_tensor("x1d", [NT, DM], F32, kind=kind_scr).ap()
    h2Td = nc.dram_tensor("h2Td", [8, 128, NT], BF16, kind="Internal").ap()

    with ExitStack() as st:
        S = Sched(nc, st)

        cur = [st]

        def sb(name, shape, dt):
            return cur[0].enter_context(nc.sbuf_tensor(name, list(shape), dt)), S.tile(name)

        def ps(name, shape, dt):
            return cur[0].enter_context(nc.psum_tensor(name, list(shape), dt)), S.tile(name)

        ident_f, T_identf = sb("ident_f", [128, 128], F32)
        ident_b, T_identb = sb("ident_b", [128, 128], BF16)
        ones_b, T_ones = sb("ones_b", [128, 128], BF16)
        iota_t, T_iota = sb("iota_t", [128, 128], F32)
        T_const = S.tile("consts")

        S.op("pool", [], [T_iota], lambda e: e.iota(iota_t[:], pattern=[[1, 128]], base=0, channel_multiplier=-1,
                                                    allow_small_or_imprecise_dtypes=True))
        S.op("dve", [T_iota], [T_identf], lambda e: e.tensor_scalar(out=ident_f[:], in0=iota_t[:], scalar1=0.0,
                                                                     scalar2=None, op0=ALU.is_equal))
        S.op("dve", [T_identf], [T_identb], lambda e: e.tensor_copy(out=ident_b[:], in_=ident_f[:]))
        S.op("pool", [], [T_ones], lambda e: e.memset(ones_b[:], 1.0))

        eps_t, T_eps = sb("eps_t", [128, 1], F32)
        S.op("pool", [], [T_eps], lambda e: e.memset(eps_t[:], EPS))
        EPS_AP = eps_t
        iota_r, T_iotar = sb("iota_r", [128, 128], F32)
        S.op("pool", [], [T_iotar], lambda e: e.iota(iota_r[:], pattern=[[1, 128]], base=0, channel_multiplier=0,
                                                     allow_small_or_imprecise_dtypes=True))
        st1 = ExitStack()
        cur[0] = st1
        w_in_b, T_win = sb("w_in_b", [128, 8, 2560], BF16)
        w_out_b, T_wout = sb("w_out_b", [128, 8, DM], BF16)
        diag, T_diag = sb("diag", [128, 124, 128], BF16)
        gmix_s, _ = sb("gmix_s", [128, 8], F32)
        convw_s, _ = sb("convw_s", [128, 4, 31], F32)
        cvec_s, _ = sb("cvec_s", [128, 12], F32)
        lam_s, _ = sb("lam_s", [128, 256], F32)
        gsub_s, _ = sb("gsub_s", [128, 128], F32)
        relb_s, _ = sb("relb_s", [128, 128], F32)
        bk_s, _ = sb("bk_s", [128, 2, 128], F32)
        Tb, T_Tb = sb("Tb", [128, 4, 2, 128], F32)
        eqm, T_eqm = sb("eqm", [128, 2, 128], F32)
        small, T_small = sb("small", [128, 16], F32)
        stage0, T_st0 = sb("stage0", [128, 1024], F32)
        stage1, T_st1 = sb("stage1", [128, 1024], F32)

        for dst, src in ((gmix_s, gmix), (convw_s, convw), (cvec_s, cvec), (bk_s, bkc)):
            S.dma("sp", [], [T_const], lambda e, d=dst, s_=src: e.dma_start(out=d[:], in_=s_))
        for dst, src, n in ((lam_s, lam, 256), (gsub_s, subg, 128), (relb_s, relb, 128)):
            S.dma("sp", [], [T_const], lambda e, d=dst, s_=src, n=n: e.dma_start(out=d[:], in_=s_.to_broadcast([128, n])))

        stg = [(stage0, T_st0), (stage1, T_st1)]
        i = 0
        for kc in range(8):
            for (a0, a1) in ((0, 1024), (1024, 2048), (2048, 2560)):
                stt, T_s = stg[i % 2]
                i += 1
                S.dma("sp", [], [T_s], lambda e, stt=stt, kc=kc, a0=a0, a1=a1: e.dma_start(
                    out=stt[:, 0:a1 - a0], in_=w_in[kc * 128:(kc + 1) * 128, a0:a1]))
                S.op("dve", [T_s, T_const], [T_win], lambda e, stt=stt, kc=kc, a0=a0, a1=a1: e.tensor_scalar(
                    out=w_in_b[:, kc, a0:a1], in0=stt[:, 0:a1 - a0], scalar1=gmix_s[:, kc:kc + 1],
                    scalar2=None, op0=ALU.mult))
        for kc in range(8):
            stt, T_s = stg[i % 2]
            i += 1
            S.dma("sp", [], [T_s], lambda e, stt=stt, kc=kc: e.dma_start(
                out=stt[:, 0:DM], in_=w_out[kc * 128:(kc + 1) * 128, :]))
            S.op("act", [T_s], [T_wout], lambda e, stt=stt, kc=kc: e.copy(out=w_out_b[:, kc, :], in_=stt[:, 0:DM]))
        for w in range(31):
            for cb in range(4):
                S.op("dve", [T_const, T_identf], [T_diag], lambda e, w=w, cb=cb: e.tensor_scalar(
                    out=diag[:, w * 4 + cb, :], in0=ident_f[:], scalar1=convw_s[:, cb, w:w + 1], scalar2=None,
                    op0=ALU.mult))
        S.op("dve", [T_const], [T_eqm], lambda e: e.tensor_scalar(
            out=eqm[:], in0=bk_s[:], scalar1=32.0, scalar2=-30000.0, op0=ALU.is_equal, op1=ALU.mult))
        for h in range(4):
            S.op("dve", [T_eqm], [T_Tb], lambda e, h=h: e.tensor_copy(out=Tb[:, h, :, :], in_=eqm[:]))
        for b in range(32):
            S.op("dve", [T_const], [T_eqm], lambda e, b=b: e.tensor_scalar(
                out=eqm[:], in0=bk_s[:], scalar1=float(b), scalar2=None, op0=ALU.is_equal))
            for h in range(4):
                S.op("dve", [T_eqm, T_const, T_Tb], [T_Tb], lambda e, b=b, h=h: e.scalar_tensor_tensor(
                    out=Tb[:, h, :, :], in0=eqm[:], scalar=relb_s[:, b * 4 + h:b * 4 + h + 1], in1=Tb[:, h, :, :],
                    op0=ALU.mult, op1=ALU.add))
        S.op("dve", [T_const], [T_eqm], lambda e: e.tensor_tensor(
            out=eqm[:, 0, :].rearrange("p (a b) -> p a b", a=2), in0=lam_s[:].rearrange("p (a b c) -> p a b c", a=2, b=2)[:, :, 0, :],
            in1=lam_s[:].rearrange("p (a b c) -> p a b c", a=2, b=2)[:, :, 1, :], op=ALU.mult))
        S.op("dve", [T_eqm], [T_small], lambda e: e.reduce_sum(
            out=small[:, 0:2], in_=eqm[:, 0, :].rearrange("p (a b) -> p a b", a=2), axis=AX.X))
        S.op("act", [T_small], [T_small], lambda e: e.activation(out=small[:, 2:4], in_=small[:, 0:2], func=AF.Exp))
        S.op("dve", [T_small], [T_small], lambda e: e.tensor_tensor(
            out=small[:, 4:5], in0=small[:, 3:4], in1=small[:, 2:3], op=ALU.subtract))
        S.op("dve", [T_small], [T_small], lambda e: e.tensor_scalar(
            out=small[:, 4:5], in0=small[:, 4:5], scalar1=-LAM_INIT, scalar2=None, op0=ALU.add))
        S.op("dve", [T_const], [T_const], lambda e: e.tensor_scalar(
            out=gsub_s[:], in0=gsub_s[:], scalar1=1.0 - LAM_INIT, scalar2=None, op0=ALU.mult))
        neg_lam = small[:, 4:5]

        fT, T_fT = sb("fT", [128, 8, 512], BF16)
        xt, T_xt = sb("xt", [128, DM], F32)
        xr, T_xr = xt, T_xt
        junk, T_junk = sb("junk", [128, DM], BF16)
        hb, T_hb = sb("hb", [128, DM], BF16)
        stat, T_stat = sb("stat", [128, 8], F32)
        aT, T_aT = sb("aT", [128, 4, 30 + 512], BF16)
        sig, T_sig = sb("sig", [128, 512], F32)
        a32, T_a32 = sb("a32", [128, 512], F32)
        qT, T_qT = sb("qT", [128, 4, 512], BF16)
        kT, T_kT = sb("kT", [128, 4, SEQ + 64], BF16)
        vaug, T_v = sb("vaug", [128, 17, 4, 130], BF16)
        catT, T_cat = sb("catT", [128, 8, 512], BF16)
        zq, T_zq = sb("zq", [128, 512], BF16)
        zk32, T_zk32 = sb("zk32", [128, 512], F32)
        zkb, T_zkb = sb("zkb", [128, 512], BF16)
        zv32, T_zv32 = sb("zv32", [128, 512], F32)
        PTT = [sb(f"PT{i}", [128, 512], BF16) for i in range(2)]
        TMP = [sb(f"tmpb{i}", [128, 128], F32) for i in range(2)]
        att, T_att = sb("att", [128, 128], F32)
        attb, T_attb = sb("attb", [128, 512], BF16)
        astat, T_astat = sb("astat", [128, 8], F32)
        y32, T_y32 = sb("y32", [128, 4, 512], F32)
        ybf, T_ybf = sb("ybf", [128, 4, 512], BF16)
        ysq, T_ysq = sb("ysq", [128, 4, 512], BF16)
        mu, T_mu = sig, T_sig
        rs, T_rs = a32, T_a32
        ctail, T_ctail = zk32, T_zk32
        cst32, T_cst32 = sb("cst32", [128, 4, 30], F32)

        pT, T_pT = ps("pT", [128, 1024], BF16)
        pM = [ps(f"pM{i}", [128, 512], F32) for i in range(2)]
        pSS = [ps(f"pS{i}", [128, 512], F32) for i in range(2)]
        pO, T_pO = ps("pO", [128, 2, 256], F32)
        pX = [ps(f"pX{i}", [128, 512], F32) for i in range(2)]
        pm_i = [0]
        pOO = [(pO, T_pO), (pM[0][0][:].rearrange("p (a b) -> p a b", a=2), pM[0][1])]

        def next_pM():
            pm_i[0] += 1
            return pM[pm_i[0] % 2]

        S.op("pool", [], [T_v], lambda e: e.memset(vaug[:], 1.0))

        def rms_to_bf16(n, src, T_src, dst_b, T_dst, col):
            S.op("act", [T_src], [T_junk, T_stat], lambda e: e.activation(
                out=junk[0:n, :], in_=src[0:n, :], func=AF.Square, accum_out=stat[0:n, col:col + 1]))
            S.op("act", [T_stat], [T_stat], lambda e: e.activation(
                out=stat[0:n, col + 1:col + 2], in_=stat[0:n, col:col + 1], func=AF.Sqrt, scale=1.0 / DM, bias=EPS_AP[0:n, :]))
            S.op("dve", [T_stat], [T_stat], lambda e: e.reciprocal(
                out=stat[0:n, col + 1:col + 2], in_=stat[0:n, col + 1:col + 2]))
            S.op("dve", [T_src, T_stat], [T_dst], lambda e: e.tensor_scalar(
                out=dst_b[0:n, :], in0=src[0:n, :], scalar1=stat[0:n, col + 1:col + 2], scalar2=None, op0=ALU.mult))

        def transpose_to_fT(n, src_b, T_src, c0):
            for kc in range(8):
                S.op("pe", [T_src, T_identb], [T_pT], lambda e, kc=kc: e.transpose(
                    out=pT[:, kc * 128:kc * 128 + n], in_=src_b[0:n, kc * 128:(kc + 1) * 128], identity=ident_b[0:n, 0:n]))
            S.op("act", [T_pT], [T_fT], lambda e: e.copy(
                out=fT[:, :, c0:c0 + n], in_=pT[:].rearrange("p (k c) -> p k c", k=8)[:, :, 0:n]))

        seqs = []
        for s_ in range(n_pseq):
            seqs.append(("p", xp[s_], SEQ, kp[s_], vp[s_], cp[s_], s_ * SEQ))
        seqs.append(("s", xs, 64, ks, vs, cs, n_pseq * SEQ))

        S.barrier()
        import os
        STOP = int(os.environ.get("KSTOP", "99"))
        if STOP <= 0:
            seqs = []

        for (kind, xd, ntok, kd, vd, cd, tok0) in seqs:
            past = SEQ if kind == "s" else 0
            if kind == "p":
                S.op("pool", [], [T_aT], lambda e: e.memset(aT[:, :, 0:30], 0.0))
            else:
                S.dma("sp", [], [T_cst32], lambda e: e.dma_start(out=cst32[:], in_=scT))
                S.op("dve", [T_cst32], [T_aT], lambda e: e.tensor_copy(out=aT[:, :, 0:30], in_=cst32[:]))
                for h in range(4):
                    for hf in range(2):
                        stt, T_s = stg[(h * 2 + hf) % 2]
                        S.dma("sp", [], [T_s], lambda e, stt=stt, h=h, hf=hf: e.dma_start(
                            out=stt[:, 0:1024], in_=ckT[h, :, hf * 1024:(hf + 1) * 1024]))
                        S.op("act", [T_s], [T_kT], lambda e, stt=stt, h=h, hf=hf: e.copy(
                            out=kT[:, h, hf * 1024:(hf + 1) * 1024], in_=stt[:, 0:1024]))
                for blk in range(16):
                    stt, T_s = stg[blk % 2]
                    S.dma("sp", [], [T_s], lambda e, stt=stt, blk=blk: e.dma_start(
                        out=stt[:, 0:512], in_=cv[blk * 128:(blk + 1) * 128, :]))
                    S.op("dve", [T_s], [T_v], lambda e, stt=stt, blk=blk: e.tensor_copy(
                        out=vaug[:, blk, :, 0:128], in_=stt[:, 0:512].rearrange("p (h e) -> p h e", h=4)))

            ngroups = (ntok + 511) // 512
            for g in range(ngroups):
                g0 = g * 512
                N = min(512, ntok - g0)
                tiles = [(c0, min(128, N - c0)) for c0 in range(0, N, 128)]
                last_group = (g == ngroups - 1)

                for (c0, n) in tiles:
                    S.dma("sp", [], [T_xt], lambda e, c0=c0, n=n: e.dma_start(out=xt[0:n, :], in_=xd[g0 + c0:g0 + c0 + n, :]))
                    rms_to_bf16(n, xt, T_xt, hb, T_hb, 0)
                    transpose_to_fT(n, hb, T_hb, c0)

                if STOP <= 1:
                    continue
                for (c0, n) in tiles:
                    blk = (past + g0 + c0) // 128
                    kcol = past + g0 + c0
                    for j in range(int(os.environ.get('KJ', '3'))):
                        pm, T_pm = next_pM()
                        for kc in range(8):
                            S.op("pe", [T_fT, T_win], [T_pm], lambda e, pm=pm, kc=kc, j=j, c0=c0, n=n: e.matmul(
                                pm[0:n, :], lhsT=fT[:, kc, c0:c0 + n], rhs=w_in_b[:, kc, 1024 + j * 512:1024 + (j + 1) * 512],
                                start=(kc == 0), stop=(kc == 7)))
                        if j == 0:
                            S.op("act", [T_pm], [T_zq], lambda e, pm=pm, n=n: e.activation(
                                out=zq[0:n, :], in_=pm[0:n, :], func=AF.Copy, scale=0.125))
                            for h in range(4):
                                S.op("pe", [T_zq, T_identb], [T_pT], lambda e, h=h, n=n: e.transpose(
                                    out=pT[:, h * 128:h * 128 + n], in_=zq[0:n, h * 128:(h + 1) * 128], identity=ident_b[0:n, 0:n]))
                            S.op("dve", [T_pT], [T_qT], lambda e, c0=c0, n=n: e.tensor_copy(
                                out=qT[:, :, c0:c0 + n], in_=pT[:, 0:512].rearrange("p (k c) -> p k c", k=4)[:, :, 0:n]))
                        elif j == 1:
                            if not os.environ.get("K1A"):
                                S.op("dve", [T_pm], [T_zk32], lambda e, pm=pm, n=n: e.tensor_copy(out=zk32[0:n, :], in_=pm[0:n, :]))
                            S.op("act", [T_zk32], [T_zkb], lambda e, pm=pm, n=n: e.copy(out=zkb[0:n, :], in_=zk32[0:n, :]))
                            if not os.environ.get("NOKD"):
                                S.dma("sp", [T_zk32], [], lambda e, c0=c0, n=n: e.dma_start(
                                    out=kd[g0 + c0:g0 + c0 + n, :], in_=zk32[0:n, :]))
                            for h in range(0 if os.environ.get("K1B") else 4):
                                S.op("pe", [T_zkb, T_identb], [T_pT], lambda e, h=h, n=n: e.transpose(
                                    out=pT[:, 512 + h * 128:512 + h * 128 + n], in_=zkb[0:n, h * 128:(h + 1) * 128],
                                    identity=ident_b[0:n, 0:n]))
                            if not os.environ.get("K1C"):
                              S.op("dve", [T_pT], [T_kT], lambda e, kcol=kcol, n=n: e.tensor_copy(
                                out=kT[:, :, kcol:kcol + n], in_=pT[:, 512:1024].rearrange("p (k c) -> p k c", k=4)[:, :, 0:n]))
                        else:
                            S.op("dve", [T_pm], [T_zv32], lambda e, pm=pm, n=n: e.tensor_copy(out=zv32[0:n, :], in_=pm[0:n, :]))
                            S.op("act", [T_zv32], [T_v], lambda e, pm=pm, n=n, blk=blk: e.copy(
                                out=vaug[0:n, blk, :, 0:128], in_=zv32[0:n, :].rearrange("p (h e) -> p h e", h=4)))
                            S.dma("sp", [T_zv32], [], lambda e, c0=c0, n=n: e.dma_start(
                                out=vd[g0 + c0:g0 + c0 + n, :], in_=zv32[0:n, :]))

                if STOP <= 2:
                    continue
                for cb in range(4):
                    pa, T_pa = next_pM()
                    pg, T_pg = next_pM()
                    for kc in range(8):
                        S.op("pe", [T_fT, T_win], [T_pa], lambda e, pa=pa, kc=kc, cb=cb: e.matmul(
                            pa[:, 0:N], lhsT=w_in_b[:, kc, cb * 128:(cb + 1) * 128], rhs=fT[:, kc, 0:N],
                            start=(kc == 0), stop=(kc == 7)))
                    for kc in range(8):
                        S.op("pe", [T_fT, T_win], [T_pg], lambda e, pg=pg, kc=kc, cb=cb: e.matmul(
                            pg[:, 0:N], lhsT=w_in_b[:, kc, 512 + cb * 128:512 + (cb + 1) * 128], rhs=fT[:, kc, 0:N],
                            start=(kc == 0), stop=(kc == 7)))
                    S.op("act", [T_pg], [T_sig], lambda e, pg=pg: e.activation(out=sig[:, 0:N], in_=pg[:, 0:N], func=AF.Sigmoid))
                    S.op("dve", [T_pa, T_sig], [T_a32], lambda e, pa=pa: e.tensor_tensor(
                        out=a32[:, 0:N], in0=pa[:, 0:N], in1=sig[:, 0:N], op=ALU.mult))
                    S.op("pool", [T_a32], [T_aT], lambda e, cb=cb: e.tensor_copy(out=aT[:, cb, 30:30 + N], in_=a32[:, 0:N]))
                    if last_group:
                        pm, T_pm = pX[0]
                        S.op("pe", [T_a32, T_identf], [T_pm], lambda e, pm=pm, cb=cb: e.transpose(
                            out=pm[0:30, cb * 128:(cb + 1) * 128], in_=a32[:, N - 30:N], identity=ident_f[:]))
                if last_group:
                    pm, T_pm = pX[0]
                    S.op("act", [T_pm], [T_ctail], lambda e, pm=pm: e.copy(out=ctail[0:30, :], in_=pm[0:30, :]))
                    S.dma("sp", [T_ctail], [], lambda e: e.dma_start(out=cd, in_=ctail[0:30, :]))

                if STOP <= 3:
                    continue
                items = []
                for (c0, n) in tiles:
                    qi = (g0 + c0) // 128
                    if kind == "p":
                        far = list(range(0, max(qi - 1, 0)))
                        near = ([(qi - 1, 128, 1)] if qi >= 1 else []) + [(qi, 128, 0)]
                    else:
                        far = list(range(0, 15))
                        near = [(15, 128, 1), (16, 64, 0)]
                    nblk = len(far) + len(near)
                    for h in range(4):
                        for m in range(2):
                            done = 0
                            for f0 in range(0, len(far), 4):
                                chunk = far[f0:f0 + 4]
                                items.append(dict(kind="far", c0=c0, n=n, h=h, m=m, blks=chunk, done=done, nblk=nblk))
                                done += len(chunk)
                            for (blk, nk, bkind) in near:
                                items.append(dict(kind="near", c0=c0, n=n, h=h, m=m, blk=blk, nk=nk, bkind=bkind,
                                                  done=done, nblk=nblk))
                                done += 1
                        items[-1]["head_end"] = True
                    items[-1]["tile_end"] = True

                def emit_qk(k, it):
                    pS_, T_pS_ = pSS[k % 2]
                    PT_, T_PT_ = PTT[k % 2]
                    c0, n, h, m = it["c0"], it["n"], it["h"], it["m"]
                    mrow = slice(m * 64, (m + 1) * 64)
                    if it["kind"] == "far":
                        for j, blk in enumerate(it["blks"]):
                            S.op("pe", [T_kT, T_qT], [T_pS_], lambda e, j=j, blk=blk: e.matmul(
                                pS_[:, j * n:(j + 1) * n], lhsT=kT[mrow, h, blk * 128:(blk + 1) * 128],
                                rhs=qT[mrow, h, c0:c0 + n], start=True, stop=True))
                        cn = len(it["blks"]) * n
                        S.op("act", [T_pS_, T_const], [T_PT_], lambda e: e.activation(
                            out=PT_[:, 0:cn], in_=pS_[:, 0:cn], func=AF.Exp, bias=relb_s[:, 60 + h:61 + h]))
                    else:
                        blk, nk, bkind = it["blk"], it["nk"], it["bkind"]
                        tb_, T_tb_ = TMP[k % 2]
                        S.op("pe", [T_kT, T_qT], [T_pS_], lambda e: e.matmul(
                            pS_[0:nk, 0:n], lhsT=kT[mrow, h, blk * 128:blk * 128 + nk],
                            rhs=qT[mrow, h, c0:c0 + n], start=True, stop=True))
                        S.op("dve", [T_pS_, T_Tb], [T_tb_], lambda e: e.tensor_tensor(
                            out=tb_[0:nk, 0:n], in0=pS_[0:nk, 0:n], in1=Tb[0:nk, h, bkind, 0:n], op=ALU.add))
                        S.op("act", [T_tb_], [T_PT_], lambda e: e.activation(
                            out=PT_[0:nk, 0:n], in_=tb_[0:nk, 0:n], func=AF.Exp))

                def emit_pv(k, it):
                    PT_, T_PT_ = PTT[k % 2]
                    c0, n, h, m = it["c0"], it["n"], it["h"], it["m"]
                    qi_ = (g0 + c0) // 128
                    pO_, T_pO_ = pOO[(qi_ * 4 + h) % 2]
                    nblk = it["nblk"]
                    if it["kind"] == "far":
                        for j, blk in enumerate(it["blks"]):
                            dn = it["done"] + j
                            S.op("pe", [T_PT_, T_v], [T_pO_], lambda e, j=j, blk=blk, dn=dn: e.matmul(
                                pO_[0:n, m, 0:129], lhsT=PT_[:, j * n:(j + 1) * n], rhs=vaug[:, blk, h, 0:129],
                                start=(dn == 0), stop=(dn == nblk - 1)))
                    else:
                        blk, nk = it["blk"], it["nk"]
                        dn = it["done"]
                        S.op("pe", [T_PT_, T_v], [T_pO_], lambda e: e.matmul(
                            pO_[0:n, m, 0:129], lhsT=PT_[0:nk, 0:n], rhs=vaug[0:nk, blk, h, 0:129],
                            start=(dn == 0), stop=(dn == nblk - 1)))
                    if it.get("head_end"):
                        S.op("dve", [T_pO_], [T_astat], lambda e: e.reciprocal(
                            out=astat[0:n, 0:2], in_=pO_[0:n, :, 128:129].rearrange("p a b -> p (a b)")))
                        S.op("dve", [T_astat, T_small], [T_astat], lambda e: e.tensor_tensor(
                            out=astat[0:n, 2:3], in0=astat[0:n, 1:2], in1=neg_lam[0:n, :], op=ALU.mult))
                        S.op("dve", [T_pO_, T_astat], [T_att], lambda e: e.tensor_scalar(
                            out=att[0:n, :], in0=pO_[0:n, 0, 0:128], scalar1=astat[0:n, 0:1], scalar2=None, op0=ALU.mult))
                        S.op("dve", [T_pO_, T_astat, T_att], [T_att], lambda e: e.scalar_tensor_tensor(
                            out=att[0:n, :], in0=pO_[0:n, 1, 0:128], scalar=astat[0:n, 2:3], in1=att[0:n, :],
                            op0=ALU.mult, op1=ALU.add))
                        S.op("act", [T_att], [T_junk, T_astat], lambda e: e.activation(
                            out=junk[0:n, 0:128], in_=att[0:n, :], func=AF.Square, accum_out=astat[0:n, 3:4]))
                        S.op("act", [T_astat, T_eps], [T_astat], lambda e: e.activation(
                            out=astat[0:n, 4:5], in_=astat[0:n, 3:4], func=AF.Sqrt, scale=1.0 / 128, bias=eps_t[0:n, :]))
                        S.op("dve", [T_astat], [T_astat], lambda e: e.reciprocal(out=astat[0:n, 4:5], in_=astat[0:n, 4:5]))
                        S.op("dve", [T_att, T_astat, T_const], [T_attb], lambda e: e.scalar_tensor_tensor(
                            out=attb[0:n, h * 128:(h + 1) * 128], in0=att[0:n, :], scalar=astat[0:n, 4:5], in1=gsub_s[0:n, :],
                            op0=ALU.mult, op1=ALU.mult))
                    if it.get("tile_end"):
                        for hh in range(4):
                            S.op("pe", [T_attb, T_identb], [T_pT], lambda e, hh=hh: e.transpose(
                                out=pT[:, hh * 128:hh * 128 + n], in_=attb[0:n, hh * 128:(hh + 1) * 128], identity=ident_b[0:n, 0:n]))
                        S.op("act", [T_pT], [T_cat], lambda e: e.copy(
                            out=catT[:, 4:8, c0:c0 + n], in_=pT[:, 0:512].rearrange("p (k c) -> p k c", k=4)[:, :, 0:n]))

                for k, it in enumerate(items):
                    emit_qk(k, it)
                    if k >= 1:
                        emit_pv(k - 1, items[k - 1])
                emit_pv(len(items) - 1, items[-1])

                for cb in range(4):
                    pm, T_pm = next_pM()
                    for w in range(31):
                        S.op("pe", [T_aT, T_diag], [T_pm], lambda e, pm=pm, w=w, cb=cb: e.matmul(
                            pm[:, 0:N], lhsT=diag[:, w * 4 + cb, :], rhs=aT[:, cb, w:w + N], start=(w == 0), stop=(w == 30)))
                    S.op("act", [T_pm, T_const], [T_y32], lambda e, pm=pm, cb=cb: e.activation(
                        out=y32[:, cb, 0:N], in_=pm[:, 0:N], func=AF.Identity, bias=cvec_s[:, cb:cb + 1]))
                    S.op("act", [T_pm, T_const], [T_ysq], lambda e, pm=pm, cb=cb: e.activation(
                        out=ysq[:, cb, 0:N], in_=pm[:, 0:N], func=AF.Square, bias=cvec_s[:, cb:cb + 1]))
                    S.op("pool", [T_y32], [T_ybf], lambda e, cb=cb: e.tensor_copy(out=ybf[:, cb, 0:N], in_=y32[:, cb, 0:N]))
                p1, T_p1 = pX[0]
                p2, T_p2 = pX[1]
                for cb in range(4):
                    S.op("pe", [T_ybf, T_ones], [T_p1], lambda e, cb=cb: e.matmul(
                        p1[:, 0:N], lhsT=ones_b[:], rhs=ybf[:, cb, 0:N], start=(cb == 0), stop=(cb == 3)))
                for cb in range(4):
                    S.op("pe", [T_ysq, T_ones], [T_p2], lambda e, cb=cb: e.matmul(
                        p2[:, 0:N], lhsT=ones_b[:], rhs=ysq[:, cb, 0:N], start=(cb == 0), stop=(cb == 3)))
                S.op("dve", [T_p1], [T_mu], lambda e: e.tensor_scalar(
                    out=mu[:, 0:N], in0=p1[:, 0:N], scalar1=1.0 / 512, scalar2=None, op0=ALU.mult))
                S.op("dve", [T_mu], [T_rs], lambda e: e.tensor_tensor(out=rs[:, 0:N], in0=mu[:, 0:N], in1=mu[:, 0:N], op=ALU.mult))
                S.op("dve", [T_p2, T_rs], [T_rs], lambda e: e.scalar_tensor_tensor(
                    out=rs[:, 0:N], in0=p2[:, 0:N], scalar=1.0 / 512, in1=rs[:, 0:N], op0=ALU.mult, op1=ALU.subtract))
                S.op("act", [T_rs, T_eps], [T_rs], lambda e: e.activation(
                    out=rs[:, 0:N], in_=rs[:, 0:N], func=AF.Sqrt, bias=eps_t[:, :]))
                S.op("dve", [T_rs], [T_rs], lambda e: e.reciprocal(out=rs[:, 0:N], in_=rs[:, 0:N]))
                for cb in range(4):
                    S.op("dve", [T_y32, T_mu], [T_y32], lambda e, cb=cb: e.tensor_tensor(
                        out=y32[:, cb, 0:N], in0=y32[:, cb, 0:N], in1=mu[:, 0:N], op=ALU.subtract))
                    S.op("pool", [T_y32, T_rs], [T_y32], lambda e, cb=cb: e.tensor_tensor(
                        out=y32[:, cb, 0:N], in0=y32[:, cb, 0:N], in1=rs[:, 0:N], op=ALU.mult))
                    S.op("act", [T_y32, T_const], [T_cat], lambda e, cb=cb: e.activation(
                        out=catT[:, cb, 0:N], in_=y32[:, cb, 0:N], func=AF.Silu,
                        scale=cvec_s[:, 4 + cb:5 + cb], bias=cvec_s[:, 8 + cb:9 + cb]))
                if not last_group:
                    S.op("pool", [T_aT], [T_aT], lambda e: e.tensor_copy(out=aT[:, :, 0:30], in_=aT[:, :, N:N + 30]))

                if STOP <= 5:
                    continue
                for (c0, n) in tiles:
                    S.dma("sp", [], [T_xr], lambda e, c0=c0, n=n: e.dma_start(out=xr[0:n, :], in_=xd[g0 + c0:g0 + c0 + n, :]))
                    for hf in range(2):
                        po, T_po = pX[hf]
                        for kc in range(8):
                            S.op("pe", [T_cat, T_wout], [T_po], lambda e, po=po, kc=kc, hf=hf, c0=c0, n=n: e.matmul(
                                po[0:n, :], lhsT=catT[:, kc, c0:c0 + n], rhs=w_out_b[:, kc, hf * 512:(hf + 1) * 512],
                                start=(kc == 0), stop=(kc == 7)))
                        S.op("dve", [T_po, T_xr], [T_xr], lambda e, po=po, hf=hf, n=n: e.tensor_tensor(
                            out=xr[0:n, hf * 512:(hf + 1) * 512], in0=po[0:n, :], in1=xr[0:n, hf * 512:(hf + 1) * 512], op=ALU.add))
                    S.dma("sp", [T_xr], [], lambda e, c0=c0, n=n: e.dma_start(
                        out=x1d[tok0 + g0 + c0:tok0 + g0 + c0 + n, :], in_=xr[0:n, :]))
                    rms_to_bf16(n, xr, T_xr, hb, T_hb, 2)
                    transpose_to_fT(n, hb, T_hb, c0)
                for kc in range(8):
                    S.dma("sp", [T_fT], [], lambda e, kc=kc: e.dma_start(
                        out=h2Td[kc, :, tok0 + g0:tok0 + g0 + N], in_=fT[:, kc, 0:N]))

        S.barrier()
        st1.close()
        st2 = ExitStack()
        st.enter_context(st2)
        cur[0] = st2
        TG = 256
        NCH = 64
        if with_peer:
            ijwd = nc.dram_tensor("ijwd", [128, 3, NT], F32, kind="Internal").ap()
            uscr_t = nc.dram_tensor("uscr", [NCH, 128, 8 * 256], BF16, kind="Internal").ap()
            vscr_t = nc.dram_tensor("vscr", [NCH, 128, 2 * DM], BF16, kind="Internal").ap()
            uscr = [uscr_t[ic].rearrange("p (k e) -> p k e", k=8) for ic in range(NCH)]
            vscr = [vscr_t[ic].rearrange("p (b d) -> p b d", b=2) for ic in range(NCH)]
            T_uscr = [S.tile(f"uscr{ic}") for ic in range(NCH)]
            T_vscr = [S.tile(f"vscr{ic}") for ic in range(NCH)]
            T_ijwd = S.tile("ijwd")

            gfin_s, T_gfin = sb("gfin_s", [128, DM], F32)
            iota_b, T_iotab = sb("iota_b", [128, 128], BF16)
            T_c2 = S.tile("consts2")
            S.dma("sp", [], [T_c2], lambda e: e.dma_start(out=gfin_s[:], in_=gfin.to_broadcast([128, DM])))
            S.op("dve", [T_iotar], [T_iotab], lambda e: e.tensor_copy(out=iota_b[:], in_=iota_r[:]))

            st2a = ExitStack()
            cur[0] = st2a
            TA = 512
            wq_b, T_wq = sb("wq_b", [128, 8, 2048], BF16)
            keys_b, T_keys = sb("keys_b", [128, 16, 128], BF16)
            gffn_s, T_gffn = sb("gffn_s", [128, 8], F32)
            sg0, T_sg0 = sb("sg0", [128, 1024], F32)
            sg1, T_sg1 = sb("sg1", [128, 1024], F32)
            h2g, T_h2g = sb("h2g", [128, 8, TA], BF16)
            qryT, T_qry = sb("qryT", [128, 16, TA], BF16)
            s_sb, T_ssb = sb("s_sb", [128, 16, 128], F32)
            wk, T_wk = sb("wk", [128, 256], F32)
            A_, T_A = sb("A_", [128, 16, 16], F32)
            Iu, T_Iu = sb("Iu", [128, 16, 16], U32)
            If, T_If = sb("If", [128, 16, 16], F32)
            cand, T_cand = sb("cand", [128, 8, 256], F32)
            C_, T_C = sb("C_", [128, 8, 16], F32)
            pos, T_pos = sb("pos", [128, 8, 16], U32)
            ku, T_ku = sb("ku", [128, 2, 128], U32)
            kf, T_kf = sb("kf", [128, 2, 128], F32)
            E_, T_E = sb("E_", [128, 8, 16], F32)
            gst, T_gst = sb("gst", [128, 32], F32)
            oh, T_oh = sb("oh", [128, 8, 16, 16], F32)
            ijw, T_ijw = sb("ijw", [128, 3, 128], F32)
            ijT = [sb(f"ijT{i}", [128, 3, 128], F32) for i in range(2)]
            stu = [sb(f"stu{i}", [128, 8, 256], BF16) for i in range(2)]
            stv = [sb(f"stv{i}", [128, 2, DM], BF16) for i in range(2)]
            pGa = [ps(f"pGa{i}", [128, 512], F32) for i in range(4)]
            pga_i = [0]

            def next_pGa():
                pga_i[0] += 1
                return pGa[pga_i[0] % 4]

            S.dma("sp", [], [T_c2], lambda e: e.dma_start(out=gffn_s[:], in_=gffn))
            S.dma("pool", [], [T_keys], lambda e: e.dma_start(out=keys_b[:], in_=keysT.rearrange("r d n -> d r n")))
            sgs = [(sg0, T_sg0), (sg1, T_sg1)]
            ii = 0
            for kc in range(8):
                for hf in range(2):
                    stt, T_s = sgs[ii % 2]
                    ii += 1
                    S.dma("sp", [], [T_s], lambda e, stt=stt, kc=kc, hf=hf: e.dma_start(
                        out=stt[:], in_=wq[kc * 128:(kc + 1) * 128, hf * 1024:(hf + 1) * 1024]))
                    S.op("dve", [T_s, T_c2], [T_wq], lambda e, stt=stt, kc=kc, hf=hf: e.tensor_scalar(
                        out=wq_b[:, kc, hf * 1024:(hf + 1) * 1024], in0=stt[:], scalar1=gffn_s[:, kc:kc + 1],
                        scalar2=None, op0=ALU.mult))

            conv_i = [0]

            def convert_chunk():
                ic = conv_i[0]
                if ic >= NCH:
                    return
                conv_i[0] += 1
                (su, T_su), (sv, T_sv) = stu[ic % 2], stv[ic % 2]
                for k4 in range(2):
                    S.dma("pool", [], [T_su], lambda e, k4=k4: e.dma_start(
                        out=su[:, k4 * 4:(k4 + 1) * 4, :],
                        in_=uT[k4 * 512:(k4 + 1) * 512, ic * 256:(ic + 1) * 256].rearrange("(k p) e -> p k e", p=128)))
                S.dma("pool", [], [T_sv], lambda e: e.dma_start(
                    out=sv[:], in_=pv[ic * 256:(ic + 1) * 256, :].rearrange("(b p) d -> p b d", p=128)))
                S.dma("sp", [T_su], [T_uscr[ic]], lambda e: e.dma_start(out=uscr[ic], in_=su[:]), key=T_su)
                S.dma("sp", [T_sv], [T_vscr[ic]], lambda e: e.dma_start(out=vscr[ic], in_=sv[:]), key=T_sv)

            tile_ctr = 0
            for t0 in range(0, NT, TA):
                N = min(TA, NT - t0)
                tiles = [(c0, min(128, N - c0)) for c0 in range(0, N, 128)]
                S.dma("sp", [], [T_h2g], lambda e: e.dma_start(
                    out=h2g[:, :, 0:N], in_=h2Td[:, :, t0:t0 + N].rearrange("k p t -> p k t")))
                for blk in range(16):
                    pg, T_pg = next_pGa()
                    for kc in range(8):
                        S.op("pe", [T_wq, T_h2g], [T_pg], lambda e, pg=pg, kc=kc, blk=blk: e.matmul(
                            pg[:, 0:N], lhsT=wq_b[:, kc, blk * 128:(blk + 1) * 128], rhs=h2g[:, kc, 0:N],
                            start=(kc == 0), stop=(kc == 7)))
                    S.op("act", [T_pg], [T_qry], lambda e, pg=pg, blk=blk: e.copy(out=qryT[:, blk, 0:N], in_=pg[:, 0:N]))
                for (c0, n) in tiles:
                    convert_chunk()
                    convert_chunk()
                    for q4 in range(4):
                        pg, T_pg = next_pGa()
                        for j in range(4):
                            rp = q4 * 4 + j
                            S.op("pe", [T_qry, T_keys], [T_pg], lambda e, pg=pg, j=j, rp=rp: e.matmul(
                                pg[0:n, j * 128:(j + 1) * 128], lhsT=qryT[:, rp, c0:c0 + n], rhs=keys_b[:, rp, :],
                                start=True, stop=True))
                        S.op("act", [T_pg], [T_ssb], lambda e, pg=pg, q4=q4: e.copy(
                            out=s_sb[0:n, q4 * 4:(q4 + 1) * 4, :], in_=pg[0:n, :].rearrange("p (a b) -> p a b", a=4)))
                    for rp in range(16):
                        S.op("dve", [T_ssb], [T_A], lambda e, rp=rp: e.max(out=A_[0:n, rp, 0:8], in_=s_sb[0:n, rp, :]))
                        S.op("dve", [T_ssb, T_A], [T_Iu], lambda e, rp=rp: e.max_index(
                            out=Iu[0:n, rp, 0:8], in_max=A_[0:n, rp, 0:8], in_values=s_sb[0:n, rp, :]))
                        S.op("dve", [T_ssb, T_A], [T_wk], lambda e, rp=rp: e.match_replace(
                            out=wk[0:n, 0:128], in_to_replace=A_[0:n, rp, 0:8], in_values=s_sb[0:n, rp, :], imm_value=-1e30))
                        S.op("dve", [T_wk], [T_A], lambda e, rp=rp: e.max(out=A_[0:n, rp, 8:16], in_=wk[0:n, 0:128]))
                        S.op("dve", [T_wk, T_A], [T_Iu], lambda e, rp=rp: e.max_index(
                            out=Iu[0:n, rp, 8:16], in_max=A_[0:n, rp, 8:16], in_values=wk[0:n, 0:128]))
                    S.op("dve", [T_Iu], [T_If], lambda e: e.tensor_copy(out=If[0:n], in_=Iu[0:n]))
                    A4 = A_[0:n].rearrange("p (r a) k -> p r a k", a=2)
                    I4 = If[0:n].rearrange("p (r a) k -> p r a k", a=2)
                    S.op("dve", [T_A], [T_cand], lambda e: e.tensor_tensor(
                        out=cand[0:n].rearrange("p r (a b) -> p r a b", a=16),
                        in0=A4[:, :, 0, :].unsqueeze(3).to_broadcast([n, 8, 16, 16]),
                        in1=A4[:, :, 1, :].unsqueeze(2).to_broadcast([n, 8, 16, 16]), op=ALU.add))
                    for r in range(8):
                        S.op("dve", [T_cand], [T_C], lambda e, r=r: e.max(out=C_[0:n, r, 0:8], in_=cand[0:n, r, :]))
                        S.op("dve", [T_cand, T_C], [T_pos], lambda e, r=r: e.max_index(
                            out=pos[0:n, r, 0:8], in_max=C_[0:n, r, 0:8], in_values=cand[0:n, r, :]))
                        S.op("dve", [T_cand, T_C], [T_wk], lambda e, r=r: e.match_replace(
                            out=wk[0:n, :], in_to_replace=C_[0:n, r, 0:8], in_values=cand[0:n, r, :], imm_value=-1e30))
                        S.op("dve", [T_wk], [T_C], lambda e, r=r: e.max(out=C_[0:n, r, 8:16], in_=wk[0:n, :]))
                        S.op("dve", [T_wk, T_C], [T_pos], lambda e, r=r: e.max_index(
                            out=pos[0:n, r, 8:16], in_max=C_[0:n, r, 8:16], in_values=wk[0:n, :]))
                    S.op("dve", [T_C], [T_gst], lambda e: e.tensor_scalar(
                        out=gst[0:n, 0:8], in0=C_[0:n, :, 0], scalar1=-1.0, scalar2=None, op0=ALU.mult))
                    for r in range(8):
                        S.op("act", [T_C, T_gst], [T_E, T_gst], lambda e, r=r: e.activation(
                            out=E_[0:n, r, :], in_=C_[0:n, r, :], func=AF.Exp, bias=gst[0:n, r:r + 1],
                            accum_out=gst[0:n, 8 + r:9 + r]))
                    S.op("dve", [T_gst], [T_gst], lambda e: e.reciprocal(out=gst[0:n, 16:24], in_=gst[0:n, 8:16]))
                    S.op("dve", [T_E, T_gst], [T_ijw], lambda e: e.tensor_tensor(
                        out=ijw[0:n, 2, :].rearrange("p (r k) -> p r k", r=8), in0=E_[0:n],
                        in1=gst[0:n, 16:24].unsqueeze(2).to_broadcast([n, 8, 16]), op=ALU.mult))
                    S.op("dve", [T_pos], [T_ku], lambda e: e.tensor_single_scalar(
                        out=ku[0:n, 0, :], in_=pos[0:n].rearrange("p r k -> p (r k)"), scalar=4, op=ALU.logical_shift_right))
                    S.op("dve", [T_pos], [T_ku], lambda e: e.tensor_single_scalar(
                        out=ku[0:n, 1, :], in_=pos[0:n].rearrange("p r k -> p (r k)"), scalar=15, op=ALU.bitwise_and))
                    S.op("dve", [T_ku], [T_kf], lambda e: e.tensor_copy(out=kf[0:n], in_=ku[0:n]))
                    for a in range(2):
                        S.op("dve", [T_kf, T_iotar], [T_oh], lambda e, a=a: e.tensor_tensor(
                            out=oh[0:n],
                            in0=kf[0:n, a, :].rearrange("p (r k) -> p r k", r=8).unsqueeze(3).to_broadcast([n, 8, 16, 16]),
                            in1=iota_r[0:n, 0:16].unsqueeze(1).unsqueeze(1).to_broadcast([n, 8, 16, 16]), op=ALU.is_equal))
                        S.op("dve", [T_oh, T_If], [T_oh], lambda e, a=a: e.tensor_tensor(
                            out=oh[0:n], in0=oh[0:n],
                            in1=I4[:, :, a, :].unsqueeze(2).to_broadcast([n, 8, 16, 16]), op=ALU.mult))
                        S.op("dve", [T_oh], [T_ijw], lambda e, a=a: e.reduce_sum(
                            out=ijw[0:n, a, :].rearrange("p (r k) -> p r k", r=8), in_=oh[0:n], axis=AX.X))
                    pg, T_pg = next_pGa()
                    for a in range(3):
                        S.op("pe", [T_ijw, T_identf], [T_pg], lambda e, pg=pg, a=a: e.transpose(
                            out=pg[:, a * 128:a * 128 + n], in_=ijw[0:n, a, :], identity=ident_f[0:n, 0:n]))
                    (it_, T_it) = ijT[tile_ctr % 2]
                    tile_ctr += 1
                    S.op("act", [T_pg], [T_it], lambda e, pg=pg, it_=it_: e.copy(
                        out=it_[:, :, 0:n], in_=pg[:, 0:384].rearrange("p (a t) -> p a t", a=3)[:, :, 0:n]))
                    S.dma("sp", [T_it], [T_ijwd], lambda e, it_=it_: e.dma_start(
                        out=ijwd[:, :, t0 + c0:t0 + c0 + n], in_=it_[:, :, 0:n]), key=T_it)
            while conv_i[0] < NCH:
                convert_chunk()
            S.barrier()
            st2a.close()

            st2b = ExitStack()
            st.enter_context(st2b)
            cur[0] = st2b
            Gall = [sb(f"Gall{i}", [128, 128, TG], BF16) for i in range(2)]
            ubuf = [sb(f"ubuf{i}", [128, 8, 256], BF16) for i in range(2)]
            vbuf = [sb(f"vbuf{i}", [128, 2, DM], BF16) for i in range(3)]
            h2g2 = [sb(f"h2g2_{i}", [128, 8, TG], BF16) for i in range(2)]
            ijg = [sb(f"ijg{i}", [128, 3, TG], F32) for i in range(2)]
            P4 = [sb(f"P4_{i}", [128, 4, 128], BF16) for i in range(3)]
            Q4 = [sb(f"Q4_{i}", [128, 4, 128], BF16) for i in range(3)]
            gbuf = [sb(f"gbuf{i}", [128, TG], F32) for i in range(3)]
            cbuf = [sb(f"cbuf{i}", [128, TG], BF16) for i in range(3)]
            x2, T_x2 = sb("x2", [128, DM], F32)
            junk2, T_junk2 = sb("junk2", [128, DM], BF16)
            st2s, T_st2s = sb("st2s", [128, 4], F32)
            pY = [[ps(f"pY{t}{h}", [128, 512], F32) for h in range(2)] for t in range(2)]
            pA = [ps(f"pA{i}", [128, 512], F32) for i in range(3)]
            pG = [ps(f"pG{i}", [128, 512], F32) for i in range(1)]

            groups = [(t0, min(TG, NT - t0)) for t0 in range(0, NT, TG)]
            MAXG = int(os.environ.get("KGROUPS", "999"))
            groups = groups[:MAXG]
            NG = len(groups)
            MULT_ENG = os.environ.get("KMULT", "pool")

            def gtiles(g):
                N = groups[g][1]
                return [(c0, min(128, N - c0)) for c0 in range(0, N, 128)]

            def load_group(g):
                t0, N = groups[g]
                (hg, T_hg), (ij, T_ij) = h2g2[g % 2], ijg[g % 2]
                S.dma("sp", [], [T_hg], lambda e: e.dma_start(
                    out=hg[:, :, 0:N], in_=h2Td[:, :, t0:t0 + N].rearrange("k p t -> p k t")))
                S.dma("sp", [T_ijwd], [T_ij], lambda e: e.dma_start(out=ij[:, :, 0:N], in_=ijwd[:, :, t0:t0 + N]))

            def p10_dve(g, b):
                (ij, T_ij) = ijg[g % 2]
                (p4, T_p4), (q4_, T_q4) = P4[b % 3], Q4[b % 3]
                for u in range(4):
                    t = b * 4 + u
                    S.op("dve", [T_ij, T_iotab], [T_p4], lambda e, u=u, t=t: e.tensor_scalar(
                        out=p4[:, u, :], in0=iota_b[:], scalar1=ij[:, 0, t:t + 1], scalar2=None, op0=ALU.is_equal))
                    S.op("dve", [T_ij, T_iotab], [T_q4], lambda e, u=u, t=t: e.tensor_scalar(
                        out=q4_[:, u, :], in0=iota_b[:], scalar1=ij[:, 1, t:t + 1], scalar2=ij[:, 2, t:t + 1],
                        op0=ALU.is_equal, op1=ALU.mult))

            def p10_pe(g, b):
                (p4, T_p4), (q4_, T_q4) = P4[b % 3], Q4[b % 3]
                (ga, T_ga) = Gall[g % 2]
                pg, T_pg = pG[0]
                for u in range(4):
                    S.op("pe", [T_p4, T_q4], [T_pg], lambda e, u=u: e.matmul(
                        pg[:, u * 128:(u + 1) * 128], lhsT=q4_[:, u, :], rhs=p4[:, u, :], start=True, stop=True))
                S.op("act", [T_pg], [T_ga], lambda e: e.copy(
                    out=ga[:, :, b * 4:b * 4 + 4], in_=pg[:, :].rearrange("p (t i) -> p i t", t=4)))

            class P10:
                def __init__(self, g):
                    self.g = g
                    self.nb = groups[g][1] // 4
                    self.d = 0
                    self.p = 0

                def step(self):
                    if self.d < self.nb:
                        p10_dve(self.g, self.d)
                        self.d += 1
                        if self.d - self.p >= 3:
                            p10_pe(self.g, self.p)
                            self.p += 1
                    elif self.p < self.nb:
                        p10_pe(self.g, self.p)
                        self.p += 1

                def flush(self):
                    while self.p < self.nb:
                        if self.d < self.nb and self.d - self.p < 3:
                            p10_dve(self.g, self.d)
                            self.d += 1
                        else:
                            p10_pe(self.g, self.p)
                            self.p += 1

            chunk_ctr = [0]

            def load_chunk(ic):
                c = chunk_ctr[0]
                chunk_ctr[0] += 1
                (ub, T_ub), (vb, T_vb) = ubuf[c % 2], vbuf[c % 3]
                S.dma("sp", [T_uscr[ic]], [T_ub], lambda e: e.dma_start(out=ub[:], in_=uscr[ic]))
                S.dma("sp", [T_vscr[ic]], [T_vb], lambda e: e.dma_start(out=vb[:], in_=vscr[ic]))

            def stage_u(g, i):
                N = groups[g][1]
                c = g * NCH + i // 2
                ib = i % 2
                (ub, T_ub) = ubuf[c % 2]
                (hg, T_hg) = h2g2[g % 2]
                (ga, T_ga) = Gall[g % 2]
                gidx = g * 128 + i
                pa, T_pa = pA[gidx % 3]
                gb, T_gb = gbuf[gidx % 3]
                cb_, T_cb = cbuf[gidx % 3]
                for kc in range(8):
                    S.op("pe", [T_ub, T_hg], [T_pa], lambda e, kc=kc: e.matmul(
                        pa[:, 0:N], lhsT=ub[:, kc, ib * 128:(ib + 1) * 128], rhs=hg[:, kc, 0:N],
                        start=(kc == 0), stop=(kc == 7)))
                S.op("act", [T_pa], [T_gb], lambda e: e.activation(out=gb[:, 0:N], in_=pa[:, 0:N], func=AF.Gelu))
                S.op(MULT_ENG, [T_gb, T_ga], [T_cb], lambda e: e.tensor_tensor(
                    out=cb_[:, 0:N], in0=gb[:, 0:N], in1=ga[:, i, 0:N], op=ALU.mult))

            def stage_v(g, i):
                c = g * NCH + i // 2
                ib = i % 2
                (vb, T_vb) = vbuf[c % 3]
                cb_, T_cb = cbuf[(g * 128 + i) % 3]
                for ti, (c0, n) in enumerate(gtiles(g)):
                    for hf in range(2):
                        py, T_py = pY[ti][hf]
                        S.op("pe", [T_cb, T_vb], [T_py], lambda e, py=py, c0=c0, n=n, hf=hf: e.matmul(
                            py[0:n, :], lhsT=cb_[:, c0:c0 + n], rhs=vb[:, ib, hf * 512:(hf + 1) * 512],
                            start=(i == 0), stop=(i == 127)))

            def epilogue(g):
                t0, N = groups[g]
                for ti, (c0, n) in enumerate(gtiles(g)):
                    tg = t0 + c0
                    S.dma("sp", [], [T_x2], lambda e, tg=tg, n=n: e.dma_start(out=x2[0:n, :], in_=x1d[tg:tg + n, :]))
                    for hf in range(2):
                        py, T_py = pY[ti][hf]
                        S.op("dve", [T_py, T_x2], [T_x2], lambda e, py=py, hf=hf, n=n: e.tensor_tensor(
                            out=x2[0:n, hf * 512:(hf + 1) * 512], in0=py[0:n, :], in1=x2[0:n, hf * 512:(hf + 1) * 512], op=ALU.add))
                    S.op("act", [T_x2], [T_junk2, T_st2s], lambda e, n=n: e.activation(
                        out=junk2[0:n, :], in_=x2[0:n, :], func=AF.Square, accum_out=st2s[0:n, 0:1]))
                    S.op("act", [T_st2s, T_eps], [T_st2s], lambda e, n=n: e.activation(
                        out=st2s[0:n, 1:2], in_=st2s[0:n, 0:1], func=AF.Sqrt, scale=1.0 / DM, bias=eps_t[0:n, :]))
                    S.op("dve", [T_st2s], [T_st2s], lambda e, n=n: e.reciprocal(out=st2s[0:n, 1:2], in_=st2s[0:n, 1:2]))
                    S.op("dve", [T_x2, T_st2s, T_c2], [T_x2], lambda e, n=n: e.scalar_tensor_tensor(
                        out=x2[0:n, :], in0=x2[0:n, :], scalar=st2s[0:n, 1:2], in1=gfin_s[0:n, :], op0=ALU.mult, op1=ALU.mult))
                    if tg < n_pseq * SEQ:
                        dst = yp[tg // SEQ][tg % SEQ:tg % SEQ + n, :]
                    else:
                        dst = ys[0:n, :]
                    S.dma("sp", [T_x2], [], lambda e, dst=dst, n=n: e.dma_start(out=dst, in_=x2[0:n, :]))

            load_group(0)
            load_chunk(0)
            pz = P10(0)
            pz.flush()
            seq = [(g, i) for g in range(NG) for i in range(128)]
            nxt = None

            def emit_v(idx):
                g_, i_ = seq[idx]
                stage_v(g_, i_)
                if i_ == 127:
                    epilogue(g_)

            for idx, (g, i) in enumerate(seq):
                if i == 0:
                    nxt = None
                    if g + 1 < NG:
                        load_group(g + 1)
                        nxt = P10(g + 1)
                stage_u(g, i)
                if idx >= 2:
                    emit_v(idx - 2)
                if i % 2 == 0:
                    ic_next = i // 2 + 1
                    if ic_next < NCH:
                        load_chunk(ic_next)
                    elif g + 1 < NG:
                        load_chunk(0)
                if nxt is not None and i % 2 == 1:
                    nxt.step()
                if i == 127 and nxt is not None:
                    nxt.flush()
            emit_v(len(seq) - 2)
            emit_v(len(seq) - 1)

        S.barrier()
        S.finish("sp")
        print("ops per engine:", S.nops, "sems:", S.nsem)
    return nc


def _prep_shared(inp):
    f = lambda a: np.ascontiguousarray(np.asarray(a, dtype=np.float32))
    sh = {}
    sh["w_in"] = f(inp["w_in"][0])
    sh["gmix"] = f(inp["g_mix"][0].reshape(8, 128).T)
    sh["convw"] = f(inp["conv_w"][0].reshape(31, 4, 128).transpose(2, 1, 0))
    sh["cvec"] = f(np.concatenate([inp["conv_b"][0].reshape(4, 128).T, inp["conv_ln_g"][0].reshape(4, 128).T,
                                   inp["conv_ln_b"][0].reshape(4, 128).T], axis=1))
    sh["lam"] = f(np.stack([inp["lambda_q1"][0], inp["lambda_k1"][0], inp["lambda_q2"][0], inp["lambda_k2"][0]]).reshape(1, 256))
    sh["subg"] = f(inp["subln_g"][0].reshape(1, 128))
    sh["relb"] = f(inp["rel_bias"].reshape(1, 128))
    sh["w_out"] = f(inp["w_out"][0])
    sh["gffn"] = f(inp["g_ffn"][0].reshape(8, 128).T)
    sh["wq"] = f(inp["w_query"][0])
    sh["keysT"] = f(inp["sub_keys"][0].reshape(16, 128, 128).transpose(0, 2, 1))
    sh["uT"] = f(inp["peer_u"][0].T)
    sh["pv"] = f(inp["peer_v"][0])
    sh["gfin"] = f(inp["g_final"].reshape(1, DM))
    sh["bkc"] = _bucket_tiles()
    return sh


def kernel(**inp):
    f = lambda a: np.ascontiguousarray(np.asarray(a, dtype=np.float32))
    sh = _prep_shared(inp)
    nc = build_program(2)
    in_maps = []
    for c in range(NCORES):
        m = dict(sh)
        m["xp"] = f(inp["x_prompt"][2 * c:2 * c + 2])
        m["xs"] = f(inp["x_sample"][c])
        m["ckT"] = f(np.asarray(inp["cache_k"][0, c]).reshape(SEQ, 4, 128).transpose(1, 2, 0))
        m["cv"] = f(np.asarray(inp["cache_v"][0, c]).reshape(SEQ, 512))
        m["scT"] = f(np.asarray(inp["state_conv"][0, c]).reshape(30, 4, 128).transpose(2, 1, 0))
        in_maps.append(m)
    res = run_bass_kernel_spmd(nc, in_maps, core_ids=list(range(NCORES)))
    R = res.results
    y_prompt = np.concatenate([r["yp"] for r in R], axis=0)
    y_sample = np.stack([r["ys"] for r in R], axis=0)
    k_prompt = np.concatenate([r["kp"] for r in R], axis=0).reshape(1, 16, SEQ, 4, 2, 64)
    v_prompt = np.concatenate([r["vp"] for r in R], axis=0).reshape(1, 16, SEQ, 4, 128)
    c_prompt = np.concatenate([r["cp"] for r in R], axis=0).reshape(1, 16, 30, 512)
    k_sample = np.stack([r["ks"] for r in R], axis=0).reshape(1, 8, 64, 4, 2, 64)
    v_sample = np.stack([r["vs"] for r in R], axis=0).reshape(1, 8, 64, 4, 128)
    c_sample = np.stack([r["cs"] for r in R], axis=0).reshape(1, 8, 30, 512)
    return (y_prompt, y_sample, k_prompt, v_prompt, c_prompt, k_sample, v_sample, c_sample)
```

```python
import math
import os
from contextlib import ExitStack

import numpy as np
import concourse.bass as bass
import concourse.mybir as mybir
from concourse.bass_utils import run_bass_kernel_spmd

F32 = mybir.dt.float32
BF16 = mybir.dt.bfloat16
U32 = mybir.dt.uint32
AF = mybir.ActivationFunctionType
ALU = mybir.AluOpType
AX = mybir.AxisListType

EPS = 1e-6
LAM_INIT = 0.8 - 0.6 * math.exp(-0.3 * 0)
NCORES = 8
SEQ = 2048
DM = 1024
NEXP_SIDE = 128

SEM_LIMIT = 30000


class Counter:
    def __init__(self, S, name):
        self.S = S
        self.name = name
        self.epoch = 0
        self.val = 0
        self.sem = S.new_sem(f"{name}_e0")

    def bump(self, inc):
        if self.val + inc > SEM_LIMIT:
            self.epoch += 1
            self.val = 0
            self.sem = self.S.new_sem(f"{self.name}_e{self.epoch}")
        self.val += inc
        return (self.sem, self.val, self.name, self.epoch)


class Tile:
    __slots__ = ("name", "w", "r", "dmac")

    def __init__(self, name):
        self.name = name
        self.w = None
        self.r = []
        self.dmac = None


class Sched:
    def __init__(self, nc, stack):
        self.nc = nc
        self.stack = stack
        self.nsem = 0
        self.engs = {"pe": nc.tensor, "act": nc.scalar, "dve": nc.vector,
                     "pool": nc.gpsimd, "sp": nc.sync}
        self.cnt = {k: Counter(self, k) for k in self.engs}
        self.known = {k: {} for k in self.engs}
        self.nops = {k: 0 for k in self.engs}
        self.tiles = []

    def new_sem(self, name):
        self.nsem += 1
        return self.stack.enter_context(self.nc.semaphore(f"s{self.nsem}_{name}"))

    def tile(self, name):
        t = Tile(name)
        self.tiles.append(t)
        return t

    def _wait(self, e, ev):
        sem, val, name, epoch = ev
        key = (name, epoch)
        if self.known[e].get(key, 0) >= val:
            return
        self.known[e][key] = val
        self.engs[e].wait_ge(sem, val)

    def _deps(self, reads, writes):
        evs = []
        for t in reads:
            if t.w is not None:
                evs.append(t.w)
        for t in writes:
            if t.w is not None:
                evs.append(t.w)
            evs.extend(t.r)
        return evs

    def op(self, e, reads, writes, fn):
        for ev in self._deps(reads, writes):
            if ev[2] == e:
                if e == "pe":
                    continue
                if ev[3] == self.cnt[e].epoch and self.cnt[e].val - ev[1] >= 2:
                    continue
            self._wait(e, ev)
        ins = fn(self.engs[e])
        ev = self.cnt[e].bump(1)
        ins.then_inc(ev[0], 1)
        self.nops[e] += 1
        self._mark(ev, reads, writes)
        return ev

    def _mark(self, ev, reads, writes):
        k = (ev[2], ev[3])
        for t in reads:
            t.r = [x for x in t.r if (x[2], x[3]) != k]
            t.r.append(ev)
        for t in writes:
            t.w = ev
            t.r = []

    def dma(self, q, reads, writes, fn, key=None):
        kt = key or (writes[0] if writes else reads[0])
        if kt.dmac is None:
            kt.dmac = Counter(self, "d_" + kt.name)
        for ev in self._deps(reads, writes):
            self._wait(q, ev)
        ins = fn(self.engs[q])
        ev = kt.dmac.bump(16)
        ins.then_inc(ev[0], 16)
        self.nops[q] += 1
        self._mark(ev, reads, writes)
        return ev

    def _all_events(self):
        evs = {}
        for t in self.tiles:
            for ev in ([t.w] if t.w else []) + t.r:
                k = (ev[2], ev[3])
                if k not in evs or evs[k][1] < ev[1]:
                    evs[k] = ev
        return evs

    def barrier(self):
        evs = self._all_events()
        for e in self.engs:
            for ev in evs.values():
                self._wait(e, ev)
        for t in self.tiles:
            t.w = None
            t.r = []

    def finish(self, e="sp"):
        for ev in self._all_events().values():
            self._wait(e, ev)


def _bucket_np(rel):
    nb = 16
    max_exact = 8
    ret = np.where(rel > 0, nb, 0)
    n = np.abs(rel)
    nf = np.maximum(n, 1).astype(np.float32)
    large = max_exact + (np.log(nf / max_exact) / math.log(128 / max_exact) * (nb - max_exact)).astype(np.int32)
    large = np.minimum(large, nb - 1)
    return ret + np.where(n < max_exact, n, large)


def _bucket_tiles():
    k = np.arange(128)[:, None]
    q = np.arange(128)[None, :]
    b0 = _bucket_np(k - q).astype(np.float32)
    masked = (k // 64) > (q // 64)
    b0 = np.where(masked, 32.0, b0)
    b1 = _bucket_np(k - q - 128).astype(np.float32)
    return np.stack([b0, b1], axis=1).astype(np.float32)


def build_program(n_pseq=2, with_peer=True, dbg=False):
    nc = bass.Bass("TRN2", target_bir_lowering=False)
    NT = n_pseq * SEQ + 64

    def din(name, shape, dt=F32):
        return nc.dram_tensor(name, list(shape), dt, kind="ExternalInput").ap()

    def dout(name, shape, dt=F32):
        return nc.dram_tensor(name, list(shape), dt, kind="ExternalOutput").ap()

    xp = din("xp", [n_pseq, SEQ, DM])
    xs = din("xs", [64, DM])
    ckT = din("ckT", [4, 128, SEQ])
    cv = din("cv", [SEQ, 512])
    scT = din("scT", [128, 4, 30])
    w_in = din("w_in", [DM, 2560])
    gmix = din("gmix", [128, 8])
    convw = din("convw", [128, 4, 31])
    cvec = din("cvec", [128, 12])
    lam = din("lam", [1, 256])
    subg = din("subg", [1, 128])
    relb = din("relb", [1, 128])
    w_out = din("w_out", [DM, DM])
    gffn = din("gffn", [128, 8])
    wq = din("wq", [DM, 2048])
    keysT = din("keysT", [16, 128, 128])
    uT = din("uT", [DM, 16384])
    pv = din("pv", [16384, DM])
    gfin = din("gfin", [1, DM])
    bkc = din("bkc", [128, 2, 128])

    yp = dout("yp", [n_pseq, SEQ, DM])
    ys = dout("ys", [64, DM])
    kp = dout("kp", [n_pseq, SEQ, 512])
    vp = dout("vp", [n_pseq, SEQ, 512])
    cp = dout("cp", [n_pseq, 30, 512])
    ks = dout("ks", [64, 512])
    vs = dout("vs", [64, 512])
    cs = dout("cs", [30, 512])

    kind_scr = "ExternalOutput" if dbg else "Internal"
    x1d = nc.dram_tensor("x1d", [NT, DM], F32, kind=kind_scr).ap()
    h2Td = nc.dram_tensor("h2Td", [8, 128, NT], BF16, kind="Internal").ap()

    with ExitStack() as st:
        S = Sched(nc, st)

        cur = [st]

        def sb(name, shape, dt):
            return cur[0].enter_context(nc.sbuf_tensor(name, list(shape), dt)), S.tile(name)

        def ps(name, shape, dt):
            return cur[0].enter_context(nc.psum_tensor(name, list(shape), dt)), S.tile(name)

        ident_f, T_identf = sb("ident_f", [128, 128], F32)
        ident_b, T_identb = sb("ident_b", [128, 128], BF16)
        ones_b, T_ones = sb("ones_b", [128, 128], BF16)
        iota_t, T_iota = sb("iota_t", [128, 128], F32)
        T_const = S.tile("consts")

        S.op("pool", [], [T_iota], lambda e: e.iota(iota_t[:], pattern=[[1, 128]], base=0, channel_multiplier=-1,
                                                    allow_small_or_imprecise_dtypes=True))
        S.op("dve", [T_iota], [T_identf], lambda e: e.tensor_scalar(out=ident_f[:], in0=iota_t[:], scalar1=0.0,
                                                                     scalar2=None, op0=ALU.is_equal))
        S.op("dve", [T_identf], [T_identb], lambda e: e.tensor_copy(out=ident_b[:], in_=ident_f[:]))
        S.op("pool", [], [T_ones], lambda e: e.memset(ones_b[:], 1.0))

        eps_t, T_eps = sb("eps_t", [128, 1], F32)
        S.op("pool", [], [T_eps], lambda e: e.memset(eps_t[:], EPS))
        EPS_AP = eps_t
        iota_r, T_iotar = sb("iota_r", [128, 128], F32)
        S.op("pool", [], [T_iotar], lambda e: e.iota(iota_r[:], pattern=[[1, 128]], base=0, channel_multiplier=0,
                                                     allow_small_or_imprecise_dtypes=True))
        st1 = ExitStack()
        cur[0] = st1
        w_in_b, T_win = sb("w_in_b", [128, 8, 2560], BF16)
        w_out_b, T_wout = sb("w_out_b", [128, 8, DM], BF16)
        diag, T_diag = sb("diag", [128, 124, 128], BF16)
        gmix_s, _ = sb("gmix_s", [128, 8], F32)
        convw_s, _ = sb("convw_s", [128, 4, 31], F32)
        cvec_s, _ = sb("cvec_s", [128, 12], F32)
        lam_s, _ = sb("lam_s", [128, 256], F32)
        gsub_s, _ = sb("gsub_s", [128, 128], F32)
        relb_s, _ = sb("relb_s", [128, 128], F32)
        bk_s, _ = sb("bk_s", [128, 2, 128], F32)
        Tb, T_Tb = sb("Tb", [128, 4, 2, 128], F32)
        eqm, T_eqm = sb("eqm", [128, 2, 128], F32)
        small, T_small = sb("small", [128, 16], F32)
        stage0, T_st0 = sb("stage0", [128, 1024], F32)
        stage1, T_st1 = sb("stage1", [128, 1024], F32)

        for dst, src in ((gmix_s, gmix), (convw_s, convw), (cvec_s, cvec), (bk_s, bkc)):
            S.dma("sp", [], [T_const], lambda e, d=dst, s_=src: e.dma_start(out=d[:], in_=s_))
        for dst, src, n in ((lam_s, lam, 256), (gsub_s, subg, 128), (relb_s, relb, 128)):
            S.dma("sp", [], [T_const], lambda e, d=dst, s_=src, n=n: e.dma_start(out=d[:], in_=s_.to_broadcast([128, n])))

        stg = [(stage0, T_st0), (stage1, T_st1)]
        i = 0
        for kc in range(8):
            for (a0, a1) in ((0, 1024), (1024, 2048), (2048, 2560)):
                stt, T_s = stg[i % 2]
                i += 1
                S.dma("sp", [], [T_s], lambda e, stt=stt, kc=kc, a0=a0, a1=a1: e.dma_start(
                    out=stt[:, 0:a1 - a0], in_=w_in[kc * 128:(kc + 1) * 128, a0:a1]))
                S.op("dve", [T_s, T_const], [T_win], lambda e, stt=stt, kc=kc, a0=a0, a1=a1: e.tensor_scalar(
                    out=w_in_b[:, kc, a0:a1], in0=stt[:, 0:a1 - a0], scalar1=gmix_s[:, kc:kc + 1],
                    scalar2=None, op0=ALU.mult))
        for kc in range(8):
            stt, T_s = stg[i % 2]
            i += 1
            S.dma("sp", [], [T_s], lambda e, stt=stt, kc=kc: e.dma_start(
                out=stt[:, 0:DM], in_=w_out[kc * 128:(kc + 1) * 128, :]))
            S.op("act", [T_s], [T_wout], lambda e, stt=stt, kc=kc: e.copy(out=w_out_b[:, kc, :], in_=stt[:, 0:DM]))
        for w in range(31):
            for cb in range(4):
                S.op("dve", [T_const, T_identf], [T_diag], lambda e, w=w, cb=cb: e.tensor_scalar(
                    out=diag[:, w * 4 + cb, :], in0=ident_f[:], scalar1=convw_s[:, cb, w:w + 1], scalar2=None,
                    op0=ALU.mult))
        S.op("dve", [T_const], [T_eqm], lambda e: e.tensor_scalar(
            out=eqm[:], in0=bk_s[:], scalar1=32.0, scalar2=-30000.0, op0=ALU.is_equal, op1=ALU.mult))
        for h in range(4):
            S.op("dve", [T_eqm], [T_Tb], lambda e, h=h: e.tensor_copy(out=Tb[:, h, :, :], in_=eqm[:]))
        for b in range(32):
            S.op("dve", [T_const], [T_eqm], lambda e, b=b: e.tensor_scalar(
                out=eqm[:], in0=bk_s[:], scalar1=float(b), scalar2=None, op0=ALU.is_equal))
            for h in range(4):
                S.op("dve", [T_eqm, T_const, T_Tb], [T_Tb], lambda e, b=b, h=h: e.scalar_tensor_tensor(
                    out=Tb[:, h, :, :], in0=eqm[:], scalar=relb_s[:, b * 4 + h:b * 4 + h + 1], in1=Tb[:, h, :, :],
                    op0=ALU.mult, op1=ALU.add))
        S.op("dve", [T_const], [T_eqm], lambda e: e.tensor_tensor(
            out=eqm[:, 0, :].rearrange("p (a b) -> p a b", a=2), in0=lam_s[:].rearrange("p (a b c) -> p a b c", a=2, b=2)[:, :, 0, :],
            in1=lam_s[:].rearrange("p (a b c) -> p a b c", a=2, b=2)[:, :, 1, :], op=ALU.mult))
        S.op("dve", [T_eqm], [T_small], lambda e: e.reduce_sum(
            out=small[:, 0:2], in_=eqm[:, 0, :].rearrange("p (a b) -> p a b", a=2), axis=AX.X))
        S.op("act", [T_small], [T_small], lambda e: e.activation(out=small[:, 2:4], in_=small[:, 0:2], func=AF.Exp))
        S.op("dve", [T_small], [T_small], lambda e: e.tensor_tensor(
            out=small[:, 4:5], in0=small[:, 3:4], in1=small[:, 2:3], op=ALU.subtract))
        S.op("dve", [T_small], [T_small], lambda e: e.tensor_scalar(
            out=small[:, 4:5], in0=small[:, 4:5], scalar1=-LAM_INIT, scalar2=None, op0=ALU.add))
        S.op("dve", [T_const], [T_const], lambda e: e.tensor_scalar(
            out=gsub_s[:], in0=gsub_s[:], scalar1=1.0 - LAM_INIT, scalar2=None, op0=ALU.mult))
        neg_lam = small[:, 4:5]

        fT, T_fT = sb("fT", [128, 8, 512], BF16)
        xt, T_xt = sb("xt", [128, DM], F32)
        xr, T_xr = xt, T_xt
        junk, T_junk = sb("junk", [128, DM], BF16)
        hb, T_hb = sb("hb", [128, DM], BF16)
        stat, T_stat = sb("stat", [128, 8], F32)
        aT, T_aT = sb("aT", [128, 4, 30 + 512], BF16)
        sig, T_sig = sb("sig", [128, 512], F32)
        a32, T_a32 = sb("a32", [128, 512], F32)
        qT, T_qT = sb("qT", [128, 4, 512], BF16)
        kT, T_kT = sb("kT", [128, 4, SEQ + 64], BF16)
        vaug, T_v = sb("vaug", [128, 17, 4, 130], BF16)
        catT, T_cat = sb("catT", [128, 8, 512], BF16)
        zq, T_zq = sb("zq", [128, 512], BF16)
        zk32, T_zk32 = sb("zk32", [128, 512], F32)
        zkb, T_zkb = sb("zkb", [128, 512], BF16)
        zv32, T_zv32 = sb("zv32", [128, 512], F32)
        PTT = [sb(f"PT{i}", [128, 512], BF16) for i in range(3)]
        TMP = [sb(f"tmpb{i}", [128, 128], F32) for i in range(3)]
        att, T_att = sb("att", [128, 128], F32)
        attb, T_attb = sb("attb", [128, 512], BF16)
        astat, T_astat = sb("astat", [128, 8], F32)
        y32, T_y32 = sb("y32", [128, 4, 512], F32)
        ybf, T_ybf = sb("ybf", [128, 4, 512], BF16)
        ysq, T_ysq = sb("ysq", [128, 4, 512], BF16)
        mu, T_mu = sig, T_sig
        rs, T_rs = a32, T_a32
        ctail, T_ctail = zk32, T_zk32
        cst32, T_cst32 = sb("cst32", [128, 4, 30], F32)

        pT, T_pT = ps("pT", [128, 1024], BF16)
        pM = [ps(f"pM{i}", [128, 512], F32) for i in range(2)]
        pSS = [ps(f"pS{i}", [128, 512], F32) for i in range(2)]
        pO, T_pO = ps("pO", [128, 2, 256], F32)
        pX = [ps(f"pX{i}", [128, 512], F32) for i in range(2)]
        pm_i = [0]
        pOO = [(pO, T_pO), (pM[0][0][:].rearrange("p (a b) -> p a b", a=2), pM[0][1])]
        pSS = pSS + [pX[1]]

        def next_pM():
            pm_i[0] += 1
            return pM[pm_i[0] % 2]

        S.op("pool", [], [T_v], lambda e: e.memset(vaug[:], 1.0))

        def rms_to_bf16(n, src, T_src, dst_b, T_dst, col):
            S.op("act", [T_src], [T_junk, T_stat], lambda e: e.activation(
                out=junk[0:n, :], in_=src[0:n, :], func=AF.Square, accum_out=stat[0:n, col:col + 1]))
            S.op("act", [T_stat], [T_stat], lambda e: e.activation(
                out=stat[0:n, col + 1:col + 2], in_=stat[0:n, col:col + 1], func=AF.Sqrt, scale=1.0 / DM, bias=EPS_AP[0:n, :]))
            S.op("dve", [T_stat], [T_stat], lambda e: e.reciprocal(
                out=stat[0:n, col + 1:col + 2], in_=stat[0:n, col + 1:col + 2]))
            S.op("dve", [T_src, T_stat], [T_dst], lambda e: e.tensor_scalar(
                out=dst_b[0:n, :], in0=src[0:n, :], scalar1=stat[0:n, col + 1:col + 2], scalar2=None, op0=ALU.mult))

        def transpose_to_fT(n, src_b, T_src, c0):
            for kc in range(8):
                S.op("pe", [T_src, T_identb], [T_pT], lambda e, kc=kc: e.transpose(
                    out=pT[:, kc * 128:kc * 128 + n], in_=src_b[0:n, kc * 128:(kc + 1) * 128], identity=ident_b[0:n, 0:n]))
            S.op("act", [T_pT], [T_fT], lambda e: e.copy(
                out=fT[:, :, c0:c0 + n], in_=pT[:].rearrange("p (k c) -> p k c", k=8)[:, :, 0:n]))

        seqs = []
        for s_ in range(n_pseq):
            seqs.append(("p", xp[s_], SEQ, kp[s_], vp[s_], cp[s_], s_ * SEQ))
        seqs.append(("s", xs, 64, ks, vs, cs, n_pseq * SEQ))

        S.barrier()
        import os
        STOP = int(os.environ.get("KSTOP", "99"))
        if STOP <= 0:
            seqs = []

        for (kind, xd, ntok, kd, vd, cd, tok0) in seqs:
            past = SEQ if kind == "s" else 0
            if kind == "p":
                S.op("pool", [], [T_aT], lambda e: e.memset(aT[:, :, 0:30], 0.0))
            else:
                S.dma("sp", [], [T_cst32], lambda e: e.dma_start(out=cst32[:], in_=scT))
                S.op("dve", [T_cst32], [T_aT], lambda e: e.tensor_copy(out=aT[:, :, 0:30], in_=cst32[:]))
                for h in range(4):
                    for hf in range(2):
                        stt, T_s = stg[(h * 2 + hf) % 2]
                        S.dma("sp", [], [T_s], lambda e, stt=stt, h=h, hf=hf: e.dma_start(
                            out=stt[:, 0:1024], in_=ckT[h, :, hf * 1024:(hf + 1) * 1024]))
                        S.op("act", [T_s], [T_kT], lambda e, stt=stt, h=h, hf=hf: e.copy(
                            out=kT[:, h, hf * 1024:(hf + 1) * 1024], in_=stt[:, 0:1024]))
                for blk in range(16):
                    stt, T_s = stg[blk % 2]
                    S.dma("sp", [], [T_s], lambda e, stt=stt, blk=blk: e.dma_start(
                        out=stt[:, 0:512], in_=cv[blk * 128:(blk + 1) * 128, :]))
                    S.op("dve", [T_s], [T_v], lambda e, stt=stt, blk=blk: e.tensor_copy(
                        out=vaug[:, blk, :, 0:128], in_=stt[:, 0:512].rearrange("p (h e) -> p h e", h=4)))

            ngroups = (ntok + 511) // 512
            for g in range(ngroups):
                g0 = g * 512
                N = min(512, ntok - g0)
                tiles = [(c0, min(128, N - c0)) for c0 in range(0, N, 128)]
                last_group = (g == ngroups - 1)

                for (c0, n) in tiles:
                    S.dma("sp", [], [T_xt], lambda e, c0=c0, n=n: e.dma_start(out=xt[0:n, :], in_=xd[g0 + c0:g0 + c0 + n, :]))
                    rms_to_bf16(n, xt, T_xt, hb, T_hb, 0)
                    transpose_to_fT(n, hb, T_hb, c0)

                if STOP <= 1:
                    continue
                for (c0, n) in tiles:
                    blk = (past + g0 + c0) // 128
                    kcol = past + g0 + c0
                    for j in range(int(os.environ.get('KJ', '3'))):
                        pm, T_pm = next_pM()
                        for kc in range(8):
                            S.op("pe", [T_fT, T_win], [T_pm], lambda e, pm=pm, kc=kc, j=j, c0=c0, n=n: e.matmul(
                                pm[0:n, :], lhsT=fT[:, kc, c0:c0 + n], rhs=w_in_b[:, kc, 1024 + j * 512:1024 + (j + 1) * 512],
                                start=(kc == 0), stop=(kc == 7)))
                        if j == 0:
                            S.op("act", [T_pm], [T_zq], lambda e, pm=pm, n=n: e.activation(
                                out=zq[0:n, :], in_=pm[0:n, :], func=AF.Copy, scale=0.125))
                            for h in range(4):
                                S.op("pe", [T_zq, T_identb], [T_pT], lambda e, h=h, n=n: e.transpose(
                                    out=pT[:, h * 128:h * 128 + n], in_=zq[0:n, h * 128:(h + 1) * 128], identity=ident_b[0:n, 0:n]))
                            S.op("dve", [T_pT], [T_qT], lambda e, c0=c0, n=n: e.tensor_copy(
                                out=qT[:, :, c0:c0 + n], in_=pT[:, 0:512].rearrange("p (k c) -> p k c", k=4)[:, :, 0:n]))
                        elif j == 1:
                            if not os.environ.get("K1A"):
                                S.op("dve", [T_pm], [T_zk32], lambda e, pm=pm, n=n: e.tensor_copy(out=zk32[0:n, :], in_=pm[0:n, :]))
                            S.op("act", [T_zk32], [T_zkb], lambda e, pm=pm, n=n: e.copy(out=zkb[0:n, :], in_=zk32[0:n, :]))
                            if not os.environ.get("NOKD"):
                                S.dma("sp", [T_zk32], [], lambda e, c0=c0, n=n: e.dma_start(
                                    out=kd[g0 + c0:g0 + c0 + n, :], in_=zk32[0:n, :]))
                            for h in range(0 if os.environ.get("K1B") else 4):
                                S.op("pe", [T_zkb, T_identb], [T_pT], lambda e, h=h, n=n: e.transpose(
                                    out=pT[:, 512 + h * 128:512 + h * 128 + n], in_=zkb[0:n, h * 128:(h + 1) * 128],
                                    identity=ident_b[0:n, 0:n]))
                            if not os.environ.get("K1C"):
                              S.op("dve", [T_pT], [T_kT], lambda e, kcol=kcol, n=n: e.tensor_copy(
                                out=kT[:, :, kcol:kcol + n], in_=pT[:, 512:1024].rearrange("p (k c) -> p k c", k=4)[:, :, 0:n]))
                        else:
                            S.op("dve", [T_pm], [T_zv32], lambda e, pm=pm, n=n: e.tensor_copy(out=zv32[0:n, :], in_=pm[0:n, :]))
                            S.op("act", [T_zv32], [T_v], lambda e, pm=pm, n=n, blk=blk: e.copy(
                                out=vaug[0:n, blk, :, 0:128], in_=zv32[0:n, :].rearrange("p (h e) -> p h e", h=4)))
                            S.dma("sp", [T_zv32], [], lambda e, c0=c0, n=n: e.dma_start(
                                out=vd[g0 + c0:g0 + c0 + n, :], in_=zv32[0:n, :]))

                if STOP <= 2:
                    continue
                for cb in range(4):
                    pa, T_pa = next_pM()
                    pg, T_pg = next_pM()
                    for kc in range(8):
                        S.op("pe", [T_fT, T_win], [T_pa], lambda e, pa=pa, kc=kc, cb=cb: e.matmul(
                            pa[:, 0:N], lhsT=w_in_b[:, kc, cb * 128:(cb + 1) * 128], rhs=fT[:, kc, 0:N],
                            start=(kc == 0), stop=(kc == 7)))
                    for kc in range(8):
                        S.op("pe", [T_fT, T_win], [T_pg], lambda e, pg=pg, kc=kc, cb=cb: e.matmul(
                            pg[:, 0:N], lhsT=w_in_b[:, kc, 512 + cb * 128:512 + (cb + 1) * 128], rhs=fT[:, kc, 0:N],
                            start=(kc == 0), stop=(kc == 7)))
                    S.op("act", [T_pg], [T_sig], lambda e, pg=pg: e.activation(out=sig[:, 0:N], in_=pg[:, 0:N], func=AF.Sigmoid))
                    S.op("dve", [T_pa, T_sig], [T_a32], lambda e, pa=pa: e.tensor_tensor(
                        out=a32[:, 0:N], in0=pa[:, 0:N], in1=sig[:, 0:N], op=ALU.mult))
                    S.op("pool", [T_a32], [T_aT], lambda e, cb=cb: e.tensor_copy(out=aT[:, cb, 30:30 + N], in_=a32[:, 0:N]))
                    if last_group:
                        pm, T_pm = pX[0]
                        S.op("pe", [T_a32, T_identf], [T_pm], lambda e, pm=pm, cb=cb: e.transpose(
                            out=pm[0:30, cb * 128:(cb + 1) * 128], in_=a32[:, N - 30:N], identity=ident_f[:]))
                if last_group:
                    pm, T_pm = pX[0]
                    S.op("act", [T_pm], [T_ctail], lambda e, pm=pm: e.copy(out=ctail[0:30, :], in_=pm[0:30, :]))
                    S.dma("sp", [T_ctail], [], lambda e: e.dma_start(out=cd, in_=ctail[0:30, :]))

                if STOP <= 3:
                    continue
                items = []
                for (c0, n) in tiles:
                    qi = (g0 + c0) // 128
                    if kind == "p":
                        far = list(range(0, max(qi - 1, 0)))
                        near = ([(qi - 1, 128, 1)] if qi >= 1 else []) + [(qi, 128, 0)]
                    else:
                        far = list(range(0, 15))
                        near = [(15, 128, 1), (16, 64, 0)]
                    nblk = len(far) + len(near)
                    for h in range(4):
                        for m in range(2):
                            done = 0
                            for f0 in range(0, len(far), 4):
                                chunk = far[f0:f0 + 4]
                                items.append(dict(kind="far", c0=c0, n=n, h=h, m=m, blks=chunk, done=done, nblk=nblk))
                                done += len(chunk)
                            for (blk, nk, bkind) in near:
                                items.append(dict(kind="near", c0=c0, n=n, h=h, m=m, blk=blk, nk=nk, bkind=bkind,
                                                  done=done, nblk=nblk))
                                done += 1
                        items[-1]["head_end"] = True
                    items[-1]["tile_end"] = True

                def emit_qk(k, it):
                    pS_, T_pS_ = pSS[k % 3]
                    PT_, T_PT_ = PTT[k % 3]
                    c0, n, h, m = it["c0"], it["n"], it["h"], it["m"]
                    mrow = slice(m * 64, (m + 1) * 64)
                    if it["kind"] == "far":
                        for j, blk in enumerate(it["blks"]):
                            S.op("pe", [T_kT, T_qT], [T_pS_], lambda e, j=j, blk=blk: e.matmul(
                                pS_[:, j * n:(j + 1) * n], lhsT=kT[mrow, h, blk * 128:(blk + 1) * 128],
                                rhs=qT[mrow, h, c0:c0 + n], start=True, stop=True))
                        cn = len(it["blks"]) * n
                        S.op("act", [T_pS_, T_const], [T_PT_], lambda e: e.activation(
                            out=PT_[:, 0:cn], in_=pS_[:, 0:cn], func=AF.Exp, bias=relb_s[:, 60 + h:61 + h]))
                    else:
                        blk, nk, bkind = it["blk"], it["nk"], it["bkind"]
                        tb_, T_tb_ = TMP[k % 3]
                        S.op("pe", [T_kT, T_qT], [T_pS_], lambda e: e.matmul(
                            pS_[0:nk, 0:n], lhsT=kT[mrow, h, blk * 128:blk * 128 + nk],
                            rhs=qT[mrow, h, c0:c0 + n], start=True, stop=True))
                        S.op("dve", [T_pS_, T_Tb], [T_tb_], lambda e: e.tensor_tensor(
                            out=tb_[0:nk, 0:n], in0=pS_[0:nk, 0:n], in1=Tb[0:nk, h, bkind, 0:n], op=ALU.add))
                        S.op("act", [T_tb_], [T_PT_], lambda e: e.activation(
                            out=PT_[0:nk, 0:n], in_=tb_[0:nk, 0:n], func=AF.Exp))

                def emit_pv(k, it):
                    PT_, T_PT_ = PTT[k % 3]
                    c0, n, h, m = it["c0"], it["n"], it["h"], it["m"]
                    qi_ = (g0 + c0) // 128
                    pO_, T_pO_ = pOO[(qi_ * 4 + h) % 2]
                    nblk = it["nblk"]
                    if it["kind"] == "far":
                        for j, blk in enumerate(it["blks"]):
                            dn = it["done"] + j
                            S.op("pe", [T_PT_, T_v], [T_pO_], lambda e, j=j, blk=blk, dn=dn: e.matmul(
                                pO_[0:n, m, 0:129], lhsT=PT_[:, j * n:(j + 1) * n], rhs=vaug[:, blk, h, 0:129],
                                start=(dn == 0), stop=(dn == nblk - 1)))
                    else:
                        blk, nk = it["blk"], it["nk"]
                        dn = it["done"]
                        S.op("pe", [T_PT_, T_v], [T_pO_], lambda e: e.matmul(
                            pO_[0:n, m, 0:129], lhsT=PT_[0:nk, 0:n], rhs=vaug[0:nk, blk, h, 0:129],
                            start=(dn == 0), stop=(dn == nblk - 1)))
                    if it.get("head_end"):
                        S.op("dve", [T_pO_], [T_astat], lambda e: e.reciprocal(
                            out=astat[0:n, 0:2], in_=pO_[0:n, :, 128:129].rearrange("p a b -> p (a b)")))
                        S.op("dve", [T_astat, T_small], [T_astat], lambda e: e.tensor_tensor(
                            out=astat[0:n, 2:3], in0=astat[0:n, 1:2], in1=neg_lam[0:n, :], op=ALU.mult))
                        S.op("dve", [T_pO_, T_astat], [T_att], lambda e: e.tensor_scalar(
                            out=att[0:n, :], in0=pO_[0:n, 0, 0:128], scalar1=astat[0:n, 0:1], scalar2=None, op0=ALU.mult))
                        S.op("dve", [T_pO_, T_astat, T_att], [T_att], lambda e: e.scalar_tensor_tensor(
                            out=att[0:n, :], in0=pO_[0:n, 1, 0:128], scalar=astat[0:n, 2:3], in1=att[0:n, :],
                            op0=ALU.mult, op1=ALU.add))
                        S.op("act", [T_att], [T_junk, T_astat], lambda e: e.activation(
                            out=junk[0:n, 0:128], in_=att[0:n, :], func=AF.Square, accum_out=astat[0:n, 3:4]))
                        S.op("act", [T_astat, T_eps], [T_astat], lambda e: e.activation(
                            out=astat[0:n, 4:5], in_=astat[0:n, 3:4], func=AF.Sqrt, scale=1.0 / 128, bias=eps_t[0:n, :]))
                        S.op("dve", [T_astat], [T_astat], lambda e: e.reciprocal(out=astat[0:n, 4:5], in_=astat[0:n, 4:5]))
                        S.op("dve", [T_att, T_astat, T_const], [T_attb], lambda e: e.scalar_tensor_tensor(
                            out=attb[0:n, h * 128:(h + 1) * 128], in0=att[0:n, :], scalar=astat[0:n, 4:5], in1=gsub_s[0:n, :],
                            op0=ALU.mult, op1=ALU.mult))
                    if it.get("tile_end"):
                        for hh in range(4):
                            S.op("pe", [T_attb, T_identb], [T_pT], lambda e, hh=hh: e.transpose(
                                out=pT[:, hh * 128:hh * 128 + n], in_=attb[0:n, hh * 128:(hh + 1) * 128], identity=ident_b[0:n, 0:n]))
                        S.op("act", [T_pT], [T_cat], lambda e: e.copy(
                            out=catT[:, 4:8, c0:c0 + n], in_=pT[:, 0:512].rearrange("p (k c) -> p k c", k=4)[:, :, 0:n]))

                for k, it in enumerate(items):
                    emit_qk(k, it)
                    if k >= 2:
                        emit_pv(k - 2, items[k - 2])
                for k in range(max(len(items) - 2, 0), len(items)):
                    emit_pv(k, items[k])

                for cb in range(4):
                    pm, T_pm = next_pM()
                    for w in range(31):
                        S.op("pe", [T_aT, T_diag], [T_pm], lambda e, pm=pm, w=w, cb=cb: e.matmul(
                            pm[:, 0:N], lhsT=diag[:, w * 4 + cb, :], rhs=aT[:, cb, w:w + N], start=(w == 0), stop=(w == 30)))
                    S.op("act", [T_pm, T_const], [T_y32], lambda e, pm=pm, cb=cb: e.activation(
                        out=y32[:, cb, 0:N], in_=pm[:, 0:N], func=AF.Identity, bias=cvec_s[:, cb:cb + 1]))
                    S.op("act", [T_pm, T_const], [T_ysq], lambda e, pm=pm, cb=cb: e.activation(
                        out=ysq[:, cb, 0:N], in_=pm[:, 0:N], func=AF.Square, bias=cvec_s[:, cb:cb + 1]))
                    S.op("pool", [T_y32], [T_ybf], lambda e, cb=cb: e.tensor_copy(out=ybf[:, cb, 0:N], in_=y32[:, cb, 0:N]))
                p1, T_p1 = pX[0]
                p2, T_p2 = pX[1]
                for cb in range(4):
                    S.op("pe", [T_ybf, T_ones], [T_p1], lambda e, cb=cb: e.matmul(
                        p1[:, 0:N], lhsT=ones_b[:], rhs=ybf[:, cb, 0:N], start=(cb == 0), stop=(cb == 3)))
                for cb in range(4):
                    S.op("pe", [T_ysq, T_ones], [T_p2], lambda e, cb=cb: e.matmul(
                        p2[:, 0:N], lhsT=ones_b[:], rhs=ysq[:, cb, 0:N], start=(cb == 0), stop=(cb == 3)))
                S.op("dve", [T_p1], [T_mu], lambda e: e.tensor_scalar(
                    out=mu[:, 0:N], in0=p1[:, 0:N], scalar1=1.0 / 512, scalar2=None, op0=ALU.mult))
                S.op("dve", [T_mu], [T_rs], lambda e: e.tensor_tensor(out=rs[:, 0:N], in0=mu[:, 0:N], in1=mu[:, 0:N], op=ALU.mult))
                S.op("dve", [T_p2, T_rs], [T_rs], lambda e: e.scalar_tensor_tensor(
                    out=rs[:, 0:N], in0=p2[:, 0:N], scalar=1.0 / 512, in1=rs[:, 0:N], op0=ALU.mult, op1=ALU.subtract))
                S.op("act", [T_rs, T_eps], [T_rs], lambda e: e.activation(
                    out=rs[:, 0:N], in_=rs[:, 0:N], func=AF.Sqrt, bias=eps_t[:, :]))
                S.op("dve", [T_rs], [T_rs], lambda e: e.reciprocal(out=rs[:, 0:N], in_=rs[:, 0:N]))
                for cb in range(4):
                    S.op("dve", [T_y32, T_mu], [T_y32], lambda e, cb=cb: e.tensor_tensor(
                        out=y32[:, cb, 0:N], in0=y32[:, cb, 0:N], in1=mu[:, 0:N], op=ALU.subtract))
                    S.op("pool", [T_y32, T_rs], [T_y32], lambda e, cb=cb: e.tensor_tensor(
                        out=y32[:, cb, 0:N], in0=y32[:, cb, 0:N], in1=rs[:, 0:N], op=ALU.mult))
                    S.op("act", [T_y32, T_const], [T_cat], lambda e, cb=cb: e.activation(
                        out=catT[:, cb, 0:N], in_=y32[:, cb, 0:N], func=AF.Silu,
                        scale=cvec_s[:, 4 + cb:5 + cb], bias=cvec_s[:, 8 + cb:9 + cb]))
                if not last_group:
                    S.op("pool", [T_aT], [T_aT], lambda e: e.tensor_copy(out=aT[:, :, 0:30], in_=aT[:, :, N:N + 30]))

                if STOP <= 5:
                    continue
                for (c0, n) in tiles:
                    S.dma("sp", [], [T_xr], lambda e, c0=c0, n=n: e.dma_start(out=xr[0:n, :], in_=xd[g0 + c0:g0 + c0 + n, :]))
                    for hf in range(2):
                        po, T_po = pX[hf]
                        for kc in range(8):
                            S.op("pe", [T_cat, T_wout], [T_po], lambda e, po=po, kc=kc, hf=hf, c0=c0, n=n: e.matmul(
                                po[0:n, :], lhsT=catT[:, kc, c0:c0 + n], rhs=w_out_b[:, kc, hf * 512:(hf + 1) * 512],
                                start=(kc == 0), stop=(kc == 7)))
                        S.op("dve", [T_po, T_xr], [T_xr], lambda e, po=po, hf=hf, n=n: e.tensor_tensor(
                            out=xr[0:n, hf * 512:(hf + 1) * 512], in0=po[0:n, :], in1=xr[0:n, hf * 512:(hf + 1) * 512], op=ALU.add))
                    S.dma("sp", [T_xr], [], lambda e, c0=c0, n=n: e.dma_start(
                        out=x1d[tok0 + g0 + c0:tok0 + g0 + c0 + n, :], in_=xr[0:n, :]))
                    rms_to_bf16(n, xr, T_xr, hb, T_hb, 2)
                    transpose_to_fT(n, hb, T_hb, c0)
                for kc in range(8):
                    S.dma("sp", [T_fT], [], lambda e, kc=kc: e.dma_start(
                        out=h2Td[kc, :, tok0 + g0:tok0 + g0 + N], in_=fT[:, kc, 0:N]))

        S.barrier()
        st1.close()
        st2 = ExitStack()
        st.enter_context(st2)
        cur[0] = st2
        TG = 256
        NCH = 64
        if with_peer:
            ijwd = nc.dram_tensor("ijwd", [128, 3, NT], F32, kind="Internal").ap()
            uscr_t = nc.dram_tensor("uscr", [NCH, 128, 8 * 256], BF16, kind="Internal").ap()
            vscr_t = nc.dram_tensor("vscr", [NCH, 128, 2 * DM], BF16, kind="Internal").ap()
            uscr = [uscr_t[ic].rearrange("p (k e) -> p k e", k=8) for ic in range(NCH)]
            vscr = [vscr_t[ic].rearrange("p (b d) -> p b d", b=2) for ic in range(NCH)]
            T_uscr = [S.tile(f"uscr{ic}") for ic in range(NCH)]
            T_vscr = [S.tile(f"vscr{ic}") for ic in range(NCH)]
            T_ijwd = S.tile("ijwd")

            gfin_s, T_gfin = sb("gfin_s", [128, DM], F32)
            iota_b, T_iotab = sb("iota_b", [128, 128], BF16)
            T_c2 = S.tile("consts2")
            S.dma("sp", [], [T_c2], lambda e: e.dma_start(out=gfin_s[:], in_=gfin.to_broadcast([128, DM])))
            S.op("dve", [T_iotar], [T_iotab], lambda e: e.tensor_copy(out=iota_b[:], in_=iota_r[:]))

            st2a = ExitStack()
            cur[0] = st2a
            TA = 512
            wq_b, T_wq = sb("wq_b", [128, 8, 2048], BF16)
            keys_b, T_keys = sb("keys_b", [128, 16, 128], BF16)
            gffn_s, T_gffn = sb("gffn_s", [128, 8], F32)
            sg0, T_sg0 = sb("sg0", [128, 1024], F32)
            sg1, T_sg1 = sb("sg1", [128, 1024], F32)
            h2g, T_h2g = sb("h2g", [128, 8, TA], BF16)
            qryT, T_qry = sb("qryT", [128, 16, TA], BF16)
            s_sb, T_ssb = sb("s_sb", [128, 16, 128], F32)
            wk4l = [sb(f"wk4_{i}", [128, 256], F32) for i in range(4)]
            wk4 = [x[0] for x in wk4l]
            T_wk4 = [x[1] for x in wk4l]
            T_A4 = [S.tile(f"A4_{i}") for i in range(4)]
            T_I4 = [S.tile(f"I4_{i}") for i in range(4)]
            T_C4 = [S.tile(f"C4_{i}") for i in range(4)]
            T_P4 = [S.tile(f"P4t_{i}") for i in range(4)]
            A_, T_A = sb("A_", [128, 16, 16], F32)
            Iu, T_Iu = sb("Iu", [128, 16, 16], U32)
            If, T_If = sb("If", [128, 16, 16], F32)
            cand, T_cand = sb("cand", [128, 8, 256], F32)
            C_, T_C = sb("C_", [128, 8, 16], F32)
            pos, T_pos = sb("pos", [128, 8, 16], U32)
            ku, T_ku = sb("ku", [128, 2, 128], U32)
            kf, T_kf = sb("kf", [128, 2, 128], F32)
            E_, T_E = sb("E_", [128, 8, 16], F32)
            gst, T_gst = sb("gst", [128, 32], F32)
            oh, T_oh = sb("oh", [128, 8, 16, 16], F32)
            ijw, T_ijw = sb("ijw", [128, 3, 128], F32)
            ijT = [sb(f"ijT{i}", [128, 3, 128], F32) for i in range(2)]
            stu = [sb(f"stu{i}", [128, 8, 256], BF16) for i in range(2)]
            stv = [sb(f"stv{i}", [128, 2, DM], BF16) for i in range(2)]
            pGa = [ps(f"pGa{i}", [128, 512], F32) for i in range(4)]
            pga_i = [0]

            def next_pGa():
                pga_i[0] += 1
                return pGa[pga_i[0] % 4]

            S.dma("sp", [], [T_c2], lambda e: e.dma_start(out=gffn_s[:], in_=gffn))
            S.dma("pool", [], [T_keys], lambda e: e.dma_start(out=keys_b[:], in_=keysT.rearrange("r d n -> d r n")))
            sgs = [(sg0, T_sg0), (sg1, T_sg1)]
            ii = 0
            for kc in range(8):
                for hf in range(2):
                    stt, T_s = sgs[ii % 2]
                    ii += 1
                    S.dma("sp", [], [T_s], lambda e, stt=stt, kc=kc, hf=hf: e.dma_start(
                        out=stt[:], in_=wq[kc * 128:(kc + 1) * 128, hf * 1024:(hf + 1) * 1024]))
                    S.op("dve", [T_s, T_c2], [T_wq], lambda e, stt=stt, kc=kc, hf=hf: e.tensor_scalar(
                        out=wq_b[:, kc, hf * 1024:(hf + 1) * 1024], in0=stt[:], scalar1=gffn_s[:, kc:kc + 1],
                        scalar2=None, op0=ALU.mult))

            conv_i = [0]

            def convert_chunk():
                ic = conv_i[0]
                if ic >= NCH:
                    return
                conv_i[0] += 1
                (su, T_su), (sv, T_sv) = stu[ic % 2], stv[ic % 2]
                for k4 in range(2):
                    S.dma("pool", [], [T_su], lambda e, k4=k4: e.dma_start(
                        out=su[:, k4 * 4:(k4 + 1) * 4, :],
                        in_=uT[k4 * 512:(k4 + 1) * 512, ic * 256:(ic + 1) * 256].rearrange("(k p) e -> p k e", p=128)))
                S.dma("pool", [], [T_sv], lambda e: e.dma_start(
                    out=sv[:], in_=pv[ic * 256:(ic + 1) * 256, :].rearrange("(b p) d -> p b d", p=128)))
                S.dma("sp", [T_su], [T_uscr[ic]], lambda e: e.dma_start(out=uscr[ic], in_=su[:]), key=T_su)
                S.dma("sp", [T_sv], [T_vscr[ic]], lambda e: e.dma_start(out=vscr[ic], in_=sv[:]), key=T_sv)

            tile_ctr = 0
            for t0 in range(0, NT, TA):
                N = min(TA, NT - t0)
                tiles = [(c0, min(128, N - c0)) for c0 in range(0, N, 128)]
                S.dma("sp", [], [T_h2g], lambda e: e.dma_start(
                    out=h2g[:, :, 0:N], in_=h2Td[:, :, t0:t0 + N].rearrange("k p t -> p k t")))
                for blk in range(16):
                    pg, T_pg = next_pGa()
                    for kc in range(8):
                        S.op("pe", [T_wq, T_h2g], [T_pg], lambda e, pg=pg, kc=kc, blk=blk: e.matmul(
                            pg[:, 0:N], lhsT=wq_b[:, kc, blk * 128:(blk + 1) * 128], rhs=h2g[:, kc, 0:N],
                            start=(kc == 0), stop=(kc == 7)))
                    S.op("act", [T_pg], [T_qry], lambda e, pg=pg, blk=blk: e.copy(out=qryT[:, blk, 0:N], in_=pg[:, 0:N]))
                for (c0, n) in tiles:
                    convert_chunk()
                    convert_chunk()
                    for q4 in range(4):
                        pg, T_pg = next_pGa()
                        for j in range(4):
                            rp = q4 * 4 + j
                            S.op("pe", [T_qry, T_keys], [T_pg], lambda e, pg=pg, j=j, rp=rp: e.matmul(
                                pg[0:n, j * 128:(j + 1) * 128], lhsT=qryT[:, rp, c0:c0 + n], rhs=keys_b[:, rp, :],
                                start=True, stop=True))
                        S.op("act", [T_pg], [T_ssb], lambda e, pg=pg, q4=q4: e.copy(
                            out=s_sb[0:n, q4 * 4:(q4 + 1) * 4, :], in_=pg[0:n, :].rearrange("p (a b) -> p a b", a=4)))
                    for rp0 in range(0, 16, 4):
                        rps = list(range(rp0, rp0 + 4))
                        for rp in rps:
                            S.op("dve", [T_ssb], [T_A4[rp % 4]], lambda e, rp=rp: e.max(out=A_[0:n, rp, 0:8], in_=s_sb[0:n, rp, :]))
                        for rp in rps:
                            S.op("dve", [T_ssb, T_A4[rp % 4]], [T_I4[rp % 4]], lambda e, rp=rp: e.max_index(
                                out=Iu[0:n, rp, 0:8], in_max=A_[0:n, rp, 0:8], in_values=s_sb[0:n, rp, :]))
                        for rp in rps:
                            S.op("dve", [T_ssb, T_A4[rp % 4]], [T_wk4[rp % 4]], lambda e, rp=rp: e.match_replace(
                                out=wk4[rp % 4][0:n, 0:128], in_to_replace=A_[0:n, rp, 0:8], in_values=s_sb[0:n, rp, :], imm_value=-1e30))
                        for rp in rps:
                            S.op("dve", [T_wk4[rp % 4]], [T_A4[rp % 4]], lambda e, rp=rp: e.max(out=A_[0:n, rp, 8:16], in_=wk4[rp % 4][0:n, 0:128]))
                        for rp in rps:
                            S.op("dve", [T_wk4[rp % 4], T_A4[rp % 4]], [T_I4[rp % 4]], lambda e, rp=rp: e.max_index(
                                out=Iu[0:n, rp, 8:16], in_max=A_[0:n, rp, 8:16], in_values=wk4[rp % 4][0:n, 0:128]))
                    S.op("dve", T_I4, [T_If], lambda e: e.tensor_copy(out=If[0:n], in_=Iu[0:n]))
                    A4 = A_[0:n].rearrange("p (r a) k -> p r a k", a=2)
                    I4 = If[0:n].rearrange("p (r a) k -> p r a k", a=2)
                    S.op("dve", T_A4, [T_cand], lambda e: e.tensor_tensor(
                        out=cand[0:n].rearrange("p r (a b) -> p r a b", a=16),
                        in0=A4[:, :, 0, :].unsqueeze(3).to_broadcast([n, 8, 16, 16]),
                        in1=A4[:, :, 1, :].unsqueeze(2).to_broadcast([n, 8, 16, 16]), op=ALU.add))
                    for r0 in range(0, 8, 4):
                        rs_ = list(range(r0, r0 + 4))
                        for r in rs_:
                            S.op("dve", [T_cand], [T_C4[r % 4]], lambda e, r=r: e.max(out=C_[0:n, r, 0:8], in_=cand[0:n, r, :]))
                        for r in rs_:
                            S.op("dve", [T_cand, T_C4[r % 4]], [T_P4[r % 4]], lambda e, r=r: e.max_index(
                                out=pos[0:n, r, 0:8], in_max=C_[0:n, r, 0:8], in_values=cand[0:n, r, :]))
                        for r in rs_:
                            S.op("dve", [T_cand, T_C4[r % 4]], [T_wk4[r % 4]], lambda e, r=r: e.match_replace(
                                out=wk4[r % 4][0:n, :], in_to_replace=C_[0:n, r, 0:8], in_values=cand[0:n, r, :], imm_value=-1e30))
                        for r in rs_:
                            S.op("dve", [T_wk4[r % 4]], [T_C4[r % 4]], lambda e, r=r: e.max(out=C_[0:n, r, 8:16], in_=wk4[r % 4][0:n, :]))
                        for r in rs_:
                            S.op("dve", [T_wk4[r % 4], T_C4[r % 4]], [T_P4[r % 4]], lambda e, r=r: e.max_index(
                                out=pos[0:n, r, 8:16], in_max=C_[0:n, r, 8:16], in_values=wk4[r % 4][0:n, :]))
                    S.op("dve", T_C4, [T_gst], lambda e: e.tensor_scalar(
                        out=gst[0:n, 0:8], in0=C_[0:n, :, 0], scalar1=-1.0, scalar2=None, op0=ALU.mult))
                    for r in range(8):
                        S.op("act", T_C4 + [T_gst], [T_E, T_gst], lambda e, r=r: e.activation(
                            out=E_[0:n, r, :], in_=C_[0:n, r, :], func=AF.Exp, bias=gst[0:n, r:r + 1],
                            accum_out=gst[0:n, 8 + r:9 + r]))
                    S.op("dve", [T_gst], [T_gst], lambda e: e.reciprocal(out=gst[0:n, 16:24], in_=gst[0:n, 8:16]))
                    S.op("dve", [T_E, T_gst], [T_ijw], lambda e: e.tensor_tensor(
                        out=ijw[0:n, 2, :].rearrange("p (r k) -> p r k", r=8), in0=E_[0:n],
                        in1=gst[0:n, 16:24].unsqueeze(2).to_broadcast([n, 8, 16]), op=ALU.mult))
                    S.op("dve", T_P4, [T_ku], lambda e: e.tensor_single_scalar(
                        out=ku[0:n, 0, :], in_=pos[0:n].rearrange("p r k -> p (r k)"), scalar=4, op=ALU.logical_shift_right))
                    S.op("dve", T_P4, [T_ku], lambda e: e.tensor_single_scalar(
                        out=ku[0:n, 1, :], in_=pos[0:n].rearrange("p r k -> p (r k)"), scalar=15, op=ALU.bitwise_and))
                    S.op("dve", [T_ku], [T_kf], lambda e: e.tensor_copy(out=kf[0:n], in_=ku[0:n]))
                    for a in range(2):
                        S.op("dve", [T_kf, T_iotar], [T_oh], lambda e, a=a: e.tensor_tensor(
                            out=oh[0:n],
                            in0=kf[0:n, a, :].rearrange("p (r k) -> p r k", r=8).unsqueeze(3).to_broadcast([n, 8, 16, 16]),
                            in1=iota_r[0:n, 0:16].unsqueeze(1).unsqueeze(1).to_broadcast([n, 8, 16, 16]), op=ALU.is_equal))
                        S.op("dve", [T_oh, T_If], [T_oh], lambda e, a=a: e.tensor_tensor(
                            out=oh[0:n], in0=oh[0:n],
                            in1=I4[:, :, a, :].unsqueeze(2).to_broadcast([n, 8, 16, 16]), op=ALU.mult))
                        S.op("dve", [T_oh], [T_ijw], lambda e, a=a: e.reduce_sum(
                            out=ijw[0:n, a, :].rearrange("p (r k) -> p r k", r=8), in_=oh[0:n], axis=AX.X))
                    pg, T_pg = next_pGa()
                    for a in range(3):
                        S.op("pe", [T_ijw, T_identf], [T_pg], lambda e, pg=pg, a=a: e.transpose(
                            out=pg[:, a * 128:a * 128 + n], in_=ijw[0:n, a, :], identity=ident_f[0:n, 0:n]))
                    (it_, T_it) = ijT[tile_ctr % 2]
                    tile_ctr += 1
                    S.op("act", [T_pg], [T_it], lambda e, pg=pg, it_=it_: e.copy(
                        out=it_[:, :, 0:n], in_=pg[:, 0:384].rearrange("p (a t) -> p a t", a=3)[:, :, 0:n]))
                    S.dma("sp", [T_it], [T_ijwd], lambda e, it_=it_: e.dma_start(
                        out=ijwd[:, :, t0 + c0:t0 + c0 + n], in_=it_[:, :, 0:n]), key=T_it)
            while conv_i[0] < NCH:
                convert_chunk()
            S.barrier()
            st2a.close()

            st2b = ExitStack()
            st.enter_context(st2b)
            cur[0] = st2b
            Gall = [sb(f"Gall{i}", [128, 128, TG], BF16) for i in range(2)]
            ubuf = [sb(f"ubuf{i}", [128, 8, 256], BF16) for i in range(2)]
            vbuf = [sb(f"vbuf{i}", [128, 2, DM], BF16) for i in range(3)]
            h2g2 = [sb(f"h2g2_{i}", [128, 8, TG], BF16) for i in range(2)]
            ijg = [sb(f"ijg{i}", [128, 3, TG], F32) for i in range(2)]
            P4 = [sb(f"P4_{i}", [128, 4, 128], BF16) for i in range(3)]
            Q4 = [sb(f"Q4_{i}", [128, 4, 128], BF16) for i in range(3)]
            gbuf = [sb(f"gbuf{i}", [128, TG], F32) for i in range(3)]
            cbuf = [sb(f"cbuf{i}", [128, TG], BF16) for i in range(3)]
            x2, T_x2 = sb("x2", [128, DM], F32)
            junk2, T_junk2 = sb("junk2", [128, DM], BF16)
            st2s, T_st2s = sb("st2s", [128, 4], F32)
            pY = [[ps(f"pY{t}{h}", [128, 512], F32) for h in range(2)] for t in range(2)]
            pA = [ps(f"pA{i}", [128, 512], F32) for i in range(3)]
            pG = [ps(f"pG{i}", [128, 512], F32) for i in range(1)]

            groups = [(t0, min(TG, NT - t0)) for t0 in range(0, NT, TG)]
            MAXG = int(os.environ.get("KGROUPS", "999"))
            groups = groups[:MAXG]
            NG = len(groups)
            MULT_ENG = os.environ.get("KMULT", "pool")

            def gtiles(g):
                N = groups[g][1]
                return [(c0, min(128, N - c0)) for c0 in range(0, N, 128)]

            def load_group(g):
                t0, N = groups[g]
                (hg, T_hg), (ij, T_ij) = h2g2[g % 2], ijg[g % 2]
                S.dma("sp", [], [T_hg], lambda e: e.dma_start(
                    out=hg[:, :, 0:N], in_=h2Td[:, :, t0:t0 + N].rearrange("k p t -> p k t")))
                S.dma("sp", [T_ijwd], [T_ij], lambda e: e.dma_start(out=ij[:, :, 0:N], in_=ijwd[:, :, t0:t0 + N]))

            def p10_dve(g, b):
                (ij, T_ij) = ijg[g % 2]
                (p4, T_p4), (q4_, T_q4) = P4[b % 3], Q4[b % 3]
                for u in range(4):
                    t = b * 4 + u
                    S.op("dve", [T_ij, T_iotab], [T_p4], lambda e, u=u, t=t: e.tensor_scalar(
                        out=p4[:, u, :], in0=iota_b[:], scalar1=ij[:, 0, t:t + 1], scalar2=None, op0=ALU.is_equal))
                    S.op("dve", [T_ij, T_iotab], [T_q4], lambda e, u=u, t=t: e.tensor_scalar(
                        out=q4_[:, u, :], in0=iota_b[:], scalar1=ij[:, 1, t:t + 1], scalar2=ij[:, 2, t:t + 1],
                        op0=ALU.is_equal, op1=ALU.mult))

            def p10_pe(g, b):
                (p4, T_p4), (q4_, T_q4) = P4[b % 3], Q4[b % 3]
                (ga, T_ga) = Gall[g % 2]
                pg, T_pg = pG[0]
                for u in range(4):
                    S.op("pe", [T_p4, T_q4], [T_pg], lambda e, u=u: e.matmul(
                        pg[:, u * 128:(u + 1) * 128], lhsT=q4_[:, u, :], rhs=p4[:, u, :], start=True, stop=True))
                S.op("act", [T_pg], [T_ga], lambda e: e.copy(
                    out=ga[:, :, b * 4:b * 4 + 4], in_=pg[:, :].rearrange("p (t i) -> p i t", t=4)))

            class P10:
                def __init__(self, g):
                    self.g = g
                    self.nb = groups[g][1] // 4
                    self.d = 0
                    self.p = 0

                def step(self):
                    if self.d < self.nb:
                        p10_dve(self.g, self.d)
                        self.d += 1
                        if self.d - self.p >= 3:
                            p10_pe(self.g, self.p)
                            self.p += 1
                    elif self.p < self.nb:
                        p10_pe(self.g, self.p)
                        self.p += 1

                def flush(self):
                    while self.p < self.nb:
                        if self.d < self.nb and self.d - self.p < 3:
                            p10_dve(self.g, self.d)
                            self.d += 1
                        else:
                            p10_pe(self.g, self.p)
                            self.p += 1

            chunk_ctr = [0]

            def load_chunk(ic):
                c = chunk_ctr[0]
                chunk_ctr[0] += 1
                (ub, T_ub), (vb, T_vb) = ubuf[c % 2], vbuf[c % 3]
                S.dma("sp", [T_uscr[ic]], [T_ub], lambda e: e.dma_start(out=ub[:], in_=uscr[ic]))
                S.dma("sp", [T_vscr[ic]], [T_vb], lambda e: e.dma_start(out=vb[:], in_=vscr[ic]))

            def stage_u(g, i):
                N = groups[g][1]
                c = g * NCH + i // 2
                ib = i % 2
                (ub, T_ub) = ubuf[c % 2]
                (hg, T_hg) = h2g2[g % 2]
                (ga, T_ga) = Gall[g % 2]
                gidx = g * 128 + i
                pa, T_pa = pA[gidx % 3]
                gb, T_gb = gbuf[gidx % 3]
                cb_, T_cb = cbuf[gidx % 3]
                for kc in range(8):
                    S.op("pe", [T_ub, T_hg], [T_pa], lambda e, kc=kc: e.matmul(
                        pa[:, 0:N], lhsT=ub[:, kc, ib * 128:(ib + 1) * 128], rhs=hg[:, kc, 0:N],
                        start=(kc == 0), stop=(kc == 7)))
                S.op("act", [T_pa], [T_gb], lambda e: e.activation(out=gb[:, 0:N], in_=pa[:, 0:N], func=AF.Gelu))
                S.op(MULT_ENG, [T_gb, T_ga], [T_cb], lambda e: e.tensor_tensor(
                    out=cb_[:, 0:N], in0=gb[:, 0:N], in1=ga[:, i, 0:N], op=ALU.mult))

            def stage_v(g, i):
                c = g * NCH + i // 2
                ib = i % 2
                (vb, T_vb) = vbuf[c % 3]
                cb_, T_cb = cbuf[(g * 128 + i) % 3]
                for ti, (c0, n) in enumerate(gtiles(g)):
                    for hf in range(2):
                        py, T_py = pY[ti][hf]
                        S.op("pe", [T_cb, T_vb], [T_py], lambda e, py=py, c0=c0, n=n, hf=hf: e.matmul(
                            py[0:n, :], lhsT=cb_[:, c0:c0 + n], rhs=vb[:, ib, hf * 512:(hf + 1) * 512],
                            start=(i == 0), stop=(i == 127)))

            def epilogue(g):
                t0, N = groups[g]
                for ti, (c0, n) in enumerate(gtiles(g)):
                    tg = t0 + c0
                    S.dma("sp", [], [T_x2], lambda e, tg=tg, n=n: e.dma_start(out=x2[0:n, :], in_=x1d[tg:tg + n, :]))
                    for hf in range(2):
                        py, T_py = pY[ti][hf]
                        S.op("dve", [T_py, T_x2], [T_x2], lambda e, py=py, hf=hf, n=n: e.tensor_tensor(
                            out=x2[0:n, hf * 512:(hf + 1) * 512], in0=py[0:n, :], in1=x2[0:n, hf * 512:(hf + 1) * 512], op=ALU.add))
                    S.op("act", [T_x2], [T_junk2, T_st2s], lambda e, n=n: e.activation(
                        out=junk2[0:n, :], in_=x2[0:n, :], func=AF.Square, accum_out=st2s[0:n, 0:1]))
                    S.op("act", [T_st2s, T_eps], [T_st2s], lambda e, n=n: e.activation(
                        out=st2s[0:n, 1:2], in_=st2s[0:n, 0:1], func=AF.Sqrt, scale=1.0 / DM, bias=eps_t[0:n, :]))
                    S.op("dve", [T_st2s], [T_st2s], lambda e, n=n: e.reciprocal(out=st2s[0:n, 1:2], in_=st2s[0:n, 1:2]))
                    S.op("dve", [T_x2, T_st2s, T_c2], [T_x2], lambda e, n=n: e.scalar_tensor_tensor(
                        out=x2[0:n, :], in0=x2[0:n, :], scalar=st2s[0:n, 1:2], in1=gfin_s[0:n, :], op0=ALU.mult, op1=ALU.mult))
                    if tg < n_pseq * SEQ:
                        dst = yp[tg // SEQ][tg % SEQ:tg % SEQ + n, :]
                    else:
                        dst = ys[0:n, :]
                    S.dma("sp", [T_x2], [], lambda e, dst=dst, n=n: e.dma_start(out=dst, in_=x2[0:n, :]))

            load_group(0)
            load_chunk(0)
            pz = P10(0)
            pz.flush()
            seq = [(g, i) for g in range(NG) for i in range(128)]
            nxt = None

            def emit_v(idx):
                g_, i_ = seq[idx]
                stage_v(g_, i_)
                if i_ == 127:
                    epilogue(g_)

            for idx, (g, i) in enumerate(seq):
                if i == 0:
                    nxt = None
                    if g + 1 < NG:
                        load_group(g + 1)
                        nxt = P10(g + 1)
                stage_u(g, i)
                if idx >= 2:
                    emit_v(idx - 2)
                if i % 2 == 0:
                    ic_next = i // 2 + 1
                    if ic_next < NCH:
                        load_chunk(ic_next)
                    elif g + 1 < NG:
                        load_chunk(0)
                if nxt is not None and i % 2 == 1:
                    nxt.step()
                if i == 127 and nxt is not None:
                    nxt.flush()
            emit_v(len(seq) - 2)
            emit_v(len(seq) - 1)

        S.barrier()
        S.finish("sp")
        print("ops per engine:", S.nops, "sems:", S.nsem)
    return nc


def _prep_shared(inp):
    f = lambda a: np.ascontiguousarray(np.asarray(a, dtype=np.float32))
    sh = {}
    sh["w_in"] = f(inp["w_in"][0])
    sh["gmix"] = f(inp["g_mix"][0].reshape(8, 128).T)
    sh["convw"] = f(inp["conv_w"][0].reshape(31, 4, 128).transpose(2, 1, 0))
    sh["cvec"] = f(np.concatenate([inp["conv_b"][0].reshape(4, 128).T, inp["conv_ln_g"][0].reshape(4, 128).T,
                                   inp["conv_ln_b"][0].reshape(4, 128).T], axis=1))
    sh["lam"] = f(np.stack([inp["lambda_q1"][0], inp["lambda_k1"][0], inp["lambda_q2"][0], inp["lambda_k2"][0]]).reshape(1, 256))
    sh["subg"] = f(inp["subln_g"][0].reshape(1, 128))
    sh["relb"] = f(inp["rel_bias"].reshape(1, 128))
    sh["w_out"] = f(inp["w_out"][0])
    sh["gffn"] = f(inp["g_ffn"][0].reshape(8, 128).T)
    sh["wq"] = f(inp["w_query"][0])
    sh["keysT"] = f(inp["sub_keys"][0].reshape(16, 128, 128).transpose(0, 2, 1))
    sh["uT"] = f(inp["peer_u"][0].T)
    sh["pv"] = f(inp["peer_v"][0])
    sh["gfin"] = f(inp["g_final"].reshape(1, DM))
    sh["bkc"] = _bucket_tiles()
    return sh


def kernel(**inp):
    f = lambda a: np.ascontiguousarray(np.asarray(a, dtype=np.float32))
    sh = _prep_shared(inp)
    nc = build_program(2)
    in_maps = []
    for c in range(NCORES):
        m = dict(sh)
        m["xp"] = f(inp["x_prompt"][2 * c:2 * c + 2])
        m["xs"] = f(inp["x_sample"][c])
        m["ckT"] = f(np.asarray(inp["cache_k"][0, c]).reshape(SEQ, 4, 128).transpose(1, 2, 0))
        m["cv"] = f(np.asarray(inp["cache_v"][0, c]).reshape(SEQ, 512))
        m["scT"] = f(np.asarray(inp["state_conv"][0, c]).reshape(30, 4, 128).transpose(2, 1, 0))
        in_maps.append(m)
    res = run_bass_kernel_spmd(nc, in_maps, core_ids=list(range(NCORES)))
    R = res.results
    y_prompt = np.concatenate([r["yp"] for r in R], axis=0)
    y_sample = np.stack([r["ys"] for r in R], axis=0)
    k_prompt = np.concatenate([r["kp"] for r in R], axis=0).reshape(1, 16, SEQ, 4, 2, 64)
    v_prompt = np.concatenate([r["vp"] for r in R], axis=0).reshape(1, 16, SEQ, 4, 128)
    c_prompt = np.concatenate([r["cp"] for r in R], axis=0).reshape(1, 16, 30, 512)
    k_sample = np.stack([r["ks"] for r in R], axis=0).reshape(1, 8, 64, 4, 2, 64)
    v_sample = np.stack([r["vs"] for r in R], axis=0).reshape(1, 8, 64, 4, 128)
    c_sample = np.stack([r["cs"] for r in R], axis=0).reshape(1, 8, 30, 512)
    return (y_prompt, y_sample, k_prompt, v_prompt, c_prompt, k_sample, v_sample, c_sample)
```

```python
import math
import os
from contextlib import ExitStack

import numpy as np
import concourse.bass as bass
import concourse.mybir as mybir
from concourse.bass_utils import run_bass_kernel_spmd

F32 = mybir.dt.float32
BF16 = mybir.dt.bfloat16
U32 = mybir.dt.uint32
AF = mybir.ActivationFunctionType
ALU = mybir.AluOpType
AX = mybir.AxisListType

EPS = 1e-6
LAM_INIT = 0.8 - 0.6 * math.exp(-0.3 * 0)
NCORES = 8
SEQ = 2048
DM = 1024
NEXP_SIDE = 128

SEM_LIMIT = 30000


class Counter:
    def __init__(self, S, name):
        self.S = S
        self.name = name
        self.epoch = 0
        self.val = 0
        self.sem = S.new_sem(f"{name}_e0")

    def bump(self, inc):
        if self.val + inc > SEM_LIMIT:
            self.epoch += 1
            self.val = 0
            self.sem = self.S.new_sem(f"{self.name}_e{self.epoch}")
        self.val += inc
        return (self.sem, self.val, self.name, self.epoch)


class Tile:
    __slots__ = ("name", "w", "r", "dmac")

    def __init__(self, name):
        self.name = name
        self.w = None
        self.r = []
        self.dmac = None


class Sched:
    def __init__(self, nc, stack):
        self.nc = nc
        self.stack = stack
        self.nsem = 0
        self.engs = {"pe": nc.tensor, "act": nc.scalar, "dve": nc.vector,
                     "pool": nc.gpsimd, "sp": nc.sync}
        self.cnt = {k: Counter(self, k) for k in self.engs}
        self.known = {k: {} for k in self.engs}
        self.nops = {k: 0 for k in self.engs}
        self.tiles = []
        self.cap = None

    def new_sem(self, name):
        self.nsem += 1
        return self.stack.enter_context(self.nc.semaphore(f"s{self.nsem}_{name}"))

    def tile(self, name):
        t = Tile(name)
        self.tiles.append(t)
        return t

    def _wait(self, e, ev):
        sem, val, name, epoch = ev
        key = (name, epoch)
        if self.known[e].get(key, 0) >= val:
            return
        self.known[e][key] = val
        self.engs[e].wait_ge(sem, val)

    def _deps(self, reads, writes):
        evs = []
        for t in reads:
            if t.w is not None:
                evs.append(t.w)
        for t in writes:
            if t.w is not None:
                evs.append(t.w)
            evs.extend(t.r)
        return evs

    def op(self, e, reads, writes, fn):
        if self.cap is not None:
            self.cap.append(("op", e, reads, writes, fn, None))
            return None
        return self._op(e, reads, writes, fn)

    def dma(self, q, reads, writes, fn, key=None):
        if self.cap is not None:
            self.cap.append(("dma", q, reads, writes, fn, key))
            return None
        return self._dma(q, reads, writes, fn, key)

    def replay(self, item):
        kind, e, reads, writes, fn, key = item
        if kind == "op":
            return self._op(e, reads, writes, fn)
        return self._dma(e, reads, writes, fn, key)

    def _op(self, e, reads, writes, fn):
        for ev in self._deps(reads, writes):
            if ev[2] == e:
                if e == "pe":
                    continue
                if ev[3] == self.cnt[e].epoch and self.cnt[e].val - ev[1] >= 2:
                    continue
            self._wait(e, ev)
        ins = fn(self.engs[e])
        ev = self.cnt[e].bump(1)
        ins.then_inc(ev[0], 1)
        self.nops[e] += 1
        self._mark(ev, reads, writes)
        return ev

    def _mark(self, ev, reads, writes):
        k = (ev[2], ev[3])
        for t in reads:
            t.r = [x for x in t.r if (x[2], x[3]) != k]
            t.r.append(ev)
        for t in writes:
            t.w = ev
            t.r = []

    def _dma(self, q, reads, writes, fn, key=None):
        kt = key or (writes[0] if writes else reads[0])
        if kt.dmac is None:
            kt.dmac = Counter(self, "d_" + kt.name)
        for ev in self._deps(reads, writes):
            self._wait(q, ev)
        ins = fn(self.engs[q])
        ev = kt.dmac.bump(16)
        ins.then_inc(ev[0], 16)
        self.nops[q] += 1
        self._mark(ev, reads, writes)
        return ev

    def _all_events(self):
        evs = {}
        for t in self.tiles:
            for ev in ([t.w] if t.w else []) + t.r:
                k = (ev[2], ev[3])
                if k not in evs or evs[k][1] < ev[1]:
                    evs[k] = ev
        return evs

    def barrier(self):
        evs = self._all_events()
        for e in self.engs:
            for ev in evs.values():
                self._wait(e, ev)
        for t in self.tiles:
            t.w = None
            t.r = []

    def finish(self, e="sp"):
        for ev in self._all_events().values():
            self._wait(e, ev)


def _bucket_np(rel):
    nb = 16
    max_exact = 8
    ret = np.where(rel > 0, nb, 0)
    n = np.abs(rel)
    nf = np.maximum(n, 1).astype(np.float32)
    large = max_exact + (np.log(nf / max_exact) / math.log(128 / max_exact) * (nb - max_exact)).astype(np.int32)
    large = np.minimum(large, nb - 1)
    return ret + np.where(n < max_exact, n, large)


def _bucket_tiles():
    k = np.arange(128)[:, None]
    q = np.arange(128)[None, :]
    b0 = _bucket_np(k - q).astype(np.float32)
    masked = (k // 64) > (q // 64)
    b0 = np.where(masked, 32.0, b0)
    b1 = _bucket_np(k - q - 128).astype(np.float32)
    return np.stack([b0, b1], axis=1).astype(np.float32)


def build_program(n_pseq=2, with_peer=True, dbg=False):
    nc = bass.Bass("TRN2", target_bir_lowering=False)
    NT = n_pseq * SEQ + 64

    def din(name, shape, dt=F32):
        return nc.dram_tensor(name, list(shape), dt, kind="ExternalInput").ap()

    def dout(name, shape, dt=F32):
        return nc.dram_tensor(name, list(shape), dt, kind="ExternalOutput").ap()

    xp = din("xp", [n_pseq, SEQ, DM])
    xs = din("xs", [64, DM])
    ckT = din("ckT", [4, 128, SEQ])
    cv = din("cv", [SEQ, 512])
    scT = din("scT", [128, 4, 30])
    w_in = din("w_in", [DM, 2560])
    gmix = din("gmix", [128, 8])
    convw = din("convw", [128, 4, 31])
    cvec = din("cvec", [128, 12])
    lam = din("lam", [1, 256])
    subg = din("subg", [1, 128])
    relb = din("relb", [1, 128])
    w_out = din("w_out", [DM, DM])
    gffn = din("gffn", [128, 8])
    wq = din("wq", [DM, 2048])
    keysT = din("keysT", [16, 128, 128])
    uT = din("uT", [DM, 16384])
    pv = din("pv", [16384, DM])
    gfin = din("gfin", [1, DM])
    bkc = din("bkc", [128, 2, 128])

    yp = dout("yp", [n_pseq, SEQ, DM])
    ys = dout("ys", [64, DM])
    kp = dout("kp", [n_pseq, SEQ, 512])
    vp = dout("vp", [n_pseq, SEQ, 512])
    cp = dout("cp", [n_pseq, 30, 512])
    ks = dout("ks", [64, 512])
    vs = dout("vs", [64, 512])
    cs = dout("cs", [30, 512])

    kind_scr = "ExternalOutput" if dbg else "Internal"
    x1d = nc.dram_tensor("x1d", [NT, DM], F32, kind=kind_scr).ap()
    h2Td = nc.dram_tensor("h2Td", [8, 128, NT], BF16, kind="Internal").ap()

    with ExitStack() as st:
        S = Sched(nc, st)

        cur = [st]

        def sb(name, shape, dt):
            return cur[0].enter_context(nc.sbuf_tensor(name, list(shape), dt)), S.tile(name)

        def ps(name, shape, dt):
            return cur[0].enter_context(nc.psum_tensor(name, list(shape), dt)), S.tile(name)

        ident_f, T_identf = sb("ident_f", [128, 128], F32)
        ident_b, T_identb = sb("ident_b", [128, 128], BF16)
        ones_b, T_ones = sb("ones_b", [128, 128], BF16)
        iota_t, T_iota = sb("iota_t", [128, 128], F32)
        T_const = S.tile("consts")

        S.op("pool", [], [T_iota], lambda e: e.iota(iota_t[:], pattern=[[1, 128]], base=0, channel_multiplier=-1,
                                                    allow_small_or_imprecise_dtypes=True))
        S.op("dve", [T_iota], [T_identf], lambda e: e.tensor_scalar(out=ident_f[:], in0=iota_t[:], scalar1=0.0,
                                                                     scalar2=None, op0=ALU.is_equal))
        S.op("dve", [T_identf], [T_identb], lambda e: e.tensor_copy(out=ident_b[:], in_=ident_f[:]))
        S.op("pool", [], [T_ones], lambda e: e.memset(ones_b[:], 1.0))

        eps_t, T_eps = sb("eps_t", [128, 1], F32)
        S.op("pool", [], [T_eps], lambda e: e.memset(eps_t[:], EPS))
        EPS_AP = eps_t
        iota_r, T_iotar = sb("iota_r", [128, 128], F32)
        S.op("pool", [], [T_iotar], lambda e: e.iota(iota_r[:], pattern=[[1, 128]], base=0, channel_multiplier=0,
                                                     allow_small_or_imprecise_dtypes=True))
        st1 = ExitStack()
        cur[0] = st1
        w_in_b, T_win = sb("w_in_b", [128, 8, 2560], BF16)
        w_out_b, T_wout = sb("w_out_b", [128, 8, DM], BF16)
        diag, T_diag = sb("diag", [128, 124, 128], BF16)
        gmix_s, _ = sb("gmix_s", [128, 8], F32)
        convw_s, _ = sb("convw_s", [128, 4, 31], F32)
        cvec_s, _ = sb("cvec_s", [128, 12], F32)
        lam_s, _ = sb("lam_s", [128, 256], F32)
        gsub_s, _ = sb("gsub_s", [128, 128], F32)
        relb_s, _ = sb("relb_s", [128, 128], F32)
        bk_s, _ = sb("bk_s", [128, 2, 128], F32)
        Tb, T_Tb = sb("Tb", [128, 4, 2, 128], F32)
        eqm, T_eqm = sb("eqm", [128, 2, 128], F32)
        small, T_small = sb("small", [128, 16], F32)
        stage0, T_st0 = sb("stage0", [128, 1024], F32)
        stage1, T_st1 = sb("stage1", [128, 1024], F32)

        for dst, src in ((gmix_s, gmix), (convw_s, convw), (cvec_s, cvec), (bk_s, bkc)):
            S.dma("sp", [], [T_const], lambda e, d=dst, s_=src: e.dma_start(out=d[:], in_=s_))
        for dst, src, n in ((lam_s, lam, 256), (gsub_s, subg, 128), (relb_s, relb, 128)):
            S.dma("sp", [], [T_const], lambda e, d=dst, s_=src, n=n: e.dma_start(out=d[:], in_=s_.to_broadcast([128, n])))

        stg = [(stage0, T_st0), (stage1, T_st1)]
        i = 0
        for kc in range(8):
            for (a0, a1) in ((0, 1024), (1024, 2048), (2048, 2560)):
                stt, T_s = stg[i % 2]
                i += 1
                S.dma("sp", [], [T_s], lambda e, stt=stt, kc=kc, a0=a0, a1=a1: e.dma_start(
                    out=stt[:, 0:a1 - a0], in_=w_in[kc * 128:(kc + 1) * 128, a0:a1]))
                S.op("dve", [T_s, T_const], [T_win], lambda e, stt=stt, kc=kc, a0=a0, a1=a1: e.tensor_scalar(
                    out=w_in_b[:, kc, a0:a1], in0=stt[:, 0:a1 - a0], scalar1=gmix_s[:, kc:kc + 1],
                    scalar2=None, op0=ALU.mult))
        for kc in range(8):
            stt, T_s = stg[i % 2]
            i += 1
            S.dma("sp", [], [T_s], lambda e, stt=stt, kc=kc: e.dma_start(
                out=stt[:, 0:DM], in_=w_out[kc * 128:(kc + 1) * 128, :]))
            S.op("act", [T_s], [T_wout], lambda e, stt=stt, kc=kc: e.copy(out=w_out_b[:, kc, :], in_=stt[:, 0:DM]))
        for w in range(31):
            for cb in range(4):
                S.op("dve", [T_const, T_identf], [T_diag], lambda e, w=w, cb=cb: e.tensor_scalar(
                    out=diag[:, w * 4 + cb, :], in0=ident_f[:], scalar1=convw_s[:, cb, w:w + 1], scalar2=None,
                    op0=ALU.mult))
        S.op("dve", [T_const], [T_eqm], lambda e: e.tensor_scalar(
            out=eqm[:], in0=bk_s[:], scalar1=32.0, scalar2=-30000.0, op0=ALU.is_equal, op1=ALU.mult))
        for h in range(4):
            S.op("dve", [T_eqm], [T_Tb], lambda e, h=h: e.tensor_copy(out=Tb[:, h, :, :], in_=eqm[:]))
        for b in range(32):
            S.op("dve", [T_const], [T_eqm], lambda e, b=b: e.tensor_scalar(
                out=eqm[:], in0=bk_s[:], scalar1=float(b), scalar2=None, op0=ALU.is_equal))
            for h in range(4):
                S.op("dve", [T_eqm, T_const, T_Tb], [T_Tb], lambda e, b=b, h=h: e.scalar_tensor_tensor(
                    out=Tb[:, h, :, :], in0=eqm[:], scalar=relb_s[:, b * 4 + h:b * 4 + h + 1], in1=Tb[:, h, :, :],
                    op0=ALU.mult, op1=ALU.add))
        S.op("dve", [T_const], [T_eqm], lambda e: e.tensor_tensor(
            out=eqm[:, 0, :].rearrange("p (a b) -> p a b", a=2), in0=lam_s[:].rearrange("p (a b c) -> p a b c", a=2, b=2)[:, :, 0, :],
            in1=lam_s[:].rearrange("p (a b c) -> p a b c", a=2, b=2)[:, :, 1, :], op=ALU.mult))
        S.op("dve", [T_eqm], [T_small], lambda e: e.reduce_sum(
            out=small[:, 0:2], in_=eqm[:, 0, :].rearrange("p (a b) -> p a b", a=2), axis=AX.X))
        S.op("act", [T_small], [T_small], lambda e: e.activation(out=small[:, 2:4], in_=small[:, 0:2], func=AF.Exp))
        S.op("dve", [T_small], [T_small], lambda e: e.tensor_tensor(
            out=small[:, 4:5], in0=small[:, 3:4], in1=small[:, 2:3], op=ALU.subtract))
        S.op("dve", [T_small], [T_small], lambda e: e.tensor_scalar(
            out=small[:, 4:5], in0=small[:, 4:5], scalar1=-LAM_INIT, scalar2=None, op0=ALU.add))
        S.op("dve", [T_const], [T_const], lambda e: e.tensor_scalar(
            out=gsub_s[:], in0=gsub_s[:], scalar1=1.0 - LAM_INIT, scalar2=None, op0=ALU.mult))
        neg_lam = small[:, 4:5]

        fT, T_fT = sb("fT", [128, 8, 512], BF16)
        xt, T_xt = sb("xt", [128, DM], F32)
        xr, T_xr = xt, T_xt
        junk, T_junk = sb("junk", [128, DM], BF16)
        hb, T_hb = sb("hb", [128, DM], BF16)
        stat, T_stat = sb("stat", [128, 8], F32)
        aT, T_aT = sb("aT", [128, 4, 30 + 512], BF16)
        sig, T_sig = sb("sig", [128, 512], F32)
        a32, T_a32 = sb("a32", [128, 512], F32)
        qT, T_qT = sb("qT", [128, 4, 512], BF16)
        kT, T_kT = sb("kT", [128, 4, SEQ + 64], BF16)
        vaug, T_v = sb("vaug", [128, 17, 4, 130], BF16)
        catT, T_cat = sb("catT", [128, 8, 512], BF16)
        zq, T_zq = sb("zq", [128, 512], BF16)
        zk32, T_zk32 = sb("zk32", [128, 512], F32)
        zkb, T_zkb = sb("zkb", [128, 512], BF16)
        zv32, T_zv32 = sb("zv32", [128, 512], F32)
        PTT = [sb(f"PT{i}", [128, 512], BF16) for i in range(3)]
        TMP = [sb(f"tmpb{i}", [128, 128], F32) for i in range(3)]
        att, T_att = sb("att", [128, 128], F32)
        attb, T_attb = sb("attb", [128, 512], BF16)
        astat, T_astat = sb("astat", [128, 8], F32)
        y32, T_y32 = sb("y32", [128, 4, 512], F32)
        ybf, T_ybf = sb("ybf", [128, 4, 512], BF16)
        ysq, T_ysq = sb("ysq", [128, 4, 512], BF16)
        mu, T_mu = sig, T_sig
        rs, T_rs = a32, T_a32
        ctail, T_ctail = zk32, T_zk32
        cst32, T_cst32 = sb("cst32", [128, 4, 30], F32)

        pT, T_pT = ps("pT", [128, 1024], BF16)
        pM = [ps(f"pM{i}", [128, 512], F32) for i in range(2)]
        pSS = [ps(f"pS{i}", [128, 512], F32) for i in range(2)]
        pO, T_pO = ps("pO", [128, 2, 256], F32)
        pX = [ps(f"pX{i}", [128, 512], F32) for i in range(2)]
        pm_i = [0]
        pOO = [(pO, T_pO), (pM[0][0][:].rearrange("p (a b) -> p a b", a=2), pM[0][1])]
        pSS = pSS + [pX[1]]

        def next_pM():
            pm_i[0] += 1
            return pM[pm_i[0] % 2]

        S.op("pool", [], [T_v], lambda e: e.memset(vaug[:], 1.0))

        def rms_to_bf16(n, src, T_src, dst_b, T_dst, col):
            S.op("act", [T_src], [T_junk, T_stat], lambda e: e.activation(
                out=junk[0:n, :], in_=src[0:n, :], func=AF.Square, accum_out=stat[0:n, col:col + 1]))
            S.op("act", [T_stat], [T_stat], lambda e: e.activation(
                out=stat[0:n, col + 1:col + 2], in_=stat[0:n, col:col + 1], func=AF.Sqrt, scale=1.0 / DM, bias=EPS_AP[0:n, :]))
            S.op("dve", [T_stat], [T_stat], lambda e: e.reciprocal(
                out=stat[0:n, col + 1:col + 2], in_=stat[0:n, col + 1:col + 2]))
            S.op("dve", [T_src, T_stat], [T_dst], lambda e: e.tensor_scalar(
                out=dst_b[0:n, :], in0=src[0:n, :], scalar1=stat[0:n, col + 1:col + 2], scalar2=None, op0=ALU.mult))

        def transpose_to_fT(n, src_b, T_src, c0):
            for kc in range(8):
                S.op("pe", [T_src, T_identb], [T_pT], lambda e, kc=kc: e.transpose(
                    out=pT[:, kc * 128:kc * 128 + n], in_=src_b[0:n, kc * 128:(kc + 1) * 128], identity=ident_b[0:n, 0:n]))
            S.op("act", [T_pT], [T_fT], lambda e: e.copy(
                out=fT[:, :, c0:c0 + n], in_=pT[:].rearrange("p (k c) -> p k c", k=8)[:, :, 0:n]))

        seqs = []
        for s_ in range(n_pseq):
            seqs.append(("p", xp[s_], SEQ, kp[s_], vp[s_], cp[s_], s_ * SEQ))
        seqs.append(("s", xs, 64, ks, vs, cs, n_pseq * SEQ))

        S.barrier()
        import os
        STOP = int(os.environ.get("KSTOP", "99"))
        if STOP <= 0:
            seqs = []

        for (kind, xd, ntok, kd, vd, cd, tok0) in seqs:
            past = SEQ if kind == "s" else 0
            if kind == "p":
                S.op("pool", [], [T_aT], lambda e: e.memset(aT[:, :, 0:30], 0.0))
            else:
                S.dma("sp", [], [T_cst32], lambda e: e.dma_start(out=cst32[:], in_=scT))
                S.op("dve", [T_cst32], [T_aT], lambda e: e.tensor_copy(out=aT[:, :, 0:30], in_=cst32[:]))
                for h in range(4):
                    for hf in range(2):
                        stt, T_s = stg[(h * 2 + hf) % 2]
                        S.dma("sp", [], [T_s], lambda e, stt=stt, h=h, hf=hf: e.dma_start(
                            out=stt[:, 0:1024], in_=ckT[h, :, hf * 1024:(hf + 1) * 1024]))
                        S.op("act", [T_s], [T_kT], lambda e, stt=stt, h=h, hf=hf: e.copy(
                            out=kT[:, h, hf * 1024:(hf + 1) * 1024], in_=stt[:, 0:1024]))
                for blk in range(16):
                    stt, T_s = stg[blk % 2]
                    S.dma("sp", [], [T_s], lambda e, stt=stt, blk=blk: e.dma_start(
                        out=stt[:, 0:512], in_=cv[blk * 128:(blk + 1) * 128, :]))
                    S.op("dve", [T_s], [T_v], lambda e, stt=stt, blk=blk: e.tensor_copy(
                        out=vaug[:, blk, :, 0:128], in_=stt[:, 0:512].rearrange("p (h e) -> p h e", h=4)))

            ngroups = (ntok + 511) // 512
            for g in range(ngroups):
                g0 = g * 512
                N = min(512, ntok - g0)
                tiles = [(c0, min(128, N - c0)) for c0 in range(0, N, 128)]
                last_group = (g == ngroups - 1)

                for (c0, n) in tiles:
                    S.dma("sp", [], [T_xt], lambda e, c0=c0, n=n: e.dma_start(out=xt[0:n, :], in_=xd[g0 + c0:g0 + c0 + n, :]))
                    rms_to_bf16(n, xt, T_xt, hb, T_hb, 0)
                    transpose_to_fT(n, hb, T_hb, c0)

                if STOP <= 1:
                    continue
                for (c0, n) in tiles:
                    blk = (past + g0 + c0) // 128
                    kcol = past + g0 + c0
                    for j in range(int(os.environ.get('KJ', '3'))):
                        pm, T_pm = next_pM()
                        for kc in range(8):
                            S.op("pe", [T_fT, T_win], [T_pm], lambda e, pm=pm, kc=kc, j=j, c0=c0, n=n: e.matmul(
                                pm[0:n, :], lhsT=fT[:, kc, c0:c0 + n], rhs=w_in_b[:, kc, 1024 + j * 512:1024 + (j + 1) * 512],
                                start=(kc == 0), stop=(kc == 7)))
                        if j == 0:
                            S.op("act", [T_pm], [T_zq], lambda e, pm=pm, n=n: e.activation(
                                out=zq[0:n, :], in_=pm[0:n, :], func=AF.Copy, scale=0.125))
                            for h in range(4):
                                S.op("pe", [T_zq, T_identb], [T_pT], lambda e, h=h, n=n: e.transpose(
                                    out=pT[:, h * 128:h * 128 + n], in_=zq[0:n, h * 128:(h + 1) * 128], identity=ident_b[0:n, 0:n]))
                            S.op("dve", [T_pT], [T_qT], lambda e, c0=c0, n=n: e.tensor_copy(
                                out=qT[:, :, c0:c0 + n], in_=pT[:, 0:512].rearrange("p (k c) -> p k c", k=4)[:, :, 0:n]))
                        elif j == 1:
                            if not os.environ.get("K1A"):
                                S.op("dve", [T_pm], [T_zk32], lambda e, pm=pm, n=n: e.tensor_copy(out=zk32[0:n, :], in_=pm[0:n, :]))
                            S.op("act", [T_zk32], [T_zkb], lambda e, pm=pm, n=n: e.copy(out=zkb[0:n, :], in_=zk32[0:n, :]))
                            if not os.environ.get("NOKD"):
                                S.dma("sp", [T_zk32], [], lambda e, c0=c0, n=n: e.dma_start(
                                    out=kd[g0 + c0:g0 + c0 + n, :], in_=zk32[0:n, :]))
                            for h in range(0 if os.environ.get("K1B") else 4):
                                S.op("pe", [T_zkb, T_identb], [T_pT], lambda e, h=h, n=n: e.transpose(
                                    out=pT[:, 512 + h * 128:512 + h * 128 + n], in_=zkb[0:n, h * 128:(h + 1) * 128],
                                    identity=ident_b[0:n, 0:n]))
                            if not os.environ.get("K1C"):
                              S.op("dve", [T_pT], [T_kT], lambda e, kcol=kcol, n=n: e.tensor_copy(
                                out=kT[:, :, kcol:kcol + n], in_=pT[:, 512:1024].rearrange("p (k c) -> p k c", k=4)[:, :, 0:n]))
                        else:
                            S.op("dve", [T_pm], [T_zv32], lambda e, pm=pm, n=n: e.tensor_copy(out=zv32[0:n, :], in_=pm[0:n, :]))
                            S.op("act", [T_zv32], [T_v], lambda e, pm=pm, n=n, blk=blk: e.copy(
                                out=vaug[0:n, blk, :, 0:128], in_=zv32[0:n, :].rearrange("p (h e) -> p h e", h=4)))
                            S.dma("sp", [T_zv32], [], lambda e, c0=c0, n=n: e.dma_start(
                                out=vd[g0 + c0:g0 + c0 + n, :], in_=zv32[0:n, :]))

                if STOP <= 2:
                    continue
                for cb in range(4):
                    pa, T_pa = next_pM()
                    pg, T_pg = next_pM()
                    for kc in range(8):
                        S.op("pe", [T_fT, T_win], [T_pa], lambda e, pa=pa, kc=kc, cb=cb: e.matmul(
                            pa[:, 0:N], lhsT=w_in_b[:, kc, cb * 128:(cb + 1) * 128], rhs=fT[:, kc, 0:N],
                            start=(kc == 0), stop=(kc == 7)))
                    for kc in range(8):
                        S.op("pe", [T_fT, T_win], [T_pg], lambda e, pg=pg, kc=kc, cb=cb: e.matmul(
                            pg[:, 0:N], lhsT=w_in_b[:, kc, 512 + cb * 128:512 + (cb + 1) * 128], rhs=fT[:, kc, 0:N],
                            start=(kc == 0), stop=(kc == 7)))
                    S.op("act", [T_pg], [T_sig], lambda e, pg=pg: e.activation(out=sig[:, 0:N], in_=pg[:, 0:N], func=AF.Sigmoid))
                    S.op("dve", [T_pa, T_sig], [T_a32], lambda e, pa=pa: e.tensor_tensor(
                        out=a32[:, 0:N], in0=pa[:, 0:N], in1=sig[:, 0:N], op=ALU.mult))
                    S.op("pool", [T_a32], [T_aT], lambda e, cb=cb: e.tensor_copy(out=aT[:, cb, 30:30 + N], in_=a32[:, 0:N]))
                    if last_group:
                        pm, T_pm = pX[0]
                        S.op("pe", [T_a32, T_identf], [T_pm], lambda e, pm=pm, cb=cb: e.transpose(
                            out=pm[0:30, cb * 128:(cb + 1) * 128], in_=a32[:, N - 30:N], identity=ident_f[:]))
                if last_group:
                    pm, T_pm = pX[0]
                    S.op("act", [T_pm], [T_ctail], lambda e, pm=pm: e.copy(out=ctail[0:30, :], in_=pm[0:30, :]))
                    S.dma("sp", [T_ctail], [], lambda e: e.dma_start(out=cd, in_=ctail[0:30, :]))

                if STOP <= 3:
                    continue
                S.cap = []
                for cb in range(4):
                    pm, T_pm = pM[1]
                    for w in range(31):
                        S.op("pe", [T_aT, T_diag], [T_pm], lambda e, pm=pm, w=w, cb=cb: e.matmul(
                            pm[:, 0:N], lhsT=diag[:, w * 4 + cb, :], rhs=aT[:, cb, w:w + N], start=(w == 0), stop=(w == 30)))
                    S.op("act", [T_pm, T_const], [T_y32], lambda e, pm=pm, cb=cb: e.activation(
                        out=y32[:, cb, 0:N], in_=pm[:, 0:N], func=AF.Identity, bias=cvec_s[:, cb:cb + 1]))
                    S.op("act", [T_pm, T_const], [T_ysq], lambda e, pm=pm, cb=cb: e.activation(
                        out=ysq[:, cb, 0:N], in_=pm[:, 0:N], func=AF.Square, bias=cvec_s[:, cb:cb + 1]))
                    S.op("pool", [T_y32], [T_ybf], lambda e, cb=cb: e.tensor_copy(out=ybf[:, cb, 0:N], in_=y32[:, cb, 0:N]))
                p1, T_p1 = pX[0]
                p2, T_p2 = pX[0]
                for cb in range(4):
                    S.op("pe", [T_ybf, T_ones], [T_p1], lambda e, cb=cb: e.matmul(
                        p1[:, 0:N], lhsT=ones_b[:], rhs=ybf[:, cb, 0:N], start=(cb == 0), stop=(cb == 3)))
                S.op("dve", [T_p1], [T_mu], lambda e: e.tensor_scalar(
                    out=mu[:, 0:N], in0=p1[:, 0:N], scalar1=1.0 / 512, scalar2=None, op0=ALU.mult))
                for cb in range(4):
                    S.op("pe", [T_ysq, T_ones], [T_p2], lambda e, cb=cb: e.matmul(
                        p2[:, 0:N], lhsT=ones_b[:], rhs=ysq[:, cb, 0:N], start=(cb == 0), stop=(cb == 3)))
                S.op("dve", [T_mu], [T_rs], lambda e: e.tensor_tensor(out=rs[:, 0:N], in0=mu[:, 0:N], in1=mu[:, 0:N], op=ALU.mult))
                S.op("dve", [T_p2, T_rs], [T_rs], lambda e: e.scalar_tensor_tensor(
                    out=rs[:, 0:N], in0=p2[:, 0:N], scalar=1.0 / 512, in1=rs[:, 0:N], op0=ALU.mult, op1=ALU.subtract))
                S.op("act", [T_rs, T_eps], [T_rs], lambda e: e.activation(
                    out=rs[:, 0:N], in_=rs[:, 0:N], func=AF.Sqrt, bias=eps_t[:, :]))
                S.op("dve", [T_rs], [T_rs], lambda e: e.reciprocal(out=rs[:, 0:N], in_=rs[:, 0:N]))
                for cb in range(4):
                    S.op("dve", [T_y32, T_mu], [T_y32], lambda e, cb=cb: e.tensor_tensor(
                        out=y32[:, cb, 0:N], in0=y32[:, cb, 0:N], in1=mu[:, 0:N], op=ALU.subtract))
                    S.op("pool", [T_y32, T_rs], [T_y32], lambda e, cb=cb: e.tensor_tensor(
                        out=y32[:, cb, 0:N], in0=y32[:, cb, 0:N], in1=rs[:, 0:N], op=ALU.mult))
                    S.op("act", [T_y32, T_const], [T_cat], lambda e, cb=cb: e.activation(
                        out=catT[:, cb, 0:N], in_=y32[:, cb, 0:N], func=AF.Silu,
                        scale=cvec_s[:, 4 + cb:5 + cb], bias=cvec_s[:, 8 + cb:9 + cb]))
                if not last_group:
                    S.op("pool", [T_aT], [T_aT], lambda e: e.tensor_copy(out=aT[:, :, 0:30], in_=aT[:, :, N:N + 30]))

                capD = S.cap
                S.cap = None
                items = []
                for (c0, n) in tiles:
                    qi = (g0 + c0) // 128
                    if kind == "p":
                        far = list(range(0, max(qi - 1, 0)))
                        near = ([(qi - 1, 128, 1)] if qi >= 1 else []) + [(qi, 128, 0)]
                    else:
                        far = list(range(0, 15))
                        near = [(15, 128, 1), (16, 64, 0)]
                    nblk = len(far) + len(near)
                    for h in range(4):
                        for m in range(2):
                            done = 0
                            for f0 in range(0, len(far), 4):
                                chunk = far[f0:f0 + 4]
                                items.append(dict(kind="far", c0=c0, n=n, h=h, m=m, blks=chunk, done=done, nblk=nblk))
                                done += len(chunk)
                            for (blk, nk, bkind) in near:
                                items.append(dict(kind="near", c0=c0, n=n, h=h, m=m, blk=blk, nk=nk, bkind=bkind,
                                                  done=done, nblk=nblk))
                                done += 1
                        items[-1]["head_end"] = True
                    items[-1]["tile_end"] = True

                def emit_qk(k, it):
                    pS_, T_pS_ = pSS[k % 3]
                    PT_, T_PT_ = PTT[k % 3]
                    c0, n, h, m = it["c0"], it["n"], it["h"], it["m"]
                    mrow = slice(m * 64, (m + 1) * 64)
                    if it["kind"] == "far":
                        for j, blk in enumerate(it["blks"]):
                            S.op("pe", [T_kT, T_qT], [T_pS_], lambda e, j=j, blk=blk: e.matmul(
                                pS_[:, j * n:(j + 1) * n], lhsT=kT[mrow, h, blk * 128:(blk + 1) * 128],
                                rhs=qT[mrow, h, c0:c0 + n], start=True, stop=True))
                        cn = len(it["blks"]) * n
                        S.op("act", [T_pS_, T_const], [T_PT_], lambda e: e.activation(
                            out=PT_[:, 0:cn], in_=pS_[:, 0:cn], func=AF.Exp, bias=relb_s[:, 60 + h:61 + h]))
                    else:
                        blk, nk, bkind = it["blk"], it["nk"], it["bkind"]
                        tb_, T_tb_ = TMP[k % 3]
                        S.op("pe", [T_kT, T_qT], [T_pS_], lambda e: e.matmul(
                            pS_[0:nk, 0:n], lhsT=kT[mrow, h, blk * 128:blk * 128 + nk],
                            rhs=qT[mrow, h, c0:c0 + n], start=True, stop=True))
                        S.op("dve", [T_pS_, T_Tb], [T_tb_], lambda e: e.tensor_tensor(
                            out=tb_[0:nk, 0:n], in0=pS_[0:nk, 0:n], in1=Tb[0:nk, h, bkind, 0:n], op=ALU.add))
                        S.op("act", [T_tb_], [T_PT_], lambda e: e.activation(
                            out=PT_[0:nk, 0:n], in_=tb_[0:nk, 0:n], func=AF.Exp))

                def emit_pv(k, it):
                    PT_, T_PT_ = PTT[k % 3]
                    c0, n, h, m = it["c0"], it["n"], it["h"], it["m"]
                    qi_ = (g0 + c0) // 128
                    pO_, T_pO_ = pOO[(qi_ * 4 + h) % 2]
                    nblk = it["nblk"]
                    if it["kind"] == "far":
                        for j, blk in enumerate(it["blks"]):
                            dn = it["done"] + j
                            S.op("pe", [T_PT_, T_v], [T_pO_], lambda e, j=j, blk=blk, dn=dn: e.matmul(
                                pO_[0:n, m, 0:129], lhsT=PT_[:, j * n:(j + 1) * n], rhs=vaug[:, blk, h, 0:129],
                                start=(dn == 0), stop=(dn == nblk - 1)))
                    else:
                        blk, nk = it["blk"], it["nk"]
                        dn = it["done"]
                        S.op("pe", [T_PT_, T_v], [T_pO_], lambda e: e.matmul(
                            pO_[0:n, m, 0:129], lhsT=PT_[0:nk, 0:n], rhs=vaug[0:nk, blk, h, 0:129],
                            start=(dn == 0), stop=(dn == nblk - 1)))
                    if it.get("head_end"):
                        S.op("dve", [T_pO_], [T_astat], lambda e: e.reciprocal(
                            out=astat[0:n, 0:2], in_=pO_[0:n, :, 128:129].rearrange("p a b -> p (a b)")))
                        S.op("dve", [T_astat, T_small], [T_astat], lambda e: e.tensor_tensor(
                            out=astat[0:n, 2:3], in0=astat[0:n, 1:2], in1=neg_lam[0:n, :], op=ALU.mult))
                        S.op("dve", [T_pO_, T_astat], [T_att], lambda e: e.tensor_scalar(
                            out=att[0:n, :], in0=pO_[0:n, 0, 0:128], scalar1=astat[0:n, 0:1], scalar2=None, op0=ALU.mult))
                        S.op("dve", [T_pO_, T_astat, T_att], [T_att], lambda e: e.scalar_tensor_tensor(
                            out=att[0:n, :], in0=pO_[0:n, 1, 0:128], scalar=astat[0:n, 2:3], in1=att[0:n, :],
                            op0=ALU.mult, op1=ALU.add))
                        S.op("act", [T_att], [T_junk, T_astat], lambda e: e.activation(
                            out=junk[0:n, 0:128], in_=att[0:n, :], func=AF.Square, accum_out=astat[0:n, 3:4]))
                        S.op("act", [T_astat, T_eps], [T_astat], lambda e: e.activation(
                            out=astat[0:n, 4:5], in_=astat[0:n, 3:4], func=AF.Sqrt, scale=1.0 / 128, bias=eps_t[0:n, :]))
                        S.op("dve", [T_astat], [T_astat], lambda e: e.reciprocal(out=astat[0:n, 4:5], in_=astat[0:n, 4:5]))
                        S.op("dve", [T_att, T_astat, T_const], [T_attb], lambda e: e.scalar_tensor_tensor(
                            out=attb[0:n, h * 128:(h + 1) * 128], in0=att[0:n, :], scalar=astat[0:n, 4:5], in1=gsub_s[0:n, :],
                            op0=ALU.mult, op1=ALU.mult))
                    if it.get("tile_end"):
                        for hh in range(4):
                            S.op("pe", [T_attb, T_identb], [T_pT], lambda e, hh=hh: e.transpose(
                                out=pT[:, hh * 128:hh * 128 + n], in_=attb[0:n, hh * 128:(hh + 1) * 128], identity=ident_b[0:n, 0:n]))
                        S.op("act", [T_pT], [T_cat], lambda e: e.copy(
                            out=catT[:, 4:8, c0:c0 + n], in_=pT[:, 0:512].rearrange("p (k c) -> p k c", k=4)[:, :, 0:n]))

                per_item = -(-len(capD) // max(len(items) - 4, 1))
                cpos = 0
                for k, it in enumerate(items):
                    emit_qk(k, it)
                    if k >= 2:
                        emit_pv(k - 2, items[k - 2])
                    for _ in range(per_item):
                        if cpos < len(capD):
                            S.replay(capD[cpos])
                            cpos += 1
                for k in range(max(len(items) - 2, 0), len(items)):
                    emit_pv(k, items[k])
                while cpos < len(capD):
                    S.replay(capD[cpos])
                    cpos += 1

                if STOP <= 5:
                    continue
                for (c0, n) in tiles:
                    S.dma("sp", [], [T_xr], lambda e, c0=c0, n=n: e.dma_start(out=xr[0:n, :], in_=xd[g0 + c0:g0 + c0 + n, :]))
                    for hf in range(2):
                        po, T_po = pX[hf]
                        for kc in range(8):
                            S.op("pe", [T_cat, T_wout], [T_po], lambda e, po=po, kc=kc, hf=hf, c0=c0, n=n: e.matmul(
                                po[0:n, :], lhsT=catT[:, kc, c0:c0 + n], rhs=w_out_b[:, kc, hf * 512:(hf + 1) * 512],
                                start=(kc == 0), stop=(kc == 7)))
                        S.op("dve", [T_po, T_xr], [T_xr], lambda e, po=po, hf=hf, n=n: e.tensor_tensor(
                            out=xr[0:n, hf * 512:(hf + 1) * 512], in0=po[0:n, :], in1=xr[0:n, hf * 512:(hf + 1) * 512], op=ALU.add))
                    S.dma("sp", [T_xr], [], lambda e, c0=c0, n=n: e.dma_start(
                        out=x1d[tok0 + g0 + c0:tok0 + g0 + c0 + n, :], in_=xr[0:n, :]))
                    rms_to_bf16(n, xr, T_xr, hb, T_hb, 2)
                    transpose_to_fT(n, hb, T_hb, c0)
                for kc in range(8):
                    S.dma("sp", [T_fT], [], lambda e, kc=kc: e.dma_start(
                        out=h2Td[kc, :, tok0 + g0:tok0 + g0 + N], in_=fT[:, kc, 0:N]))

        S.barrier()
        st1.close()
        st2 = ExitStack()
        st.enter_context(st2)
        cur[0] = st2
        TG = 256
        NCH = 64
        if with_peer:
            ijwd = nc.dram_tensor("ijwd", [128, 3, NT], F32, kind="Internal").ap()
            uscr_t = nc.dram_tensor("uscr", [NCH, 128, 8 * 256], BF16, kind="Internal").ap()
            vscr_t = nc.dram_tensor("vscr", [NCH, 128, 2 * DM], BF16, kind="Internal").ap()
            uscr = [uscr_t[ic].rearrange("p (k e) -> p k e", k=8) for ic in range(NCH)]
            vscr = [vscr_t[ic].rearrange("p (b d) -> p b d", b=2) for ic in range(NCH)]
            T_uscr = [S.tile(f"uscr{ic}") for ic in range(NCH)]
            T_vscr = [S.tile(f"vscr{ic}") for ic in range(NCH)]
            T_ijwd = S.tile("ijwd")

            gfin_s, T_gfin = sb("gfin_s", [128, DM], F32)
            iota_b, T_iotab = sb("iota_b", [128, 128], BF16)
            T_c2 = S.tile("consts2")
            S.dma("sp", [], [T_c2], lambda e: e.dma_start(out=gfin_s[:], in_=gfin.to_broadcast([128, DM])))
            S.op("dve", [T_iotar], [T_iotab], lambda e: e.tensor_copy(out=iota_b[:], in_=iota_r[:]))

            st2a = ExitStack()
            cur[0] = st2a
            TA = 512
            wq_b, T_wq = sb("wq_b", [128, 8, 2048], BF16)
            keys_b, T_keys = sb("keys_b", [128, 16, 128], BF16)
            gffn_s, T_gffn = sb("gffn_s", [128, 8], F32)
            sg0, T_sg0 = sb("sg0", [128, 1024], F32)
            sg1, T_sg1 = sb("sg1", [128, 1024], F32)
            h2g, T_h2g = sb("h2g", [128, 8, TA], BF16)
            qryT, T_qry = sb("qryT", [128, 16, TA], BF16)
            s_sb, T_ssb = sb("s_sb", [128, 16, 128], F32)
            wk4l = [sb(f"wk4_{i}", [128, 256], F32) for i in range(4)]
            wk4 = [x[0] for x in wk4l]
            T_wk4 = [x[1] for x in wk4l]
            T_A4 = [S.tile(f"A4_{i}") for i in range(4)]
            T_I4 = [S.tile(f"I4_{i}") for i in range(4)]
            T_C4 = [S.tile(f"C4_{i}") for i in range(4)]
            T_P4 = [S.tile(f"P4t_{i}") for i in range(4)]
            A_, T_A = sb("A_", [128, 16, 16], F32)
            Iu, T_Iu = sb("Iu", [128, 16, 16], U32)
            If, T_If = sb("If", [128, 16, 16], F32)
            cand, T_cand = sb("cand", [128, 8, 256], F32)
            C_, T_C = sb("C_", [128, 8, 16], F32)
            pos, T_pos = sb("pos", [128, 8, 16], U32)
            ku, T_ku = sb("ku", [128, 2, 128], U32)
            kf, T_kf = sb("kf", [128, 2, 128], F32)
            E_, T_E = sb("E_", [128, 8, 16], F32)
            gst, T_gst = sb("gst", [128, 32], F32)
            oh, T_oh = sb("oh", [128, 8, 16, 16], F32)
            ijw, T_ijw = sb("ijw", [128, 3, 128], F32)
            ijT = [sb(f"ijT{i}", [128, 3, 128], F32) for i in range(2)]
            stu = [sb(f"stu{i}", [128, 8, 256], BF16) for i in range(2)]
            stv = [sb(f"stv{i}", [128, 2, DM], BF16) for i in range(2)]
            pGa = [ps(f"pGa{i}", [128, 512], F32) for i in range(4)]
            pga_i = [0]

            def next_pGa():
                pga_i[0] += 1
                return pGa[pga_i[0] % 4]

            S.dma("sp", [], [T_c2], lambda e: e.dma_start(out=gffn_s[:], in_=gffn))
            S.dma("pool", [], [T_keys], lambda e: e.dma_start(out=keys_b[:], in_=keysT.rearrange("r d n -> d r n")))
            sgs = [(sg0, T_sg0), (sg1, T_sg1)]
            ii = 0
            for kc in range(8):
                for hf in range(2):
                    stt, T_s = sgs[ii % 2]
                    ii += 1
                    S.dma("sp", [], [T_s], lambda e, stt=stt, kc=kc, hf=hf: e.dma_start(
                        out=stt[:], in_=wq[kc * 128:(kc + 1) * 128, hf * 1024:(hf + 1) * 1024]))
                    S.op("dve", [T_s, T_c2], [T_wq], lambda e, stt=stt, kc=kc, hf=hf: e.tensor_scalar(
                        out=wq_b[:, kc, hf * 1024:(hf + 1) * 1024], in0=stt[:], scalar1=gffn_s[:, kc:kc + 1],
                        scalar2=None, op0=ALU.mult))

            conv_i = [0]

            def convert_chunk():
                ic = conv_i[0]
                if ic >= NCH:
                    return
                conv_i[0] += 1
                (su, T_su), (sv, T_sv) = stu[ic % 2], stv[ic % 2]
                for k4 in range(2):
                    S.dma("pool", [], [T_su], lambda e, k4=k4: e.dma_start(
                        out=su[:, k4 * 4:(k4 + 1) * 4, :],
                        in_=uT[k4 * 512:(k4 + 1) * 512, ic * 256:(ic + 1) * 256].rearrange("(k p) e -> p k e", p=128)))
                S.dma("pool", [], [T_sv], lambda e: e.dma_start(
                    out=sv[:], in_=pv[ic * 256:(ic + 1) * 256, :].rearrange("(b p) d -> p b d", p=128)))
                S.dma("sp", [T_su], [T_uscr[ic]], lambda e: e.dma_start(out=uscr[ic], in_=su[:]), key=T_su)
                S.dma("sp", [T_sv], [T_vscr[ic]], lambda e: e.dma_start(out=vscr[ic], in_=sv[:]), key=T_sv)

            tile_ctr = 0
            for t0 in range(0, NT, TA):
                N = min(TA, NT - t0)
                tiles = [(c0, min(128, N - c0)) for c0 in range(0, N, 128)]
                S.dma("sp", [], [T_h2g], lambda e: e.dma_start(
                    out=h2g[:, :, 0:N], in_=h2Td[:, :, t0:t0 + N].rearrange("k p t -> p k t")))
                for blk in range(16):
                    pg, T_pg = next_pGa()
                    for kc in range(8):
                        S.op("pe", [T_wq, T_h2g], [T_pg], lambda e, pg=pg, kc=kc, blk=blk: e.matmul(
                            pg[:, 0:N], lhsT=wq_b[:, kc, blk * 128:(blk + 1) * 128], rhs=h2g[:, kc, 0:N],
                            start=(kc == 0), stop=(kc == 7)))
                    S.op("act", [T_pg], [T_qry], lambda e, pg=pg, blk=blk: e.copy(out=qryT[:, blk, 0:N], in_=pg[:, 0:N]))
                for (c0, n) in tiles:
                    convert_chunk()
                    convert_chunk()
                    for q4 in range(4):
                        pg, T_pg = next_pGa()
                        for j in range(4):
                            rp = q4 * 4 + j
                            S.op("pe", [T_qry, T_keys], [T_pg], lambda e, pg=pg, j=j, rp=rp: e.matmul(
                                pg[0:n, j * 128:(j + 1) * 128], lhsT=qryT[:, rp, c0:c0 + n], rhs=keys_b[:, rp, :],
                                start=True, stop=True))
                        S.op("act", [T_pg], [T_ssb], lambda e, pg=pg, q4=q4: e.copy(
                            out=s_sb[0:n, q4 * 4:(q4 + 1) * 4, :], in_=pg[0:n, :].rearrange("p (a b) -> p a b", a=4)))
                    for rp0 in range(0, 16, 4):
                        rps = list(range(rp0, rp0 + 4))
                        for rp in rps:
                            S.op("dve", [T_ssb], [T_A4[rp % 4]], lambda e, rp=rp: e.max(out=A_[0:n, rp, 0:8], in_=s_sb[0:n, rp, :]))
                        for rp in rps:
                            S.op("dve", [T_ssb, T_A4[rp % 4]], [T_I4[rp % 4]], lambda e, rp=rp: e.max_index(
                                out=Iu[0:n, rp, 0:8], in_max=A_[0:n, rp, 0:8], in_values=s_sb[0:n, rp, :]))
                        for rp in rps:
                            S.op("dve", [T_ssb, T_A4[rp % 4]], [T_wk4[rp % 4]], lambda e, rp=rp: e.match_replace(
                                out=wk4[rp % 4][0:n, 0:128], in_to_replace=A_[0:n, rp, 0:8], in_values=s_sb[0:n, rp, :], imm_value=-1e30))
                        for rp in rps:
                            S.op("dve", [T_wk4[rp % 4]], [T_A4[rp % 4]], lambda e, rp=rp: e.max(out=A_[0:n, rp, 8:16], in_=wk4[rp % 4][0:n, 0:128]))
                        for rp in rps:
                            S.op("dve", [T_wk4[rp % 4], T_A4[rp % 4]], [T_I4[rp % 4]], lambda e, rp=rp: e.max_index(
                                out=Iu[0:n, rp, 8:16], in_max=A_[0:n, rp, 8:16], in_values=wk4[rp % 4][0:n, 0:128]))
                    S.op("dve", T_I4, [T_If], lambda e: e.tensor_copy(out=If[0:n], in_=Iu[0:n]))
                    A4 = A_[0:n].rearrange("p (r a) k -> p r a k", a=2)
                    I4 = If[0:n].rearrange("p (r a) k -> p r a k", a=2)
                    S.op("dve", T_A4, [T_cand], lambda e: e.tensor_tensor(
                        out=cand[0:n].rearrange("p r (a b) -> p r a b", a=16),
                        in0=A4[:, :, 0, :].unsqueeze(3).to_broadcast([n, 8, 16, 16]),
                        in1=A4[:, :, 1, :].unsqueeze(2).to_broadcast([n, 8, 16, 16]), op=ALU.add))
                    for r0 in range(0, 8, 4):
                        rs_ = list(range(r0, r0 + 4))
                        for r in rs_:
                            S.op("dve", [T_cand], [T_C4[r % 4]], lambda e, r=r: e.max(out=C_[0:n, r, 0:8], in_=cand[0:n, r, :]))
                        for r in rs_:
                            S.op("dve", [T_cand, T_C4[r % 4]], [T_P4[r % 4]], lambda e, r=r: e.max_index(
                                out=pos[0:n, r, 0:8], in_max=C_[0:n, r, 0:8], in_values=cand[0:n, r, :]))
                        for r in rs_:
                            S.op("dve", [T_cand, T_C4[r % 4]], [T_wk4[r % 4]], lambda e, r=r: e.match_replace(
                                out=wk4[r % 4][0:n, :], in_to_replace=C_[0:n, r, 0:8], in_values=cand[0:n, r, :], imm_value=-1e30))
                        for r in rs_:
                            S.op("dve", [T_wk4[r % 4]], [T_C4[r % 4]], lambda e, r=r: e.max(out=C_[0:n, r, 8:16], in_=wk4[r % 4][0:n, :]))
                        for r in rs_:
                            S.op("dve", [T_wk4[r % 4], T_C4[r % 4]], [T_P4[r % 4]], lambda e, r=r: e.max_index(
                                out=pos[0:n, r, 8:16], in_max=C_[0:n, r, 8:16], in_values=wk4[r % 4][0:n, :]))
                    S.op("dve", T_C4, [T_gst], lambda e: e.tensor_scalar(
                        out=gst[0:n, 0:8], in0=C_[0:n, :, 0], scalar1=-1.0, scalar2=None, op0=ALU.mult))
                    for r in range(8):
                        S.op("act", T_C4 + [T_gst], [T_E, T_gst], lambda e, r=r: e.activation(
                            out=E_[0:n, r, :], in_=C_[0:n, r, :], func=AF.Exp, bias=gst[0:n, r:r + 1],
                            accum_out=gst[0:n, 8 + r:9 + r]))
                    S.op("dve", [T_gst], [T_gst], lambda e: e.reciprocal(out=gst[0:n, 16:24], in_=gst[0:n, 8:16]))
                    S.op("dve", [T_E, T_gst], [T_ijw], lambda e: e.tensor_tensor(
                        out=ijw[0:n, 2, :].rearrange("p (r k) -> p r k", r=8), in0=E_[0:n],
                        in1=gst[0:n, 16:24].unsqueeze(2).to_broadcast([n, 8, 16]), op=ALU.mult))
                    S.op("dve", T_P4, [T_ku], lambda e: e.tensor_single_scalar(
                        out=ku[0:n, 0, :], in_=pos[0:n].rearrange("p r k -> p (r k)"), scalar=4, op=ALU.logical_shift_right))
                    S.op("dve", T_P4, [T_ku], lambda e: e.tensor_single_scalar(
                        out=ku[0:n, 1, :], in_=pos[0:n].rearrange("p r k -> p (r k)"), scalar=15, op=ALU.bitwise_and))
                    S.op("dve", [T_ku], [T_kf], lambda e: e.tensor_copy(out=kf[0:n], in_=ku[0:n]))
                    for a in range(2):
                        S.op("dve", [T_kf, T_iotar], [T_oh], lambda e, a=a: e.tensor_tensor(
                            out=oh[0:n],
                            in0=kf[0:n, a, :].rearrange("p (r k) -> p r k", r=8).unsqueeze(3).to_broadcast([n, 8, 16, 16]),
                            in1=iota_r[0:n, 0:16].unsqueeze(1).unsqueeze(1).to_broadcast([n, 8, 16, 16]), op=ALU.is_equal))
                        S.op("dve", [T_oh, T_If], [T_oh], lambda e, a=a: e.tensor_tensor(
                            out=oh[0:n], in0=oh[0:n],
                            in1=I4[:, :, a, :].unsqueeze(2).to_broadcast([n, 8, 16, 16]), op=ALU.mult))
                        S.op("dve", [T_oh], [T_ijw], lambda e, a=a: e.reduce_sum(
                            out=ijw[0:n, a, :].rearrange("p (r k) -> p r k", r=8), in_=oh[0:n], axis=AX.X))
                    pg, T_pg = next_pGa()
                    for a in range(3):
                        S.op("pe", [T_ijw, T_identf], [T_pg], lambda e, pg=pg, a=a: e.transpose(
                            out=pg[:, a * 128:a * 128 + n], in_=ijw[0:n, a, :], identity=ident_f[0:n, 0:n]))
                    (it_, T_it) = ijT[tile_ctr % 2]
                    tile_ctr += 1
                    S.op("act", [T_pg], [T_it], lambda e, pg=pg, it_=it_: e.copy(
                        out=it_[:, :, 0:n], in_=pg[:, 0:384].rearrange("p (a t) -> p a t", a=3)[:, :, 0:n]))
                    S.dma("sp", [T_it], [T_ijwd], lambda e, it_=it_: e.dma_start(
                        out=ijwd[:, :, t0 + c0:t0 + c0 + n], in_=it_[:, :, 0:n]), key=T_it)
            while conv_i[0] < NCH:
                convert_chunk()
            S.barrier()
            st2a.close()

            st2b = ExitStack()
            st.enter_context(st2b)
            cur[0] = st2b
            Gall = [sb(f"Gall{i}", [128, 128, TG], BF16) for i in range(2)]
            ubuf = [sb(f"ubuf{i}", [128, 8, 256], BF16) for i in range(2)]
            vbuf = [sb(f"vbuf{i}", [128, 2, DM], BF16) for i in range(3)]
            h2g2 = [sb(f"h2g2_{i}", [128, 8, TG], BF16) for i in range(2)]
            ijg = [sb(f"ijg{i}", [128, 3, TG], F32) for i in range(2)]
            P4 = [sb(f"P4_{i}", [128, 4, 128], BF16) for i in range(3)]
            Q4 = [sb(f"Q4_{i}", [128, 4, 128], BF16) for i in range(3)]
            gbuf = [sb(f"gbuf{i}", [128, TG], F32) for i in range(3)]
            cbuf = [sb(f"cbuf{i}", [128, TG], BF16) for i in range(3)]
            x2l = [sb(f"x2_{i}", [128, DM], F32) for i in range(2)]
            junk2, T_junk2 = sb("junk2", [128, DM], BF16)
            st2s, T_st2s = sb("st2s", [128, 8], F32)
            pY = [[ps(f"pY{t}{h}", [128, 512], F32) for h in range(2)] for t in range(2)]
            pA = [ps(f"pA{i}", [128, 512], F32) for i in range(3)]
            pG = [ps(f"pG{i}", [128, 512], F32) for i in range(1)]

            groups = [(t0, min(TG, NT - t0)) for t0 in range(0, NT, TG)]
            MAXG = int(os.environ.get("KGROUPS", "999"))
            groups = groups[:MAXG]
            NG = len(groups)
            MULT_ENG = os.environ.get("KMULT", "pool")

            def gtiles(g):
                N = groups[g][1]
                return [(c0, min(128, N - c0)) for c0 in range(0, N, 128)]

            def load_group(g):
                t0, N = groups[g]
                (hg, T_hg), (ij, T_ij) = h2g2[g % 2], ijg[g % 2]
                S.dma("sp", [], [T_hg], lambda e: e.dma_start(
                    out=hg[:, :, 0:N], in_=h2Td[:, :, t0:t0 + N].rearrange("k p t -> p k t")))
                S.dma("sp", [T_ijwd], [T_ij], lambda e: e.dma_start(out=ij[:, :, 0:N], in_=ijwd[:, :, t0:t0 + N]))

            def p10_dve(g, b):
                (ij, T_ij) = ijg[g % 2]
                (p4, T_p4), (q4_, T_q4) = P4[b % 3], Q4[b % 3]
                for u in range(4):
                    t = b * 4 + u
                    S.op("dve", [T_ij, T_iotab], [T_p4], lambda e, u=u, t=t: e.tensor_scalar(
                        out=p4[:, u, :], in0=iota_b[:], scalar1=ij[:, 0, t:t + 1], scalar2=None, op0=ALU.is_equal))
                    S.op("dve", [T_ij, T_iotab], [T_q4], lambda e, u=u, t=t: e.tensor_scalar(
                        out=q4_[:, u, :], in0=iota_b[:], scalar1=ij[:, 1, t:t + 1], scalar2=ij[:, 2, t:t + 1],
                        op0=ALU.is_equal, op1=ALU.mult))

            def p10_pe(g, b):
                (p4, T_p4), (q4_, T_q4) = P4[b % 3], Q4[b % 3]
                (ga, T_ga) = Gall[g % 2]
                pg, T_pg = pG[0]
                for u in range(4):
                    S.op("pe", [T_p4, T_q4], [T_pg], lambda e, u=u: e.matmul(
                        pg[:, u * 128:(u + 1) * 128], lhsT=q4_[:, u, :], rhs=p4[:, u, :], start=True, stop=True))
                S.op("act", [T_pg], [T_ga], lambda e: e.copy(
                    out=ga[:, :, b * 4:b * 4 + 4], in_=pg[:, :].rearrange("p (t i) -> p i t", t=4)))

            class P10:
                def __init__(self, g):
                    self.g = g
                    self.nb = groups[g][1] // 4
                    self.d = 0
                    self.p = 0

                def step(self):
                    if self.d < self.nb:
                        p10_dve(self.g, self.d)
                        self.d += 1
                        if self.d - self.p >= 3:
                            p10_pe(self.g, self.p)
                            self.p += 1
                    elif self.p < self.nb:
                        p10_pe(self.g, self.p)
                        self.p += 1

                def flush(self):
                    while self.p < self.nb:
                        if self.d < self.nb and self.d - self.p < 3:
                            p10_dve(self.g, self.d)
                            self.d += 1
                        else:
                            p10_pe(self.g, self.p)
                            self.p += 1

            chunk_ctr = [0]

            def load_chunk(ic):
                c = chunk_ctr[0]
                chunk_ctr[0] += 1
                (ub, T_ub), (vb, T_vb) = ubuf[c % 2], vbuf[c % 3]
                S.dma("sp", [T_uscr[ic]], [T_ub], lambda e: e.dma_start(out=ub[:], in_=uscr[ic]))
                S.dma("sp", [T_vscr[ic]], [T_vb], lambda e: e.dma_start(out=vb[:], in_=vscr[ic]))

            def stage_u(g, i):
                N = groups[g][1]
                c = g * NCH + i // 2
                ib = i % 2
                (ub, T_ub) = ubuf[c % 2]
                (hg, T_hg) = h2g2[g % 2]
                (ga, T_ga) = Gall[g % 2]
                gidx = g * 128 + i
                pa, T_pa = pA[gidx % 3]
                gb, T_gb = gbuf[gidx % 3]
                cb_, T_cb = cbuf[gidx % 3]
                for kc in range(8):
                    S.op("pe", [T_ub, T_hg], [T_pa], lambda e, kc=kc: e.matmul(
                        pa[:, 0:N], lhsT=ub[:, kc, ib * 128:(ib + 1) * 128], rhs=hg[:, kc, 0:N],
                        start=(kc == 0), stop=(kc == 7)))
                S.op("act", [T_pa], [T_gb], lambda e: e.activation(out=gb[:, 0:N], in_=pa[:, 0:N], func=AF.Gelu))
                S.op(MULT_ENG, [T_gb, T_ga], [T_cb], lambda e: e.tensor_tensor(
                    out=cb_[:, 0:N], in0=gb[:, 0:N], in1=ga[:, i, 0:N], op=ALU.mult))

            def stage_v(g, i):
                c = g * NCH + i // 2
                ib = i % 2
                (vb, T_vb) = vbuf[c % 3]
                cb_, T_cb = cbuf[(g * 128 + i) % 3]
                for ti, (c0, n) in enumerate(gtiles(g)):
                    for hf in range(2):
                        py, T_py = pY[ti][hf]
                        S.op("pe", [T_cb, T_vb], [T_py], lambda e, py=py, c0=c0, n=n, hf=hf: e.matmul(
                            py[0:n, :], lhsT=cb_[:, c0:c0 + n], rhs=vb[:, ib, hf * 512:(hf + 1) * 512],
                            start=(i == 0), stop=(i == 127)))

            def prefetch_x1(g):
                t0, N = groups[g]
                for ti, (c0, n) in enumerate(gtiles(g)):
                    x2, T_x2 = x2l[ti]
                    tg = t0 + c0
                    S.dma("sp", [], [T_x2], lambda e, tg=tg, n=n, x2=x2: e.dma_start(out=x2[0:n, :], in_=x1d[tg:tg + n, :]))

            def epilogue(g):
                t0, N = groups[g]
                for ti, (c0, n) in enumerate(gtiles(g)):
                    x2, T_x2 = x2l[ti]
                    for hf in range(2):
                        py, T_py = pY[ti][hf]
                        S.op("dve", [T_py, T_x2], [T_x2], lambda e, py=py, hf=hf, n=n, x2=x2: e.tensor_tensor(
                            out=x2[0:n, hf * 512:(hf + 1) * 512], in0=py[0:n, :], in1=x2[0:n, hf * 512:(hf + 1) * 512], op=ALU.add))
                for ti, (c0, n) in enumerate(gtiles(g)):
                    x2, T_x2 = x2l[ti]
                    tg = t0 + c0
                    k0 = ti * 4
                    S.op("act", [T_x2], [T_junk2, T_st2s], lambda e, n=n, x2=x2, k0=k0: e.activation(
                        out=junk2[0:n, :], in_=x2[0:n, :], func=AF.Square, accum_out=st2s[0:n, k0:k0 + 1]))
                    S.op("act", [T_st2s, T_eps], [T_st2s], lambda e, n=n, k0=k0: e.activation(
                        out=st2s[0:n, k0 + 1:k0 + 2], in_=st2s[0:n, k0:k0 + 1], func=AF.Sqrt, scale=1.0 / DM, bias=eps_t[0:n, :]))
                    S.op("dve", [T_st2s], [T_st2s], lambda e, n=n, k0=k0: e.reciprocal(out=st2s[0:n, k0 + 1:k0 + 2], in_=st2s[0:n, k0 + 1:k0 + 2]))
                    S.op("dve", [T_x2, T_st2s, T_c2], [T_x2], lambda e, n=n, x2=x2, k0=k0: e.scalar_tensor_tensor(
                        out=x2[0:n, :], in0=x2[0:n, :], scalar=st2s[0:n, k0 + 1:k0 + 2], in1=gfin_s[0:n, :], op0=ALU.mult, op1=ALU.mult))
                    if tg < n_pseq * SEQ:
                        dst = yp[tg // SEQ][tg % SEQ:tg % SEQ + n, :]
                    else:
                        dst = ys[0:n, :]
                    S.dma("sp", [T_x2], [], lambda e, dst=dst, n=n, x2=x2: e.dma_start(out=dst, in_=x2[0:n, :]))

            load_group(0)
            load_chunk(0)
            pz = P10(0)
            pz.flush()
            seq = [(g, i) for g in range(NG) for i in range(128)]
            nxt = None

            def emit_v(idx):
                g_, i_ = seq[idx]
                stage_v(g_, i_)
                if i_ == 127:
                    epilogue(g_)

            for idx, (g, i) in enumerate(seq):
                if i == 0:
                    nxt = None
                    if g + 1 < NG:
                        load_group(g + 1)
                        nxt = P10(g + 1)
                stage_u(g, i)
                if idx >= 2:
                    emit_v(idx - 2)
                if i % 2 == 0:
                    ic_next = i // 2 + 1
                    if ic_next < NCH:
                        load_chunk(ic_next)
                    elif g + 1 < NG:
                        load_chunk(0)
                if nxt is not None and (i % 2 == 1 or i in (8, 16, 24, 32)):
                    nxt.step()
                if i == 64:
                    prefetch_x1(g)
                if i == 127 and nxt is not None:
                    nxt.flush()
            emit_v(len(seq) - 2)
            emit_v(len(seq) - 1)

        S.barrier()
        S.finish("sp")
        print("ops per engine:", S.nops, "sems:", S.nsem)
    return nc


def _prep_shared(inp):
    f = lambda a: np.ascontiguousarray(np.asarray(a, dtype=np.float32))
    sh = {}
    sh["w_in"] = f(inp["w_in"][0])
    sh["gmix"] = f(inp["g_mix"][0].reshape(8, 128).T)
    sh["convw"] = f(inp["conv_w"][0].reshape(31, 4, 128).transpose(2, 1, 0))
    sh["cvec"] = f(np.concatenate([inp["conv_b"][0].reshape(4, 128).T, inp["conv_ln_g"][0].reshape(4, 128).T,
                                   inp["conv_ln_b"][0].reshape(4, 128).T], axis=1))
    sh["lam"] = f(np.stack([inp["lambda_q1"][0], inp["lambda_k1"][0], inp["lambda_q2"][0], inp["lambda_k2"][0]]).reshape(1, 256))
    sh["subg"] = f(inp["subln_g"][0].reshape(1, 128))
    sh["relb"] = f(inp["rel_bias"].reshape(1, 128))
    sh["w_out"] = f(inp["w_out"][0])
    sh["gffn"] = f(inp["g_ffn"][0].reshape(8, 128).T)
    sh["wq"] = f(inp["w_query"][0])
    sh["keysT"] = f(inp["sub_keys"][0].reshape(16, 128, 128).transpose(0, 2, 1))
    sh["uT"] = f(inp["peer_u"][0].T)
    sh["pv"] = f(inp["peer_v"][0])
    sh["gfin"] = f(inp["g_final"].reshape(1, DM))
    sh["bkc"] = _bucket_tiles()
    return sh


def kernel(**inp):
    f = lambda a: np.ascontiguousarray(np.asarray(a, dtype=np.float32))
    sh = _prep_shared(inp)
    nc = build_program(2)
    in_maps = []
    for c in range(NCORES):
        m = dict(sh)
        m["xp"] = f(inp["x_prompt"][2 * c:2 * c + 2])
        m["xs"] = f(inp["x_sample"][c])
        m["ckT"] = f(np.asarray(inp["cache_k"][0, c]).reshape(SEQ, 4, 128).transpose(1, 2, 0))
        m["cv"] = f(np.asarray(inp["cache_v"][0, c]).reshape(SEQ, 512))
        m["scT"] = f(np.asarray(inp["state_conv"][0, c]).reshape(30, 4, 128).transpose(2, 1, 0))
        in_maps.append(m)
    res = run_bass_kernel_spmd(nc, in_maps, core_ids=list(range(NCORES)))
    R = res.results
    y_prompt = np.concatenate([r["yp"] for r in R], axis=0)
    y_sample = np.stack([r["ys"] for r in R], axis=0)
    k_prompt = np.concatenate([r["kp"] for r in R], axis=0).reshape(1, 16, SEQ, 4, 2, 64)
    v_prompt = np.concatenate([r["vp"] for r in R], axis=0).reshape(1, 16, SEQ, 4, 128)
    c_prompt = np.concatenate([r["cp"] for r in R], axis=0).reshape(1, 16, 30, 512)
    k_sample = np.stack([r["ks"] for r in R], axis=0).reshape(1, 8, 64, 4, 2, 64)
    v_sample = np.stack([r["vs"] for r in R], axis=0).reshape(1, 8, 64, 4, 128)
    c_sample = np.stack([r["cs"] for r in R], axis=0).reshape(1, 8, 30, 512)
    return (y_prompt, y_sample, k_prompt, v_prompt, c_prompt, k_sample, v_sample, c_sample)
```

```python
import math
import os
from contextlib import ExitStack

import numpy as np
import concourse.bass as bass
import concourse.mybir as mybir
from concourse.bass_utils import run_bass_kernel_spmd

F32 = mybir.dt.float32
BF16 = mybir.dt.bfloat16
U32 = mybir.dt.uint32
AF = mybir.ActivationFunctionType
ALU = mybir.AluOpType
AX = mybir.AxisListType

EPS = 1e-6
LAM_INIT = 0.8 - 0.6 * math.exp(-0.3 * 0)
NCORES = 8
SEQ = 2048
DM = 1024
NEXP_SIDE = 128

SEM_LIMIT = 30000


class Counter:
    def __init__(self, S, name):
        self.S = S
        self.name = name
        self.epoch = 0
        self.val = 0
        self.sem = S.new_sem(f"{name}_e0")

    def bump(self, inc):
        if self.val + inc > SEM_LIMIT:
            self.epoch += 1
            self.val = 0
            self.sem = self.S.new_sem(f"{self.name}_e{self.epoch}")
        self.val += inc
        return (self.sem, self.val, self.name, self.epoch)


class Tile:
    __slots__ = ("name", "w", "r", "dmac")

    def __init__(self, name):
        self.name = name
        self.w = None
        self.r = []
        self.dmac = None


class Sched:
    def __init__(self, nc, stack):
        self.nc = nc
        self.stack = stack
        self.nsem = 0
        self.engs = {"pe": nc.tensor, "act": nc.scalar, "dve": nc.vector,
                     "pool": nc.gpsimd, "sp": nc.sync}
        self.cnt = {k: Counter(self, k) for k in self.engs}
        self.known = {k: {} for k in self.engs}
        self.nops = {k: 0 for k in self.engs}
        self.tiles = []
        self.cap = None
        self.snaps = {}

    def new_sem(self, name):
        self.nsem += 1
        return self.stack.enter_context(self.nc.semaphore(f"s{self.nsem}_{name}"))

    def tile(self, name):
        t = Tile(name)
        self.tiles.append(t)
        return t

    def _wait(self, e, ev):
        sem, val, name, epoch = ev
        key = (name, epoch)
        if self.known[e].get(key, 0) >= val:
            return
        self.known[e][key] = val
        self.engs[e].wait_ge(sem, val)
        snap = self.snaps.get((name, epoch, val))
        if snap:
            ke = self.known[e]
            for k2, v2 in snap.items():
                if ke.get(k2, 0) < v2:
                    ke[k2] = v2

    def _deps(self, reads, writes):
        evs = []
        for t in reads:
            if t.w is not None:
                evs.append(t.w)
        for t in writes:
            if t.w is not None:
                evs.append(t.w)
            evs.extend(t.r)
        return evs

    def op(self, e, reads, writes, fn):
        if self.cap is not None:
            self.cap.append(("op", e, reads, writes, fn, None))
            return None
        return self._op(e, reads, writes, fn)

    def dma(self, q, reads, writes, fn, key=None):
        if self.cap is not None:
            self.cap.append(("dma", q, reads, writes, fn, key))
            return None
        return self._dma(q, reads, writes, fn, key)

    def replay(self, item):
        kind, e, reads, writes, fn, key = item
        if kind == "op":
            return self._op(e, reads, writes, fn)
        return self._dma(e, reads, writes, fn, key)

    def _op(self, e, reads, writes, fn):
        for ev in self._deps(reads, writes):
            if ev[2] == e:
                if e == "pe":
                    continue
                if ev[3] == self.cnt[e].epoch and self.cnt[e].val - ev[1] >= 2:
                    continue
            self._wait(e, ev)
        ins = fn(self.engs[e])
        ev = self.cnt[e].bump(1)
        ins.then_inc(ev[0], 1)
        self.snaps[(ev[2], ev[3], ev[1])] = dict(self.known[e])
        self.nops[e] += 1
        self._mark(ev, reads, writes)
        return ev

    def _mark(self, ev, reads, writes):
        k = (ev[2], ev[3])
        for t in reads:
            t.r = [x for x in t.r if (x[2], x[3]) != k]
            t.r.append(ev)
        for t in writes:
            t.w = ev
            t.r = []

    def _dma(self, q, reads, writes, fn, key=None):
        kt = key or (writes[0] if writes else reads[0])
        if kt.dmac is None:
            kt.dmac = Counter(self, "d_" + kt.name)
        for ev in self._deps(reads, writes):
            self._wait(q, ev)
        ins = fn(self.engs[q])
        ev = kt.dmac.bump(16)
        ins.then_inc(ev[0], 16)
        self.snaps[(ev[2], ev[3], ev[1])] = dict(self.known[q])
        self.nops[q] += 1
        self._mark(ev, reads, writes)
        return ev

    def _all_events(self):
        evs = {}
        for t in self.tiles:
            for ev in ([t.w] if t.w else []) + t.r:
                k = (ev[2], ev[3])
                if k not in evs or evs[k][1] < ev[1]:
                    evs[k] = ev
        return evs

    def barrier(self):
        evs = self._all_events()
        for e in self.engs:
            for ev in evs.values():
                self._wait(e, ev)
        for t in self.tiles:
            t.w = None
            t.r = []

    def finish(self, e="sp"):
        for ev in self._all_events().values():
            self._wait(e, ev)


def _bucket_np(rel):
    nb = 16
    max_exact = 8
    ret = np.where(rel > 0, nb, 0)
    n = np.abs(rel)
    nf = np.maximum(n, 1).astype(np.float32)
    large = max_exact + (np.log(nf / max_exact) / math.log(128 / max_exact) * (nb - max_exact)).astype(np.int32)
    large = np.minimum(large, nb - 1)
    return ret + np.where(n < max_exact, n, large)


def _bucket_tiles():
    k = np.arange(128)[:, None]
    q = np.arange(128)[None, :]
    b0 = _bucket_np(k - q).astype(np.float32)
    masked = (k // 64) > (q // 64)
    b0 = np.where(masked, 32.0, b0)
    b1 = _bucket_np(k - q - 128).astype(np.float32)
    return np.stack([b0, b1], axis=1).astype(np.float32)


def build_program(n_pseq=2, with_peer=True, dbg=False):
    nc = bass.Bass("TRN2", target_bir_lowering=False)
    NT = n_pseq * SEQ + 64

    def din(name, shape, dt=F32):
        return nc.dram_tensor(name, list(shape), dt, kind="ExternalInput").ap()

    def dout(name, shape, dt=F32):
        return nc.dram_tensor(name, list(shape), dt, kind="ExternalOutput").ap()

    xp = din("xp", [n_pseq, SEQ, DM])
    xs = din("xs", [64, DM])
    ckT = din("ckT", [4, 128, SEQ])
    cv = din("cv", [SEQ, 512])
    scT = din("scT", [128, 4, 30])
    w_in = din("w_in", [DM, 2560])
    gmix = din("gmix", [128, 8])
    convw = din("convw", [128, 4, 31])
    cvec = din("cvec", [128, 12])
    lam = din("lam", [1, 256])
    subg = din("subg", [1, 128])
    relb = din("relb", [1, 128])
    w_out = din("w_out", [DM, DM])
    gffn = din("gffn", [128, 8])
    wq = din("wq", [DM, 2048])
    keysT = din("keysT", [16, 128, 128])
    uT = din("uT", [DM, 16384])
    pv = din("pv", [16384, DM])
    gfin = din("gfin", [1, DM])
    bkc = din("bkc", [128, 2, 128])

    yp = dout("yp", [n_pseq, SEQ, DM])
    ys = dout("ys", [64, DM])
    kp = dout("kp", [n_pseq, SEQ, 512])
    vp = dout("vp", [n_pseq, SEQ, 512])
    cp = dout("cp", [n_pseq, 30, 512])
    ks = dout("ks", [64, 512])
    vs = dout("vs", [64, 512])
    cs = dout("cs", [30, 512])

    kind_scr = "ExternalOutput" if dbg else "Internal"
    x1d = nc.dram_tensor("x1d", [NT, DM], F32, kind=kind_scr).ap()
    h2Td = nc.dram_tensor("h2Td", [8, 128, NT], BF16, kind="Internal").ap()

    with ExitStack() as st:
        S = Sched(nc, st)

        cur = [st]

        def sb(name, shape, dt):
            return cur[0].enter_context(nc.sbuf_tensor(name, list(shape), dt)), S.tile(name)

        def ps(name, shape, dt):
            return cur[0].enter_context(nc.psum_tensor(name, list(shape), dt)), S.tile(name)

        ident_f, T_identf = sb("ident_f", [128, 128], F32)
        ident_b, T_identb = sb("ident_b", [128, 128], BF16)
        ones_b, T_ones = sb("ones_b", [128, 128], BF16)
        iota_t, T_iota = sb("iota_t", [128, 128], F32)
        T_const = S.tile("consts")

        S.op("pool", [], [T_iota], lambda e: e.iota(iota_t[:], pattern=[[1, 128]], base=0, channel_multiplier=-1,
                                                    allow_small_or_imprecise_dtypes=True))
        S.op("dve", [T_iota], [T_identf], lambda e: e.tensor_scalar(out=ident_f[:], in0=iota_t[:], scalar1=0.0,
                                                                     scalar2=None, op0=ALU.is_equal))
        S.op("dve", [T_identf], [T_identb], lambda e: e.tensor_copy(out=ident_b[:], in_=ident_f[:]))
        S.op("pool", [], [T_ones], lambda e: e.memset(ones_b[:], 1.0))

        eps_t, T_eps = sb("eps_t", [128, 1], F32)
        S.op("pool", [], [T_eps], lambda e: e.memset(eps_t[:], EPS))
        EPS_AP = eps_t
        iota_r, T_iotar = sb("iota_r", [128, 128], F32)
        S.op("pool", [], [T_iotar], lambda e: e.iota(iota_r[:], pattern=[[1, 128]], base=0, channel_multiplier=0,
                                                     allow_small_or_imprecise_dtypes=True))
        st1 = ExitStack()
        cur[0] = st1
        w_in_b, T_win = sb("w_in_b", [128, 8, 2560], BF16)
        w_out_b, T_wout = sb("w_out_b", [128, 8, DM], BF16)
        diag, T_diag = sb("diag", [128, 124, 128], BF16)
        gmix_s, _ = sb("gmix_s", [128, 8], F32)
        convw_s, _ = sb("convw_s", [128, 4, 31], F32)
        cvec_s, _ = sb("cvec_s", [128, 12], F32)
        lam_s, _ = sb("lam_s", [128, 256], F32)
        gsub_s, _ = sb("gsub_s", [128, 128], F32)
        relb_s, _ = sb("relb_s", [128, 128], F32)
        bk_s, _ = sb("bk_s", [128, 2, 128], F32)
        Tb, T_Tb = sb("Tb", [128, 4, 2, 128], F32)
        eqm, T_eqm = sb("eqm", [128, 2, 128], F32)
        small, T_small = sb("small", [128, 16], F32)
        stage0, T_st0 = sb("stage0", [128, 1024], F32)
        stage1, T_st1 = sb("stage1", [128, 1024], F32)

        for dst, src in ((gmix_s, gmix), (convw_s, convw), (cvec_s, cvec), (bk_s, bkc)):
            S.dma("sp", [], [T_const], lambda e, d=dst, s_=src: e.dma_start(out=d[:], in_=s_))
        for dst, src, n in ((lam_s, lam, 256), (gsub_s, subg, 128), (relb_s, relb, 128)):
            S.dma("sp", [], [T_const], lambda e, d=dst, s_=src, n=n: e.dma_start(out=d[:], in_=s_.to_broadcast([128, n])))

        stg = [(stage0, T_st0), (stage1, T_st1)]
        i = 0
        for kc in range(8):
            for (a0, a1) in ((0, 1024), (1024, 2048), (2048, 2560)):
                stt, T_s = stg[i % 2]
                i += 1
                S.dma("sp", [], [T_s], lambda e, stt=stt, kc=kc, a0=a0, a1=a1: e.dma_start(
                    out=stt[:, 0:a1 - a0], in_=w_in[kc * 128:(kc + 1) * 128, a0:a1]))
                S.op("dve", [T_s, T_const], [T_win], lambda e, stt=stt, kc=kc, a0=a0, a1=a1: e.tensor_scalar(
                    out=w_in_b[:, kc, a0:a1], in0=stt[:, 0:a1 - a0], scalar1=gmix_s[:, kc:kc + 1],
                    scalar2=None, op0=ALU.mult))
        for kc in range(8):
            stt, T_s = stg[i % 2]
            i += 1
            S.dma("sp", [], [T_s], lambda e, stt=stt, kc=kc: e.dma_start(
                out=stt[:, 0:DM], in_=w_out[kc * 128:(kc + 1) * 128, :]))
            S.op("act", [T_s], [T_wout], lambda e, stt=stt, kc=kc: e.copy(out=w_out_b[:, kc, :], in_=stt[:, 0:DM]))
        for w in range(31):
            for cb in range(4):
                S.op("dve", [T_const, T_identf], [T_diag], lambda e, w=w, cb=cb: e.tensor_scalar(
                    out=diag[:, w * 4 + cb, :], in0=ident_f[:], scalar1=convw_s[:, cb, w:w + 1], scalar2=None,
                    op0=ALU.mult))
        S.op("dve", [T_const], [T_eqm], lambda e: e.tensor_scalar(
            out=eqm[:], in0=bk_s[:], scalar1=32.0, scalar2=-30000.0, op0=ALU.is_equal, op1=ALU.mult))
        for h in range(4):
            S.op("dve", [T_eqm], [T_Tb], lambda e, h=h: e.tensor_copy(out=Tb[:, h, :, :], in_=eqm[:]))
        for b in range(32):
            S.op("dve", [T_const], [T_eqm], lambda e, b=b: e.tensor_scalar(
                out=eqm[:], in0=bk_s[:], scalar1=float(b), scalar2=None, op0=ALU.is_equal))
            for h in range(4):
                S.op("dve", [T_eqm, T_const, T_Tb], [T_Tb], lambda e, b=b, h=h: e.scalar_tensor_tensor(
                    out=Tb[:, h, :, :], in0=eqm[:], scalar=relb_s[:, b * 4 + h:b * 4 + h + 1], in1=Tb[:, h, :, :],
                    op0=ALU.mult, op1=ALU.add))
        S.op("dve", [T_const], [T_eqm], lambda e: e.tensor_tensor(
            out=eqm[:, 0, :].rearrange("p (a b) -> p a b", a=2), in0=lam_s[:].rearrange("p (a b c) -> p a b c", a=2, b=2)[:, :, 0, :],
            in1=lam_s[:].rearrange("p (a b c) -> p a b c", a=2, b=2)[:, :, 1, :], op=ALU.mult))
        S.op("dve", [T_eqm], [T_small], lambda e: e.reduce_sum(
            out=small[:, 0:2], in_=eqm[:, 0, :].rearrange("p (a b) -> p a b", a=2), axis=AX.X))
        S.op("act", [T_small], [T_small], lambda e: e.activation(out=small[:, 2:4], in_=small[:, 0:2], func=AF.Exp))
        S.op("dve", [T_small], [T_small], lambda e: e.tensor_tensor(
            out=small[:, 4:5], in0=small[:, 3:4], in1=small[:, 2:3], op=ALU.subtract))
        S.op("dve", [T_small], [T_small], lambda e: e.tensor_scalar(
            out=small[:, 4:5], in0=small[:, 4:5], scalar1=-LAM_INIT, scalar2=None, op0=ALU.add))
        S.op("dve", [T_const], [T_const], lambda e: e.tensor_scalar(
            out=gsub_s[:], in0=gsub_s[:], scalar1=1.0 - LAM_INIT, scalar2=None, op0=ALU.mult))
        neg_lam = small[:, 4:5]

        fT, T_fT = sb("fT", [128, 8, 512], BF16)
        xt, T_xt = sb("xt", [128, DM], F32)
        xr, T_xr = xt, T_xt
        junk, T_junk = sb("junk", [128, DM], BF16)
        hb, T_hb = sb("hb", [128, DM], BF16)
        stat, T_stat = sb("stat", [128, 8], F32)
        aT, T_aT = sb("aT", [128, 4, 30 + 512], BF16)
        sig, T_sig = sb("sig", [128, 512], F32)
        a32, T_a32 = sb("a32", [128, 512], F32)
        qT, T_qT = sb("qT", [128, 4, 512], BF16)
        kT, T_kT = sb("kT", [128, 4, SEQ + 64], BF16)
        vaug, T_v = sb("vaug", [128, 17, 4, 130], BF16)
        catT, T_cat = sb("catT", [128, 8, 512], BF16)
        zq, T_zq = sb("zq", [128, 512], BF16)
        zk32, T_zk32 = sb("zk32", [128, 512], F32)
        zkb, T_zkb = sb("zkb", [128, 512], BF16)
        zv32, T_zv32 = sb("zv32", [128, 512], F32)
        PTT = [sb(f"PT{i}", [128, 512], BF16) for i in range(3)]
        TMP = [sb(f"tmpb{i}", [128, 128], F32) for i in range(3)]
        att, T_att = sb("att", [128, 128], F32)
        sqj, T_sq = sb("sqj", [128, 128], F32)
        attb, T_attb = sb("attb", [128, 512], BF16)
        astat, T_astat = sb("astat", [128, 8], F32)
        y32, T_y32 = sb("y32", [128, 4, 512], F32)
        ybf, T_ybf = sb("ybf", [128, 4, 512], BF16)
        ysq, T_ysq = sb("ysq", [128, 4, 512], BF16)
        mu, T_mu = sig, T_sig
        rs, T_rs = a32, T_a32
        ctail, T_ctail = zk32, T_zk32
        cst32, T_cst32 = sb("cst32", [128, 4, 30], F32)

        pT, T_pT = ps("pT", [128, 1024], BF16)
        pM = [ps(f"pM{i}", [128, 512], F32) for i in range(2)]
        pSS = [ps(f"pS{i}", [128, 512], F32) for i in range(2)]
        pO, T_pO = ps("pO", [128, 2, 256], F32)
        pX = [ps(f"pX{i}", [128, 512], F32) for i in range(2)]
        pm_i = [0]
        pOO = [(pO, T_pO), (pM[0][0][:].rearrange("p (a b) -> p a b", a=2), pM[0][1])]
        pSS = pSS + [pX[1]]

        def next_pM():
            pm_i[0] += 1
            return pM[pm_i[0] % 2]

        S.op("pool", [], [T_v], lambda e: e.memset(vaug[:], 1.0))

        def rms_to_bf16(n, src, T_src, dst_b, T_dst, col):
            S.op("act", [T_src], [T_junk, T_stat], lambda e: e.activation(
                out=junk[0:n, :], in_=src[0:n, :], func=AF.Square, accum_out=stat[0:n, col:col + 1]))
            S.op("act", [T_stat], [T_stat], lambda e: e.activation(
                out=stat[0:n, col + 1:col + 2], in_=stat[0:n, col:col + 1], func=AF.Sqrt, scale=1.0 / DM, bias=EPS_AP[0:n, :]))
            S.op("dve", [T_stat], [T_stat], lambda e: e.reciprocal(
                out=stat[0:n, col + 1:col + 2], in_=stat[0:n, col + 1:col + 2]))
            S.op("dve", [T_src, T_stat], [T_dst], lambda e: e.tensor_scalar(
                out=dst_b[0:n, :], in0=src[0:n, :], scalar1=stat[0:n, col + 1:col + 2], scalar2=None, op0=ALU.mult))

        def transpose_to_fT(n, src_b, T_src, c0):
            for kc in range(8):
                S.op("pe", [T_src, T_identb], [T_pT], lambda e, kc=kc: e.transpose(
                    out=pT[:, kc * 128:kc * 128 + n], in_=src_b[0:n, kc * 128:(kc + 1) * 128], identity=ident_b[0:n, 0:n]))
            S.op("act", [T_pT], [T_fT], lambda e: e.copy(
                out=fT[:, :, c0:c0 + n], in_=pT[:].rearrange("p (k c) -> p k c", k=8)[:, :, 0:n]))

        seqs = []
        for s_ in range(n_pseq):
            seqs.append(("p", xp[s_], SEQ, kp[s_], vp[s_], cp[s_], s_ * SEQ))
        seqs.append(("s", xs, 64, ks, vs, cs, n_pseq * SEQ))

        S.barrier()
        import os
        STOP = int(os.environ.get("KSTOP", "99"))
        if STOP <= 0:
            seqs = []

        for (kind, xd, ntok, kd, vd, cd, tok0) in seqs:
            past = SEQ if kind == "s" else 0
            if kind == "p":
                S.op("pool", [], [T_aT], lambda e: e.memset(aT[:, :, 0:30], 0.0))
            else:
                S.dma("sp", [], [T_cst32], lambda e: e.dma_start(out=cst32[:], in_=scT))
                S.op("dve", [T_cst32], [T_aT], lambda e: e.tensor_copy(out=aT[:, :, 0:30], in_=cst32[:]))
                for h in range(4):
                    for hf in range(2):
                        stt, T_s = stg[(h * 2 + hf) % 2]
                        S.dma("sp", [], [T_s], lambda e, stt=stt, h=h, hf=hf: e.dma_start(
                            out=stt[:, 0:1024], in_=ckT[h, :, hf * 1024:(hf + 1) * 1024]))
                        S.op("act", [T_s], [T_kT], lambda e, stt=stt, h=h, hf=hf: e.copy(
                            out=kT[:, h, hf * 1024:(hf + 1) * 1024], in_=stt[:, 0:1024]))
                for blk in range(16):
                    stt, T_s = stg[blk % 2]
                    S.dma("sp", [], [T_s], lambda e, stt=stt, blk=blk: e.dma_start(
                        out=stt[:, 0:512], in_=cv[blk * 128:(blk + 1) * 128, :]))
                    S.op("dve", [T_s], [T_v], lambda e, stt=stt, blk=blk: e.tensor_copy(
                        out=vaug[:, blk, :, 0:128], in_=stt[:, 0:512].rearrange("p (h e) -> p h e", h=4)))

            ngroups = (ntok + 511) // 512
            for g in range(ngroups):
                g0 = g * 512
                N = min(512, ntok - g0)
                tiles = [(c0, min(128, N - c0)) for c0 in range(0, N, 128)]
                last_group = (g == ngroups - 1)

                for (c0, n) in tiles:
                    S.dma("sp", [], [T_xt], lambda e, c0=c0, n=n: e.dma_start(out=xt[0:n, :], in_=xd[g0 + c0:g0 + c0 + n, :]))
                    rms_to_bf16(n, xt, T_xt, hb, T_hb, 0)
                    transpose_to_fT(n, hb, T_hb, c0)

                if STOP <= 1:
                    continue
                for (c0, n) in tiles:
                    blk = (past + g0 + c0) // 128
                    kcol = past + g0 + c0
                    for j in range(int(os.environ.get('KJ', '3'))):
                        pm, T_pm = next_pM()
                        for kc in range(8):
                            S.op("pe", [T_fT, T_win], [T_pm], lambda e, pm=pm, kc=kc, j=j, c0=c0, n=n: e.matmul(
                                pm[0:n, :], lhsT=fT[:, kc, c0:c0 + n], rhs=w_in_b[:, kc, 1024 + j * 512:1024 + (j + 1) * 512],
                                start=(kc == 0), stop=(kc == 7)))
                        if j == 0:
                            S.op("act", [T_pm], [T_zq], lambda e, pm=pm, n=n: e.activation(
                                out=zq[0:n, :], in_=pm[0:n, :], func=AF.Copy, scale=0.125))
                            for h in range(4):
                                S.op("pe", [T_zq, T_identb], [T_pT], lambda e, h=h, n=n: e.transpose(
                                    out=pT[:, h * 128:h * 128 + n], in_=zq[0:n, h * 128:(h + 1) * 128], identity=ident_b[0:n, 0:n]))
                            S.op("dve", [T_pT], [T_qT], lambda e, c0=c0, n=n: e.tensor_copy(
                                out=qT[:, :, c0:c0 + n], in_=pT[:, 0:512].rearrange("p (k c) -> p k c", k=4)[:, :, 0:n]))
                        elif j == 1:
                            if not os.environ.get("K1A"):
                                S.op("dve", [T_pm], [T_zk32], lambda e, pm=pm, n=n: e.tensor_copy(out=zk32[0:n, :], in_=pm[0:n, :]))
                            S.op("act", [T_zk32], [T_zkb], lambda e, pm=pm, n=n: e.copy(out=zkb[0:n, :], in_=zk32[0:n, :]))
                            if not os.environ.get("NOKD"):
                                S.dma("sp", [T_zk32], [], lambda e, c0=c0, n=n: e.dma_start(
                                    out=kd[g0 + c0:g0 + c0 + n, :], in_=zk32[0:n, :]))
                            for h in range(0 if os.environ.get("K1B") else 4):
                                S.op("pe", [T_zkb, T_identb], [T_pT], lambda e, h=h, n=n: e.transpose(
                                    out=pT[:, 512 + h * 128:512 + h * 128 + n], in_=zkb[0:n, h * 128:(h + 1) * 128],
                                    identity=ident_b[0:n, 0:n]))
                            if not os.environ.get("K1C"):
                              S.op("dve", [T_pT], [T_kT], lambda e, kcol=kcol, n=n: e.tensor_copy(
                                out=kT[:, :, kcol:kcol + n], in_=pT[:, 512:1024].rearrange("p (k c) -> p k c", k=4)[:, :, 0:n]))
                        else:
                            S.op("dve", [T_pm], [T_zv32], lambda e, pm=pm, n=n: e.tensor_copy(out=zv32[0:n, :], in_=pm[0:n, :]))
                            S.op("act", [T_zv32], [T_v], lambda e, pm=pm, n=n, blk=blk: e.copy(
                                out=vaug[0:n, blk, :, 0:128], in_=zv32[0:n, :].rearrange("p (h e) -> p h e", h=4)))
                            S.dma("sp", [T_zv32], [], lambda e, c0=c0, n=n: e.dma_start(
                                out=vd[g0 + c0:g0 + c0 + n, :], in_=zv32[0:n, :]))

                if STOP <= 2:
                    continue
                for cb in range(4):
                    pa, T_pa = next_pM()
                    pg, T_pg = next_pM()
                    for kc in range(8):
                        S.op("pe", [T_fT, T_win], [T_pa], lambda e, pa=pa, kc=kc, cb=cb: e.matmul(
                            pa[:, 0:N], lhsT=w_in_b[:, kc, cb * 128:(cb + 1) * 128], rhs=fT[:, kc, 0:N],
                            start=(kc == 0), stop=(kc == 7)))
                    for kc in range(8):
                        S.op("pe", [T_fT, T_win], [T_pg], lambda e, pg=pg, kc=kc, cb=cb: e.matmul(
                            pg[:, 0:N], lhsT=w_in_b[:, kc, 512 + cb * 128:512 + (cb + 1) * 128], rhs=fT[:, kc, 0:N],
                            start=(kc == 0), stop=(kc == 7)))
                    S.op("act", [T_pg], [T_sig], lambda e, pg=pg: e.activation(out=sig[:, 0:N], in_=pg[:, 0:N], func=AF.Sigmoid))
                    S.op("dve", [T_pa, T_sig], [T_a32], lambda e, pa=pa: e.tensor_tensor(
                        out=a32[:, 0:N], in0=pa[:, 0:N], in1=sig[:, 0:N], op=ALU.mult))
                    S.op("pool", [T_a32], [T_aT], lambda e, cb=cb: e.tensor_copy(out=aT[:, cb, 30:30 + N], in_=a32[:, 0:N]))
                    if last_group:
                        pm, T_pm = pX[0]
                        S.op("pe", [T_a32, T_identf], [T_pm], lambda e, pm=pm, cb=cb: e.transpose(
                            out=pm[0:30, cb * 128:(cb + 1) * 128], in_=a32[:, N - 30:N], identity=ident_f[:]))
                if last_group:
                    pm, T_pm = pX[0]
                    S.op("act", [T_pm], [T_ctail], lambda e, pm=pm: e.copy(out=ctail[0:30, :], in_=pm[0:30, :]))
                    S.dma("sp", [T_ctail], [], lambda e: e.dma_start(out=cd, in_=ctail[0:30, :]))

                if STOP <= 3:
                    continue
                S.cap = []
                for cb in range(4):
                    pm, T_pm = pM[1]
                    for w in range(31):
                        S.op("pe", [T_aT, T_diag], [T_pm], lambda e, pm=pm, w=w, cb=cb: e.matmul(
                            pm[:, 0:N], lhsT=diag[:, w * 4 + cb, :], rhs=aT[:, cb, w:w + N], start=(w == 0), stop=(w == 30)))
                    S.op("act", [T_pm, T_const], [T_y32], lambda e, pm=pm, cb=cb: e.activation(
                        out=y32[:, cb, 0:N], in_=pm[:, 0:N], func=AF.Identity, bias=cvec_s[:, cb:cb + 1]))
                    S.op("act", [T_pm, T_const], [T_ysq], lambda e, pm=pm, cb=cb: e.activation(
                        out=ysq[:, cb, 0:N], in_=pm[:, 0:N], func=AF.Square, bias=cvec_s[:, cb:cb + 1]))
                    S.op("pool", [T_y32], [T_ybf], lambda e, cb=cb: e.tensor_copy(out=ybf[:, cb, 0:N], in_=y32[:, cb, 0:N]))
                p1, T_p1 = pX[0]
                p2, T_p2 = pX[0]
                for cb in range(4):
                    S.op("pe", [T_ybf, T_ones], [T_p1], lambda e, cb=cb: e.matmul(
                        p1[:, 0:N], lhsT=ones_b[:], rhs=ybf[:, cb, 0:N], start=(cb == 0), stop=(cb == 3)))
                S.op("dve", [T_p1], [T_mu], lambda e: e.tensor_scalar(
                    out=mu[:, 0:N], in0=p1[:, 0:N], scalar1=1.0 / 512, scalar2=None, op0=ALU.mult))
                for cb in range(4):
                    S.op("pe", [T_ysq, T_ones], [T_p2], lambda e, cb=cb: e.matmul(
                        p2[:, 0:N], lhsT=ones_b[:], rhs=ysq[:, cb, 0:N], start=(cb == 0), stop=(cb == 3)))
                S.op("dve", [T_mu], [T_rs], lambda e: e.tensor_tensor(out=rs[:, 0:N], in0=mu[:, 0:N], in1=mu[:, 0:N], op=ALU.mult))
                S.op("dve", [T_p2, T_rs], [T_rs], lambda e: e.scalar_tensor_tensor(
                    out=rs[:, 0:N], in0=p2[:, 0:N], scalar=1.0 / 512, in1=rs[:, 0:N], op0=ALU.mult, op1=ALU.subtract))
                S.op("act", [T_rs, T_eps], [T_rs], lambda e: e.activation(
                    out=rs[:, 0:N], in_=rs[:, 0:N], func=AF.Sqrt, bias=eps_t[:, :]))
                S.op("dve", [T_rs], [T_rs], lambda e: e.reciprocal(out=rs[:, 0:N], in_=rs[:, 0:N]))
                for cb in range(4):
                    S.op("dve", [T_y32, T_mu], [T_y32], lambda e, cb=cb: e.tensor_tensor(
                        out=y32[:, cb, 0:N], in0=y32[:, cb, 0:N], in1=mu[:, 0:N], op=ALU.subtract))
                    S.op("pool", [T_y32, T_rs], [T_y32], lambda e, cb=cb: e.tensor_tensor(
                        out=y32[:, cb, 0:N], in0=y32[:, cb, 0:N], in1=rs[:, 0:N], op=ALU.mult))
                    S.op("act", [T_y32, T_const], [T_cat], lambda e, cb=cb: e.activation(
                        out=catT[:, cb, 0:N], in_=y32[:, cb, 0:N], func=AF.Silu,
                        scale=cvec_s[:, 4 + cb:5 + cb], bias=cvec_s[:, 8 + cb:9 + cb]))
                if not last_group:
                    S.op("pool", [T_aT], [T_aT], lambda e: e.tensor_copy(out=aT[:, :, 0:30], in_=aT[:, :, N:N + 30]))

                capD = S.cap
                S.cap = None
                items = []
                for (c0, n) in tiles:
                    qi = (g0 + c0) // 128
                    if kind == "p":
                        far = list(range(0, max(qi - 1, 0)))
                        near = ([(qi - 1, 128, 1)] if qi >= 1 else []) + [(qi, 128, 0)]
                    else:
                        far = list(range(0, 15))
                        near = [(15, 128, 1), (16, 64, 0)]
                    nblk = len(far) + len(near)
                    for h in range(4):
                        for m in range(2):
                            done = 0
                            for f0 in range(0, len(far), 4):
                                chunk = far[f0:f0 + 4]
                                items.append(dict(kind="far", c0=c0, n=n, h=h, m=m, blks=chunk, done=done, nblk=nblk))
                                done += len(chunk)
                            for (blk, nk, bkind) in near:
                                items.append(dict(kind="near", c0=c0, n=n, h=h, m=m, blk=blk, nk=nk, bkind=bkind,
                                                  done=done, nblk=nblk))
                                done += 1
                        items[-1]["head_end"] = True
                    items[-1]["tile_end"] = True

                def emit_qk(k, it):
                    pS_, T_pS_ = pSS[k % 3]
                    PT_, T_PT_ = PTT[k % 3]
                    c0, n, h, m = it["c0"], it["n"], it["h"], it["m"]
                    mrow = slice(m * 64, (m + 1) * 64)
                    if it["kind"] == "far":
                        for j, blk in enumerate(it["blks"]):
                            S.op("pe", [T_kT, T_qT], [T_pS_], lambda e, j=j, blk=blk: e.matmul(
                                pS_[:, j * n:(j + 1) * n], lhsT=kT[mrow, h, blk * 128:(blk + 1) * 128],
                                rhs=qT[mrow, h, c0:c0 + n], start=True, stop=True))
                        cn = len(it["blks"]) * n
                        S.op("act", [T_pS_, T_const], [T_PT_], lambda e: e.activation(
                            out=PT_[:, 0:cn], in_=pS_[:, 0:cn], func=AF.Exp, bias=relb_s[:, 60 + h:61 + h]))
                    else:
                        blk, nk, bkind = it["blk"], it["nk"], it["bkind"]
                        tb_, T_tb_ = TMP[k % 3]
                        S.op("pe", [T_kT, T_qT], [T_pS_], lambda e: e.matmul(
                            pS_[0:nk, 0:n], lhsT=kT[mrow, h, blk * 128:blk * 128 + nk],
                            rhs=qT[mrow, h, c0:c0 + n], start=True, stop=True))
                        S.op("dve", [T_pS_, T_Tb], [T_tb_], lambda e: e.tensor_tensor(
                            out=tb_[0:nk, 0:n], in0=pS_[0:nk, 0:n], in1=Tb[0:nk, h, bkind, 0:n], op=ALU.add))
                        S.op("act", [T_tb_], [T_PT_], lambda e: e.activation(
                            out=PT_[0:nk, 0:n], in_=tb_[0:nk, 0:n], func=AF.Exp))

                def emit_pv(k, it):
                    PT_, T_PT_ = PTT[k % 3]
                    c0, n, h, m = it["c0"], it["n"], it["h"], it["m"]
                    qi_ = (g0 + c0) // 128
                    pO_, T_pO_ = pOO[(qi_ * 4 + h) % 2]
                    nblk = it["nblk"]
                    if it["kind"] == "far":
                        for j, blk in enumerate(it["blks"]):
                            dn = it["done"] + j
                            S.op("pe", [T_PT_, T_v], [T_pO_], lambda e, j=j, blk=blk, dn=dn: e.matmul(
                                pO_[0:n, m, 0:129], lhsT=PT_[:, j * n:(j + 1) * n], rhs=vaug[:, blk, h, 0:129],
                                start=(dn == 0), stop=(dn == nblk - 1)))
                    else:
                        blk, nk = it["blk"], it["nk"]
                        dn = it["done"]
                        S.op("pe", [T_PT_, T_v], [T_pO_], lambda e: e.matmul(
                            pO_[0:n, m, 0:129], lhsT=PT_[0:nk, 0:n], rhs=vaug[0:nk, blk, h, 0:129],
                            start=(dn == 0), stop=(dn == nblk - 1)))
                    if it.get("head_end"):
                        S.op("dve", [T_pO_], [T_astat], lambda e: e.reciprocal(
                            out=astat[0:n, 0:2], in_=pO_[0:n, :, 128:129].rearrange("p a b -> p (a b)")))
                        S.op("dve", [T_astat, T_small], [T_astat], lambda e: e.tensor_tensor(
                            out=astat[0:n, 2:3], in0=astat[0:n, 1:2], in1=neg_lam[0:n, :], op=ALU.mult))
                        S.op("dve", [T_pO_, T_astat], [T_att], lambda e: e.tensor_scalar(
                            out=att[0:n, :], in0=pO_[0:n, 0, 0:128], scalar1=astat[0:n, 0:1], scalar2=None, op0=ALU.mult))
                        S.op("dve", [T_pO_, T_astat, T_att], [T_att], lambda e: e.scalar_tensor_tensor(
                            out=att[0:n, :], in0=pO_[0:n, 1, 0:128], scalar=astat[0:n, 2:3], in1=att[0:n, :],
                            op0=ALU.mult, op1=ALU.add))
                        S.op("dve", [T_att], [T_sq, T_astat], lambda e: e.scalar_tensor_tensor(
                            out=sqj[0:n, :], in0=att[0:n, :], scalar=1.0, in1=att[0:n, :], op0=ALU.mult, op1=ALU.mult,
                            accum_out=astat[0:n, 3:4]))
                        S.op("act", [T_astat, T_eps], [T_astat], lambda e: e.activation(
                            out=astat[0:n, 5:6], in_=astat[0:n, 3:4], func=AF.Ln, scale=1.0 / 128, bias=eps_t[0:n, :]))
                        S.op("act", [T_astat], [T_astat], lambda e: e.activation(
                            out=astat[0:n, 4:5], in_=astat[0:n, 5:6], func=AF.Exp, scale=-0.5))
                        S.op("dve", [T_att, T_astat, T_const], [T_attb], lambda e: e.scalar_tensor_tensor(
                            out=attb[0:n, h * 128:(h + 1) * 128], in0=att[0:n, :], scalar=astat[0:n, 4:5], in1=gsub_s[0:n, :],
                            op0=ALU.mult, op1=ALU.mult))
                    if it.get("tile_end"):
                        for hh in range(4):
                            S.op("pe", [T_attb, T_identb], [T_pT], lambda e, hh=hh: e.transpose(
                                out=pT[:, hh * 128:hh * 128 + n], in_=attb[0:n, hh * 128:(hh + 1) * 128], identity=ident_b[0:n, 0:n]))
                        S.op("act", [T_pT], [T_cat], lambda e: e.copy(
                            out=catT[:, 4:8, c0:c0 + n], in_=pT[:, 0:512].rearrange("p (k c) -> p k c", k=4)[:, :, 0:n]))

                per_item = -(-len(capD) // max(len(items) - 4, 1))
                cpos = 0
                for k, it in enumerate(items):
                    emit_qk(k, it)
                    if k >= 2:
                        emit_pv(k - 2, items[k - 2])
                    for _ in range(per_item):
                        if cpos < len(capD):
                            S.replay(capD[cpos])
                            cpos += 1
                for k in range(max(len(items) - 2, 0), len(items)):
                    emit_pv(k, items[k])
                while cpos < len(capD):
                    S.replay(capD[cpos])
                    cpos += 1

                if STOP <= 5:
                    continue
                for (c0, n) in tiles:
                    S.dma("sp", [], [T_xr], lambda e, c0=c0, n=n: e.dma_start(out=xr[0:n, :], in_=xd[g0 + c0:g0 + c0 + n, :]))
                    for hf in range(2):
                        po, T_po = pX[hf]
                        for kc in range(8):
                            S.op("pe", [T_cat, T_wout], [T_po], lambda e, po=po, kc=kc, hf=hf, c0=c0, n=n: e.matmul(
                                po[0:n, :], lhsT=catT[:, kc, c0:c0 + n], rhs=w_out_b[:, kc, hf * 512:(hf + 1) * 512],
                                start=(kc == 0), stop=(kc == 7)))
                        S.op("dve", [T_po, T_xr], [T_xr], lambda e, po=po, hf=hf, n=n: e.tensor_tensor(
                            out=xr[0:n, hf * 512:(hf + 1) * 512], in0=po[0:n, :], in1=xr[0:n, hf * 512:(hf + 1) * 512], op=ALU.add))
                    S.dma("sp", [T_xr], [], lambda e, c0=c0, n=n: e.dma_start(
                        out=x1d[tok0 + g0 + c0:tok0 + g0 + c0 + n, :], in_=xr[0:n, :]))
                    rms_to_bf16(n, xr, T_xr, hb, T_hb, 2)
                    transpose_to_fT(n, hb, T_hb, c0)
                for kc in range(8):
                    S.dma("sp", [T_fT], [], lambda e, kc=kc: e.dma_start(
                        out=h2Td[kc, :, tok0 + g0:tok0 + g0 + N], in_=fT[:, kc, 0:N]))

        S.barrier()
        st1.close()
        st2 = ExitStack()
        st.enter_context(st2)
        cur[0] = st2
        TG = 256
        NCH = 64
        if with_peer:
            ijwd = nc.dram_tensor("ijwd", [128, 3, NT], F32, kind="Internal").ap()
            uscr_t = nc.dram_tensor("uscr", [NCH, 128, 8 * 256], BF16, kind="Internal").ap()
            vscr_t = nc.dram_tensor("vscr", [NCH, 128, 2 * DM], BF16, kind="Internal").ap()
            uscr = [uscr_t[ic].rearrange("p (k e) -> p k e", k=8) for ic in range(NCH)]
            vscr = [vscr_t[ic].rearrange("p (b d) -> p b d", b=2) for ic in range(NCH)]
            T_uscr = [S.tile(f"uscr{ic}") for ic in range(NCH)]
            T_vscr = [S.tile(f"vscr{ic}") for ic in range(NCH)]
            T_ijwd = S.tile("ijwd")

            gfin_s, T_gfin = sb("gfin_s", [128, DM], F32)
            iota_b, T_iotab = sb("iota_b", [128, 128], BF16)
            T_c2 = S.tile("consts2")
            S.dma("sp", [], [T_c2], lambda e: e.dma_start(out=gfin_s[:], in_=gfin.to_broadcast([128, DM])))
            S.op("dve", [T_iotar], [T_iotab], lambda e: e.tensor_copy(out=iota_b[:], in_=iota_r[:]))

            st2a = ExitStack()
            cur[0] = st2a
            TA = 512
            wq_b, T_wq = sb("wq_b", [128, 8, 2048], BF16)
            keys_b, T_keys = sb("keys_b", [128, 16, 128], BF16)
            gffn_s, T_gffn = sb("gffn_s", [128, 8], F32)
            sg0, T_sg0 = sb("sg0", [128, 1024], F32)
            sg1, T_sg1 = sb("sg1", [128, 1024], F32)
            h2g, T_h2g = sb("h2g", [128, 8, TA], BF16)
            qryT, T_qry = sb("qryT", [128, 16, TA], BF16)
            s_sb, T_ssb = sb("s_sb", [128, 16, 128], F32)
            wk4l = [sb(f"wk4_{i}", [128, 256], F32) for i in range(4)]
            wk4 = [x[0] for x in wk4l]
            T_wk4 = [x[1] for x in wk4l]
            T_A4 = [S.tile(f"A4_{i}") for i in range(4)]
            T_I4 = [S.tile(f"I4_{i}") for i in range(4)]
            T_C4 = [S.tile(f"C4_{i}") for i in range(4)]
            T_P4 = [S.tile(f"P4t_{i}") for i in range(4)]
            A_, T_A = sb("A_", [128, 16, 16], F32)
            Iu, T_Iu = sb("Iu", [128, 16, 16], U32)
            If, T_If = sb("If", [128, 16, 16], F32)
            cand, T_cand = sb("cand", [128, 8, 256], F32)
            C_, T_C = sb("C_", [128, 8, 16], F32)
            pos, T_pos = sb("pos", [128, 8, 16], U32)
            ku, T_ku = sb("ku", [128, 2, 128], U32)
            kf, T_kf = sb("kf", [128, 2, 128], F32)
            E_, T_E = sb("E_", [128, 8, 16], F32)
            gst, T_gst = sb("gst", [128, 32], F32)
            oh, T_oh = sb("oh", [128, 8, 16, 16], F32)
            ijw, T_ijw = sb("ijw", [128, 3, 128], F32)
            ijT = [sb(f"ijT{i}", [128, 3, 128], F32) for i in range(2)]
            stu = [sb(f"stu{i}", [128, 8, 256], BF16) for i in range(2)]
            stv = [sb(f"stv{i}", [128, 2, DM], BF16) for i in range(2)]
            pGa = [ps(f"pGa{i}", [128, 512], F32) for i in range(4)]
            pga_i = [0]

            def next_pGa():
                pga_i[0] += 1
                return pGa[pga_i[0] % 4]

            S.dma("sp", [], [T_c2], lambda e: e.dma_start(out=gffn_s[:], in_=gffn))
            S.dma("pool", [], [T_keys], lambda e: e.dma_start(out=keys_b[:], in_=keysT.rearrange("r d n -> d r n")))
            sgs = [(sg0, T_sg0), (sg1, T_sg1)]
            ii = 0
            for kc in range(8):
                for hf in range(2):
                    stt, T_s = sgs[ii % 2]
                    ii += 1
                    S.dma("sp", [], [T_s], lambda e, stt=stt, kc=kc, hf=hf: e.dma_start(
                        out=stt[:], in_=wq[kc * 128:(kc + 1) * 128, hf * 1024:(hf + 1) * 1024]))
                    S.op("dve", [T_s, T_c2], [T_wq], lambda e, stt=stt, kc=kc, hf=hf: e.tensor_scalar(
                        out=wq_b[:, kc, hf * 1024:(hf + 1) * 1024], in0=stt[:], scalar1=gffn_s[:, kc:kc + 1],
                        scalar2=None, op0=ALU.mult))

            conv_i = [0]

            def convert_chunk():
                ic = conv_i[0]
                if ic >= NCH:
                    return
                conv_i[0] += 1
                (su, T_su), (sv, T_sv) = stu[ic % 2], stv[ic % 2]
                for k4 in range(2):
                    S.dma("pool", [], [T_su], lambda e, k4=k4: e.dma_start(
                        out=su[:, k4 * 4:(k4 + 1) * 4, :],
                        in_=uT[k4 * 512:(k4 + 1) * 512, ic * 256:(ic + 1) * 256].rearrange("(k p) e -> p k e", p=128)))
                S.dma("pool", [], [T_sv], lambda e: e.dma_start(
                    out=sv[:], in_=pv[ic * 256:(ic + 1) * 256, :].rearrange("(b p) d -> p b d", p=128)))
                S.dma("sp", [T_su], [T_uscr[ic]], lambda e: e.dma_start(out=uscr[ic], in_=su[:]), key=T_su)
                S.dma("sp", [T_sv], [T_vscr[ic]], lambda e: e.dma_start(out=vscr[ic], in_=sv[:]), key=T_sv)

            tile_ctr = 0
            for t0 in range(0, NT, TA):
                N = min(TA, NT - t0)
                tiles = [(c0, min(128, N - c0)) for c0 in range(0, N, 128)]
                S.dma("sp", [], [T_h2g], lambda e: e.dma_start(
                    out=h2g[:, :, 0:N], in_=h2Td[:, :, t0:t0 + N].rearrange("k p t -> p k t")))
                for blk in range(16):
                    pg, T_pg = next_pGa()
                    for kc in range(8):
                        S.op("pe", [T_wq, T_h2g], [T_pg], lambda e, pg=pg, kc=kc, blk=blk: e.matmul(
                            pg[:, 0:N], lhsT=wq_b[:, kc, blk * 128:(blk + 1) * 128], rhs=h2g[:, kc, 0:N],
                            start=(kc == 0), stop=(kc == 7)))
                    S.op("act", [T_pg], [T_qry], lambda e, pg=pg, blk=blk: e.copy(out=qryT[:, blk, 0:N], in_=pg[:, 0:N]))
                for (c0, n) in tiles:
                    convert_chunk()
                    convert_chunk()
                    for q4 in range(4):
                        pg, T_pg = next_pGa()
                        for j in range(4):
                            rp = q4 * 4 + j
                            S.op("pe", [T_qry, T_keys], [T_pg], lambda e, pg=pg, j=j, rp=rp: e.matmul(
                                pg[0:n, j * 128:(j + 1) * 128], lhsT=qryT[:, rp, c0:c0 + n], rhs=keys_b[:, rp, :],
                                start=True, stop=True))
                        S.op("act", [T_pg], [T_ssb], lambda e, pg=pg, q4=q4: e.copy(
                            out=s_sb[0:n, q4 * 4:(q4 + 1) * 4, :], in_=pg[0:n, :].rearrange("p (a b) -> p a b", a=4)))
                    for rp0 in range(0, 16, 4):
                        rps = list(range(rp0, rp0 + 4))
                        for rp in rps:
                            S.op("dve", [T_ssb], [T_A4[rp % 4]], lambda e, rp=rp: e.max(out=A_[0:n, rp, 0:8], in_=s_sb[0:n, rp, :]))
                        for rp in rps:
                            S.op("dve", [T_ssb, T_A4[rp % 4]], [T_I4[rp % 4]], lambda e, rp=rp: e.max_index(
                                out=Iu[0:n, rp, 0:8], in_max=A_[0:n, rp, 0:8], in_values=s_sb[0:n, rp, :]))
                        for rp in rps:
                            S.op("dve", [T_ssb, T_A4[rp % 4]], [T_wk4[rp % 4]], lambda e, rp=rp: e.match_replace(
                                out=wk4[rp % 4][0:n, 0:128], in_to_replace=A_[0:n, rp, 0:8], in_values=s_sb[0:n, rp, :], imm_value=-1e30))
                        for rp in rps:
                            S.op("dve", [T_wk4[rp % 4]], [T_A4[rp % 4]], lambda e, rp=rp: e.max(out=A_[0:n, rp, 8:16], in_=wk4[rp % 4][0:n, 0:128]))
                        for rp in rps:
                            S.op("dve", [T_wk4[rp % 4], T_A4[rp % 4]], [T_I4[rp % 4]], lambda e, rp=rp: e.max_index(
                                out=Iu[0:n, rp, 8:16], in_max=A_[0:n, rp, 8:16], in_values=wk4[rp % 4][0:n, 0:128]))
                    S.op("dve", T_I4, [T_If], lambda e: e.tensor_copy(out=If[0:n], in_=Iu[0:n]))
                    A4 = A_[0:n].rearrange("p (r a) k -> p r a k", a=2)
                    I4 = If[0:n].rearrange("p (r a) k -> p r a k", a=2)
                    S.op("dve", T_A4, [T_cand], lambda e: e.tensor_tensor(
                        out=cand[0:n].rearrange("p r (a b) -> p r a b", a=16),
                        in0=A4[:, :, 0, :].unsqueeze(3).to_broadcast([n, 8, 16, 16]),
                        in1=A4[:, :, 1, :].unsqueeze(2).to_broadcast([n, 8, 16, 16]), op=ALU.add))
                    for r0 in range(0, 8, 4):
                        rs_ = list(range(r0, r0 + 4))
                        for r in rs_:
                            S.op("dve", [T_cand], [T_C4[r % 4]], lambda e, r=r: e.max(out=C_[0:n, r, 0:8], in_=cand[0:n, r, :]))
                        for r in rs_:
                            S.op("dve", [T_cand, T_C4[r % 4]], [T_P4[r % 4]], lambda e, r=r: e.max_index(
                                out=pos[0:n, r, 0:8], in_max=C_[0:n, r, 0:8], in_values=cand[0:n, r, :]))
                        for r in rs_:
                            S.op("dve", [T_cand, T_C4[r % 4]], [T_wk4[r % 4]], lambda e, r=r: e.match_replace(
                                out=wk4[r % 4][0:n, :], in_to_replace=C_[0:n, r, 0:8], in_values=cand[0:n, r, :], imm_value=-1e30))
                        for r in rs_:
                            S.op("dve", [T_wk4[r % 4]], [T_C4[r % 4]], lambda e, r=r: e.max(out=C_[0:n, r, 8:16], in_=wk4[r % 4][0:n, :]))
                        for r in rs_:
                            S.op("dve", [T_wk4[r % 4], T_C4[r % 4]], [T_P4[r % 4]], lambda e, r=r: e.max_index(
                                out=pos[0:n, r, 8:16], in_max=C_[0:n, r, 8:16], in_values=wk4[r % 4][0:n, :]))
                    S.op("dve", T_C4, [T_gst], lambda e: e.tensor_scalar(
                        out=gst[0:n, 0:8], in0=C_[0:n, :, 0], scalar1=-1.0, scalar2=None, op0=ALU.mult))
                    for r in range(8):
                        S.op("act", T_C4 + [T_gst], [T_E, T_gst], lambda e, r=r: e.activation(
                            out=E_[0:n, r, :], in_=C_[0:n, r, :], func=AF.Exp, bias=gst[0:n, r:r + 1],
                            accum_out=gst[0:n, 8 + r:9 + r]))
                    S.op("dve", [T_gst], [T_gst], lambda e: e.reciprocal(out=gst[0:n, 16:24], in_=gst[0:n, 8:16]))
                    S.op("dve", [T_E, T_gst], [T_ijw], lambda e: e.tensor_tensor(
                        out=ijw[0:n, 2, :].rearrange("p (r k) -> p r k", r=8), in0=E_[0:n],
                        in1=gst[0:n, 16:24].unsqueeze(2).to_broadcast([n, 8, 16]), op=ALU.mult))
                    S.op("dve", T_P4, [T_ku], lambda e: e.tensor_single_scalar(
                        out=ku[0:n, 0, :], in_=pos[0:n].rearrange("p r k -> p (r k)"), scalar=4, op=ALU.logical_shift_right))
                    S.op("dve", T_P4, [T_ku], lambda e: e.tensor_single_scalar(
                        out=ku[0:n, 1, :], in_=pos[0:n].rearrange("p r k -> p (r k)"), scalar=15, op=ALU.bitwise_and))
                    S.op("dve", [T_ku], [T_kf], lambda e: e.tensor_copy(out=kf[0:n], in_=ku[0:n]))
                    for a in range(2):
                        S.op("dve", [T_kf, T_iotar], [T_oh], lambda e, a=a: e.tensor_tensor(
                            out=oh[0:n],
                            in0=kf[0:n, a, :].rearrange("p (r k) -> p r k", r=8).unsqueeze(3).to_broadcast([n, 8, 16, 16]),
                            in1=iota_r[0:n, 0:16].unsqueeze(1).unsqueeze(1).to_broadcast([n, 8, 16, 16]), op=ALU.is_equal))
                        S.op("dve", [T_oh, T_If], [T_oh], lambda e, a=a: e.tensor_tensor(
                            out=oh[0:n], in0=oh[0:n],
                            in1=I4[:, :, a, :].unsqueeze(2).to_broadcast([n, 8, 16, 16]), op=ALU.mult))
                        S.op("dve", [T_oh], [T_ijw], lambda e, a=a: e.reduce_sum(
                            out=ijw[0:n, a, :].rearrange("p (r k) -> p r k", r=8), in_=oh[0:n], axis=AX.X))
                    pg, T_pg = next_pGa()
                    for a in range(3):
                        S.op("pe", [T_ijw, T_identf], [T_pg], lambda e, pg=pg, a=a: e.transpose(
                            out=pg[:, a * 128:a * 128 + n], in_=ijw[0:n, a, :], identity=ident_f[0:n, 0:n]))
                    (it_, T_it) = ijT[tile_ctr % 2]
                    tile_ctr += 1
                    S.op("act", [T_pg], [T_it], lambda e, pg=pg, it_=it_: e.copy(
                        out=it_[:, :, 0:n], in_=pg[:, 0:384].rearrange("p (a t) -> p a t", a=3)[:, :, 0:n]))
                    S.dma("sp", [T_it], [T_ijwd], lambda e, it_=it_: e.dma_start(
                        out=ijwd[:, :, t0 + c0:t0 + c0 + n], in_=it_[:, :, 0:n]), key=T_it)
            while conv_i[0] < NCH:
                convert_chunk()
            S.barrier()
            st2a.close()

            st2b = ExitStack()
            st.enter_context(st2b)
            cur[0] = st2b
            Gall = [sb(f"Gall{i}", [128, 128, TG], BF16) for i in range(2)]
            ubuf = [sb(f"ubuf{i}", [128, 8, 256], BF16) for i in range(2)]
            vbuf = [sb(f"vbuf{i}", [128, 2, DM], BF16) for i in range(3)]
            h2g2 = [sb(f"h2g2_{i}", [128, 8, TG], BF16) for i in range(2)]
            ijg = [sb(f"ijg{i}", [128, 3, TG], F32) for i in range(2)]
            P4 = [sb(f"P4_{i}", [128, 4, 128], BF16) for i in range(3)]
            Q4 = [sb(f"Q4_{i}", [128, 4, 128], BF16) for i in range(3)]
            gbuf = [sb(f"gbuf{i}", [128, TG], F32) for i in range(3)]
            cbuf = [sb(f"cbuf{i}", [128, TG], BF16) for i in range(3)]
            x2l = [sb(f"x2_{i}", [128, DM], F32) for i in range(2)]
            junk2, T_junk2 = sb("junk2", [128, DM], BF16)
            st2s, T_st2s = sb("st2s", [128, 8], F32)
            pY = [[ps(f"pY{t}{h}", [128, 512], F32) for h in range(2)] for t in range(2)]
            pA = [ps(f"pA{i}", [128, 512], F32) for i in range(3)]
            pG = [ps(f"pG{i}", [128, 512], F32) for i in range(1)]

            groups = [(t0, min(TG, NT - t0)) for t0 in range(0, NT, TG)]
            MAXG = int(os.environ.get("KGROUPS", "999"))
            groups = groups[:MAXG]
            NG = len(groups)
            MULT_ENG = os.environ.get("KMULT", "pool")

            def gtiles(g):
                N = groups[g][1]
                return [(c0, min(128, N - c0)) for c0 in range(0, N, 128)]

            def load_group(g):
                t0, N = groups[g]
                (hg, T_hg), (ij, T_ij) = h2g2[g % 2], ijg[g % 2]
                S.dma("sp", [], [T_hg], lambda e: e.dma_start(
                    out=hg[:, :, 0:N], in_=h2Td[:, :, t0:t0 + N].rearrange("k p t -> p k t")))
                S.dma("sp", [T_ijwd], [T_ij], lambda e: e.dma_start(out=ij[:, :, 0:N], in_=ijwd[:, :, t0:t0 + N]))

            def p10_dve(g, b):
                (ij, T_ij) = ijg[g % 2]
                (p4, T_p4), (q4_, T_q4) = P4[b % 3], Q4[b % 3]
                for u in range(4):
                    t = b * 4 + u
                    S.op("dve", [T_ij, T_iotab], [T_p4], lambda e, u=u, t=t: e.tensor_scalar(
                        out=p4[:, u, :], in0=iota_b[:], scalar1=ij[:, 0, t:t + 1], scalar2=None, op0=ALU.is_equal))
                    S.op("dve", [T_ij, T_iotab], [T_q4], lambda e, u=u, t=t: e.tensor_scalar(
                        out=q4_[:, u, :], in0=iota_b[:], scalar1=ij[:, 1, t:t + 1], scalar2=ij[:, 2, t:t + 1],
                        op0=ALU.is_equal, op1=ALU.mult))

            def p10_pe(g, b):
                (p4, T_p4), (q4_, T_q4) = P4[b % 3], Q4[b % 3]
                (ga, T_ga) = Gall[g % 2]
                pg, T_pg = pG[0]
                for u in range(4):
                    S.op("pe", [T_p4, T_q4], [T_pg], lambda e, u=u: e.matmul(
                        pg[:, u * 128:(u + 1) * 128], lhsT=q4_[:, u, :], rhs=p4[:, u, :], start=True, stop=True))
                S.op("act", [T_pg], [T_ga], lambda e: e.copy(
                    out=ga[:, :, b * 4:b * 4 + 4], in_=pg[:, :].rearrange("p (t i) -> p i t", t=4)))

            class P10:
                def __init__(self, g):
                    self.g = g
                    self.nb = groups[g][1] // 4
                    self.d = 0
                    self.p = 0

                def step(self):
                    if self.d < self.nb:
                        p10_dve(self.g, self.d)
                        self.d += 1
                        if self.d - self.p >= 3:
                            p10_pe(self.g, self.p)
                            self.p += 1
                    elif self.p < self.nb:
                        p10_pe(self.g, self.p)
                        self.p += 1

                def flush(self):
                    while self.p < self.nb:
                        if self.d < self.nb and self.d - self.p < 3:
                            p10_dve(self.g, self.d)
                            self.d += 1
                        else:
                            p10_pe(self.g, self.p)
                            self.p += 1

            chunk_ctr = [0]

            def load_chunk(ic):
                c = chunk_ctr[0]
                chunk_ctr[0] += 1
                (ub, T_ub), (vb, T_vb) = ubuf[c % 2], vbuf[c % 3]
                S.dma("sp", [T_uscr[ic]], [T_ub], lambda e: e.dma_start(out=ub[:], in_=uscr[ic]))
                S.dma("sp", [T_vscr[ic]], [T_vb], lambda e: e.dma_start(out=vb[:], in_=vscr[ic]))

            def stage_u(g, i):
                N = groups[g][1]
                c = g * NCH + i // 2
                ib = i % 2
                (ub, T_ub) = ubuf[c % 2]
                (hg, T_hg) = h2g2[g % 2]
                (ga, T_ga) = Gall[g % 2]
                gidx = g * 128 + i
                pa, T_pa = pA[gidx % 3]
                gb, T_gb = gbuf[gidx % 3]
                cb_, T_cb = cbuf[gidx % 3]
                for kc in range(8):
                    S.op("pe", [T_ub, T_hg], [T_pa], lambda e, kc=kc: e.matmul(
                        pa[:, 0:N], lhsT=ub[:, kc, ib * 128:(ib + 1) * 128], rhs=hg[:, kc, 0:N],
                        start=(kc == 0), stop=(kc == 7)))
                S.op("act", [T_pa], [T_gb], lambda e: e.activation(out=gb[:, 0:N], in_=pa[:, 0:N], func=AF.Gelu))
                S.op(MULT_ENG, [T_gb, T_ga], [T_cb], lambda e: e.tensor_tensor(
                    out=cb_[:, 0:N], in0=gb[:, 0:N], in1=ga[:, i, 0:N], op=ALU.mult))

            def stage_v(g, i):
                c = g * NCH + i // 2
                ib = i % 2
                (vb, T_vb) = vbuf[c % 3]
                cb_, T_cb = cbuf[(g * 128 + i) % 3]
                for ti, (c0, n) in enumerate(gtiles(g)):
                    for hf in range(2):
                        py, T_py = pY[ti][hf]
                        S.op("pe", [T_cb, T_vb], [T_py], lambda e, py=py, c0=c0, n=n, hf=hf: e.matmul(
                            py[0:n, :], lhsT=cb_[:, c0:c0 + n], rhs=vb[:, ib, hf * 512:(hf + 1) * 512],
                            start=(i == 0), stop=(i == 127)))

            def prefetch_x1(g):
                t0, N = groups[g]
                for ti, (c0, n) in enumerate(gtiles(g)):
                    x2, T_x2 = x2l[ti]
                    tg = t0 + c0
                    S.dma("sp", [], [T_x2], lambda e, tg=tg, n=n, x2=x2: e.dma_start(out=x2[0:n, :], in_=x1d[tg:tg + n, :]))

            def epilogue(g):
                t0, N = groups[g]
                for ti, (c0, n) in enumerate(gtiles(g)):
                    x2, T_x2 = x2l[ti]
                    for hf in range(2):
                        py, T_py = pY[ti][hf]
                        S.op("dve", [T_py, T_x2], [T_x2], lambda e, py=py, hf=hf, n=n, x2=x2: e.tensor_tensor(
                            out=x2[0:n, hf * 512:(hf + 1) * 512], in0=py[0:n, :], in1=x2[0:n, hf * 512:(hf + 1) * 512], op=ALU.add))
                for ti, (c0, n) in enumerate(gtiles(g)):
                    x2, T_x2 = x2l[ti]
                    tg = t0 + c0
                    k0 = ti * 4
                    S.op("act", [T_x2], [T_junk2, T_st2s], lambda e, n=n, x2=x2, k0=k0: e.activation(
                        out=junk2[0:n, :], in_=x2[0:n, :], func=AF.Square, accum_out=st2s[0:n, k0:k0 + 1]))
                    S.op("act", [T_st2s, T_eps], [T_st2s], lambda e, n=n, k0=k0: e.activation(
                        out=st2s[0:n, k0 + 1:k0 + 2], in_=st2s[0:n, k0:k0 + 1], func=AF.Sqrt, scale=1.0 / DM, bias=eps_t[0:n, :]))
                    S.op("dve", [T_st2s], [T_st2s], lambda e, n=n, k0=k0: e.reciprocal(out=st2s[0:n, k0 + 1:k0 + 2], in_=st2s[0:n, k0 + 1:k0 + 2]))
                    S.op("dve", [T_x2, T_st2s, T_c2], [T_x2], lambda e, n=n, x2=x2, k0=k0: e.scalar_tensor_tensor(
                        out=x2[0:n, :], in0=x2[0:n, :], scalar=st2s[0:n, k0 + 1:k0 + 2], in1=gfin_s[0:n, :], op0=ALU.mult, op1=ALU.mult))
                    if tg < n_pseq * SEQ:
                        dst = yp[tg // SEQ][tg % SEQ:tg % SEQ + n, :]
                    else:
                        dst = ys[0:n, :]
                    S.dma("sp", [T_x2], [], lambda e, dst=dst, n=n, x2=x2: e.dma_start(out=dst, in_=x2[0:n, :]))

            load_group(0)
            load_chunk(0)
            pz = P10(0)
            pz.flush()
            seq = [(g, i) for g in range(NG) for i in range(128)]
            nxt = None

            def emit_v(idx):
                g_, i_ = seq[idx]
                stage_v(g_, i_)
                if i_ == 127:
                    epilogue(g_)

            for idx, (g, i) in enumerate(seq):
                if i == 0:
                    nxt = None
                    if g + 1 < NG:
                        load_group(g + 1)
                        nxt = P10(g + 1)
                stage_u(g, i)
                if idx >= 2:
                    emit_v(idx - 2)
                if i % 2 == 0:
                    ic_next = i // 2 + 1
                    if ic_next < NCH:
                        load_chunk(ic_next)
                    elif g + 1 < NG:
                        load_chunk(0)
                if nxt is not None and (i % 2 == 1 or i in (8, 16, 24, 32)):
                    nxt.step()
                if i == 64:
                    prefetch_x1(g)
                if i == 127 and nxt is not None:
                    nxt.flush()
            emit_v(len(seq) - 2)
            emit_v(len(seq) - 1)

        S.barrier()
        S.finish("sp")
        print("ops per engine:", S.nops, "sems:", S.nsem)
    return nc


def _prep_shared(inp):
    f = lambda a: np.ascontiguousarray(np.asarray(a, dtype=np.float32))
    sh = {}
    sh["w_in"] = f(inp["w_in"][0])
    sh["gmix"] = f(inp["g_mix"][0].reshape(8, 128).T)
    sh["convw"] = f(inp["conv_w"][0].reshape(31, 4, 128).transpose(2, 1, 0))
    sh["cvec"] = f(np.concatenate([inp["conv_b"][0].reshape(4, 128).T, inp["conv_ln_g"][0].reshape(4, 128).T,
                                   inp["conv_ln_b"][0].reshape(4, 128).T], axis=1))
    sh["lam"] = f(np.stack([inp["lambda_q1"][0], inp["lambda_k1"][0], inp["lambda_q2"][0], inp["lambda_k2"][0]]).reshape(1, 256))
    sh["subg"] = f(inp["subln_g"][0].reshape(1, 128))
    sh["relb"] = f(inp["rel_bias"].reshape(1, 128))
    sh["w_out"] = f(inp["w_out"][0])
    sh["gffn"] = f(inp["g_ffn"][0].reshape(8, 128).T)
    sh["wq"] = f(inp["w_query"][0])
    sh["keysT"] = f(inp["sub_keys"][0].reshape(16, 128, 128).transpose(0, 2, 1))
    sh["uT"] = f(inp["peer_u"][0].T)
    sh["pv"] = f(inp["peer_v"][0])
    sh["gfin"] = f(inp["g_final"].reshape(1, DM))
    sh["bkc"] = _bucket_tiles()
    return sh


def kernel(**inp):
    f = lambda a: np.ascontiguousarray(np.asarray(a, dtype=np.float32))
    sh = _prep_shared(inp)
    nc = build_program(2)
    in_maps = []
    for c in range(NCORES):
        m = dict(sh)
        m["xp"] = f(inp["x_prompt"][2 * c:2 * c + 2])
        m["xs"] = f(inp["x_sample"][c])
        m["ckT"] = f(np.asarray(inp["cache_k"][0, c]).reshape(SEQ, 4, 128).transpose(1, 2, 0))
        m["cv"] = f(np.asarray(inp["cache_v"][0, c]).reshape(SEQ, 512))
        m["scT"] = f(np.asarray(inp["state_conv"][0, c]).reshape(30, 4, 128).transpose(2, 1, 0))
        in_maps.append(m)
    res = run_bass_kernel_spmd(nc, in_maps, core_ids=list(range(NCORES)))
    R = res.results
    y_prompt = np.concatenate([r["yp"] for r in R], axis=0)
    y_sample = np.stack([r["ys"] for r in R], axis=0)
    k_prompt = np.concatenate([r["kp"] for r in R], axis=0).reshape(1, 16, SEQ, 4, 2, 64)
    v_prompt = np.concatenate([r["vp"] for r in R], axis=0).reshape(1, 16, SEQ, 4, 128)
    c_prompt = np.concatenate([r["cp"] for r in R], axis=0).reshape(1, 16, 30, 512)
    k_sample = np.stack([r["ks"] for r in R], axis=0).reshape(1, 8, 64, 4, 2, 64)
    v_sample = np.stack([r["vs"] for r in R], axis=0).reshape(1, 8, 64, 4, 128)
    c_sample = np.stack([r["cs"] for r in R], axis=0).reshape(1, 8, 30, 512)
    return (y_prompt, y_sample, k_prompt, v_prompt, c_prompt, k_sample, v_sample, c_sample)
```

```python
import math
import os
from contextlib import ExitStack

import numpy as np
import concourse.bass as bass
import concourse.mybir as mybir
from concourse.bass_utils import run_bass_kernel_spmd

F32 = mybir.dt.float32
BF16 = mybir.dt.bfloat16
U32 = mybir.dt.uint32
AF = mybir.ActivationFunctionType
ALU = mybir.AluOpType
AX = mybir.AxisListType

EPS = 1e-6
LAM_INIT = 0.8 - 0.6 * math.exp(-0.3 * 0)
NCORES = 8
SEQ = 2048
DM = 1024
NEXP_SIDE = 128

SEM_LIMIT = 30000


class Counter:
    def __init__(self, S, name):
        self.S = S
        self.name = name
        self.epoch = 0
        self.val = 0
        self.sem = S.new_sem(f"{name}_e0")

    def bump(self, inc):
        if self.val + inc > SEM_LIMIT:
            self.epoch += 1
            self.val = 0
            self.sem = self.S.new_sem(f"{self.name}_e{self.epoch}")
        self.val += inc
        return (self.sem, self.val, self.name, self.epoch)


class Tile:
    __slots__ = ("name", "w", "r", "dmac")

    def __init__(self, name):
        self.name = name
        self.w = None
        self.r = []
        self.dmac = None


class Sched:
    def __init__(self, nc, stack):
        self.nc = nc
        self.stack = stack
        self.nsem = 0
        self.engs = {"pe": nc.tensor, "act": nc.scalar, "dve": nc.vector,
                     "pool": nc.gpsimd, "sp": nc.sync}
        self.cnt = {k: Counter(self, k) for k in self.engs}
        self.known = {k: {} for k in self.engs}
        self.nops = {k: 0 for k in self.engs}
        self.tiles = []
        self.cap = None
        self.snaps = {}

    def new_sem(self, name):
        self.nsem += 1
        return self.stack.enter_context(self.nc.semaphore(f"s{self.nsem}_{name}"))

    def tile(self, name):
        t = Tile(name)
        self.tiles.append(t)
        return t

    def _wait(self, e, ev):
        sem, val, name, epoch = ev
        key = (name, epoch)
        if self.known[e].get(key, 0) >= val:
            return
        self.known[e][key] = val
        self.engs[e].wait_ge(sem, val)
        snap = self.snaps.get((name, epoch, val))
        if snap:
            ke = self.known[e]
            for k2, v2 in snap.items():
                if ke.get(k2, 0) < v2:
                    ke[k2] = v2

    def _deps(self, reads, writes):
        evs = []
        for t in reads:
            if t.w is not None:
                evs.append(t.w)
        for t in writes:
            if t.w is not None:
                evs.append(t.w)
            evs.extend(t.r)
        return evs

    def op(self, e, reads, writes, fn):
        if self.cap is not None:
            self.cap.append(("op", e, reads, writes, fn, None))
            return None
        return self._op(e, reads, writes, fn)

    def dma(self, q, reads, writes, fn, key=None):
        if self.cap is not None:
            self.cap.append(("dma", q, reads, writes, fn, key))
            return None
        return self._dma(q, reads, writes, fn, key)

    def replay(self, item):
        kind, e, reads, writes, fn, key = item
        if kind == "op":
            return self._op(e, reads, writes, fn)
        return self._dma(e, reads, writes, fn, key)

    def _op(self, e, reads, writes, fn):
        for ev in self._deps(reads, writes):
            if ev[2] == e:
                if e == "pe":
                    continue
                if ev[3] == self.cnt[e].epoch and self.cnt[e].val - ev[1] >= 2:
                    continue
            self._wait(e, ev)
        ins = fn(self.engs[e])
        ev = self.cnt[e].bump(1)
        ins.then_inc(ev[0], 1)
        self.snaps[(ev[2], ev[3], ev[1])] = dict(self.known[e])
        self.nops[e] += 1
        self._mark(ev, reads, writes)
        return ev

    def _mark(self, ev, reads, writes):
        k = (ev[2], ev[3])
        for t in reads:
            t.r = [x for x in t.r if (x[2], x[3]) != k]
            t.r.append(ev)
        for t in writes:
            t.w = ev
            t.r = []

    def _dma(self, q, reads, writes, fn, key=None):
        kt = key or (writes[0] if writes else reads[0])
        if kt.dmac is None:
            kt.dmac = Counter(self, "d_" + kt.name)
        for ev in self._deps(reads, writes):
            self._wait(q, ev)
        ins = fn(self.engs[q])
        ev = kt.dmac.bump(16)
        ins.then_inc(ev[0], 16)
        self.snaps[(ev[2], ev[3], ev[1])] = dict(self.known[q])
        self.nops[q] += 1
        self._mark(ev, reads, writes)
        return ev

    def _all_events(self):
        evs = {}
        for t in self.tiles:
            for ev in ([t.w] if t.w else []) + t.r:
                k = (ev[2], ev[3])
                if k not in evs or evs[k][1] < ev[1]:
                    evs[k] = ev
        return evs

    def barrier(self):
        evs = self._all_events()
        for e in self.engs:
            for ev in evs.values():
                self._wait(e, ev)
        for t in self.tiles:
            t.w = None
            t.r = []

    def finish(self, e="sp"):
        for ev in self._all_events().values():
            self._wait(e, ev)


def _bucket_np(rel):
    nb = 16
    max_exact = 8
    ret = np.where(rel > 0, nb, 0)
    n = np.abs(rel)
    nf = np.maximum(n, 1).astype(np.float32)
    large = max_exact + (np.log(nf / max_exact) / math.log(128 / max_exact) * (nb - max_exact)).astype(np.int32)
    large = np.minimum(large, nb - 1)
    return ret + np.where(n < max_exact, n, large)


def _bucket_tiles():
    k = np.arange(128)[:, None]
    q = np.arange(128)[None, :]
    b0 = _bucket_np(k - q).astype(np.float32)
    masked = (k // 64) > (q // 64)
    b0 = np.where(masked, 32.0, b0)
    b1 = _bucket_np(k - q - 128).astype(np.float32)
    return np.stack([b0, b1], axis=1).astype(np.float32)


def build_program(n_pseq=2, with_peer=True, dbg=False):
    nc = bass.Bass("TRN2", target_bir_lowering=False)
    NT = n_pseq * SEQ + 64

    def din(name, shape, dt=F32):
        return nc.dram_tensor(name, list(shape), dt, kind="ExternalInput").ap()

    def dout(name, shape, dt=F32):
        return nc.dram_tensor(name, list(shape), dt, kind="ExternalOutput").ap()

    xp = din("xp", [n_pseq, SEQ, DM])
    xs = din("xs", [64, DM])
    ckT = din("ckT", [4, 128, SEQ])
    cv = din("cv", [SEQ, 512])
    scT = din("scT", [128, 4, 30])
    w_in = din("w_in", [DM, 2560])
    gmix = din("gmix", [128, 8])
    convw = din("convw", [128, 4, 31])
    cvec = din("cvec", [128, 12])
    lam = din("lam", [1, 256])
    subg = din("subg", [1, 128])
    relb = din("relb", [1, 128])
    w_out = din("w_out", [DM, DM])
    gffn = din("gffn", [128, 8])
    wq = din("wq", [DM, 2048])
    keysT = din("keysT", [16, 128, 128])
    uT = din("uT", [DM, 16384])
    pv = din("pv", [16384, DM])
    gfin = din("gfin", [1, DM])
    bkc = din("bkc", [128, 2, 128])

    yp = dout("yp", [n_pseq, SEQ, DM])
    ys = dout("ys", [64, DM])
    kp = dout("kp", [n_pseq, SEQ, 512])
    vp = dout("vp", [n_pseq, SEQ, 512])
    cp = dout("cp", [n_pseq, 30, 512])
    ks = dout("ks", [64, 512])
    vs = dout("vs", [64, 512])
    cs = dout("cs", [30, 512])

    kind_scr = "ExternalOutput" if dbg else "Internal"
    x1d = nc.dram_tensor("x1d", [NT, DM], F32, kind=kind_scr).ap()
    h2Td = nc.dram_tensor("h2Td", [8, 128, NT], BF16, kind="Internal").ap()

    with ExitStack() as st:
        S = Sched(nc, st)

        cur = [st]

        def sb(name, shape, dt):
            return cur[0].enter_context(nc.sbuf_tensor(name, list(shape), dt)), S.tile(name)

        def ps(name, shape, dt):
            return cur[0].enter_context(nc.psum_tensor(name, list(shape), dt)), S.tile(name)

        ident_f, T_identf = sb("ident_f", [128, 128], F32)
        ident_b, T_identb = sb("ident_b", [128, 128], BF16)
        ones_b, T_ones = sb("ones_b", [128, 128], BF16)
        iota_t, T_iota = sb("iota_t", [128, 128], F32)
        T_const = S.tile("consts")

        S.op("pool", [], [T_iota], lambda e: e.iota(iota_t[:], pattern=[[1, 128]], base=0, channel_multiplier=-1,
                                                    allow_small_or_imprecise_dtypes=True))
        S.op("dve", [T_iota], [T_identf], lambda e: e.tensor_scalar(out=ident_f[:], in0=iota_t[:], scalar1=0.0,
                                                                     scalar2=None, op0=ALU.is_equal))
        S.op("dve", [T_identf], [T_identb], lambda e: e.tensor_copy(out=ident_b[:], in_=ident_f[:]))
        S.op("pool", [], [T_ones], lambda e: e.memset(ones_b[:], 1.0))

        eps_t, T_eps = sb("eps_t", [128, 1], F32)
        S.op("pool", [], [T_eps], lambda e: e.memset(eps_t[:], EPS))
        EPS_AP = eps_t
        iota_r, T_iotar = sb("iota_r", [128, 128], F32)
        S.op("pool", [], [T_iotar], lambda e: e.iota(iota_r[:], pattern=[[1, 128]], base=0, channel_multiplier=0,
                                                     allow_small_or_imprecise_dtypes=True))
        st1 = ExitStack()
        cur[0] = st1
        w_in_b, T_win = sb("w_in_b", [128, 8, 2560], BF16)
        w_out_b, T_wout = sb("w_out_b", [128, 8, DM], BF16)
        diag, T_diag = sb("diag", [128, 124, 128], BF16)
        gmix_s, _ = sb("gmix_s", [128, 8], F32)
        convw_s, _ = sb("convw_s", [128, 4, 31], F32)
        cvec_s, _ = sb("cvec_s", [128, 12], F32)
        lam_s, _ = sb("lam_s", [128, 256], F32)
        gsub_s, _ = sb("gsub_s", [128, 128], F32)
        relb_s, _ = sb("relb_s", [128, 128], F32)
        bk_s, _ = sb("bk_s", [128, 2, 128], F32)
        Tb, T_Tb = sb("Tb", [128, 4, 2, 128], F32)
        eqm, T_eqm = sb("eqm", [128, 2, 128], F32)
        small, T_small = sb("small", [128, 16], F32)
        stage0, T_st0 = sb("stage0", [128, 1024], F32)
        stage1, T_st1 = sb("stage1", [128, 1024], F32)

        for dst, src in ((gmix_s, gmix), (convw_s, convw), (cvec_s, cvec), (bk_s, bkc)):
            S.dma("sp", [], [T_const], lambda e, d=dst, s_=src: e.dma_start(out=d[:], in_=s_))
        for dst, src, n in ((lam_s, lam, 256), (gsub_s, subg, 128), (relb_s, relb, 128)):
            S.dma("sp", [], [T_const], lambda e, d=dst, s_=src, n=n: e.dma_start(out=d[:], in_=s_.to_broadcast([128, n])))

        stg = [(stage0, T_st0), (stage1, T_st1)]
        i = 0
        for kc in range(8):
            for (a0, a1) in ((0, 1024), (1024, 2048), (2048, 2560)):
                stt, T_s = stg[i % 2]
                i += 1
                S.dma("sp", [], [T_s], lambda e, stt=stt, kc=kc, a0=a0, a1=a1: e.dma_start(
                    out=stt[:, 0:a1 - a0], in_=w_in[kc * 128:(kc + 1) * 128, a0:a1]))
                S.op("dve", [T_s, T_const], [T_win], lambda e, stt=stt, kc=kc, a0=a0, a1=a1: e.tensor_scalar(
                    out=w_in_b[:, kc, a0:a1], in0=stt[:, 0:a1 - a0], scalar1=gmix_s[:, kc:kc + 1],
                    scalar2=None, op0=ALU.mult))
        for kc in range(8):
            stt, T_s = stg[i % 2]
            i += 1
            S.dma("sp", [], [T_s], lambda e, stt=stt, kc=kc: e.dma_start(
                out=stt[:, 0:DM], in_=w_out[kc * 128:(kc + 1) * 128, :]))
            S.op("act", [T_s], [T_wout], lambda e, stt=stt, kc=kc: e.copy(out=w_out_b[:, kc, :], in_=stt[:, 0:DM]))
        for w in range(31):
            for cb in range(4):
                S.op("dve", [T_const, T_identf], [T_diag], lambda e, w=w, cb=cb: e.tensor_scalar(
                    out=diag[:, w * 4 + cb, :], in0=ident_f[:], scalar1=convw_s[:, cb, w:w + 1], scalar2=None,
                    op0=ALU.mult))
        S.op("dve", [T_const], [T_eqm], lambda e: e.tensor_scalar(
            out=eqm[:], in0=bk_s[:], scalar1=32.0, scalar2=-30000.0, op0=ALU.is_equal, op1=ALU.mult))
        for h in range(4):
            S.op("dve", [T_eqm], [T_Tb], lambda e, h=h: e.tensor_copy(out=Tb[:, h, :, :], in_=eqm[:]))
        for b in range(32):
            S.op("dve", [T_const], [T_eqm], lambda e, b=b: e.tensor_scalar(
                out=eqm[:], in0=bk_s[:], scalar1=float(b), scalar2=None, op0=ALU.is_equal))
            for h in range(4):
                S.op("dve", [T_eqm, T_const, T_Tb], [T_Tb], lambda e, b=b, h=h: e.scalar_tensor_tensor(
                    out=Tb[:, h, :, :], in0=eqm[:], scalar=relb_s[:, b * 4 + h:b * 4 + h + 1], in1=Tb[:, h, :, :],
                    op0=ALU.mult, op1=ALU.add))
        S.op("dve", [T_const], [T_eqm], lambda e: e.tensor_tensor(
            out=eqm[:, 0, :].rearrange("p (a b) -> p a b", a=2), in0=lam_s[:].rearrange("p (a b c) -> p a b c", a=2, b=2)[:, :, 0, :],
            in1=lam_s[:].rearrange("p (a b c) -> p a b c", a=2, b=2)[:, :, 1, :], op=ALU.mult))
        S.op("dve", [T_eqm], [T_small], lambda e: e.reduce_sum(
            out=small[:, 0:2], in_=eqm[:, 0, :].rearrange("p (a b) -> p a b", a=2), axis=AX.X))
        S.op("act", [T_small], [T_small], lambda e: e.activation(out=small[:, 2:4], in_=small[:, 0:2], func=AF.Exp))
        S.op("dve", [T_small], [T_small], lambda e: e.tensor_tensor(
            out=small[:, 4:5], in0=small[:, 3:4], in1=small[:, 2:3], op=ALU.subtract))
        S.op("dve", [T_small], [T_small], lambda e: e.tensor_scalar(
            out=small[:, 4:5], in0=small[:, 4:5], scalar1=-LAM_INIT, scalar2=None, op0=ALU.add))
        S.op("dve", [T_const], [T_const], lambda e: e.tensor_scalar(
            out=gsub_s[:], in0=gsub_s[:], scalar1=1.0 - LAM_INIT, scalar2=None, op0=ALU.mult))
        neg_lam = small[:, 4:5]

        fT, T_fT = sb("fT", [128, 8, 512], BF16)
        xt, T_xt = sb("xt", [128, DM], F32)
        xr, T_xr = xt, T_xt
        junk, T_junk = sb("junk", [128, DM], BF16)
        hb, T_hb = sb("hb", [128, DM], BF16)
        stat, T_stat = sb("stat", [128, 8], F32)
        aT, T_aT = sb("aT", [128, 4, 30 + 512], BF16)
        sig, T_sig = sb("sig", [128, 512], F32)
        a32, T_a32 = sb("a32", [128, 512], F32)
        qT, T_qT = sb("qT", [128, 4, 512], BF16)
        kT, T_kT = sb("kT", [128, 4, SEQ + 64], BF16)
        vaug, T_v = sb("vaug", [128, 17, 4, 130], BF16)
        catT, T_cat = sb("catT", [128, 8, 512], BF16)
        zq, T_zq = sb("zq", [128, 512], BF16)
        zk32, T_zk32 = sb("zk32", [128, 512], F32)
        zkb, T_zkb = sb("zkb", [128, 512], BF16)
        zv32, T_zv32 = sb("zv32", [128, 512], F32)
        PTT = [sb(f"PT{i}", [128, 512], BF16) for i in range(3)]
        TMP = [sb(f"tmpb{i}", [128, 128], F32) for i in range(3)]
        att, T_att = sb("att", [128, 128], F32)
        sqj, T_sq = sb("sqj", [128, 128], F32)
        attb, T_attb = sb("attb", [128, 512], BF16)
        astat, T_astat = sb("astat", [128, 8], F32)
        y32, T_y32 = sb("y32", [128, 4, 512], F32)
        ybf, T_ybf = sb("ybf", [128, 4, 512], BF16)
        ysq, T_ysq = sb("ysq", [128, 4, 512], BF16)
        mu, T_mu = sig, T_sig
        rs, T_rs = a32, T_a32
        ctail, T_ctail = zk32, T_zk32
        cst32, T_cst32 = sb("cst32", [128, 4, 30], F32)

        pT, T_pT = ps("pT", [128, 1024], BF16)
        pM = [ps(f"pM{i}", [128, 512], F32) for i in range(2)]
        pSS = [ps(f"pS{i}", [128, 512], F32) for i in range(2)]
        pO, T_pO = ps("pO", [128, 2, 256], F32)
        pX = [ps(f"pX{i}", [128, 512], F32) for i in range(2)]
        pm_i = [0]
        pOO = [(pO, T_pO), (pM[0][0][:].rearrange("p (a b) -> p a b", a=2), pM[0][1])]
        pSS = pSS + [pX[1]]

        def next_pM():
            pm_i[0] += 1
            return pM[pm_i[0] % 2]

        S.op("pool", [], [T_v], lambda e: e.memset(vaug[:], 1.0))

        def rms_to_bf16(n, src, T_src, dst_b, T_dst, col):
            S.op("act", [T_src], [T_junk, T_stat], lambda e: e.activation(
                out=junk[0:n, :], in_=src[0:n, :], func=AF.Square, accum_out=stat[0:n, col:col + 1]))
            S.op("act", [T_stat], [T_stat], lambda e: e.activation(
                out=stat[0:n, col + 1:col + 2], in_=stat[0:n, col:col + 1], func=AF.Sqrt, scale=1.0 / DM, bias=EPS_AP[0:n, :]))
            S.op("dve", [T_stat], [T_stat], lambda e: e.reciprocal(
                out=stat[0:n, col + 1:col + 2], in_=stat[0:n, col + 1:col + 2]))
            S.op("dve", [T_src, T_stat], [T_dst], lambda e: e.tensor_scalar(
                out=dst_b[0:n, :], in0=src[0:n, :], scalar1=stat[0:n, col + 1:col + 2], scalar2=None, op0=ALU.mult))

        def transpose_to_fT(n, src_b, T_src, c0):
            for kc in range(8):
                S.op("pe", [T_src, T_identb], [T_pT], lambda e, kc=kc: e.transpose(
                    out=pT[:, kc * 128:kc * 128 + n], in_=src_b[0:n, kc * 128:(kc + 1) * 128], identity=ident_b[0:n, 0:n]))
            S.op("act", [T_pT], [T_fT], lambda e: e.copy(
                out=fT[:, :, c0:c0 + n], in_=pT[:].rearrange("p (k c) -> p k c", k=8)[:, :, 0:n]))

        seqs = []
        for s_ in range(n_pseq):
            seqs.append(("p", xp[s_], SEQ, kp[s_], vp[s_], cp[s_], s_ * SEQ))
        seqs.append(("s", xs, 64, ks, vs, cs, n_pseq * SEQ))

        S.barrier()
        import os
        STOP = int(os.environ.get("KSTOP", "99"))
        if STOP <= 0:
            seqs = []

        for (kind, xd, ntok, kd, vd, cd, tok0) in seqs:
            past = SEQ if kind == "s" else 0
            if kind == "p":
                S.op("pool", [], [T_aT], lambda e: e.memset(aT[:, :, 0:30], 0.0))
            else:
                S.dma("sp", [], [T_cst32], lambda e: e.dma_start(out=cst32[:], in_=scT))
                S.op("dve", [T_cst32], [T_aT], lambda e: e.tensor_copy(out=aT[:, :, 0:30], in_=cst32[:]))
                for h in range(4):
                    for hf in range(2):
                        stt, T_s = stg[(h * 2 + hf) % 2]
                        S.dma("sp", [], [T_s], lambda e, stt=stt, h=h, hf=hf: e.dma_start(
                            out=stt[:, 0:1024], in_=ckT[h, :, hf * 1024:(hf + 1) * 1024]))
                        S.op("act", [T_s], [T_kT], lambda e, stt=stt, h=h, hf=hf: e.copy(
                            out=kT[:, h, hf * 1024:(hf + 1) * 1024], in_=stt[:, 0:1024]))
                for blk in range(16):
                    stt, T_s = stg[blk % 2]
                    S.dma("sp", [], [T_s], lambda e, stt=stt, blk=blk: e.dma_start(
                        out=stt[:, 0:512], in_=cv[blk * 128:(blk + 1) * 128, :]))
                    S.op("dve", [T_s], [T_v], lambda e, stt=stt, blk=blk: e.tensor_copy(
                        out=vaug[:, blk, :, 0:128], in_=stt[:, 0:512].rearrange("p (h e) -> p h e", h=4)))

            ngroups = (ntok + 511) // 512
            for g in range(ngroups):
                g0 = g * 512
                N = min(512, ntok - g0)
                tiles = [(c0, min(128, N - c0)) for c0 in range(0, N, 128)]
                last_group = (g == ngroups - 1)

                for (c0, n) in tiles:
                    S.dma("sp", [], [T_xt], lambda e, c0=c0, n=n: e.dma_start(out=xt[0:n, :], in_=xd[g0 + c0:g0 + c0 + n, :]))
                    rms_to_bf16(n, xt, T_xt, hb, T_hb, 0)
                    transpose_to_fT(n, hb, T_hb, c0)

                if STOP <= 1:
                    continue
                for (c0, n) in tiles:
                    blk = (past + g0 + c0) // 128
                    kcol = past + g0 + c0
                    for j in range(int(os.environ.get('KJ', '3'))):
                        pm, T_pm = next_pM()
                        for kc in range(8):
                            S.op("pe", [T_fT, T_win], [T_pm], lambda e, pm=pm, kc=kc, j=j, c0=c0, n=n: e.matmul(
                                pm[0:n, :], lhsT=fT[:, kc, c0:c0 + n], rhs=w_in_b[:, kc, 1024 + j * 512:1024 + (j + 1) * 512],
                                start=(kc == 0), stop=(kc == 7)))
                        if j == 0:
                            S.op("act", [T_pm], [T_zq], lambda e, pm=pm, n=n: e.activation(
                                out=zq[0:n, :], in_=pm[0:n, :], func=AF.Copy, scale=0.125))
                            for h in range(4):
                                S.op("pe", [T_zq, T_identb], [T_pT], lambda e, h=h, n=n: e.transpose(
                                    out=pT[:, h * 128:h * 128 + n], in_=zq[0:n, h * 128:(h + 1) * 128], identity=ident_b[0:n, 0:n]))
                            S.op("dve", [T_pT], [T_qT], lambda e, c0=c0, n=n: e.tensor_copy(
                                out=qT[:, :, c0:c0 + n], in_=pT[:, 0:512].rearrange("p (k c) -> p k c", k=4)[:, :, 0:n]))
                        elif j == 1:
                            if not os.environ.get("K1A"):
                                S.op("dve", [T_pm], [T_zk32], lambda e, pm=pm, n=n: e.tensor_copy(out=zk32[0:n, :], in_=pm[0:n, :]))
                            S.op("act", [T_zk32], [T_zkb], lambda e, pm=pm, n=n: e.copy(out=zkb[0:n, :], in_=zk32[0:n, :]))
                            if not os.environ.get("NOKD"):
                                S.dma("sp", [T_zk32], [], lambda e, c0=c0, n=n: e.dma_start(
                                    out=kd[g0 + c0:g0 + c0 + n, :], in_=zk32[0:n, :]))
                            for h in range(0 if os.environ.get("K1B") else 4):
                                S.op("pe", [T_zkb, T_identb], [T_pT], lambda e, h=h, n=n: e.transpose(
                                    out=pT[:, 512 + h * 128:512 + h * 128 + n], in_=zkb[0:n, h * 128:(h + 1) * 128],
                                    identity=ident_b[0:n, 0:n]))
                            if not os.environ.get("K1C"):
                              S.op("dve", [T_pT], [T_kT], lambda e, kcol=kcol, n=n: e.tensor_copy(
                                out=kT[:, :, kcol:kcol + n], in_=pT[:, 512:1024].rearrange("p (k c) -> p k c", k=4)[:, :, 0:n]))
                        else:
                            S.op("dve", [T_pm], [T_zv32], lambda e, pm=pm, n=n: e.tensor_copy(out=zv32[0:n, :], in_=pm[0:n, :]))
                            S.op("act", [T_zv32], [T_v], lambda e, pm=pm, n=n, blk=blk: e.copy(
                                out=vaug[0:n, blk, :, 0:128], in_=zv32[0:n, :].rearrange("p (h e) -> p h e", h=4)))
                            S.dma("sp", [T_zv32], [], lambda e, c0=c0, n=n: e.dma_start(
                                out=vd[g0 + c0:g0 + c0 + n, :], in_=zv32[0:n, :]))

                if STOP <= 2:
                    continue
                for cb in range(4):
                    pa, T_pa = next_pM()
                    pg, T_pg = next_pM()
                    for kc in range(8):
                        S.op("pe", [T_fT, T_win], [T_pa], lambda e, pa=pa, kc=kc, cb=cb: e.matmul(
                            pa[:, 0:N], lhsT=w_in_b[:, kc, cb * 128:(cb + 1) * 128], rhs=fT[:, kc, 0:N],
                            start=(kc == 0), stop=(kc == 7)))
                    for kc in range(8):
                        S.op("pe", [T_fT, T_win], [T_pg], lambda e, pg=pg, kc=kc, cb=cb: e.matmul(
                            pg[:, 0:N], lhsT=w_in_b[:, kc, 512 + cb * 128:512 + (cb + 1) * 128], rhs=fT[:, kc, 0:N],
                            start=(kc == 0), stop=(kc == 7)))
                    S.op("act", [T_pg], [T_sig], lambda e, pg=pg: e.activation(out=sig[:, 0:N], in_=pg[:, 0:N], func=AF.Sigmoid))
                    S.op("dve", [T_pa, T_sig], [T_a32], lambda e, pa=pa: e.tensor_tensor(
                        out=a32[:, 0:N], in0=pa[:, 0:N], in1=sig[:, 0:N], op=ALU.mult))
                    S.op("pool", [T_a32], [T_aT], lambda e, cb=cb: e.tensor_copy(out=aT[:, cb, 30:30 + N], in_=a32[:, 0:N]))
                    if last_group:
                        pm, T_pm = pX[0]
                        S.op("pe", [T_a32, T_identf], [T_pm], lambda e, pm=pm, cb=cb: e.transpose(
                            out=pm[0:30, cb * 128:(cb + 1) * 128], in_=a32[:, N - 30:N], identity=ident_f[:]))
                if last_group:
                    pm, T_pm = pX[0]
                    S.op("act", [T_pm], [T_ctail], lambda e, pm=pm: e.copy(out=ctail[0:30, :], in_=pm[0:30, :]))
                    S.dma("sp", [T_ctail], [], lambda e: e.dma_start(out=cd, in_=ctail[0:30, :]))

                if STOP <= 3:
                    continue
                S.cap = []
                for cb in range(4):
                    pm, T_pm = pM[1]
                    for w in range(31):
                        S.op("pe", [T_aT, T_diag], [T_pm], lambda e, pm=pm, w=w, cb=cb: e.matmul(
                            pm[:, 0:N], lhsT=diag[:, w * 4 + cb, :], rhs=aT[:, cb, w:w + N], start=(w == 0), stop=(w == 30)))
                    S.op("act", [T_pm, T_const], [T_y32], lambda e, pm=pm, cb=cb: e.activation(
                        out=y32[:, cb, 0:N], in_=pm[:, 0:N], func=AF.Identity, bias=cvec_s[:, cb:cb + 1]))
                    S.op("act", [T_pm, T_const], [T_ysq], lambda e, pm=pm, cb=cb: e.activation(
                        out=ysq[:, cb, 0:N], in_=pm[:, 0:N], func=AF.Square, bias=cvec_s[:, cb:cb + 1]))
                    S.op("pool", [T_y32], [T_ybf], lambda e, cb=cb: e.tensor_copy(out=ybf[:, cb, 0:N], in_=y32[:, cb, 0:N]))
                p1, T_p1 = pX[0]
                p2, T_p2 = pX[0]
                for cb in range(4):
                    S.op("pe", [T_ybf, T_ones], [T_p1], lambda e, cb=cb: e.matmul(
                        p1[:, 0:N], lhsT=ones_b[:], rhs=ybf[:, cb, 0:N], start=(cb == 0), stop=(cb == 3)))
                S.op("dve", [T_p1], [T_mu], lambda e: e.tensor_scalar(
                    out=mu[:, 0:N], in0=p1[:, 0:N], scalar1=1.0 / 512, scalar2=None, op0=ALU.mult))
                for cb in range(4):
                    S.op("pe", [T_ysq, T_ones], [T_p2], lambda e, cb=cb: e.matmul(
                        p2[:, 0:N], lhsT=ones_b[:], rhs=ysq[:, cb, 0:N], start=(cb == 0), stop=(cb == 3)))
                S.op("dve", [T_mu], [T_rs], lambda e: e.tensor_tensor(out=rs[:, 0:N], in0=mu[:, 0:N], in1=mu[:, 0:N], op=ALU.mult))
                S.op("dve", [T_p2, T_rs], [T_rs], lambda e: e.scalar_tensor_tensor(
                    out=rs[:, 0:N], in0=p2[:, 0:N], scalar=1.0 / 512, in1=rs[:, 0:N], op0=ALU.mult, op1=ALU.subtract))
                S.op("act", [T_rs, T_eps], [T_rs], lambda e: e.activation(
                    out=rs[:, 0:N], in_=rs[:, 0:N], func=AF.Sqrt, bias=eps_t[:, :]))
                S.op("dve", [T_rs], [T_rs], lambda e: e.reciprocal(out=rs[:, 0:N], in_=rs[:, 0:N]))
                for cb in range(4):
                    S.op("dve", [T_y32, T_mu], [T_y32], lambda e, cb=cb: e.tensor_tensor(
                        out=y32[:, cb, 0:N], in0=y32[:, cb, 0:N], in1=mu[:, 0:N], op=ALU.subtract))
                    S.op("pool", [T_y32, T_rs], [T_y32], lambda e, cb=cb: e.tensor_tensor(
                        out=y32[:, cb, 0:N], in0=y32[:, cb, 0:N], in1=rs[:, 0:N], op=ALU.mult))
                    S.op("act", [T_y32, T_const], [T_cat], lambda e, cb=cb: e.activation(
                        out=catT[:, cb, 0:N], in_=y32[:, cb, 0:N], func=AF.Silu,
                        scale=cvec_s[:, 4 + cb:5 + cb], bias=cvec_s[:, 8 + cb:9 + cb]))
                if not last_group:
                    S.op("pool", [T_aT], [T_aT], lambda e: e.tensor_copy(out=aT[:, :, 0:30], in_=aT[:, :, N:N + 30]))

                capD = S.cap
                S.cap = None
                items = []
                for (c0, n) in tiles:
                    qi = (g0 + c0) // 128
                    if kind == "p":
                        far = list(range(0, max(qi - 1, 0)))
                        near = ([(qi - 1, 128, 1)] if qi >= 1 else []) + [(qi, 128, 0)]
                    else:
                        far = list(range(0, 15))
                        near = [(15, 128, 1), (16, 64, 0)]
                    nblk = len(far) + len(near)
                    for h in range(4):
                        for m in range(2):
                            done = 0
                            for f0 in range(0, len(far), 4):
                                chunk = far[f0:f0 + 4]
                                items.append(dict(kind="far", c0=c0, n=n, h=h, m=m, blks=chunk, done=done, nblk=nblk))
                                done += len(chunk)
                            for (blk, nk, bkind) in near:
                                items.append(dict(kind="near", c0=c0, n=n, h=h, m=m, blk=blk, nk=nk, bkind=bkind,
                                                  done=done, nblk=nblk))
                                done += 1
                        items[-1]["head_end"] = True
                    items[-1]["tile_end"] = True

                def emit_qk(k, it):
                    pS_, T_pS_ = pSS[k % 3]
                    PT_, T_PT_ = PTT[k % 3]
                    c0, n, h, m = it["c0"], it["n"], it["h"], it["m"]
                    mrow = slice(m * 64, (m + 1) * 64)
                    if it["kind"] == "far":
                        for j, blk in enumerate(it["blks"]):
                            S.op("pe", [T_kT, T_qT], [T_pS_], lambda e, j=j, blk=blk: e.matmul(
                                pS_[:, j * n:(j + 1) * n], lhsT=kT[mrow, h, blk * 128:(blk + 1) * 128],
                                rhs=qT[mrow, h, c0:c0 + n], start=True, stop=True))
                        cn = len(it["blks"]) * n
                        S.op("act", [T_pS_, T_const], [T_PT_], lambda e: e.activation(
                            out=PT_[:, 0:cn], in_=pS_[:, 0:cn], func=AF.Exp, bias=relb_s[:, 60 + h:61 + h]))
                    else:
                        blk, nk, bkind = it["blk"], it["nk"], it["bkind"]
                        tb_, T_tb_ = TMP[k % 3]
                        S.op("pe", [T_kT, T_qT], [T_pS_], lambda e: e.matmul(
                            pS_[0:nk, 0:n], lhsT=kT[mrow, h, blk * 128:blk * 128 + nk],
                            rhs=qT[mrow, h, c0:c0 + n], start=True, stop=True))
                        S.op("dve", [T_pS_, T_Tb], [T_tb_], lambda e: e.tensor_tensor(
                            out=tb_[0:nk, 0:n], in0=pS_[0:nk, 0:n], in1=Tb[0:nk, h, bkind, 0:n], op=ALU.add))
                        S.op("act", [T_tb_], [T_PT_], lambda e: e.activation(
                            out=PT_[0:nk, 0:n], in_=tb_[0:nk, 0:n], func=AF.Exp))

                def emit_pv(k, it):
                    PT_, T_PT_ = PTT[k % 3]
                    c0, n, h, m = it["c0"], it["n"], it["h"], it["m"]
                    qi_ = (g0 + c0) // 128
                    pO_, T_pO_ = pOO[(qi_ * 4 + h) % 2]
                    nblk = it["nblk"]
                    if it["kind"] == "far":
                        for j, blk in enumerate(it["blks"]):
                            dn = it["done"] + j
                            S.op("pe", [T_PT_, T_v], [T_pO_], lambda e, j=j, blk=blk, dn=dn: e.matmul(
                                pO_[0:n, m, 0:129], lhsT=PT_[:, j * n:(j + 1) * n], rhs=vaug[:, blk, h, 0:129],
                                start=(dn == 0), stop=(dn == nblk - 1)))
                    else:
                        blk, nk = it["blk"], it["nk"]
                        dn = it["done"]
                        S.op("pe", [T_PT_, T_v], [T_pO_], lambda e: e.matmul(
                            pO_[0:n, m, 0:129], lhsT=PT_[0:nk, 0:n], rhs=vaug[0:nk, blk, h, 0:129],
                            start=(dn == 0), stop=(dn == nblk - 1)))
                    if it.get("head_end"):
                        S.op("dve", [T_pO_], [T_astat], lambda e: e.reciprocal(
                            out=astat[0:n, 0:2], in_=pO_[0:n, :, 128:129].rearrange("p a b -> p (a b)")))
                        S.op("dve", [T_astat, T_small], [T_astat], lambda e: e.tensor_tensor(
                            out=astat[0:n, 2:3], in0=astat[0:n, 1:2], in1=neg_lam[0:n, :], op=ALU.mult))
                        S.op("dve", [T_pO_, T_astat], [T_att], lambda e: e.tensor_scalar(
                            out=att[0:n, :], in0=pO_[0:n, 0, 0:128], scalar1=astat[0:n, 0:1], scalar2=None, op0=ALU.mult))
                        S.op("dve", [T_pO_, T_astat, T_att], [T_att], lambda e: e.scalar_tensor_tensor(
                            out=att[0:n, :], in0=pO_[0:n, 1, 0:128], scalar=astat[0:n, 2:3], in1=att[0:n, :],
                            op0=ALU.mult, op1=ALU.add))
                        S.op("dve", [T_att], [T_sq, T_astat], lambda e: e.scalar_tensor_tensor(
                            out=sqj[0:n, :], in0=att[0:n, :], scalar=1.0, in1=att[0:n, :], op0=ALU.mult, op1=ALU.mult,
                            accum_out=astat[0:n, 3:4]))
                        S.op("act", [T_astat, T_eps], [T_astat], lambda e: e.activation(
                            out=astat[0:n, 5:6], in_=astat[0:n, 3:4], func=AF.Ln, scale=1.0 / 128, bias=eps_t[0:n, :]))
                        S.op("act", [T_astat], [T_astat], lambda e: e.activation(
                            out=astat[0:n, 4:5], in_=astat[0:n, 5:6], func=AF.Exp, scale=-0.5))
                        S.op("dve", [T_att, T_astat, T_const], [T_attb], lambda e: e.scalar_tensor_tensor(
                            out=attb[0:n, h * 128:(h + 1) * 128], in0=att[0:n, :], scalar=astat[0:n, 4:5], in1=gsub_s[0:n, :],
                            op0=ALU.mult, op1=ALU.mult))
                    if it.get("tile_end"):
                        for hh in range(4):
                            S.op("pe", [T_attb, T_identb], [T_pT], lambda e, hh=hh: e.transpose(
                                out=pT[:, hh * 128:hh * 128 + n], in_=attb[0:n, hh * 128:(hh + 1) * 128], identity=ident_b[0:n, 0:n]))
                        S.op("act", [T_pT], [T_cat], lambda e: e.copy(
                            out=catT[:, 4:8, c0:c0 + n], in_=pT[:, 0:512].rearrange("p (k c) -> p k c", k=4)[:, :, 0:n]))

                per_item = -(-len(capD) // max(len(items) - 4, 1))
                cpos = 0
                for k, it in enumerate(items):
                    emit_qk(k, it)
                    if k >= 2:
                        emit_pv(k - 2, items[k - 2])
                    for _ in range(per_item):
                        if cpos < len(capD):
                            S.replay(capD[cpos])
                            cpos += 1
                for k in range(max(len(items) - 2, 0), len(items)):
                    emit_pv(k, items[k])
                while cpos < len(capD):
                    S.replay(capD[cpos])
                    cpos += 1

                if STOP <= 5:
                    continue
                for (c0, n) in tiles:
                    S.dma("sp", [], [T_xr], lambda e, c0=c0, n=n: e.dma_start(out=xr[0:n, :], in_=xd[g0 + c0:g0 + c0 + n, :]))
                    for hf in range(2):
                        po, T_po = pX[hf]
                        for kc in range(8):
                            S.op("pe", [T_cat, T_wout], [T_po], lambda e, po=po, kc=kc, hf=hf, c0=c0, n=n: e.matmul(
                                po[0:n, :], lhsT=catT[:, kc, c0:c0 + n], rhs=w_out_b[:, kc, hf * 512:(hf + 1) * 512],
                                start=(kc == 0), stop=(kc == 7)))
                        S.op("dve", [T_po, T_xr], [T_xr], lambda e, po=po, hf=hf, n=n: e.tensor_tensor(
                            out=xr[0:n, hf * 512:(hf + 1) * 512], in0=po[0:n, :], in1=xr[0:n, hf * 512:(hf + 1) * 512], op=ALU.add))
                    S.dma("sp", [T_xr], [], lambda e, c0=c0, n=n: e.dma_start(
                        out=x1d[tok0 + g0 + c0:tok0 + g0 + c0 + n, :], in_=xr[0:n, :]))
                    rms_to_bf16(n, xr, T_xr, hb, T_hb, 2)
                    transpose_to_fT(n, hb, T_hb, c0)
                for kc in range(8):
                    S.dma("sp", [T_fT], [], lambda e, kc=kc: e.dma_start(
                        out=h2Td[kc, :, tok0 + g0:tok0 + g0 + N], in_=fT[:, kc, 0:N]))

        S.barrier()
        st1.close()
        st2 = ExitStack()
        st.enter_context(st2)
        cur[0] = st2
        TG = 256
        NCH = 64
        if with_peer:
            ijwd = nc.dram_tensor("ijwd", [128, 3, NT], F32, kind="Internal").ap()
            uscr_t = nc.dram_tensor("uscr", [NCH, 128, 8 * 256], BF16, kind="Internal").ap()
            vscr_t = nc.dram_tensor("vscr", [NCH, 128, 2 * DM], BF16, kind="Internal").ap()
            uscr = [uscr_t[ic].rearrange("p (k e) -> p k e", k=8) for ic in range(NCH)]
            vscr = [vscr_t[ic].rearrange("p (b d) -> p b d", b=2) for ic in range(NCH)]
            T_uscr = [S.tile(f"uscr{ic}") for ic in range(NCH)]
            T_vscr = [S.tile(f"vscr{ic}") for ic in range(NCH)]
            T_ijwd = S.tile("ijwd")

            gfin_s, T_gfin = sb("gfin_s", [128, DM], F32)
            iota_b, T_iotab = sb("iota_b", [128, 128], BF16)
            T_c2 = S.tile("consts2")
            S.dma("sp", [], [T_c2], lambda e: e.dma_start(out=gfin_s[:], in_=gfin.to_broadcast([128, DM])))
            S.op("dve", [T_iotar], [T_iotab], lambda e: e.tensor_copy(out=iota_b[:], in_=iota_r[:]))

            st2a = ExitStack()
            cur[0] = st2a
            TA = 512
            wq_b, T_wq = sb("wq_b", [128, 8, 2048], BF16)
            keys_b, T_keys = sb("keys_b", [128, 16, 128], BF16)
            gffn_s, T_gffn = sb("gffn_s", [128, 8], F32)
            sg0, T_sg0 = sb("sg0", [128, 1024], F32)
            sg1, T_sg1 = sb("sg1", [128, 1024], F32)
            h2g, T_h2g = sb("h2g", [128, 8, TA], BF16)
            qryT, T_qry = sb("qryT", [128, 16, TA], BF16)
            s_sb, T_ssb = sb("s_sb", [128, 16, 128], F32)
            wk4l = [sb(f"wk4_{i}", [128, 256], F32) for i in range(4)]
            wk4 = [x[0] for x in wk4l]
            T_wk4 = [x[1] for x in wk4l]
            T_A4 = [S.tile(f"A4_{i}") for i in range(4)]
            T_I4 = [S.tile(f"I4_{i}") for i in range(4)]
            T_C4 = [S.tile(f"C4_{i}") for i in range(4)]
            T_P4 = [S.tile(f"P4t_{i}") for i in range(4)]
            A_, T_A = sb("A_", [128, 16, 16], F32)
            Iu, T_Iu = sb("Iu", [128, 16, 16], U32)
            If, T_If = sb("If", [128, 16, 16], F32)
            cand, T_cand = sb("cand", [128, 8, 256], F32)
            C_, T_C = sb("C_", [128, 8, 16], F32)
            pos, T_pos = sb("pos", [128, 8, 16], U32)
            ku, T_ku = sb("ku", [128, 2, 128], U32)
            kf, T_kf = sb("kf", [128, 2, 128], F32)
            E_, T_E = sb("E_", [128, 8, 16], F32)
            gst, T_gst = sb("gst", [128, 32], F32)
            oh, T_oh = sb("oh", [128, 8, 16, 16], F32)
            ijw, T_ijw = sb("ijw", [128, 3, 128], F32)
            ijT = [sb(f"ijT{i}", [128, 3, 128], F32) for i in range(2)]
            stu = [sb(f"stu{i}", [128, 8, 256], BF16) for i in range(2)]
            stv = [sb(f"stv{i}", [128, 2, DM], BF16) for i in range(2)]
            pGa = [ps(f"pGa{i}", [128, 512], F32) for i in range(4)]
            pga_i = [0]

            def next_pGa():
                pga_i[0] += 1
                return pGa[pga_i[0] % 4]

            S.dma("sp", [], [T_c2], lambda e: e.dma_start(out=gffn_s[:], in_=gffn))
            S.dma("pool", [], [T_keys], lambda e: e.dma_start(out=keys_b[:], in_=keysT.rearrange("r d n -> d r n")))
            sgs = [(sg0, T_sg0), (sg1, T_sg1)]
            ii = 0
            for kc in range(8):
                for hf in range(2):
                    stt, T_s = sgs[ii % 2]
                    ii += 1
                    S.dma("sp", [], [T_s], lambda e, stt=stt, kc=kc, hf=hf: e.dma_start(
                        out=stt[:], in_=wq[kc * 128:(kc + 1) * 128, hf * 1024:(hf + 1) * 1024]))
                    S.op("dve", [T_s, T_c2], [T_wq], lambda e, stt=stt, kc=kc, hf=hf: e.tensor_scalar(
                        out=wq_b[:, kc, hf * 1024:(hf + 1) * 1024], in0=stt[:], scalar1=gffn_s[:, kc:kc + 1],
                        scalar2=None, op0=ALU.mult))

            conv_i = [0]

            def convert_chunk():
                ic = conv_i[0]
                if ic >= NCH:
                    return
                conv_i[0] += 1
                (su, T_su), (sv, T_sv) = stu[ic % 2], stv[ic % 2]
                for k4 in range(2):
                    S.dma("pool", [], [T_su], lambda e, k4=k4: e.dma_start(
                        out=su[:, k4 * 4:(k4 + 1) * 4, :],
                        in_=uT[k4 * 512:(k4 + 1) * 512, ic * 256:(ic + 1) * 256].rearrange("(k p) e -> p k e", p=128)))
                S.dma("pool", [], [T_sv], lambda e: e.dma_start(
                    out=sv[:], in_=pv[ic * 256:(ic + 1) * 256, :].rearrange("(b p) d -> p b d", p=128)))
                S.dma("sp", [T_su], [T_uscr[ic]], lambda e: e.dma_start(out=uscr[ic], in_=su[:]), key=T_su)
                S.dma("sp", [T_sv], [T_vscr[ic]], lambda e: e.dma_start(out=vscr[ic], in_=sv[:]), key=T_sv)

            ssbL = [(s_sb, T_ssb), sb("s_sb1", [128, 16, 128], F32)]
            AL = [(A_, None), sb("A_1", [128, 16, 16], F32)]
            IuL = [(Iu, None), sb("Iu1", [128, 16, 16], U32)]
            IfL = [(If, T_If), sb("If1", [128, 16, 16], F32)]
            candL = [(cand, T_cand), sb("cand1", [128, 8, 256], F32)]
            T_A4L = [T_A4, [S.tile(f"A4b_{i}") for i in range(4)]]
            T_I4L = [T_I4, [S.tile(f"I4b_{i}") for i in range(4)]]
            wkAl = [sb(f"wkA_{i}", [128, 128], F32) for i in range(4)]
            P2A = os.environ.get("KPOOL2A", "pool")
            tile_ctr = [0]

            def gate_tile(t0, c0, n, pb):
                s_sb, T_ssb = ssbL[pb]
                A_ = AL[pb][0]
                Iu = IuL[pb][0]
                If, T_If = IfL[pb]
                cand, T_cand = candL[pb]
                T_A4 = T_A4L[pb]
                T_I4 = T_I4L[pb]
                S.cap = []
                for q4 in range(4):
                    pg, T_pg = next_pGa()
                    for j in range(4):
                        rp = q4 * 4 + j
                        S.op("pe", [T_qry, T_keys], [T_pg], lambda e, pg=pg, j=j, rp=rp: e.matmul(
                            pg[0:n, j * 128:(j + 1) * 128], lhsT=qryT[:, rp, c0:c0 + n], rhs=keys_b[:, rp, :],
                            start=True, stop=True))
                    S.op("act", [T_pg], [T_ssb], lambda e, pg=pg, q4=q4: e.copy(
                        out=s_sb[0:n, q4 * 4:(q4 + 1) * 4, :], in_=pg[0:n, :].rearrange("p (a b) -> p a b", a=4)))
                for rp0 in range(0, 16, 4):
                    rps = list(range(rp0, rp0 + 4))
                    for rp in rps:
                        S.op("dve", [T_ssb], [T_A4[rp % 4]], lambda e, rp=rp: e.max(out=A_[0:n, rp, 0:8], in_=s_sb[0:n, rp, :]))
                    for rp in rps:
                        S.op("dve", [T_ssb, T_A4[rp % 4]], [T_I4[rp % 4]], lambda e, rp=rp: e.max_index(
                            out=Iu[0:n, rp, 0:8], in_max=A_[0:n, rp, 0:8], in_values=s_sb[0:n, rp, :]))
                    for rp in rps:
                        S.op("dve", [T_ssb, T_A4[rp % 4]], [wkAl[rp % 4][1]], lambda e, rp=rp: e.match_replace(
                            out=wkAl[rp % 4][0][0:n, :], in_to_replace=A_[0:n, rp, 0:8], in_values=s_sb[0:n, rp, :], imm_value=-1e30))
                    for rp in rps:
                        S.op("dve", [wkAl[rp % 4][1]], [T_A4[rp % 4]], lambda e, rp=rp: e.max(out=A_[0:n, rp, 8:16], in_=wkAl[rp % 4][0][0:n, :]))
                    for rp in rps:
                        S.op("dve", [wkAl[rp % 4][1], T_A4[rp % 4]], [T_I4[rp % 4]], lambda e, rp=rp: e.max_index(
                            out=Iu[0:n, rp, 8:16], in_max=A_[0:n, rp, 8:16], in_values=wkAl[rp % 4][0][0:n, :]))
                S.op("dve", T_I4, [T_If], lambda e: e.tensor_copy(out=If[0:n], in_=Iu[0:n]))
                A4 = A_[0:n].rearrange("p (r a) k -> p r a k", a=2)
                I4 = If[0:n].rearrange("p (r a) k -> p r a k", a=2)
                S.op(P2A, T_A4, [T_cand], lambda e: e.tensor_tensor(
                    out=cand[0:n].rearrange("p r (a b) -> p r a b", a=16),
                    in0=A4[:, :, 0, :].unsqueeze(3).to_broadcast([n, 8, 16, 16]),
                    in1=A4[:, :, 1, :].unsqueeze(2).to_broadcast([n, 8, 16, 16]), op=ALU.add))
                L1 = S.cap
                S.cap = []
                for r0 in range(0, 8, 4):
                    rs_ = list(range(r0, r0 + 4))
                    for r in rs_:
                        S.op("dve", [T_cand], [T_C4[r % 4]], lambda e, r=r: e.max(out=C_[0:n, r, 0:8], in_=cand[0:n, r, :]))
                    for r in rs_:
                        S.op("dve", [T_cand, T_C4[r % 4]], [T_P4[r % 4]], lambda e, r=r: e.max_index(
                            out=pos[0:n, r, 0:8], in_max=C_[0:n, r, 0:8], in_values=cand[0:n, r, :]))
                    for r in rs_:
                        S.op("dve", [T_cand, T_C4[r % 4]], [T_wk4[r % 4]], lambda e, r=r: e.match_replace(
                            out=wk4[r % 4][0:n, :], in_to_replace=C_[0:n, r, 0:8], in_values=cand[0:n, r, :], imm_value=-1e30))
                    for r in rs_:
                        S.op("dve", [T_wk4[r % 4]], [T_C4[r % 4]], lambda e, r=r: e.max(out=C_[0:n, r, 8:16], in_=wk4[r % 4][0:n, :]))
                    for r in rs_:
                        S.op("dve", [T_wk4[r % 4], T_C4[r % 4]], [T_P4[r % 4]], lambda e, r=r: e.max_index(
                            out=pos[0:n, r, 8:16], in_max=C_[0:n, r, 8:16], in_values=wk4[r % 4][0:n, :]))
                S.op("dve", T_C4, [T_gst], lambda e: e.tensor_scalar(
                    out=gst[0:n, 0:8], in0=C_[0:n, :, 0], scalar1=-1.0, scalar2=None, op0=ALU.mult))
                for r in range(8):
                    S.op("act", T_C4 + [T_gst], [T_E, T_gst], lambda e, r=r: e.activation(
                        out=E_[0:n, r, :], in_=C_[0:n, r, :], func=AF.Exp, bias=gst[0:n, r:r + 1],
                        accum_out=gst[0:n, 8 + r:9 + r]))
                S.op("dve", [T_gst], [T_gst], lambda e: e.reciprocal(out=gst[0:n, 16:24], in_=gst[0:n, 8:16]))
                S.op("dve", [T_E, T_gst], [T_ijw], lambda e: e.tensor_tensor(
                    out=ijw[0:n, 2, :].rearrange("p (r k) -> p r k", r=8), in0=E_[0:n],
                    in1=gst[0:n, 16:24].unsqueeze(2).to_broadcast([n, 8, 16]), op=ALU.mult))
                S.op("dve", T_P4, [T_ku], lambda e: e.tensor_single_scalar(
                    out=ku[0:n, 0, :], in_=pos[0:n].rearrange("p r k -> p (r k)"), scalar=4, op=ALU.logical_shift_right))
                S.op("dve", T_P4, [T_ku], lambda e: e.tensor_single_scalar(
                    out=ku[0:n, 1, :], in_=pos[0:n].rearrange("p r k -> p (r k)"), scalar=15, op=ALU.bitwise_and))
                S.op("dve", [T_ku], [T_kf], lambda e: e.tensor_copy(out=kf[0:n], in_=ku[0:n]))
                for a in range(2):
                    S.op("dve", [T_kf, T_iotar], [T_oh], lambda e, a=a: e.tensor_tensor(
                        out=oh[0:n],
                        in0=kf[0:n, a, :].rearrange("p (r k) -> p r k", r=8).unsqueeze(3).to_broadcast([n, 8, 16, 16]),
                        in1=iota_r[0:n, 0:16].unsqueeze(1).unsqueeze(1).to_broadcast([n, 8, 16, 16]), op=ALU.is_equal))
                    S.op(P2A, [T_oh, T_If], [T_oh], lambda e, a=a: e.tensor_tensor(
                        out=oh[0:n], in0=oh[0:n],
                        in1=I4[:, :, a, :].unsqueeze(2).to_broadcast([n, 8, 16, 16]), op=ALU.mult))
                    S.op("dve", [T_oh], [T_ijw], lambda e, a=a: e.reduce_sum(
                        out=ijw[0:n, a, :].rearrange("p (r k) -> p r k", r=8), in_=oh[0:n], axis=AX.X))
                pg, T_pg = next_pGa()
                for a in range(3):
                    S.op("pe", [T_ijw, T_identf], [T_pg], lambda e, pg=pg, a=a: e.transpose(
                        out=pg[:, a * 128:a * 128 + n], in_=ijw[0:n, a, :], identity=ident_f[0:n, 0:n]))
                (it_, T_it) = ijT[tile_ctr[0] % 2]
                tile_ctr[0] += 1
                S.op("act", [T_pg], [T_it], lambda e, pg=pg, it_=it_: e.copy(
                    out=it_[:, :, 0:n], in_=pg[:, 0:384].rearrange("p (a t) -> p a t", a=3)[:, :, 0:n]))
                S.dma("sp", [T_it], [T_ijwd], lambda e, it_=it_: e.dma_start(
                    out=ijwd[:, :, t0 + c0:t0 + c0 + n], in_=it_[:, :, 0:n]), key=T_it)
                L2 = S.cap
                S.cap = None
                return L1, L2

            def interleave(La, Lb):
                na, nb = len(La), len(Lb)
                ia = ib = 0
                while ia < na or ib < nb:
                    if ib >= nb or (ia < na and ia * nb <= ib * na):
                        S.replay(La[ia])
                        ia += 1
                    else:
                        S.replay(Lb[ib])
                        ib += 1

            prevL2 = []
            xt_i = 0
            for t0 in range(0, NT, TA):
                N = min(TA, NT - t0)
                tiles = [(c0, min(128, N - c0)) for c0 in range(0, N, 128)]
                S.dma("sp", [], [T_h2g], lambda e: e.dma_start(
                    out=h2g[:, :, 0:N], in_=h2Td[:, :, t0:t0 + N].rearrange("k p t -> p k t")))
                for blk in range(16):
                    pg, T_pg = next_pGa()
                    for kc in range(8):
                        S.op("pe", [T_wq, T_h2g], [T_pg], lambda e, pg=pg, kc=kc, blk=blk: e.matmul(
                            pg[:, 0:N], lhsT=wq_b[:, kc, blk * 128:(blk + 1) * 128], rhs=h2g[:, kc, 0:N],
                            start=(kc == 0), stop=(kc == 7)))
                    S.op("act", [T_pg], [T_qry], lambda e, pg=pg, blk=blk: e.copy(out=qryT[:, blk, 0:N], in_=pg[:, 0:N]))
                for (c0, n) in tiles:
                    convert_chunk()
                    convert_chunk()
                    L1, L2 = gate_tile(t0, c0, n, xt_i % 2)
                    xt_i += 1
                    interleave(L1, prevL2)
                    prevL2 = L2
            interleave([], prevL2)
            while conv_i[0] < NCH:
                convert_chunk()
            S.barrier()
            st2a.close()

            st2b = ExitStack()
            st.enter_context(st2b)
            cur[0] = st2b
            Gall = [sb(f"Gall{i}", [128, 128, TG], BF16) for i in range(2)]
            ubuf = [sb(f"ubuf{i}", [128, 8, 256], BF16) for i in range(2)]
            vbuf = [sb(f"vbuf{i}", [128, 2, DM], BF16) for i in range(3)]
            h2g2 = [sb(f"h2g2_{i}", [128, 8, TG], BF16) for i in range(2)]
            ijg = [sb(f"ijg{i}", [128, 3, TG], F32) for i in range(2)]
            P4 = [sb(f"P4_{i}", [128, 4, 128], BF16) for i in range(3)]
            Q4 = [sb(f"Q4_{i}", [128, 4, 128], BF16) for i in range(3)]
            gbuf = [sb(f"gbuf{i}", [128, TG], F32) for i in range(3)]
            cbuf = [sb(f"cbuf{i}", [128, TG], BF16) for i in range(3)]
            x2l = [sb(f"x2_{i}", [128, DM], F32) for i in range(2)]
            junk2, T_junk2 = sb("junk2", [128, DM], BF16)
            st2s, T_st2s = sb("st2s", [128, 8], F32)
            pY = [[ps(f"pY{t}{h}", [128, 512], F32) for h in range(2)] for t in range(2)]
            pA = [ps(f"pA{i}", [128, 512], F32) for i in range(3)]
            pG = [ps(f"pG{i}", [128, 512], F32) for i in range(1)]

            groups = [(t0, min(TG, NT - t0)) for t0 in range(0, NT, TG)]
            MAXG = int(os.environ.get("KGROUPS", "999"))
            groups = groups[:MAXG]
            NG = len(groups)
            MULT_ENG = os.environ.get("KMULT", "pool")

            def gtiles(g):
                N = groups[g][1]
                return [(c0, min(128, N - c0)) for c0 in range(0, N, 128)]

            def load_group(g):
                t0, N = groups[g]
                (hg, T_hg), (ij, T_ij) = h2g2[g % 2], ijg[g % 2]
                S.dma("sp", [], [T_hg], lambda e: e.dma_start(
                    out=hg[:, :, 0:N], in_=h2Td[:, :, t0:t0 + N].rearrange("k p t -> p k t")))
                S.dma("sp", [T_ijwd], [T_ij], lambda e: e.dma_start(out=ij[:, :, 0:N], in_=ijwd[:, :, t0:t0 + N]))

            def p10_dve(g, b):
                (ij, T_ij) = ijg[g % 2]
                (p4, T_p4), (q4_, T_q4) = P4[b % 3], Q4[b % 3]
                for u in range(4):
                    t = b * 4 + u
                    S.op("dve", [T_ij, T_iotab], [T_p4], lambda e, u=u, t=t: e.tensor_scalar(
                        out=p4[:, u, :], in0=iota_b[:], scalar1=ij[:, 0, t:t + 1], scalar2=None, op0=ALU.is_equal))
                    S.op("dve", [T_ij, T_iotab], [T_q4], lambda e, u=u, t=t: e.tensor_scalar(
                        out=q4_[:, u, :], in0=iota_b[:], scalar1=ij[:, 1, t:t + 1], scalar2=ij[:, 2, t:t + 1],
                        op0=ALU.is_equal, op1=ALU.mult))

            def p10_pe(g, b):
                (p4, T_p4), (q4_, T_q4) = P4[b % 3], Q4[b % 3]
                (ga, T_ga) = Gall[g % 2]
                pg, T_pg = pG[0]
                for u in range(4):
                    S.op("pe", [T_p4, T_q4], [T_pg], lambda e, u=u: e.matmul(
                        pg[:, u * 128:(u + 1) * 128], lhsT=q4_[:, u, :], rhs=p4[:, u, :], start=True, stop=True))
                S.op("act", [T_pg], [T_ga], lambda e: e.copy(
                    out=ga[:, :, b * 4:b * 4 + 4], in_=pg[:, :].rearrange("p (t i) -> p i t", t=4)))

            class P10:
                def __init__(self, g):
                    self.g = g
                    self.nb = groups[g][1] // 4
                    self.d = 0
                    self.p = 0

                def step(self):
                    if self.d < self.nb:
                        p10_dve(self.g, self.d)
                        self.d += 1
                        if self.d - self.p >= 3:
                            p10_pe(self.g, self.p)
                            self.p += 1
                    elif self.p < self.nb:
                        p10_pe(self.g, self.p)
                        self.p += 1

                def flush(self):
                    while self.p < self.nb:
                        if self.d < self.nb and self.d - self.p < 3:
                            p10_dve(self.g, self.d)
                            self.d += 1
                        else:
                            p10_pe(self.g, self.p)
                            self.p += 1

            chunk_ctr = [0]

            def load_chunk(ic):
                c = chunk_ctr[0]
                chunk_ctr[0] += 1
                (ub, T_ub), (vb, T_vb) = ubuf[c % 2], vbuf[c % 3]
                S.dma("sp", [T_uscr[ic]], [T_ub], lambda e: e.dma_start(out=ub[:], in_=uscr[ic]))
                S.dma("sp", [T_vscr[ic]], [T_vb], lambda e: e.dma_start(out=vb[:], in_=vscr[ic]))

            def stage_u(g, i):
                N = groups[g][1]
                c = g * NCH + i // 2
                ib = i % 2
                (ub, T_ub) = ubuf[c % 2]
                (hg, T_hg) = h2g2[g % 2]
                (ga, T_ga) = Gall[g % 2]
                gidx = g * 128 + i
                pa, T_pa = pA[gidx % 3]
                gb, T_gb = gbuf[gidx % 3]
                cb_, T_cb = cbuf[gidx % 3]
                for kc in range(8):
                    S.op("pe", [T_ub, T_hg], [T_pa], lambda e, kc=kc: e.matmul(
                        pa[:, 0:N], lhsT=ub[:, kc, ib * 128:(ib + 1) * 128], rhs=hg[:, kc, 0:N],
                        start=(kc == 0), stop=(kc == 7)))
                S.op("act", [T_pa], [T_gb], lambda e: e.activation(out=gb[:, 0:N], in_=pa[:, 0:N], func=AF.Gelu))
                S.op(MULT_ENG, [T_gb, T_ga], [T_cb], lambda e: e.tensor_tensor(
                    out=cb_[:, 0:N], in0=gb[:, 0:N], in1=ga[:, i, 0:N], op=ALU.mult))

            def stage_v(g, i):
                c = g * NCH + i // 2
                ib = i % 2
                (vb, T_vb) = vbuf[c % 3]
                cb_, T_cb = cbuf[(g * 128 + i) % 3]
                for ti, (c0, n) in enumerate(gtiles(g)):
                    for hf in range(2):
                        py, T_py = pY[ti][hf]
                        S.op("pe", [T_cb, T_vb], [T_py], lambda e, py=py, c0=c0, n=n, hf=hf: e.matmul(
                            py[0:n, :], lhsT=cb_[:, c0:c0 + n], rhs=vb[:, ib, hf * 512:(hf + 1) * 512],
                            start=(i == 0), stop=(i == 127)))

            def prefetch_x1(g):
                t0, N = groups[g]
                for ti, (c0, n) in enumerate(gtiles(g)):
                    x2, T_x2 = x2l[ti]
                    tg = t0 + c0
                    S.dma("sp", [], [T_x2], lambda e, tg=tg, n=n, x2=x2: e.dma_start(out=x2[0:n, :], in_=x1d[tg:tg + n, :]))

            def epilogue(g):
                t0, N = groups[g]
                for ti, (c0, n) in enumerate(gtiles(g)):
                    x2, T_x2 = x2l[ti]
                    for hf in range(2):
                        py, T_py = pY[ti][hf]
                        S.op("dve", [T_py, T_x2], [T_x2], lambda e, py=py, hf=hf, n=n, x2=x2: e.tensor_tensor(
                            out=x2[0:n, hf * 512:(hf + 1) * 512], in0=py[0:n, :], in1=x2[0:n, hf * 512:(hf + 1) * 512], op=ALU.add))
                for ti, (c0, n) in enumerate(gtiles(g)):
                    x2, T_x2 = x2l[ti]
                    tg = t0 + c0
                    k0 = ti * 4
                    S.op("act", [T_x2], [T_junk2, T_st2s], lambda e, n=n, x2=x2, k0=k0: e.activation(
                        out=junk2[0:n, :], in_=x2[0:n, :], func=AF.Square, accum_out=st2s[0:n, k0:k0 + 1]))
                    S.op("act", [T_st2s, T_eps], [T_st2s], lambda e, n=n, k0=k0: e.activation(
                        out=st2s[0:n, k0 + 1:k0 + 2], in_=st2s[0:n, k0:k0 + 1], func=AF.Sqrt, scale=1.0 / DM, bias=eps_t[0:n, :]))
                    S.op("dve", [T_st2s], [T_st2s], lambda e, n=n, k0=k0: e.reciprocal(out=st2s[0:n, k0 + 1:k0 + 2], in_=st2s[0:n, k0 + 1:k0 + 2]))
                    S.op("dve", [T_x2, T_st2s, T_c2], [T_x2], lambda e, n=n, x2=x2, k0=k0: e.scalar_tensor_tensor(
                        out=x2[0:n, :], in0=x2[0:n, :], scalar=st2s[0:n, k0 + 1:k0 + 2], in1=gfin_s[0:n, :], op0=ALU.mult, op1=ALU.mult))
                    if tg < n_pseq * SEQ:
                        dst = yp[tg // SEQ][tg % SEQ:tg % SEQ + n, :]
                    else:
                        dst = ys[0:n, :]
                    S.dma("sp", [T_x2], [], lambda e, dst=dst, n=n, x2=x2: e.dma_start(out=dst, in_=x2[0:n, :]))

            load_group(0)
            load_chunk(0)
            pz = P10(0)
            pz.flush()
            seq = [(g, i) for g in range(NG) for i in range(128)]
            nxt = None

            def emit_v(idx):
                g_, i_ = seq[idx]
                stage_v(g_, i_)
                if i_ == 127:
                    epilogue(g_)

            for idx, (g, i) in enumerate(seq):
                if i == 0:
                    nxt = None
                    if g + 1 < NG:
                        load_group(g + 1)
                        nxt = P10(g + 1)
                stage_u(g, i)
                if idx >= 2:
                    emit_v(idx - 2)
                if i % 2 == 0:
                    ic_next = i // 2 + 1
                    if ic_next < NCH:
                        load_chunk(ic_next)
                    elif g + 1 < NG:
                        load_chunk(0)
                if nxt is not None and (i % 2 == 1 or i in (8, 16, 24, 32)):
                    nxt.step()
                if i == 64:
                    prefetch_x1(g)
                if i == 127 and nxt is not None:
                    nxt.flush()
            emit_v(len(seq) - 2)
            emit_v(len(seq) - 1)

        S.barrier()
        S.finish("sp")
        print("ops per engine:", S.nops, "sems:", S.nsem)
    return nc


def _prep_shared(inp):
    f = lambda a: np.ascontiguousarray(np.asarray(a, dtype=np.float32))
    sh = {}
    sh["w_in"] = f(inp["w_in"][0])
    sh["gmix"] = f(inp["g_mix"][0].reshape(8, 128).T)
    sh["convw"] = f(inp["conv_w"][0].reshape(31, 4, 128).transpose(2, 1, 0))
    sh["cvec"] = f(np.concatenate([inp["conv_b"][0].reshape(4, 128).T, inp["conv_ln_g"][0].reshape(4, 128).T,
                                   inp["conv_ln_b"][0].reshape(4, 128).T], axis=1))
    sh["lam"] = f(np.stack([inp["lambda_q1"][0], inp["lambda_k1"][0], inp["lambda_q2"][0], inp["lambda_k2"][0]]).reshape(1, 256))
    sh["subg"] = f(inp["subln_g"][0].reshape(1, 128))
    sh["relb"] = f(inp["rel_bias"].reshape(1, 128))
    sh["w_out"] = f(inp["w_out"][0])
    sh["gffn"] = f(inp["g_ffn"][0].reshape(8, 128).T)
    sh["wq"] = f(inp["w_query"][0])
    sh["keysT"] = f(inp["sub_keys"][0].reshape(16, 128, 128).transpose(0, 2, 1))
    sh["uT"] = f(inp["peer_u"][0].T)
    sh["pv"] = f(inp["peer_v"][0])
    sh["gfin"] = f(inp["g_final"].reshape(1, DM))
    sh["bkc"] = _bucket_tiles()
    return sh


def kernel(**inp):
    f = lambda a: np.ascontiguousarray(np.asarray(a, dtype=np.float32))
    sh = _prep_shared(inp)
    nc = build_program(2)
    in_maps = []
    for c in range(NCORES):
        m = dict(sh)
        m["xp"] = f(inp["x_prompt"][2 * c:2 * c + 2])
        m["xs"] = f(inp["x_sample"][c])
        m["ckT"] = f(np.asarray(inp["cache_k"][0, c]).reshape(SEQ, 4, 128).transpose(1, 2, 0))
        m["cv"] = f(np.asarray(inp["cache_v"][0, c]).reshape(SEQ, 512))
        m["scT"] = f(np.asarray(inp["state_conv"][0, c]).reshape(30, 4, 128).transpose(2, 1, 0))
        in_maps.append(m)
    res = run_bass_kernel_spmd(nc, in_maps, core_ids=list(range(NCORES)))
    R = res.results
    y_prompt = np.concatenate([r["yp"] for r in R], axis=0)
    y_sample = np.stack([r["ys"] for r in R], axis=0)
    k_prompt = np.concatenate([r["kp"] for r in R], axis=0).reshape(1, 16, SEQ, 4, 2, 64)
    v_prompt = np.concatenate([r["vp"] for r in R], axis=0).reshape(1, 16, SEQ, 4, 128)
    c_prompt = np.concatenate([r["cp"] for r in R], axis=0).reshape(1, 16, 30, 512)
    k_sample = np.stack([r["ks"] for r in R], axis=0).reshape(1, 8, 64, 4, 2, 64)
    v_sample = np.stack([r["vs"] for r in R], axis=0).reshape(1, 8, 64, 4, 128)
    c_sample = np.stack([r["cs"] for r in R], axis=0).reshape(1, 8, 30, 512)
    return (y_prompt, y_sample, k_prompt, v_prompt, c_prompt, k_sample, v_sample, c_sample)
```

```python
import math
import os
from contextlib import ExitStack

import numpy as np
import concourse.bass as bass
import concourse.mybir as mybir
from concourse.bass_utils import run_bass_kernel_spmd

F32 = mybir.dt.float32
BF16 = mybir.dt.bfloat16
U32 = mybir.dt.uint32
AF = mybir.ActivationFunctionType
ALU = mybir.AluOpType
AX = mybir.AxisListType

EPS = 1e-6
LAM_INIT = 0.8 - 0.6 * math.exp(-0.3 * 0)
NCORES = 8
SEQ = 2048
DM = 1024
NEXP_SIDE = 128

SEM_LIMIT = 30000


class Counter:
    def __init__(self, S, name):
        self.S = S
        self.name = name
        self.epoch = 0
        self.val = 0
        self.sem = S.new_sem(f"{name}_e0")

    def bump(self, inc):
        if self.val + inc > SEM_LIMIT:
            self.epoch += 1
            self.val = 0
            self.sem = self.S.new_sem(f"{self.name}_e{self.epoch}")
        self.val += inc
        return (self.sem, self.val, self.name, self.epoch)


class Tile:
    __slots__ = ("name", "w", "r", "dmac")

    def __init__(self, name):
        self.name = name
        self.w = None
        self.r = []
        self.dmac = None


class Sched:
    def __init__(self, nc, stack):
        self.nc = nc
        self.stack = stack
        self.nsem = 0
        self.engs = {"pe": nc.tensor, "act": nc.scalar, "dve": nc.vector,
                     "pool": nc.gpsimd, "sp": nc.sync}
        self.cnt = {k: Counter(self, k) for k in self.engs}
        self.known = {k: {} for k in self.engs}
        self.nops = {k: 0 for k in self.engs}
        self.tiles = []
        self.cap = None
        self.snaps = {}

    def new_sem(self, name):
        self.nsem += 1
        return self.stack.enter_context(self.nc.semaphore(f"s{self.nsem}_{name}"))

    def tile(self, name):
        t = Tile(name)
        self.tiles.append(t)
        return t

    def _wait(self, e, ev):
        sem, val, name, epoch = ev
        key = (name, epoch)
        if self.known[e].get(key, 0) >= val:
            return
        self.known[e][key] = val
        self.engs[e].wait_ge(sem, val)
        snap = self.snaps.get((name, epoch, val))
        if snap:
            ke = self.known[e]
            for k2, v2 in snap.items():
                if ke.get(k2, 0) < v2:
                    ke[k2] = v2

    def _deps(self, reads, writes):
        evs = []
        for t in reads:
            if t.w is not None:
                evs.append(t.w)
        for t in writes:
            if t.w is not None:
                evs.append(t.w)
            evs.extend(t.r)
        return evs

    def op(self, e, reads, writes, fn):
        if self.cap is not None:
            self.cap.append(("op", e, reads, writes, fn, None))
            return None
        return self._op(e, reads, writes, fn)

    def dma(self, q, reads, writes, fn, key=None):
        if self.cap is not None:
            self.cap.append(("dma", q, reads, writes, fn, key))
            return None
        return self._dma(q, reads, writes, fn, key)

    def replay(self, item):
        kind, e, reads, writes, fn, key = item
        if kind == "op":
            return self._op(e, reads, writes, fn)
        return self._dma(e, reads, writes, fn, key)

    def interleave(self, La, Lb):
        na, nb = len(La), len(Lb)
        ia = ib = 0
        while ia < na or ib < nb:
            if ib >= nb or (ia < na and ia * nb <= ib * na):
                self.replay(La[ia])
                ia += 1
            else:
                self.replay(Lb[ib])
                ib += 1

    def _op(self, e, reads, writes, fn):
        for ev in self._deps(reads, writes):
            if ev[2] == e:
                if e == "pe":
                    continue
                if ev[3] == self.cnt[e].epoch and self.cnt[e].val - ev[1] >= 2:
                    continue
            self._wait(e, ev)
        ins = fn(self.engs[e])
        ev = self.cnt[e].bump(1)
        ins.then_inc(ev[0], 1)
        self.snaps[(ev[2], ev[3], ev[1])] = dict(self.known[e])
        self.nops[e] += 1
        self._mark(ev, reads, writes)
        return ev

    def _mark(self, ev, reads, writes):
        k = (ev[2], ev[3])
        for t in reads:
            t.r = [x for x in t.r if (x[2], x[3]) != k]
            t.r.append(ev)
        for t in writes:
            t.w = ev
            t.r = []

    def _dma(self, q, reads, writes, fn, key=None):
        kt = key or (writes[0] if writes else reads[0])
        if kt.dmac is None:
            kt.dmac = Counter(self, "d_" + kt.name)
        for ev in self._deps(reads, writes):
            self._wait(q, ev)
        ins = fn(self.engs[q])
        ev = kt.dmac.bump(16)
        ins.then_inc(ev[0], 16)
        self.snaps[(ev[2], ev[3], ev[1])] = dict(self.known[q])
        self.nops[q] += 1
        self._mark(ev, reads, writes)
        return ev

    def _all_events(self):
        evs = {}
        for t in self.tiles:
            for ev in ([t.w] if t.w else []) + t.r:
                k = (ev[2], ev[3])
                if k not in evs or evs[k][1] < ev[1]:
                    evs[k] = ev
        return evs

    def barrier(self):
        evs = self._all_events()
        for e in self.engs:
            for ev in evs.values():
                self._wait(e, ev)
        for t in self.tiles:
            t.w = None
            t.r = []

    def finish(self, e="sp"):
        for ev in self._all_events().values():
            self._wait(e, ev)


def _bucket_np(rel):
    nb = 16
    max_exact = 8
    ret = np.where(rel > 0, nb, 0)
    n = np.abs(rel)
    nf = np.maximum(n, 1).astype(np.float32)
    large = max_exact + (np.log(nf / max_exact) / math.log(128 / max_exact) * (nb - max_exact)).astype(np.int32)
    large = np.minimum(large, nb - 1)
    return ret + np.where(n < max_exact, n, large)


def _bucket_tiles():
    k = np.arange(128)[:, None]
    q = np.arange(128)[None, :]
    b0 = _bucket_np(k - q).astype(np.float32)
    masked = (k // 64) > (q // 64)
    b0 = np.where(masked, 32.0, b0)
    b1 = _bucket_np(k - q - 128).astype(np.float32)
    return np.stack([b0, b1], axis=1).astype(np.float32)


def build_program(n_pseq=2, with_peer=True, dbg=False):
    nc = bass.Bass("TRN2", target_bir_lowering=False)
    NT = n_pseq * SEQ + 64

    def din(name, shape, dt=F32):
        return nc.dram_tensor(name, list(shape), dt, kind="ExternalInput").ap()

    def dout(name, shape, dt=F32):
        return nc.dram_tensor(name, list(shape), dt, kind="ExternalOutput").ap()

    xp = din("xp", [n_pseq, SEQ, DM])
    xs = din("xs", [64, DM])
    ckT = din("ckT", [4, 128, SEQ])
    cv = din("cv", [SEQ, 512])
    scT = din("scT", [128, 4, 30])
    w_in = din("w_in", [DM, 2560])
    gmix = din("gmix", [128, 8])
    convw = din("convw", [128, 4, 31])
    cvec = din("cvec", [128, 12])
    lam = din("lam", [1, 256])
    subg = din("subg", [1, 128])
    relb = din("relb", [1, 128])
    w_out = din("w_out", [DM, DM])
    gffn = din("gffn", [128, 8])
    wq = din("wq", [DM, 2048])
    keysT = din("keysT", [16, 128, 128])
    uT = din("uT", [DM, 16384])
    pv = din("pv", [16384, DM])
    gfin = din("gfin", [1, DM])
    bkc = din("bkc", [128, 2, 128])

    yp = dout("yp", [n_pseq, SEQ, DM])
    ys = dout("ys", [64, DM])
    kp = dout("kp", [n_pseq, SEQ, 512])
    vp = dout("vp", [n_pseq, SEQ, 512])
    cp = dout("cp", [n_pseq, 30, 512])
    ks = dout("ks", [64, 512])
    vs = dout("vs", [64, 512])
    cs = dout("cs", [30, 512])

    kind_scr = "ExternalOutput" if dbg else "Internal"
    x1d = nc.dram_tensor("x1d", [NT, DM], F32, kind=kind_scr).ap()
    h2Td = nc.dram_tensor("h2Td", [8, 128, NT], BF16, kind="Internal").ap()

    with ExitStack() as st:
        S = Sched(nc, st)

        cur = [st]

        def sb(name, shape, dt):
            return cur[0].enter_context(nc.sbuf_tensor(name, list(shape), dt)), S.tile(name)

        def ps(name, shape, dt):
            return cur[0].enter_context(nc.psum_tensor(name, list(shape), dt)), S.tile(name)

        ident_f, T_identf = sb("ident_f", [128, 128], F32)
        ident_b, T_identb = sb("ident_b", [128, 128], BF16)
        ones_b, T_ones = sb("ones_b", [128, 128], BF16)
        iota_t, T_iota = sb("iota_t", [128, 128], F32)
        T_const = S.tile("consts")

        S.op("pool", [], [T_iota], lambda e: e.iota(iota_t[:], pattern=[[1, 128]], base=0, channel_multiplier=-1,
                                                    allow_small_or_imprecise_dtypes=True))
        S.op("dve", [T_iota], [T_identf], lambda e: e.tensor_scalar(out=ident_f[:], in0=iota_t[:], scalar1=0.0,
                                                                     scalar2=None, op0=ALU.is_equal))
        S.op("dve", [T_identf], [T_identb], lambda e: e.tensor_copy(out=ident_b[:], in_=ident_f[:]))
        S.op("pool", [], [T_ones], lambda e: e.memset(ones_b[:], 1.0))

        eps_t, T_eps = sb("eps_t", [128, 1], F32)
        S.op("pool", [], [T_eps], lambda e: e.memset(eps_t[:], EPS))
        EPS_AP = eps_t
        iota_r, T_iotar = sb("iota_r", [128, 128], F32)
        S.op("pool", [], [T_iotar], lambda e: e.iota(iota_r[:], pattern=[[1, 128]], base=0, channel_multiplier=0,
                                                     allow_small_or_imprecise_dtypes=True))
        st1 = ExitStack()
        cur[0] = st1
        w_in_b, T_win = sb("w_in_b", [128, 8, 2560], BF16)
        w_out_b, T_wout = sb("w_out_b", [128, 8, DM], BF16)
        diag, T_diag = sb("diag", [128, 124, 128], BF16)
        gmix_s, _ = sb("gmix_s", [128, 8], F32)
        convw_s, _ = sb("convw_s", [128, 4, 31], F32)
        cvec_s, _ = sb("cvec_s", [128, 12], F32)
        lam_s, _ = sb("lam_s", [128, 256], F32)
        gsub_s, _ = sb("gsub_s", [128, 128], F32)
        relb_s, _ = sb("relb_s", [128, 128], F32)
        bk_s, _ = sb("bk_s", [128, 2, 128], F32)
        Tb, T_Tb = sb("Tb", [128, 4, 2, 128], F32)
        eqm, T_eqm = sb("eqm", [128, 2, 128], F32)
        small, T_small = sb("small", [128, 16], F32)
        stage0, T_st0 = sb("stage0", [128, 1024], F32)
        stage1, T_st1 = sb("stage1", [128, 1024], F32)

        for dst, src in ((gmix_s, gmix), (convw_s, convw), (cvec_s, cvec), (bk_s, bkc)):
            S.dma("sp", [], [T_const], lambda e, d=dst, s_=src: e.dma_start(out=d[:], in_=s_))
        for dst, src, n in ((lam_s, lam, 256), (gsub_s, subg, 128), (relb_s, relb, 128)):
            S.dma("sp", [], [T_const], lambda e, d=dst, s_=src, n=n: e.dma_start(out=d[:], in_=s_.to_broadcast([128, n])))

        stg = [(stage0, T_st0), (stage1, T_st1)]
        i = 0
        for kc in range(8):
            for (a0, a1) in ((0, 1024), (1024, 2048), (2048, 2560)):
                stt, T_s = stg[i % 2]
                i += 1
                S.dma("sp", [], [T_s], lambda e, stt=stt, kc=kc, a0=a0, a1=a1: e.dma_start(
                    out=stt[:, 0:a1 - a0], in_=w_in[kc * 128:(kc + 1) * 128, a0:a1]))
                S.op("dve", [T_s, T_const], [T_win], lambda e, stt=stt, kc=kc, a0=a0, a1=a1: e.tensor_scalar(
                    out=w_in_b[:, kc, a0:a1], in0=stt[:, 0:a1 - a0], scalar1=gmix_s[:, kc:kc + 1],
                    scalar2=None, op0=ALU.mult))
        for kc in range(8):
            stt, T_s = stg[i % 2]
            i += 1
            S.dma("sp", [], [T_s], lambda e, stt=stt, kc=kc: e.dma_start(
                out=stt[:, 0:DM], in_=w_out[kc * 128:(kc + 1) * 128, :]))
            S.op("act", [T_s], [T_wout], lambda e, stt=stt, kc=kc: e.copy(out=w_out_b[:, kc, :], in_=stt[:, 0:DM]))
        for w in range(31):
            for cb in range(4):
                S.op("dve", [T_const, T_identf], [T_diag], lambda e, w=w, cb=cb: e.tensor_scalar(
                    out=diag[:, w * 4 + cb, :], in0=ident_f[:], scalar1=convw_s[:, cb, w:w + 1], scalar2=None,
                    op0=ALU.mult))
        S.op("dve", [T_const], [T_eqm], lambda e: e.tensor_scalar(
            out=eqm[:], in0=bk_s[:], scalar1=32.0, scalar2=-30000.0, op0=ALU.is_equal, op1=ALU.mult))
        for h in range(4):
            S.op("dve", [T_eqm], [T_Tb], lambda e, h=h: e.tensor_copy(out=Tb[:, h, :, :], in_=eqm[:]))
        for b in range(32):
            S.op("dve", [T_const], [T_eqm], lambda e, b=b: e.tensor_scalar(
                out=eqm[:], in0=bk_s[:], scalar1=float(b), scalar2=None, op0=ALU.is_equal))
            for h in range(4):
                S.op("dve", [T_eqm, T_const, T_Tb], [T_Tb], lambda e, b=b, h=h: e.scalar_tensor_tensor(
                    out=Tb[:, h, :, :], in0=eqm[:], scalar=relb_s[:, b * 4 + h:b * 4 + h + 1], in1=Tb[:, h, :, :],
                    op0=ALU.mult, op1=ALU.add))
        S.op("dve", [T_const], [T_eqm], lambda e: e.tensor_tensor(
            out=eqm[:, 0, :].rearrange("p (a b) -> p a b", a=2), in0=lam_s[:].rearrange("p (a b c) -> p a b c", a=2, b=2)[:, :, 0, :],
            in1=lam_s[:].rearrange("p (a b c) -> p a b c", a=2, b=2)[:, :, 1, :], op=ALU.mult))
        S.op("dve", [T_eqm], [T_small], lambda e: e.reduce_sum(
            out=small[:, 0:2], in_=eqm[:, 0, :].rearrange("p (a b) -> p a b", a=2), axis=AX.X))
        S.op("act", [T_small], [T_small], lambda e: e.activation(out=small[:, 2:4], in_=small[:, 0:2], func=AF.Exp))
        S.op("dve", [T_small], [T_small], lambda e: e.tensor_tensor(
            out=small[:, 4:5], in0=small[:, 3:4], in1=small[:, 2:3], op=ALU.subtract))
        S.op("dve", [T_small], [T_small], lambda e: e.tensor_scalar(
            out=small[:, 4:5], in0=small[:, 4:5], scalar1=-LAM_INIT, scalar2=None, op0=ALU.add))
        S.op("dve", [T_const], [T_const], lambda e: e.tensor_scalar(
            out=gsub_s[:], in0=gsub_s[:], scalar1=1.0 - LAM_INIT, scalar2=None, op0=ALU.mult))
        neg_lam = small[:, 4:5]

        fT, T_fT = sb("fT", [128, 8, 512], BF16)
        T_fTk = [S.tile(f"fT_k{i}") for i in range(4)]
        xt, T_xt = sb("xt", [128, DM], F32)
        xtl = [(xt, T_xt), sb("xt1", [128, DM], F32)]
        hb, T_hb = sb("hb", [128, DM], BF16)
        stat, T_stat = sb("stat", [128, 8], F32)
        aT, T_aT = sb("aT", [128, 4, 30 + 512], BF16)
        sig, T_sig = sb("sig", [128, 512], F32)
        a32, T_a32 = sb("a32", [128, 512], F32)
        qT, T_qT = sb("qT", [128, 4, 512], BF16)
        kT, T_kT = sb("kT", [128, 4, SEQ + 64], BF16)
        vaug, T_v = sb("vaug", [128, 17, 4, 130], BF16)
        catT, T_cat = sb("catT", [128, 8, 512], BF16)
        zq, T_zq = sb("zq", [128, 512], BF16)
        zk32, T_zk32 = sb("zk32", [128, 512], F32)
        zkb, T_zkb = sb("zkb", [128, 512], BF16)
        zv32, T_zv32 = sb("zv32", [128, 512], F32)
        PTT = [sb(f"PT{i}", [128, 512], BF16) for i in range(3)]
        TMP = [sb(f"tmpb{i}", [128, 128], F32) for i in range(3)]
        att, T_att = sb("att", [128, 128], F32)
        sqj, T_sq = sb("sqj", [128, 128], F32)
        attb, T_attb = sb("attb", [128, 512], BF16)
        astat, T_astat = sb("astat", [128, 8], F32)
        y32, T_y32 = sb("y32", [128, 4, 512], F32)
        ybf, T_ybf = sb("ybf", [128, 4, 512], BF16)
        ysq, T_ysq = sb("ysq", [128, 4, 512], BF16)
        junk, T_junk = ysq[:].rearrange("p a b -> p (a b)")[:, 0:DM], T_ysq
        mu, T_mu = sig, T_sig
        rs, T_rs = a32, T_a32
        ctail, T_ctail = zk32, T_zk32
        cst32, T_cst32 = sb("cst32", [128, 4, 30], F32)

        pT, T_pT = ps("pT", [128, 1024], BF16)
        pM = [ps(f"pM{i}", [128, 512], F32) for i in range(2)]
        pSS = [ps(f"pS{i}", [128, 512], F32) for i in range(2)]
        pO, T_pO = ps("pO", [128, 2, 256], F32)
        pX = [ps(f"pX{i}", [128, 512], F32) for i in range(2)]
        pm_i = [0]
        pTA = (pX[0][0][:].bitcast(BF16), pX[0][1])
        pOO = [(pO, T_pO), (pM[0][0][:].rearrange("p (a b) -> p a b", a=2), pM[0][1])]
        pSS = pSS + [pX[1]]

        def next_pM():
            pm_i[0] += 1
            return pM[pm_i[0] % 2]

        S.op("pool", [], [T_v], lambda e: e.memset(vaug[:], 1.0))

        def rms_to_bf16(n, src, T_src, dst_b, T_dst, col):
            S.op("act", [T_src], [T_junk, T_stat], lambda e: e.activation(
                out=junk[0:n, :], in_=src[0:n, :], func=AF.Square, accum_out=stat[0:n, col:col + 1]))
            S.op("act", [T_stat], [T_stat], lambda e: e.activation(
                out=stat[0:n, col + 1:col + 2], in_=stat[0:n, col:col + 1], func=AF.Sqrt, scale=1.0 / DM, bias=EPS_AP[0:n, :]))
            S.op("dve", [T_stat], [T_stat], lambda e: e.reciprocal(
                out=stat[0:n, col + 1:col + 2], in_=stat[0:n, col + 1:col + 2]))
            S.op("dve", [T_src, T_stat], [T_dst], lambda e: e.tensor_scalar(
                out=dst_b[0:n, :], in0=src[0:n, :], scalar1=stat[0:n, col + 1:col + 2], scalar2=None, op0=ALU.mult))

        def transpose_to_fT(n, src_b, T_src, c0, T_dst=None, pbank=None):
            pT_, T_pT_ = pbank if pbank is not None else (pT, T_pT)
            for kc in range(8):
                S.op("pe", [T_src, T_identb], [T_pT_], lambda e, kc=kc: e.transpose(
                    out=pT_[:, kc * 128:kc * 128 + n], in_=src_b[0:n, kc * 128:(kc + 1) * 128], identity=ident_b[0:n, 0:n]))
            S.op("act", [T_pT_], [T_dst or T_fT], lambda e: e.copy(
                out=fT[:, :, c0:c0 + n], in_=pT_[:, :].rearrange("p (k c) -> p k c", k=8)[:, :, 0:n]))

        seqs = []
        for s_ in range(n_pseq):
            seqs.append(("p", xp[s_], SEQ, kp[s_], vp[s_], cp[s_], s_ * SEQ))
        seqs.append(("s", xs, 64, ks, vs, cs, n_pseq * SEQ))

        S.barrier()
        import os
        STOP = int(os.environ.get("KSTOP", "99"))
        if STOP <= 0:
            seqs = []

        for (kind, xd, ntok, kd, vd, cd, tok0) in seqs:
            past = SEQ if kind == "s" else 0
            if kind == "p":
                S.op("pool", [], [T_aT], lambda e: e.memset(aT[:, :, 0:30], 0.0))
            else:
                S.dma("sp", [], [T_cst32], lambda e: e.dma_start(out=cst32[:], in_=scT))
                S.op("dve", [T_cst32], [T_aT], lambda e: e.tensor_copy(out=aT[:, :, 0:30], in_=cst32[:]))
                for h in range(4):
                    for hf in range(2):
                        stt, T_s = stg[(h * 2 + hf) % 2]
                        S.dma("sp", [], [T_s], lambda e, stt=stt, h=h, hf=hf: e.dma_start(
                            out=stt[:, 0:1024], in_=ckT[h, :, hf * 1024:(hf + 1) * 1024]))
                        S.op("act", [T_s], [T_kT], lambda e, stt=stt, h=h, hf=hf: e.copy(
                            out=kT[:, h, hf * 1024:(hf + 1) * 1024], in_=stt[:, 0:1024]))
                for blk in range(16):
                    stt, T_s = stg[blk % 2]
                    S.dma("sp", [], [T_s], lambda e, stt=stt, blk=blk: e.dma_start(
                        out=stt[:, 0:512], in_=cv[blk * 128:(blk + 1) * 128, :]))
                    S.op("dve", [T_s], [T_v], lambda e, stt=stt, blk=blk: e.tensor_copy(
                        out=vaug[:, blk, :, 0:128], in_=stt[:, 0:512].rearrange("p (h e) -> p h e", h=4)))

            ngroups = (ntok + 511) // 512
            for g in range(ngroups):
                g0 = g * 512
                N = min(512, ntok - g0)
                tiles = [(c0, min(128, N - c0)) for c0 in range(0, N, 128)]
                last_group = (g == ngroups - 1)

                LA = []
                for k_, (c0, n) in enumerate(tiles):
                    S.cap = []
                    xa, T_xa = xtl[k_ % 2]
                    S.dma("sp", [], [T_xa], lambda e, c0=c0, n=n, xa=xa: e.dma_start(out=xa[0:n, :], in_=xd[g0 + c0:g0 + c0 + n, :]))
                    rms_to_bf16(n, xa, T_xa, hb, T_hb, 0)
                    transpose_to_fT(n, hb, T_hb, c0, T_fTk[k_], pbank=pTA)
                    LA.append(S.cap)
                    S.cap = None

                if STOP <= 1:
                    continue
                LB = []
                for k_, (c0, n) in enumerate(tiles):
                    S.cap = []
                    LB.append(S.cap)
                    blk = (past + g0 + c0) // 128
                    kcol = past + g0 + c0
                    for j in range(int(os.environ.get('KJ', '3'))):
                        pm, T_pm = next_pM()
                        for kc in range(8):
                            S.op("pe", [T_fTk[k_], T_win], [T_pm], lambda e, pm=pm, kc=kc, j=j, c0=c0, n=n: e.matmul(
                                pm[0:n, :], lhsT=fT[:, kc, c0:c0 + n], rhs=w_in_b[:, kc, 1024 + j * 512:1024 + (j + 1) * 512],
                                start=(kc == 0), stop=(kc == 7)))
                        if j == 0:
                            S.op("act", [T_pm], [T_zq], lambda e, pm=pm, n=n: e.activation(
                                out=zq[0:n, :], in_=pm[0:n, :], func=AF.Copy, scale=0.125))
                            for h in range(4):
                                S.op("pe", [T_zq, T_identb], [T_pT], lambda e, h=h, n=n: e.transpose(
                                    out=pT[:, h * 128:h * 128 + n], in_=zq[0:n, h * 128:(h + 1) * 128], identity=ident_b[0:n, 0:n]))
                            S.op("dve", [T_pT], [T_qT], lambda e, c0=c0, n=n: e.tensor_copy(
                                out=qT[:, :, c0:c0 + n], in_=pT[:, 0:512].rearrange("p (k c) -> p k c", k=4)[:, :, 0:n]))
                        elif j == 1:
                            if not os.environ.get("K1A"):
                                S.op("dve", [T_pm], [T_zk32], lambda e, pm=pm, n=n: e.tensor_copy(out=zk32[0:n, :], in_=pm[0:n, :]))
                            S.op("act", [T_zk32], [T_zkb], lambda e, pm=pm, n=n: e.copy(out=zkb[0:n, :], in_=zk32[0:n, :]))
                            if not os.environ.get("NOKD"):
                                S.dma("sp", [T_zk32], [], lambda e, c0=c0, n=n: e.dma_start(
                                    out=kd[g0 + c0:g0 + c0 + n, :], in_=zk32[0:n, :]))
                            for h in range(0 if os.environ.get("K1B") else 4):
                                S.op("pe", [T_zkb, T_identb], [T_pT], lambda e, h=h, n=n: e.transpose(
                                    out=pT[:, 512 + h * 128:512 + h * 128 + n], in_=zkb[0:n, h * 128:(h + 1) * 128],
                                    identity=ident_b[0:n, 0:n]))
                            if not os.environ.get("K1C"):
                              S.op("dve", [T_pT], [T_kT], lambda e, kcol=kcol, n=n: e.tensor_copy(
                                out=kT[:, :, kcol:kcol + n], in_=pT[:, 512:1024].rearrange("p (k c) -> p k c", k=4)[:, :, 0:n]))
                        else:
                            S.op("dve", [T_pm], [T_zv32], lambda e, pm=pm, n=n: e.tensor_copy(out=zv32[0:n, :], in_=pm[0:n, :]))
                            S.op("act", [T_zv32], [T_v], lambda e, pm=pm, n=n, blk=blk: e.copy(
                                out=vaug[0:n, blk, :, 0:128], in_=zv32[0:n, :].rearrange("p (h e) -> p h e", h=4)))
                            S.dma("sp", [T_zv32], [], lambda e, c0=c0, n=n: e.dma_start(
                                out=vd[g0 + c0:g0 + c0 + n, :], in_=zv32[0:n, :]))

                S.cap = None
                S.interleave(LA[0], [])
                for k_ in range(len(tiles)):
                    S.interleave(LB[k_], LA[k_ + 1] if k_ + 1 < len(tiles) else [])
                if STOP <= 2:
                    continue
                for cb in range(4):
                    pa, T_pa = next_pM()
                    pg, T_pg = next_pM()
                    for kc in range(8):
                        S.op("pe", T_fTk + [T_win], [T_pa], lambda e, pa=pa, kc=kc, cb=cb: e.matmul(
                            pa[:, 0:N], lhsT=w_in_b[:, kc, cb * 128:(cb + 1) * 128], rhs=fT[:, kc, 0:N],
                            start=(kc == 0), stop=(kc == 7)))
                    for kc in range(8):
                        S.op("pe", T_fTk + [T_win], [T_pg], lambda e, pg=pg, kc=kc, cb=cb: e.matmul(
                            pg[:, 0:N], lhsT=w_in_b[:, kc, 512 + cb * 128:512 + (cb + 1) * 128], rhs=fT[:, kc, 0:N],
                            start=(kc == 0), stop=(kc == 7)))
                    S.op("act", [T_pg], [T_sig], lambda e, pg=pg: e.activation(out=sig[:, 0:N], in_=pg[:, 0:N], func=AF.Sigmoid))
                    S.op("dve", [T_pa, T_sig], [T_a32], lambda e, pa=pa: e.tensor_tensor(
                        out=a32[:, 0:N], in0=pa[:, 0:N], in1=sig[:, 0:N], op=ALU.mult))
                    S.op("pool", [T_a32], [T_aT], lambda e, cb=cb: e.tensor_copy(out=aT[:, cb, 30:30 + N], in_=a32[:, 0:N]))
                    if last_group:
                        pm, T_pm = pX[0]
                        S.op("pe", [T_a32, T_identf], [T_pm], lambda e, pm=pm, cb=cb: e.transpose(
                            out=pm[0:30, cb * 128:(cb + 1) * 128], in_=a32[:, N - 30:N], identity=ident_f[:]))
                if last_group:
                    pm, T_pm = pX[0]
                    S.op("act", [T_pm], [T_ctail], lambda e, pm=pm: e.copy(out=ctail[0:30, :], in_=pm[0:30, :]))
                    S.dma("sp", [T_ctail], [], lambda e: e.dma_start(out=cd, in_=ctail[0:30, :]))

                if STOP <= 3:
                    continue
                S.cap = []
                for cb in range(4):
                    pm, T_pm = pM[1]
                    for w in range(31):
                        S.op("pe", [T_aT, T_diag], [T_pm], lambda e, pm=pm, w=w, cb=cb: e.matmul(
                            pm[:, 0:N], lhsT=diag[:, w * 4 + cb, :], rhs=aT[:, cb, w:w + N], start=(w == 0), stop=(w == 30)))
                    S.op("act", [T_pm, T_const], [T_y32], lambda e, pm=pm, cb=cb: e.activation(
                        out=y32[:, cb, 0:N], in_=pm[:, 0:N], func=AF.Identity, bias=cvec_s[:, cb:cb + 1]))
                    S.op("act", [T_pm, T_const], [T_ysq], lambda e, pm=pm, cb=cb: e.activation(
                        out=ysq[:, cb, 0:N], in_=pm[:, 0:N], func=AF.Square, bias=cvec_s[:, cb:cb + 1]))
                    S.op("pool", [T_y32], [T_ybf], lambda e, cb=cb: e.tensor_copy(out=ybf[:, cb, 0:N], in_=y32[:, cb, 0:N]))
                p1, T_p1 = pX[0]
                p2, T_p2 = pX[0]
                for cb in range(4):
                    S.op("pe", [T_ybf, T_ones], [T_p1], lambda e, cb=cb: e.matmul(
                        p1[:, 0:N], lhsT=ones_b[:], rhs=ybf[:, cb, 0:N], start=(cb == 0), stop=(cb == 3)))
                S.op("dve", [T_p1], [T_mu], lambda e: e.tensor_scalar(
                    out=mu[:, 0:N], in0=p1[:, 0:N], scalar1=1.0 / 512, scalar2=None, op0=ALU.mult))
                for cb in range(4):
                    S.op("pe", [T_ysq, T_ones], [T_p2], lambda e, cb=cb: e.matmul(
                        p2[:, 0:N], lhsT=ones_b[:], rhs=ysq[:, cb, 0:N], start=(cb == 0), stop=(cb == 3)))
                S.op("dve", [T_mu], [T_rs], lambda e: e.tensor_tensor(out=rs[:, 0:N], in0=mu[:, 0:N], in1=mu[:, 0:N], op=ALU.mult))
                S.op("dve", [T_p2, T_rs], [T_rs], lambda e: e.scalar_tensor_tensor(
                    out=rs[:, 0:N], in0=p2[:, 0:N], scalar=1.0 / 512, in1=rs[:, 0:N], op0=ALU.mult, op1=ALU.subtract))
                S.op("act", [T_rs, T_eps], [T_rs], lambda e: e.activation(
                    out=rs[:, 0:N], in_=rs[:, 0:N], func=AF.Sqrt, bias=eps_t[:, :]))
                S.op("dve", [T_rs], [T_rs], lambda e: e.reciprocal(out=rs[:, 0:N], in_=rs[:, 0:N]))
                for cb in range(4):
                    S.op("dve", [T_y32, T_mu], [T_y32], lambda e, cb=cb: e.tensor_tensor(
                        out=y32[:, cb, 0:N], in0=y32[:, cb, 0:N], in1=mu[:, 0:N], op=ALU.subtract))
                    S.op("pool", [T_y32, T_rs], [T_y32], lambda e, cb=cb: e.tensor_tensor(
                        out=y32[:, cb, 0:N], in0=y32[:, cb, 0:N], in1=rs[:, 0:N], op=ALU.mult))
                    S.op("act", [T_y32, T_const], [T_cat], lambda e, cb=cb: e.activation(
                        out=catT[:, cb, 0:N], in_=y32[:, cb, 0:N], func=AF.Silu,
                        scale=cvec_s[:, 4 + cb:5 + cb], bias=cvec_s[:, 8 + cb:9 + cb]))
                if not last_group:
                    S.op("pool", [T_aT], [T_aT], lambda e: e.tensor_copy(out=aT[:, :, 0:30], in_=aT[:, :, N:N + 30]))

                capD = S.cap
                S.cap = None
                items = []
                for (c0, n) in tiles:
                    qi = (g0 + c0) // 128
                    if kind == "p":
                        far = list(range(0, max(qi - 1, 0)))
                        near = ([(qi - 1, 128, 1)] if qi >= 1 else []) + [(qi, 128, 0)]
                    else:
                        far = list(range(0, 15))
                        near = [(15, 128, 1), (16, 64, 0)]
                    nblk = len(far) + len(near)
                    for h in range(4):
                        for m in range(2):
                            done = 0
                            for f0 in range(0, len(far), 4):
                                chunk = far[f0:f0 + 4]
                                items.append(dict(kind="far", c0=c0, n=n, h=h, m=m, blks=chunk, done=done, nblk=nblk))
                                done += len(chunk)
                            for (blk, nk, bkind) in near:
                                items.append(dict(kind="near", c0=c0, n=n, h=h, m=m, blk=blk, nk=nk, bkind=bkind,
                                                  done=done, nblk=nblk))
                                done += 1
                        items[-1]["head_end"] = True
                    items[-1]["tile_end"] = True

                def emit_qk(k, it):
                    pS_, T_pS_ = pSS[k % 3]
                    PT_, T_PT_ = PTT[k % 3]
                    c0, n, h, m = it["c0"], it["n"], it["h"], it["m"]
                    mrow = slice(m * 64, (m + 1) * 64)
                    if it["kind"] == "far":
                        for j, blk in enumerate(it["blks"]):
                            S.op("pe", [T_kT, T_qT], [T_pS_], lambda e, j=j, blk=blk: e.matmul(
                                pS_[:, j * n:(j + 1) * n], lhsT=kT[mrow, h, blk * 128:(blk + 1) * 128],
                                rhs=qT[mrow, h, c0:c0 + n], start=True, stop=True))
                        cn = len(it["blks"]) * n
                        S.op("act", [T_pS_, T_const], [T_PT_], lambda e: e.activation(
                            out=PT_[:, 0:cn], in_=pS_[:, 0:cn], func=AF.Exp, bias=relb_s[:, 60 + h:61 + h]))
                    else:
                        blk, nk, bkind = it["blk"], it["nk"], it["bkind"]
                        tb_, T_tb_ = TMP[k % 3]
                        S.op("pe", [T_kT, T_qT], [T_pS_], lambda e: e.matmul(
                            pS_[0:nk, 0:n], lhsT=kT[mrow, h, blk * 128:blk * 128 + nk],
                            rhs=qT[mrow, h, c0:c0 + n], start=True, stop=True))
                        S.op("dve", [T_pS_, T_Tb], [T_tb_], lambda e: e.tensor_tensor(
                            out=tb_[0:nk, 0:n], in0=pS_[0:nk, 0:n], in1=Tb[0:nk, h, bkind, 0:n], op=ALU.add))
                        S.op("act", [T_tb_], [T_PT_], lambda e: e.activation(
                            out=PT_[0:nk, 0:n], in_=tb_[0:nk, 0:n], func=AF.Exp))

                def emit_pv(k, it):
                    PT_, T_PT_ = PTT[k % 3]
                    c0, n, h, m = it["c0"], it["n"], it["h"], it["m"]
                    qi_ = (g0 + c0) // 128
                    pO_, T_pO_ = pOO[(qi_ * 4 + h) % 2]
                    nblk = it["nblk"]
                    if it["kind"] == "far":
                        for j, blk in enumerate(it["blks"]):
                            dn = it["done"] + j
                            S.op("pe", [T_PT_, T_v], [T_pO_], lambda e, j=j, blk=blk, dn=dn: e.matmul(
                                pO_[0:n, m, 0:129], lhsT=PT_[:, j * n:(j + 1) * n], rhs=vaug[:, blk, h, 0:129],
                                start=(dn == 0), stop=(dn == nblk - 1)))
                    else:
                        blk, nk = it["blk"], it["nk"]
                        dn = it["done"]
                        S.op("pe", [T_PT_, T_v], [T_pO_], lambda e: e.matmul(
                            pO_[0:n, m, 0:129], lhsT=PT_[0:nk, 0:n], rhs=vaug[0:nk, blk, h, 0:129],
                            start=(dn == 0), stop=(dn == nblk - 1)))
                    if it.get("head_end"):
                        S.op("dve", [T_pO_], [T_astat], lambda e: e.reciprocal(
                            out=astat[0:n, 0:2], in_=pO_[0:n, :, 128:129].rearrange("p a b -> p (a b)")))
                        S.op("dve", [T_astat, T_small], [T_astat], lambda e: e.tensor_tensor(
                            out=astat[0:n, 2:3], in0=astat[0:n, 1:2], in1=neg_lam[0:n, :], op=ALU.mult))
                        S.op("dve", [T_pO_, T_astat], [T_att], lambda e: e.tensor_scalar(
                            out=att[0:n, :], in0=pO_[0:n, 0, 0:128], scalar1=astat[0:n, 0:1], scalar2=None, op0=ALU.mult))
                        S.op("dve", [T_pO_, T_astat, T_att], [T_att], lambda e: e.scalar_tensor_tensor(
                            out=att[0:n, :], in0=pO_[0:n, 1, 0:128], scalar=astat[0:n, 2:3], in1=att[0:n, :],
                            op0=ALU.mult, op1=ALU.add))
                        S.op("dve", [T_att], [T_sq, T_astat], lambda e: e.scalar_tensor_tensor(
                            out=sqj[0:n, :], in0=att[0:n, :], scalar=1.0, in1=att[0:n, :], op0=ALU.mult, op1=ALU.mult,
                            accum_out=astat[0:n, 3:4]))
                        S.op("act", [T_astat, T_eps], [T_astat], lambda e: e.activation(
                            out=astat[0:n, 5:6], in_=astat[0:n, 3:4], func=AF.Ln, scale=1.0 / 128, bias=eps_t[0:n, :]))
                        S.op("act", [T_astat], [T_astat], lambda e: e.activation(
                            out=astat[0:n, 4:5], in_=astat[0:n, 5:6], func=AF.Exp, scale=-0.5))
                        S.op("dve", [T_att, T_astat, T_const], [T_attb], lambda e: e.scalar_tensor_tensor(
                            out=attb[0:n, h * 128:(h + 1) * 128], in0=att[0:n, :], scalar=astat[0:n, 4:5], in1=gsub_s[0:n, :],
                            op0=ALU.mult, op1=ALU.mult))
                    if it.get("tile_end"):
                        for hh in range(4):
                            S.op("pe", [T_attb, T_identb], [T_pT], lambda e, hh=hh: e.transpose(
                                out=pT[:, hh * 128:hh * 128 + n], in_=attb[0:n, hh * 128:(hh + 1) * 128], identity=ident_b[0:n, 0:n]))
                        S.op("act", [T_pT], [T_cat], lambda e: e.copy(
                            out=catT[:, 4:8, c0:c0 + n], in_=pT[:, 0:512].rearrange("p (k c) -> p k c", k=4)[:, :, 0:n]))

                per_item = -(-len(capD) // max(len(items) - 4, 1))
                cpos = 0
                for k, it in enumerate(items):
                    emit_qk(k, it)
                    if k >= 2:
                        emit_pv(k - 2, items[k - 2])
                    for _ in range(per_item):
                        if cpos < len(capD):
                            S.replay(capD[cpos])
                            cpos += 1
                for k in range(max(len(items) - 2, 0), len(items)):
                    emit_pv(k, items[k])
                while cpos < len(capD):
                    S.replay(capD[cpos])
                    cpos += 1

                if STOP <= 5:
                    continue
                LE1, LE2 = [], []
                for k_, (c0, n) in enumerate(tiles):
                    xr, T_xr = xtl[k_ % 2]
                    S.cap = []
                    S.dma("sp", [], [T_xr], lambda e, c0=c0, n=n, xr=xr: e.dma_start(out=xr[0:n, :], in_=xd[g0 + c0:g0 + c0 + n, :]))
                    for hf in range(2):
                        po, T_po = pX[hf]
                        for kc in range(8):
                            S.op("pe", [T_cat, T_wout], [T_po], lambda e, po=po, kc=kc, hf=hf, c0=c0, n=n: e.matmul(
                                po[0:n, :], lhsT=catT[:, kc, c0:c0 + n], rhs=w_out_b[:, kc, hf * 512:(hf + 1) * 512],
                                start=(kc == 0), stop=(kc == 7)))
                        S.op("dve", [T_po, T_xr], [T_xr], lambda e, po=po, hf=hf, n=n, xr=xr: e.tensor_tensor(
                            out=xr[0:n, hf * 512:(hf + 1) * 512], in0=po[0:n, :], in1=xr[0:n, hf * 512:(hf + 1) * 512], op=ALU.add))
                    S.dma("sp", [T_xr], [], lambda e, c0=c0, n=n, xr=xr: e.dma_start(
                        out=x1d[tok0 + g0 + c0:tok0 + g0 + c0 + n, :], in_=xr[0:n, :]))
                    LE1.append(S.cap)
                    S.cap = []
                    rms_to_bf16(n, xr, T_xr, hb, T_hb, 2)
                    transpose_to_fT(n, hb, T_hb, c0, T_fTk[k_])
                    LE2.append(S.cap)
                    S.cap = None
                S.interleave(LE1[0], [])
                for k_ in range(len(tiles)):
                    S.interleave(LE2[k_], LE1[k_ + 1] if k_ + 1 < len(tiles) else [])
                for kc in range(8):
                    S.dma("sp", T_fTk, [], lambda e, kc=kc: e.dma_start(
                        out=h2Td[kc, :, tok0 + g0:tok0 + g0 + N], in_=fT[:, kc, 0:N]), key=T_fT)

        S.barrier()
        st1.close()
        st2 = ExitStack()
        st.enter_context(st2)
        cur[0] = st2
        TG = 256
        NCH = 64
        if with_peer:
            ijwd = nc.dram_tensor("ijwd", [128, 3, NT], F32, kind="Internal").ap()
            uscr_t = nc.dram_tensor("uscr", [NCH, 128, 8 * 256], BF16, kind="Internal").ap()
            vscr_t = nc.dram_tensor("vscr", [NCH, 128, 2 * DM], BF16, kind="Internal").ap()
            uscr = [uscr_t[ic].rearrange("p (k e) -> p k e", k=8) for ic in range(NCH)]
            vscr = [vscr_t[ic].rearrange("p (b d) -> p b d", b=2) for ic in range(NCH)]
            T_uscr = [S.tile(f"uscr{ic}") for ic in range(NCH)]
            T_vscr = [S.tile(f"vscr{ic}") for ic in range(NCH)]
            T_ijwd = S.tile("ijwd")

            gfin_s, T_gfin = sb("gfin_s", [128, DM], F32)
            iota_b, T_iotab = sb("iota_b", [128, 128], BF16)
            T_c2 = S.tile("consts2")
            S.dma("sp", [], [T_c2], lambda e: e.dma_start(out=gfin_s[:], in_=gfin.to_broadcast([128, DM])))
            S.op("dve", [T_iotar], [T_iotab], lambda e: e.tensor_copy(out=iota_b[:], in_=iota_r[:]))

            st2a = ExitStack()
            cur[0] = st2a
            TA = 512
            wq_b, T_wq = sb("wq_b", [128, 8, 2048], BF16)
            keys_b, T_keys = sb("keys_b", [128, 16, 128], BF16)
            gffn_s, T_gffn = sb("gffn_s", [128, 8], F32)
            sg0, T_sg0 = sb("sg0", [128, 1024], F32)
            sg1, T_sg1 = sb("sg1", [128, 1024], F32)
            h2g, T_h2g = sb("h2g", [128, 8, TA], BF16)
            qryT, T_qry = sb("qryT", [128, 16, TA], BF16)
            s_sb, T_ssb = sb("s_sb", [128, 16, 128], F32)
            wk4l = [sb(f"wk4_{i}", [128, 256], F32) for i in range(4)]
            wk4 = [x[0] for x in wk4l]
            T_wk4 = [x[1] for x in wk4l]
            T_A4 = [S.tile(f"A4_{i}") for i in range(4)]
            T_I4 = [S.tile(f"I4_{i}") for i in range(4)]
            T_C4 = [S.tile(f"C4_{i}") for i in range(4)]
            T_P4 = [S.tile(f"P4t_{i}") for i in range(4)]
            A_, T_A = sb("A_", [128, 16, 16], F32)
            Iu, T_Iu = sb("Iu", [128, 16, 16], U32)
            If, T_If = sb("If", [128, 16, 16], F32)
            cand, T_cand = sb("cand", [128, 8, 256], F32)
            C_, T_C = sb("C_", [128, 8, 16], F32)
            pos, T_pos = sb("pos", [128, 8, 16], U32)
            ku, T_ku = sb("ku", [128, 2, 128], U32)
            kf, T_kf = sb("kf", [128, 2, 128], F32)
            E_, T_E = sb("E_", [128, 8, 16], F32)
            gst, T_gst = sb("gst", [128, 32], F32)
            oh, T_oh = sb("oh", [128, 8, 16, 16], F32)
            ijw, T_ijw = sb("ijw", [128, 3, 128], F32)
            ijT = [sb(f"ijT{i}", [128, 3, 128], F32) for i in range(2)]
            stu = [sb(f"stu{i}", [128, 8, 256], BF16) for i in range(2)]
            stv = [sb(f"stv{i}", [128, 2, DM], BF16) for i in range(2)]
            pGa = [ps(f"pGa{i}", [128, 512], F32) for i in range(4)]
            pga_i = [0]

            def next_pGa():
                pga_i[0] += 1
                return pGa[pga_i[0] % 4]

            S.dma("sp", [], [T_c2], lambda e: e.dma_start(out=gffn_s[:], in_=gffn))
            S.dma("pool", [], [T_keys], lambda e: e.dma_start(out=keys_b[:], in_=keysT.rearrange("r d n -> d r n")))
            sgs = [(sg0, T_sg0), (sg1, T_sg1)]
            ii = 0
            for kc in range(8):
                for hf in range(2):
                    stt, T_s = sgs[ii % 2]
                    ii += 1
                    S.dma("sp", [], [T_s], lambda e, stt=stt, kc=kc, hf=hf: e.dma_start(
                        out=stt[:], in_=wq[kc * 128:(kc + 1) * 128, hf * 1024:(hf + 1) * 1024]))
                    S.op("dve", [T_s, T_c2], [T_wq], lambda e, stt=stt, kc=kc, hf=hf: e.tensor_scalar(
                        out=wq_b[:, kc, hf * 1024:(hf + 1) * 1024], in0=stt[:], scalar1=gffn_s[:, kc:kc + 1],
                        scalar2=None, op0=ALU.mult))

            conv_i = [0]
            T_scw = [S.tile(f"scw_u{i}") for i in range(2)]
            T_scw2 = [S.tile(f"scw_v{i}") for i in range(2)]

            def convert_chunk():
                ic = conv_i[0]
                if ic >= NCH:
                    return
                conv_i[0] += 1
                (su, T_su), (sv, T_sv) = stu[ic % 2], stv[ic % 2]
                for k4 in range(2):
                    S.dma("pool", [], [T_su], lambda e, k4=k4: e.dma_start(
                        out=su[:, k4 * 4:(k4 + 1) * 4, :],
                        in_=uT[k4 * 512:(k4 + 1) * 512, ic * 256:(ic + 1) * 256].rearrange("(k p) e -> p k e", p=128)))
                S.dma("pool", [], [T_sv], lambda e: e.dma_start(
                    out=sv[:], in_=pv[ic * 256:(ic + 1) * 256, :].rearrange("(b p) d -> p b d", p=128)))
                S.dma("sp", [T_su], [T_uscr[ic]], lambda e: e.dma_start(out=uscr[ic], in_=su[:]), key=T_scw[ic % 2])
                S.dma("sp", [T_sv], [T_vscr[ic]], lambda e: e.dma_start(out=vscr[ic], in_=sv[:]), key=T_scw2[ic % 2])

            ssbL = [(s_sb, T_ssb), sb("s_sb1", [128, 16, 128], F32)]
            AL = [(A_, None), sb("A_1", [128, 16, 16], F32)]
            IuL = [(Iu, None), sb("Iu1", [128, 16, 16], U32)]
            IfL = [(If, T_If), sb("If1", [128, 16, 16], F32)]
            candL = [(cand, T_cand), sb("cand1", [128, 8, 256], F32)]
            T_A4L = [T_A4, [S.tile(f"A4b_{i}") for i in range(4)]]
            T_I4L = [T_I4, [S.tile(f"I4b_{i}") for i in range(4)]]
            wkAl = [sb(f"wkA_{i}", [128, 128], F32) for i in range(4)]
            P2A = os.environ.get("KPOOL2A", "pool")
            tile_ctr = [0]

            def gate_tile(t0, c0, n, pb):
                s_sb, T_ssb = ssbL[pb]
                A_ = AL[pb][0]
                Iu = IuL[pb][0]
                If, T_If = IfL[pb]
                cand, T_cand = candL[pb]
                T_A4 = T_A4L[pb]
                T_I4 = T_I4L[pb]
                S.cap = []
                for q4 in range(4):
                    pg, T_pg = next_pGa()
                    for j in range(4):
                        rp = q4 * 4 + j
                        S.op("pe", [T_qry, T_keys], [T_pg], lambda e, pg=pg, j=j, rp=rp: e.matmul(
                            pg[0:n, j * 128:(j + 1) * 128], lhsT=qryT[:, rp, c0:c0 + n], rhs=keys_b[:, rp, :],
                            start=True, stop=True))
                    S.op("act", [T_pg], [T_ssb], lambda e, pg=pg, q4=q4: e.copy(
                        out=s_sb[0:n, q4 * 4:(q4 + 1) * 4, :], in_=pg[0:n, :].rearrange("p (a b) -> p a b", a=4)))
                for rp0 in range(0, 16, 4):
                    rps = list(range(rp0, rp0 + 4))
                    for rp in rps:
                        S.op("dve", [T_ssb], [T_A4[rp % 4]], lambda e, rp=rp: e.max(out=A_[0:n, rp, 0:8], in_=s_sb[0:n, rp, :]))
                    for rp in rps:
                        S.op("dve", [T_ssb, T_A4[rp % 4]], [T_I4[rp % 4]], lambda e, rp=rp: e.max_index(
                            out=Iu[0:n, rp, 0:8], in_max=A_[0:n, rp, 0:8], in_values=s_sb[0:n, rp, :]))
                    for rp in rps:
                        S.op("dve", [T_ssb, T_A4[rp % 4]], [wkAl[rp % 4][1]], lambda e, rp=rp: e.match_replace(
                            out=wkAl[rp % 4][0][0:n, :], in_to_replace=A_[0:n, rp, 0:8], in_values=s_sb[0:n, rp, :], imm_value=-1e30))
                    for rp in rps:
                        S.op("dve", [wkAl[rp % 4][1]], [T_A4[rp % 4]], lambda e, rp=rp: e.max(out=A_[0:n, rp, 8:16], in_=wkAl[rp % 4][0][0:n, :]))
                    for rp in rps:
                        S.op("dve", [wkAl[rp % 4][1], T_A4[rp % 4]], [T_I4[rp % 4]], lambda e, rp=rp: e.max_index(
                            out=Iu[0:n, rp, 8:16], in_max=A_[0:n, rp, 8:16], in_values=wkAl[rp % 4][0][0:n, :]))
                S.op("dve", T_I4, [T_If], lambda e: e.tensor_copy(out=If[0:n], in_=Iu[0:n]))
                A4 = A_[0:n].rearrange("p (r a) k -> p r a k", a=2)
                I4 = If[0:n].rearrange("p (r a) k -> p r a k", a=2)
                S.op(P2A, T_A4, [T_cand], lambda e: e.tensor_tensor(
                    out=cand[0:n].rearrange("p r (a b) -> p r a b", a=16),
                    in0=A4[:, :, 0, :].unsqueeze(3).to_broadcast([n, 8, 16, 16]),
                    in1=A4[:, :, 1, :].unsqueeze(2).to_broadcast([n, 8, 16, 16]), op=ALU.add))
                L1 = S.cap
                S.cap = []
                for r0 in range(0, 8, 4):
                    rs_ = list(range(r0, r0 + 4))
                    for r in rs_:
                        S.op("dve", [T_cand], [T_C4[r % 4]], lambda e, r=r: e.max(out=C_[0:n, r, 0:8], in_=cand[0:n, r, :]))
                    for r in rs_:
                        S.op("dve", [T_cand, T_C4[r % 4]], [T_P4[r % 4]], lambda e, r=r: e.max_index(
                            out=pos[0:n, r, 0:8], in_max=C_[0:n, r, 0:8], in_values=cand[0:n, r, :]))
                    for r in rs_:
                        S.op("dve", [T_cand, T_C4[r % 4]], [T_wk4[r % 4]], lambda e, r=r: e.match_replace(
                            out=wk4[r % 4][0:n, :], in_to_replace=C_[0:n, r, 0:8], in_values=cand[0:n, r, :], imm_value=-1e30))
                    for r in rs_:
                        S.op("dve", [T_wk4[r % 4]], [T_C4[r % 4]], lambda e, r=r: e.max(out=C_[0:n, r, 8:16], in_=wk4[r % 4][0:n, :]))
                    for r in rs_:
                        S.op("dve", [T_wk4[r % 4], T_C4[r % 4]], [T_P4[r % 4]], lambda e, r=r: e.max_index(
                            out=pos[0:n, r, 8:16], in_max=C_[0:n, r, 8:16], in_values=wk4[r % 4][0:n, :]))
                S.op("dve", T_C4, [T_gst], lambda e: e.tensor_scalar(
                    out=gst[0:n, 0:8], in0=C_[0:n, :, 0], scalar1=-1.0, scalar2=None, op0=ALU.mult))
                for r in range(8):
                    S.op("act", T_C4 + [T_gst], [T_E, T_gst], lambda e, r=r: e.activation(
                        out=E_[0:n, r, :], in_=C_[0:n, r, :], func=AF.Exp, bias=gst[0:n, r:r + 1],
                        accum_out=gst[0:n, 8 + r:9 + r]))
                S.op("dve", [T_gst], [T_gst], lambda e: e.reciprocal(out=gst[0:n, 16:24], in_=gst[0:n, 8:16]))
                S.op("dve", [T_E, T_gst], [T_ijw], lambda e: e.tensor_tensor(
                    out=ijw[0:n, 2, :].rearrange("p (r k) -> p r k", r=8), in0=E_[0:n],
                    in1=gst[0:n, 16:24].unsqueeze(2).to_broadcast([n, 8, 16]), op=ALU.mult))
                S.op("dve", T_P4, [T_ku], lambda e: e.tensor_single_scalar(
                    out=ku[0:n, 0, :], in_=pos[0:n].rearrange("p r k -> p (r k)"), scalar=4, op=ALU.logical_shift_right))
                S.op("dve", T_P4, [T_ku], lambda e: e.tensor_single_scalar(
                    out=ku[0:n, 1, :], in_=pos[0:n].rearrange("p r k -> p (r k)"), scalar=15, op=ALU.bitwise_and))
                S.op("dve", [T_ku], [T_kf], lambda e: e.tensor_copy(out=kf[0:n], in_=ku[0:n]))
                for a in range(2):
                    S.op("dve", [T_kf, T_iotar], [T_oh], lambda e, a=a: e.tensor_tensor(
                        out=oh[0:n],
                        in0=kf[0:n, a, :].rearrange("p (r k) -> p r k", r=8).unsqueeze(3).to_broadcast([n, 8, 16, 16]),
                        in1=iota_r[0:n, 0:16].unsqueeze(1).unsqueeze(1).to_broadcast([n, 8, 16, 16]), op=ALU.is_equal))
                    S.op(P2A, [T_oh, T_If], [T_oh], lambda e, a=a: e.tensor_tensor(
                        out=oh[0:n], in0=oh[0:n],
                        in1=I4[:, :, a, :].unsqueeze(2).to_broadcast([n, 8, 16, 16]), op=ALU.mult))
                    S.op("dve", [T_oh], [T_ijw], lambda e, a=a: e.reduce_sum(
                        out=ijw[0:n, a, :].rearrange("p (r k) -> p r k", r=8), in_=oh[0:n], axis=AX.X))
                pg, T_pg = next_pGa()
                for a in range(3):
                    S.op("pe", [T_ijw, T_identf], [T_pg], lambda e, pg=pg, a=a: e.transpose(
                        out=pg[:, a * 128:a * 128 + n], in_=ijw[0:n, a, :], identity=ident_f[0:n, 0:n]))
                (it_, T_it) = ijT[tile_ctr[0] % 2]
                tile_ctr[0] += 1
                S.op("act", [T_pg], [T_it], lambda e, pg=pg, it_=it_: e.copy(
                    out=it_[:, :, 0:n], in_=pg[:, 0:384].rearrange("p (a t) -> p a t", a=3)[:, :, 0:n]))
                S.dma("sp", [T_it], [T_ijwd], lambda e, it_=it_: e.dma_start(
                    out=ijwd[:, :, t0 + c0:t0 + c0 + n], in_=it_[:, :, 0:n]), key=T_it)
                L2 = S.cap
                S.cap = None
                return L1, L2

            def interleave(La, Lb):
                na, nb = len(La), len(Lb)
                ia = ib = 0
                while ia < na or ib < nb:
                    if ib >= nb or (ia < na and ia * nb <= ib * na):
                        S.replay(La[ia])
                        ia += 1
                    else:
                        S.replay(Lb[ib])
                        ib += 1

            prevL2 = []
            xt_i = 0
            for t0 in range(0, NT, TA):
                N = min(TA, NT - t0)
                tiles = [(c0, min(128, N - c0)) for c0 in range(0, N, 128)]
                S.dma("sp", [], [T_h2g], lambda e: e.dma_start(
                    out=h2g[:, :, 0:N], in_=h2Td[:, :, t0:t0 + N].rearrange("k p t -> p k t")))
                for blk in range(16):
                    pg, T_pg = next_pGa()
                    for kc in range(8):
                        S.op("pe", [T_wq, T_h2g], [T_pg], lambda e, pg=pg, kc=kc, blk=blk: e.matmul(
                            pg[:, 0:N], lhsT=wq_b[:, kc, blk * 128:(blk + 1) * 128], rhs=h2g[:, kc, 0:N],
                            start=(kc == 0), stop=(kc == 7)))
                    S.op("act", [T_pg], [T_qry], lambda e, pg=pg, blk=blk: e.copy(out=qryT[:, blk, 0:N], in_=pg[:, 0:N]))
                for (c0, n) in tiles:
                    convert_chunk()
                    convert_chunk()
                    L1, L2 = gate_tile(t0, c0, n, xt_i % 2)
                    xt_i += 1
                    interleave(L1, prevL2)
                    prevL2 = L2
            interleave([], prevL2)
            while conv_i[0] < NCH:
                convert_chunk()
            S.barrier()
            st2a.close()

            st2b = ExitStack()
            st.enter_context(st2b)
            cur[0] = st2b
            Gall = [sb(f"Gall{i}", [128, 128, TG], BF16) for i in range(2)]
            ubuf = [sb(f"ubuf{i}", [128, 8, 256], BF16) for i in range(2)]
            vbuf = [sb(f"vbuf{i}", [128, 2, DM], BF16) for i in range(3)]
            h2g2 = [sb(f"h2g2_{i}", [128, 8, TG], BF16) for i in range(2)]
            ijg = [sb(f"ijg{i}", [128, 3, TG], F32) for i in range(2)]
            P4 = [sb(f"P4_{i}", [128, 4, 128], BF16) for i in range(3)]
            Q4 = [sb(f"Q4_{i}", [128, 4, 128], BF16) for i in range(3)]
            gbuf = [sb(f"gbuf{i}", [128, TG], F32) for i in range(3)]
            cbuf = [sb(f"cbuf{i}", [128, TG], BF16) for i in range(3)]
            x2l = [sb(f"x2_{i}", [128, DM], F32) for i in range(2)]
            junk2, T_junk2 = sb("junk2", [128, DM], BF16)
            st2s, T_st2s = sb("st2s", [128, 8], F32)
            pY = [[ps(f"pY{t}{h}", [128, 512], F32) for h in range(2)] for t in range(2)]
            pA = [ps(f"pA{i}", [128, 512], F32) for i in range(3)]
            pG = [ps(f"pG{i}", [128, 512], F32) for i in range(1)]

            groups = [(t0, min(TG, NT - t0)) for t0 in range(0, NT, TG)]
            MAXG = int(os.environ.get("KGROUPS", "999"))
            groups = groups[:MAXG]
            NG = len(groups)
            MULT_ENG = os.environ.get("KMULT", "pool")

            def gtiles(g):
                N = groups[g][1]
                return [(c0, min(128, N - c0)) for c0 in range(0, N, 128)]

            def load_group(g):
                t0, N = groups[g]
                (hg, T_hg), (ij, T_ij) = h2g2[g % 2], ijg[g % 2]
                S.dma("sp", [], [T_hg], lambda e: e.dma_start(
                    out=hg[:, :, 0:N], in_=h2Td[:, :, t0:t0 + N].rearrange("k p t -> p k t")))
                S.dma("sp", [T_ijwd], [T_ij], lambda e: e.dma_start(out=ij[:, :, 0:N], in_=ijwd[:, :, t0:t0 + N]))

            def p10_dve(g, b):
                (ij, T_ij) = ijg[g % 2]
                (p4, T_p4), (q4_, T_q4) = P4[b % 3], Q4[b % 3]
                for u in range(4):
                    t = b * 4 + u
                    S.op("dve", [T_ij, T_iotab], [T_p4], lambda e, u=u, t=t: e.tensor_scalar(
                        out=p4[:, u, :], in0=iota_b[:], scalar1=ij[:, 0, t:t + 1], scalar2=None, op0=ALU.is_equal))
                    S.op("dve", [T_ij, T_iotab], [T_q4], lambda e, u=u, t=t: e.tensor_scalar(
                        out=q4_[:, u, :], in0=iota_b[:], scalar1=ij[:, 1, t:t + 1], scalar2=ij[:, 2, t:t + 1],
                        op0=ALU.is_equal, op1=ALU.mult))

            def p10_pe(g, b):
                (p4, T_p4), (q4_, T_q4) = P4[b % 3], Q4[b % 3]
                (ga, T_ga) = Gall[g % 2]
                pg, T_pg = pG[0]
                for u in range(4):
                    S.op("pe", [T_p4, T_q4], [T_pg], lambda e, u=u: e.matmul(
                        pg[:, u * 128:(u + 1) * 128], lhsT=q4_[:, u, :], rhs=p4[:, u, :], start=True, stop=True))
                S.op("act", [T_pg], [T_ga], lambda e: e.copy(
                    out=ga[:, :, b * 4:b * 4 + 4], in_=pg[:, :].rearrange("p (t i) -> p i t", t=4)))

            class P10:
                def __init__(self, g):
                    self.g = g
                    self.nb = groups[g][1] // 4
                    self.d = 0
                    self.p = 0

                def step(self):
                    if self.d < self.nb:
                        p10_dve(self.g, self.d)
                        self.d += 1
                        if self.d - self.p >= 3:
                            p10_pe(self.g, self.p)
                            self.p += 1
                    elif self.p < self.nb:
                        p10_pe(self.g, self.p)
                        self.p += 1

                def flush(self):
                    while self.p < self.nb:
                        if self.d < self.nb and self.d - self.p < 3:
                            p10_dve(self.g, self.d)
                            self.d += 1
                        else:
                            p10_pe(self.g, self.p)
                            self.p += 1

            chunk_ctr = [0]

            def load_chunk(ic):
                c = chunk_ctr[0]
                chunk_ctr[0] += 1
                (ub, T_ub), (vb, T_vb) = ubuf[c % 2], vbuf[c % 3]
                S.dma("sp", [T_uscr[ic]], [T_ub], lambda e: e.dma_start(out=ub[:], in_=uscr[ic]))
                S.dma("sp", [T_vscr[ic]], [T_vb], lambda e: e.dma_start(out=vb[:], in_=vscr[ic]))

            def stage_u(g, i):
                N = groups[g][1]
                c = g * NCH + i // 2
                ib = i % 2
                (ub, T_ub) = ubuf[c % 2]
                (hg, T_hg) = h2g2[g % 2]
                (ga, T_ga) = Gall[g % 2]
                gidx = g * 128 + i
                pa, T_pa = pA[gidx % 3]
                gb, T_gb = gbuf[gidx % 3]
                cb_, T_cb = cbuf[gidx % 3]
                for kc in range(8):
                    S.op("pe", [T_ub, T_hg], [T_pa], lambda e, kc=kc: e.matmul(
                        pa[:, 0:N], lhsT=ub[:, kc, ib * 128:(ib + 1) * 128], rhs=hg[:, kc, 0:N],
                        start=(kc == 0), stop=(kc == 7)))
                S.op("act", [T_pa], [T_gb], lambda e: e.activation(out=gb[:, 0:N], in_=pa[:, 0:N], func=AF.Gelu))
                S.op(MULT_ENG, [T_gb, T_ga], [T_cb], lambda e: e.tensor_tensor(
                    out=cb_[:, 0:N], in0=gb[:, 0:N], in1=ga[:, i, 0:N], op=ALU.mult))

            def stage_v(g, i):
                c = g * NCH + i // 2
                ib = i % 2
                (vb, T_vb) = vbuf[c % 3]
                cb_, T_cb = cbuf[(g * 128 + i) % 3]
                for ti, (c0, n) in enumerate(gtiles(g)):
                    for hf in range(2):
                        py, T_py = pY[ti][hf]
                        S.op("pe", [T_cb, T_vb], [T_py], lambda e, py=py, c0=c0, n=n, hf=hf: e.matmul(
                            py[0:n, :], lhsT=cb_[:, c0:c0 + n], rhs=vb[:, ib, hf * 512:(hf + 1) * 512],
                            start=(i == 0), stop=(i == 127)))

            def prefetch_x1(g):
                t0, N = groups[g]
                for ti, (c0, n) in enumerate(gtiles(g)):
                    x2, T_x2 = x2l[ti]
                    tg = t0 + c0
                    S.dma("sp", [], [T_x2], lambda e, tg=tg, n=n, x2=x2: e.dma_start(out=x2[0:n, :], in_=x1d[tg:tg + n, :]))

            def epilogue(g):
                t0, N = groups[g]
                for ti, (c0, n) in enumerate(gtiles(g)):
                    x2, T_x2 = x2l[ti]
                    for hf in range(2):
                        py, T_py = pY[ti][hf]
                        S.op("dve", [T_py, T_x2], [T_x2], lambda e, py=py, hf=hf, n=n, x2=x2: e.tensor_tensor(
                            out=x2[0:n, hf * 512:(hf + 1) * 512], in0=py[0:n, :], in1=x2[0:n, hf * 512:(hf + 1) * 512], op=ALU.add))
                for ti, (c0, n) in enumerate(gtiles(g)):
                    x2, T_x2 = x2l[ti]
                    tg = t0 + c0
                    k0 = ti * 4
                    S.op("act", [T_x2], [T_junk2, T_st2s], lambda e, n=n, x2=x2, k0=k0: e.activation(
                        out=junk2[0:n, :], in_=x2[0:n, :], func=AF.Square, accum_out=st2s[0:n, k0:k0 + 1]))
                    S.op("act", [T_st2s, T_eps], [T_st2s], lambda e, n=n, k0=k0: e.activation(
                        out=st2s[0:n, k0 + 1:k0 + 2], in_=st2s[0:n, k0:k0 + 1], func=AF.Sqrt, scale=1.0 / DM, bias=eps_t[0:n, :]))
                    S.op("dve", [T_st2s], [T_st2s], lambda e, n=n, k0=k0: e.reciprocal(out=st2s[0:n, k0 + 1:k0 + 2], in_=st2s[0:n, k0 + 1:k0 + 2]))
                    S.op("dve", [T_x2, T_st2s, T_c2], [T_x2], lambda e, n=n, x2=x2, k0=k0: e.scalar_tensor_tensor(
                        out=x2[0:n, :], in0=x2[0:n, :], scalar=st2s[0:n, k0 + 1:k0 + 2], in1=gfin_s[0:n, :], op0=ALU.mult, op1=ALU.mult))
                    if tg < n_pseq * SEQ:
                        dst = yp[tg // SEQ][tg % SEQ:tg % SEQ + n, :]
                    else:
                        dst = ys[0:n, :]
                    S.dma("sp", [T_x2], [], lambda e, dst=dst, n=n, x2=x2: e.dma_start(out=dst, in_=x2[0:n, :]))

            load_group(0)
            load_chunk(0)
            pz = P10(0)
            pz.flush()
            seq = [(g, i) for g in range(NG) for i in range(128)]
            nxt = None

            def emit_v(idx):
                g_, i_ = seq[idx]
                stage_v(g_, i_)
                if i_ == 127:
                    epilogue(g_)

            for idx, (g, i) in enumerate(seq):
                if i == 0:
                    nxt = None
                    if g + 1 < NG:
                        load_group(g + 1)
                        nxt = P10(g + 1)
                stage_u(g, i)
                if idx >= 2:
                    emit_v(idx - 2)
                if i % 2 == 0:
                    ic_next = i // 2 + 1
                    if ic_next < NCH:
                        load_chunk(ic_next)
                    elif g + 1 < NG:
                        load_chunk(0)
                if nxt is not None and (i % 2 == 1 or i in (8, 16, 24, 32)):
                    nxt.step()
                if i == 64:
                    prefetch_x1(g)
                if i == 127 and nxt is not None:
                    nxt.flush()
            emit_v(len(seq) - 2)
            emit_v(len(seq) - 1)

        S.barrier()
        S.finish("sp")
        print("ops per engine:", S.nops, "sems:", S.nsem)
    return nc


def _prep_shared(inp):
    f = lambda a: np.ascontiguousarray(np.asarray(a, dtype=np.float32))
    sh = {}
    sh["w_in"] = f(inp["w_in"][0])
    sh["gmix"] = f(inp["g_mix"][0].reshape(8, 128).T)
    sh["convw"] = f(inp["conv_w"][0].reshape(31, 4, 128).transpose(2, 1, 0))
    sh["cvec"] = f(np.concatenate([inp["conv_b"][0].reshape(4, 128).T, inp["conv_ln_g"][0].reshape(4, 128).T,
                                   inp["conv_ln_b"][0].reshape(4, 128).T], axis=1))
    sh["lam"] = f(np.stack([inp["lambda_q1"][0], inp["lambda_k1"][0], inp["lambda_q2"][0], inp["lambda_k2"][0]]).reshape(1, 256))
    sh["subg"] = f(inp["subln_g"][0].reshape(1, 128))
    sh["relb"] = f(inp["rel_bias"].reshape(1, 128))
    sh["w_out"] = f(inp["w_out"][0])
    sh["gffn"] = f(inp["g_ffn"][0].reshape(8, 128).T)
    sh["wq"] = f(inp["w_query"][0])
    sh["keysT"] = f(inp["sub_keys"][0].reshape(16, 128, 128).transpose(0, 2, 1))
    sh["uT"] = f(inp["peer_u"][0].T)
    sh["pv"] = f(inp["peer_v"][0])
    sh["gfin"] = f(inp["g_final"].reshape(1, DM))
    sh["bkc"] = _bucket_tiles()
    return sh


def kernel(**inp):
    f = lambda a: np.ascontiguousarray(np.asarray(a, dtype=np.float32))
    sh = _prep_shared(inp)
    nc = build_program(2)
    in_maps = []
    for c in range(NCORES):
        m = dict(sh)
        m["xp"] = f(inp["x_prompt"][2 * c:2 * c + 2])
        m["xs"] = f(inp["x_sample"][c])
        m["ckT"] = f(np.asarray(inp["cache_k"][0, c]).reshape(SEQ, 4, 128).transpose(1, 2, 0))
        m["cv"] = f(np.asarray(inp["cache_v"][0, c]).reshape(SEQ, 512))
        m["scT"] = f(np.asarray(inp["state_conv"][0, c]).reshape(30, 4, 128).transpose(2, 1, 0))
        in_maps.append(m)
    res = run_bass_kernel_spmd(nc, in_maps, core_ids=list(range(NCORES)))
    R = res.results
    y_prompt = np.concatenate([r["yp"] for r in R], axis=0)
    y_sample = np.stack([r["ys"] for r in R], axis=0)
    k_prompt = np.concatenate([r["kp"] for r in R], axis=0).reshape(1, 16, SEQ, 4, 2, 64)
    v_prompt = np.concatenate([r["vp"] for r in R], axis=0).reshape(1, 16, SEQ, 4, 128)
    c_prompt = np.concatenate([r["cp"] for r in R], axis=0).reshape(1, 16, 30, 512)
    k_sample = np.stack([r["ks"] for r in R], axis=0).reshape(1, 8, 64, 4, 2, 64)
    v_sample = np.stack([r["vs"] for r in R], axis=0).reshape(1, 8, 64, 4, 128)
    c_sample = np.stack([r["cs"] for r in R], axis=0).reshape(1, 8, 30, 512)
    return (y_prompt, y_sample, k_prompt, v_prompt, c_prompt, k_sample, v_sample, c_sample)
```

```python
import math
import os
from contextlib import ExitStack

import numpy as np
import concourse.bass as bass
import concourse.mybir as mybir
from concourse.bass_utils import run_bass_kernel_spmd

F32 = mybir.dt.float32
BF16 = mybir.dt.bfloat16
U32 = mybir.dt.uint32
AF = mybir.ActivationFunctionType
ALU = mybir.AluOpType
AX = mybir.AxisListType

EPS = 1e-6
LAM_INIT = 0.8 - 0.6 * math.exp(-0.3 * 0)
NCORES = 8
SEQ = 2048
DM = 1024
NEXP_SIDE = 128

SEM_LIMIT = 30000


class Counter:
    def __init__(self, S, name):
        self.S = S
        self.name = name
        self.epoch = 0
        self.val = 0
        self.sem = S.new_sem(f"{name}_e0")

    def bump(self, inc):
        if self.val + inc > SEM_LIMIT:
            self.epoch += 1
            self.val = 0
            self.sem = self.S.new_sem(f"{self.name}_e{self.epoch}")
        self.val += inc
        return (self.sem, self.val, self.name, self.epoch)


class Tile:
    __slots__ = ("name", "w", "r", "dmac")

    def __init__(self, name):
        self.name = name
        self.w = None
        self.r = []
        self.dmac = None


class Sched:
    def __init__(self, nc, stack):
        self.nc = nc
        self.stack = stack
        self.nsem = 0
        self.engs = {"pe": nc.tensor, "act": nc.scalar, "dve": nc.vector,
                     "pool": nc.gpsimd, "sp": nc.sync}
        self.cnt = {k: Counter(self, k) for k in self.engs}
        self.known = {k: {} for k in self.engs}
        self.nops = {k: 0 for k in self.engs}
        self.tiles = []
        self.cap = None
        self.snaps = {}

    def new_sem(self, name):
        self.nsem += 1
        return self.stack.enter_context(self.nc.semaphore(f"s{self.nsem}_{name}"))

    def tile(self, name):
        t = Tile(name)
        self.tiles.append(t)
        return t

    def _wait(self, e, ev):
        sem, val, name, epoch = ev
        key = (name, epoch)
        if self.known[e].get(key, 0) >= val:
            return
        self.known[e][key] = val
        self.engs[e].wait_ge(sem, val)
        snap = self.snaps.get((name, epoch, val))
        if snap:
            ke = self.known[e]
            for k2, v2 in snap.items():
                if ke.get(k2, 0) < v2:
                    ke[k2] = v2

    def _deps(self, reads, writes):
        evs = []
        for t in reads:
            if t.w is not None:
                evs.append(t.w)
        for t in writes:
            if t.w is not None:
                evs.append(t.w)
            evs.extend(t.r)
        return evs

    def op(self, e, reads, writes, fn):
        if self.cap is not None:
            self.cap.append(("op", e, reads, writes, fn, None))
            return None
        return self._op(e, reads, writes, fn)

    def dma(self, q, reads, writes, fn, key=None):
        if self.cap is not None:
            self.cap.append(("dma", q, reads, writes, fn, key))
            return None
        return self._dma(q, reads, writes, fn, key)

    def replay(self, item):
        kind, e, reads, writes, fn, key = item
        if kind == "op":
            return self._op(e, reads, writes, fn)
        return self._dma(e, reads, writes, fn, key)

    def interleave(self, La, Lb):
        na, nb = len(La), len(Lb)
        ia = ib = 0
        while ia < na or ib < nb:
            if ib >= nb or (ia < na and ia * nb <= ib * na):
                self.replay(La[ia])
                ia += 1
            else:
                self.replay(Lb[ib])
                ib += 1

    def _op(self, e, reads, writes, fn):
        for ev in self._deps(reads, writes):
            if ev[2] == e and e == "pe":
                continue
            self._wait(e, ev)
        ins = fn(self.engs[e])
        ev = self.cnt[e].bump(1)
        ins.then_inc(ev[0], 1)
        self.snaps[(ev[2], ev[3], ev[1])] = dict(self.known[e])
        self.nops[e] += 1
        self._mark(ev, reads, writes)
        return ev

    def _mark(self, ev, reads, writes):
        k = (ev[2], ev[3])
        for t in reads:
            t.r = [x for x in t.r if (x[2], x[3]) != k]
            t.r.append(ev)
        for t in writes:
            t.w = ev
            t.r = []

    def _dma(self, q, reads, writes, fn, key=None):
        kt = key or (writes[0] if writes else reads[0])
        if kt.dmac is None:
            kt.dmac = Counter(self, "d_" + kt.name)
        for ev in self._deps(reads, writes):
            self._wait(q, ev)
        ins = fn(self.engs[q])
        ev = kt.dmac.bump(16)
        ins.then_inc(ev[0], 16)
        self.snaps[(ev[2], ev[3], ev[1])] = dict(self.known[q])
        self.nops[q] += 1
        self._mark(ev, reads, writes)
        return ev

    def _all_events(self):
        evs = {}
        for t in self.tiles:
            for ev in ([t.w] if t.w else []) + t.r:
                k = (ev[2], ev[3])
                if k not in evs or evs[k][1] < ev[1]:
                    evs[k] = ev
        return evs

    def barrier(self):
        evs = self._all_events()
        for e in self.engs:
            for ev in evs.values():
                self._wait(e, ev)
        for t in self.tiles:
            t.w = None
            t.r = []

    def finish(self, e="sp"):
        for ev in self._all_events().values():
            self._wait(e, ev)


def _bucket_np(rel):
    nb = 16
    max_exact = 8
    ret = np.where(rel > 0, nb, 0)
    n = np.abs(rel)
    nf = np.maximum(n, 1).astype(np.float32)
    large = max_exact + (np.log(nf / max_exact) / math.log(128 / max_exact) * (nb - max_exact)).astype(np.int32)
    large = np.minimum(large, nb - 1)
    return ret + np.where(n < max_exact, n, large)


def _bucket_tiles():
    k = np.arange(128)[:, None]
    q = np.arange(128)[None, :]
    b0 = _bucket_np(k - q).astype(np.float32)
    masked = (k // 64) > (q // 64)
    b0 = np.where(masked, 32.0, b0)
    b1 = _bucket_np(k - q - 128).astype(np.float32)
    return np.stack([b0, b1], axis=1).astype(np.float32)


def build_program(n_pseq=2, with_peer=True, dbg=False):
    nc = bass.Bass("TRN2", target_bir_lowering=False)
    NT = n_pseq * SEQ + 64

    def din(name, shape, dt=F32):
        return nc.dram_tensor(name, list(shape), dt, kind="ExternalInput").ap()

    def dout(name, shape, dt=F32):
        return nc.dram_tensor(name, list(shape), dt, kind="ExternalOutput").ap()

    xp = din("xp", [n_pseq, SEQ, DM])
    xs = din("xs", [64, DM])
    ckT = din("ckT", [4, 128, SEQ])
    cv = din("cv", [SEQ, 512])
    scT = din("scT", [128, 4, 30])
    w_in = din("w_in", [DM, 2560])
    gmix = din("gmix", [128, 8])
    convw = din("convw", [128, 4, 31])
    cvec = din("cvec", [128, 12])
    lam = din("lam", [1, 256])
    subg = din("subg", [1, 128])
    relb = din("relb", [1, 128])
    w_out = din("w_out", [DM, DM])
    gffn = din("gffn", [128, 8])
    wq = din("wq", [DM, 2048])
    keysT = din("keysT", [16, 128, 128])
    uT = din("uT", [DM, 16384])
    pv = din("pv", [16384, DM])
    gfin = din("gfin", [1, DM])
    bkc = din("bkc", [128, 2, 128])

    yp = dout("yp", [n_pseq, SEQ, DM])
    ys = dout("ys", [64, DM])
    kp = dout("kp", [n_pseq, SEQ, 512])
    vp = dout("vp", [n_pseq, SEQ, 512])
    cp = dout("cp", [n_pseq, 30, 512])
    ks = dout("ks", [64, 512])
    vs = dout("vs", [64, 512])
    cs = dout("cs", [30, 512])

    kind_scr = "ExternalOutput" if dbg else "Internal"
    x1d = nc.dram_tensor("x1d", [NT, DM], F32, kind=kind_scr).ap()
    h2Td = nc.dram_tensor("h2Td", [8, 128, NT], BF16, kind="Internal").ap()

    with ExitStack() as st:
        S = Sched(nc, st)

        cur = [st]

        def sb(name, shape, dt):
            return cur[0].enter_context(nc.sbuf_tensor(name, list(shape), dt)), S.tile(name)

        def ps(name, shape, dt):
            return cur[0].enter_context(nc.psum_tensor(name, list(shape), dt)), S.tile(name)

        ident_f, T_identf = sb("ident_f", [128, 128], F32)
        ident_b, T_identb = sb("ident_b", [128, 128], BF16)
        ones_b, T_ones = sb("ones_b", [128, 128], BF16)
        iota_t, T_iota = sb("iota_t", [128, 128], F32)
        T_const = S.tile("consts")

        S.op("pool", [], [T_iota], lambda e: e.iota(iota_t[:], pattern=[[1, 128]], base=0, channel_multiplier=-1,
                                                    allow_small_or_imprecise_dtypes=True))
        S.op("dve", [T_iota], [T_identf], lambda e: e.tensor_scalar(out=ident_f[:], in0=iota_t[:], scalar1=0.0,
                                                                     scalar2=None, op0=ALU.is_equal))
        S.op("dve", [T_identf], [T_identb], lambda e: e.tensor_copy(out=ident_b[:], in_=ident_f[:]))
        S.op("pool", [], [T_ones], lambda e: e.memset(ones_b[:], 1.0))

        eps_t, T_eps = sb("eps_t", [128, 1], F32)
        S.op("pool", [], [T_eps], lambda e: e.memset(eps_t[:], EPS))
        EPS_AP = eps_t
        iota_r, T_iotar = sb("iota_r", [128, 128], F32)
        S.op("pool", [], [T_iotar], lambda e: e.iota(iota_r[:], pattern=[[1, 128]], base=0, channel_multiplier=0,
                                                     allow_small_or_imprecise_dtypes=True))
        st1 = ExitStack()
        cur[0] = st1
        w_in_b, T_win = sb("w_in_b", [128, 8, 2560], BF16)
        w_out_b, T_wout = sb("w_out_b", [128, 8, DM], BF16)
        diag, T_diag = sb("diag", [128, 124, 128], BF16)
        gmix_s, _ = sb("gmix_s", [128, 8], F32)
        convw_s, _ = sb("convw_s", [128, 4, 31], F32)
        cvec_s, _ = sb("cvec_s", [128, 12], F32)
        lam_s, _ = sb("lam_s", [128, 256], F32)
        gsub_s, _ = sb("gsub_s", [128, 128], F32)
        relb_s, _ = sb("relb_s", [128, 128], F32)
        bk_s, _ = sb("bk_s", [128, 2, 128], F32)
        Tb, T_Tb = sb("Tb", [128, 4, 2, 128], F32)
        eqm, T_eqm = sb("eqm", [128, 2, 128], F32)
        small, T_small = sb("small", [128, 16], F32)
        stage0, T_st0 = sb("stage0", [128, 1024], F32)
        stage1, T_st1 = sb("stage1", [128, 1024], F32)

        for dst, src in ((gmix_s, gmix), (convw_s, convw), (cvec_s, cvec), (bk_s, bkc)):
            S.dma("sp", [], [T_const], lambda e, d=dst, s_=src: e.dma_start(out=d[:], in_=s_))
        for dst, src, n in ((lam_s, lam, 256), (gsub_s, subg, 128), (relb_s, relb, 128)):
            S.dma("sp", [], [T_const], lambda e, d=dst, s_=src, n=n: e.dma_start(out=d[:], in_=s_.to_broadcast([128, n])))

        stg = [(stage0, T_st0), (stage1, T_st1)]
        i = 0
        for kc in range(8):
            for (a0, a1) in ((0, 1024), (1024, 2048), (2048, 2560)):
                stt, T_s = stg[i % 2]
                i += 1
                S.dma("sp", [], [T_s], lambda e, stt=stt, kc=kc, a0=a0, a1=a1: e.dma_start(
                    out=stt[:, 0:a1 - a0], in_=w_in[kc * 128:(kc + 1) * 128, a0:a1]))
                S.op("dve", [T_s, T_const], [T_win], lambda e, stt=stt, kc=kc, a0=a0, a1=a1: e.tensor_scalar(
                    out=w_in_b[:, kc, a0:a1], in0=stt[:, 0:a1 - a0], scalar1=gmix_s[:, kc:kc + 1],
                    scalar2=None, op0=ALU.mult))
        for kc in range(8):
            stt, T_s = stg[i % 2]
            i += 1
            S.dma("sp", [], [T_s], lambda e, stt=stt, kc=kc: e.dma_start(
                out=stt[:, 0:DM], in_=w_out[kc * 128:(kc + 1) * 128, :]))
            S.op("act", [T_s], [T_wout], lambda e, stt=stt, kc=kc: e.copy(out=w_out_b[:, kc, :], in_=stt[:, 0:DM]))
        for w in range(31):
            for cb in range(4):
                S.op("dve", [T_const, T_identf], [T_diag], lambda e, w=w, cb=cb: e.tensor_scalar(
                    out=diag[:, w * 4 + cb, :], in0=ident_f[:], scalar1=convw_s[:, cb, w:w + 1], scalar2=None,
                    op0=ALU.mult))
        S.op("dve", [T_const], [T_eqm], lambda e: e.tensor_scalar(
            out=eqm[:], in0=bk_s[:], scalar1=32.0, scalar2=-30000.0, op0=ALU.is_equal, op1=ALU.mult))
        for h in range(4):
            S.op("dve", [T_eqm], [T_Tb], lambda e, h=h: e.tensor_copy(out=Tb[:, h, :, :], in_=eqm[:]))
        for b in range(32):
            S.op("dve", [T_const], [T_eqm], lambda e, b=b: e.tensor_scalar(
                out=eqm[:], in0=bk_s[:], scalar1=float(b), scalar2=None, op0=ALU.is_equal))
            for h in range(4):
                S.op("dve", [T_eqm, T_const, T_Tb], [T_Tb], lambda e, b=b, h=h: e.scalar_tensor_tensor(
                    out=Tb[:, h, :, :], in0=eqm[:], scalar=relb_s[:, b * 4 + h:b * 4 + h + 1], in1=Tb[:, h, :, :],
                    op0=ALU.mult, op1=ALU.add))
        S.op("dve", [T_const], [T_eqm], lambda e: e.tensor_tensor(
            out=eqm[:, 0, :].rearrange("p (a b) -> p a b", a=2), in0=lam_s[:].rearrange("p (a b c) -> p a b c", a=2, b=2)[:, :, 0, :],
            in1=lam_s[:].rearrange("p (a b c) -> p a b c", a=2, b=2)[:, :, 1, :], op=ALU.mult))
        S.op("dve", [T_eqm], [T_small], lambda e: e.reduce_sum(
            out=small[:, 0:2], in_=eqm[:, 0, :].rearrange("p (a b) -> p a b", a=2), axis=AX.X))
        S.op("act", [T_small], [T_small], lambda e: e.activation(out=small[:, 2:4], in_=small[:, 0:2], func=AF.Exp))
        S.op("dve", [T_small], [T_small], lambda e: e.tensor_tensor(
            out=small[:, 4:5], in0=small[:, 3:4], in1=small[:, 2:3], op=ALU.subtract))
        S.op("dve", [T_small], [T_small], lambda e: e.tensor_scalar(
            out=small[:, 4:5], in0=small[:, 4:5], scalar1=-LAM_INIT, scalar2=None, op0=ALU.add))
        S.op("dve", [T_const], [T_const], lambda e: e.tensor_scalar(
            out=gsub_s[:], in0=gsub_s[:], scalar1=1.0 - LAM_INIT, scalar2=None, op0=ALU.mult))
        neg_lam = small[:, 4:5]

        fT, T_fT = sb("fT", [128, 8, 512], BF16)
        T_fTk = [S.tile(f"fT_k{i}") for i in range(4)]
        xt, T_xt = sb("xt", [128, DM], F32)
        xtl = [(xt, T_xt), sb("xt1", [128, DM], F32)]
        hb, T_hb = sb("hb", [128, DM], BF16)
        stat, T_stat = sb("stat", [128, 8], F32)
        aT, T_aT = sb("aT", [128, 4, 30 + 512], BF16)
        sig, T_sig = sb("sig", [128, 512], F32)
        a32, T_a32 = sb("a32", [128, 512], F32)
        qT, T_qT = sb("qT", [128, 4, 512], BF16)
        kT, T_kT = sb("kT", [128, 4, SEQ + 64], BF16)
        vaug, T_v = sb("vaug", [128, 17, 4, 130], BF16)
        catT, T_cat = sb("catT", [128, 8, 512], BF16)
        zq, T_zq = sb("zq", [128, 512], BF16)
        zk32, T_zk32 = sb("zk32", [128, 512], F32)
        zkb, T_zkb = sb("zkb", [128, 512], BF16)
        zv32, T_zv32 = sb("zv32", [128, 512], F32)
        PTT = [sb(f"PT{i}", [128, 512], BF16) for i in range(3)]
        TMP = [sb(f"tmpb{i}", [128, 128], F32) for i in range(3)]
        att, T_att = sb("att", [128, 128], F32)
        sqj, T_sq = sb("sqj", [128, 128], F32)
        attb, T_attb = sb("attb", [128, 512], BF16)
        astat, T_astat = sb("astat", [128, 8], F32)
        y32, T_y32 = sb("y32", [128, 4, 512], F32)
        ybf, T_ybf = sb("ybf", [128, 4, 512], BF16)
        ysq, T_ysq = sb("ysq", [128, 4, 512], BF16)
        junk, T_junk = ysq[:].rearrange("p a b -> p (a b)")[:, 0:DM], T_ysq
        mu, T_mu = sig, T_sig
        rs, T_rs = a32, T_a32
        ctail, T_ctail = zk32, T_zk32
        cst32, T_cst32 = sb("cst32", [128, 4, 30], F32)

        pT, T_pT = ps("pT", [128, 1024], BF16)
        pM = [ps(f"pM{i}", [128, 512], F32) for i in range(2)]
        pSS = [ps(f"pS{i}", [128, 512], F32) for i in range(2)]
        pO, T_pO = ps("pO", [128, 2, 256], F32)
        pX = [ps(f"pX{i}", [128, 512], F32) for i in range(2)]
        pm_i = [0]
        pTA = (pX[0][0][:].bitcast(BF16), pX[0][1])
        pOO = [(pO, T_pO), (pM[0][0][:].rearrange("p (a b) -> p a b", a=2), pM[0][1])]
        pSS = pSS + [pX[1]]

        def next_pM():
            pm_i[0] += 1
            return pM[pm_i[0] % 2]

        S.op("pool", [], [T_v], lambda e: e.memset(vaug[:], 1.0))

        def rms_to_bf16(n, src, T_src, dst_b, T_dst, col):
            S.op("act", [T_src], [T_junk, T_stat], lambda e: e.activation(
                out=junk[0:n, :], in_=src[0:n, :], func=AF.Square, accum_out=stat[0:n, col:col + 1]))
            S.op("act", [T_stat], [T_stat], lambda e: e.activation(
                out=stat[0:n, col + 1:col + 2], in_=stat[0:n, col:col + 1], func=AF.Sqrt, scale=1.0 / DM, bias=EPS_AP[0:n, :]))
            S.op("dve", [T_stat], [T_stat], lambda e: e.reciprocal(
                out=stat[0:n, col + 1:col + 2], in_=stat[0:n, col + 1:col + 2]))
            S.op("dve", [T_src, T_stat], [T_dst], lambda e: e.tensor_scalar(
                out=dst_b[0:n, :], in0=src[0:n, :], scalar1=stat[0:n, col + 1:col + 2], scalar2=None, op0=ALU.mult))

        def transpose_to_fT(n, src_b, T_src, c0, T_dst=None, pbank=None):
            pT_, T_pT_ = pbank if pbank is not None else (pT, T_pT)
            for kc in range(8):
                S.op("pe", [T_src, T_identb], [T_pT_], lambda e, kc=kc: e.transpose(
                    out=pT_[:, kc * 128:kc * 128 + n], in_=src_b[0:n, kc * 128:(kc + 1) * 128], identity=ident_b[0:n, 0:n]))
            S.op("act", [T_pT_], [T_dst or T_fT], lambda e: e.copy(
                out=fT[:, :, c0:c0 + n], in_=pT_[:, :].rearrange("p (k c) -> p k c", k=8)[:, :, 0:n]))

        seqs = []
        for s_ in range(n_pseq):
            seqs.append(("p", xp[s_], SEQ, kp[s_], vp[s_], cp[s_], s_ * SEQ))
        seqs.append(("s", xs, 64, ks, vs, cs, n_pseq * SEQ))

        S.barrier()
        import os
        STOP = int(os.environ.get("KSTOP", "99"))
        if STOP <= 0:
            seqs = []

        for (kind, xd, ntok, kd, vd, cd, tok0) in seqs:
            past = SEQ if kind == "s" else 0
            if kind == "p":
                S.op("pool", [], [T_aT], lambda e: e.memset(aT[:, :, 0:30], 0.0))
            else:
                S.dma("sp", [], [T_cst32], lambda e: e.dma_start(out=cst32[:], in_=scT))
                S.op("dve", [T_cst32], [T_aT], lambda e: e.tensor_copy(out=aT[:, :, 0:30], in_=cst32[:]))
                for h in range(4):
                    for hf in range(2):
                        stt, T_s = stg[(h * 2 + hf) % 2]
                        S.dma("sp", [], [T_s], lambda e, stt=stt, h=h, hf=hf: e.dma_start(
                            out=stt[:, 0:1024], in_=ckT[h, :, hf * 1024:(hf + 1) * 1024]))
                        S.op("act", [T_s], [T_kT], lambda e, stt=stt, h=h, hf=hf: e.copy(
                            out=kT[:, h, hf * 1024:(hf + 1) * 1024], in_=stt[:, 0:1024]))
                for blk in range(16):
                    stt, T_s = stg[blk % 2]
                    S.dma("sp", [], [T_s], lambda e, stt=stt, blk=blk: e.dma_start(
                        out=stt[:, 0:512], in_=cv[blk * 128:(blk + 1) * 128, :]))
                    S.op("dve", [T_s], [T_v], lambda e, stt=stt, blk=blk: e.tensor_copy(
                        out=vaug[:, blk, :, 0:128], in_=stt[:, 0:512].rearrange("p (h e) -> p h e", h=4)))

            ngroups = (ntok + 511) // 512
            for g in range(ngroups):
                g0 = g * 512
                N = min(512, ntok - g0)
                tiles = [(c0, min(128, N - c0)) for c0 in range(0, N, 128)]
                last_group = (g == ngroups - 1)

                LA = []
                for k_, (c0, n) in enumerate(tiles):
                    S.cap = []
                    xa, T_xa = xtl[k_ % 2]
                    S.dma("sp", [], [T_xa], lambda e, c0=c0, n=n, xa=xa: e.dma_start(out=xa[0:n, :], in_=xd[g0 + c0:g0 + c0 + n, :]))
                    rms_to_bf16(n, xa, T_xa, hb, T_hb, 0)
                    transpose_to_fT(n, hb, T_hb, c0, T_fTk[k_], pbank=pTA)
                    LA.append(S.cap)
                    S.cap = None

                if STOP <= 1:
                    continue
                LB = []
                for k_, (c0, n) in enumerate(tiles):
                    S.cap = []
                    LB.append(S.cap)
                    blk = (past + g0 + c0) // 128
                    kcol = past + g0 + c0
                    for j in range(int(os.environ.get('KJ', '3'))):
                        pm, T_pm = next_pM()
                        for kc in range(8):
                            S.op("pe", [T_fTk[k_], T_win], [T_pm], lambda e, pm=pm, kc=kc, j=j, c0=c0, n=n: e.matmul(
                                pm[0:n, :], lhsT=fT[:, kc, c0:c0 + n], rhs=w_in_b[:, kc, 1024 + j * 512:1024 + (j + 1) * 512],
                                start=(kc == 0), stop=(kc == 7)))
                        if j == 0:
                            S.op("act", [T_pm], [T_zq], lambda e, pm=pm, n=n: e.activation(
                                out=zq[0:n, :], in_=pm[0:n, :], func=AF.Copy, scale=0.125))
                            for h in range(4):
                                S.op("pe", [T_zq, T_identb], [T_pT], lambda e, h=h, n=n: e.transpose(
                                    out=pT[:, h * 128:h * 128 + n], in_=zq[0:n, h * 128:(h + 1) * 128], identity=ident_b[0:n, 0:n]))
                            S.op("dve", [T_pT], [T_qT], lambda e, c0=c0, n=n: e.tensor_copy(
                                out=qT[:, :, c0:c0 + n], in_=pT[:, 0:512].rearrange("p (k c) -> p k c", k=4)[:, :, 0:n]))
                        elif j == 1:
                            if not os.environ.get("K1A"):
                                S.op("dve", [T_pm], [T_zk32], lambda e, pm=pm, n=n: e.tensor_copy(out=zk32[0:n, :], in_=pm[0:n, :]))
                            S.op("act", [T_zk32], [T_zkb], lambda e, pm=pm, n=n: e.copy(out=zkb[0:n, :], in_=zk32[0:n, :]))
                            if not os.environ.get("NOKD"):
                                S.dma("sp", [T_zk32], [], lambda e, c0=c0, n=n: e.dma_start(
                                    out=kd[g0 + c0:g0 + c0 + n, :], in_=zk32[0:n, :]))
                            for h in range(0 if os.environ.get("K1B") else 4):
                                S.op("pe", [T_zkb, T_identb], [T_pT], lambda e, h=h, n=n: e.transpose(
                                    out=pT[:, 512 + h * 128:512 + h * 128 + n], in_=zkb[0:n, h * 128:(h + 1) * 128],
                                    identity=ident_b[0:n, 0:n]))
                            if not os.environ.get("K1C"):
                              S.op("dve", [T_pT], [T_kT], lambda e, kcol=kcol, n=n: e.tensor_copy(
                                out=kT[:, :, kcol:kcol + n], in_=pT[:, 512:1024].rearrange("p (k c) -> p k c", k=4)[:, :, 0:n]))
                        else:
                            S.op("dve", [T_pm], [T_zv32], lambda e, pm=pm, n=n: e.tensor_copy(out=zv32[0:n, :], in_=pm[0:n, :]))
                            S.op("act", [T_zv32], [T_v], lambda e, pm=pm, n=n, blk=blk: e.copy(
                                out=vaug[0:n, blk, :, 0:128], in_=zv32[0:n, :].rearrange("p (h e) -> p h e", h=4)))
                            S.dma("sp", [T_zv32], [], lambda e, c0=c0, n=n: e.dma_start(
                                out=vd[g0 + c0:g0 + c0 + n, :], in_=zv32[0:n, :]))

                S.cap = None
                S.interleave(LA[0], [])
                for k_ in range(len(tiles)):
                    S.interleave(LB[k_], LA[k_ + 1] if k_ + 1 < len(tiles) else [])
                if STOP <= 2:
                    continue
                for cb in range(4):
                    pa, T_pa = next_pM()
                    pg, T_pg = next_pM()
                    for kc in range(8):
                        S.op("pe", T_fTk + [T_win], [T_pa], lambda e, pa=pa, kc=kc, cb=cb: e.matmul(
                            pa[:, 0:N], lhsT=w_in_b[:, kc, cb * 128:(cb + 1) * 128], rhs=fT[:, kc, 0:N],
                            start=(kc == 0), stop=(kc == 7)))
                    for kc in range(8):
                        S.op("pe", T_fTk + [T_win], [T_pg], lambda e, pg=pg, kc=kc, cb=cb: e.matmul(
                            pg[:, 0:N], lhsT=w_in_b[:, kc, 512 + cb * 128:512 + (cb + 1) * 128], rhs=fT[:, kc, 0:N],
                            start=(kc == 0), stop=(kc == 7)))
                    S.op("act", [T_pg], [T_sig], lambda e, pg=pg: e.activation(out=sig[:, 0:N], in_=pg[:, 0:N], func=AF.Sigmoid))
                    S.op("dve", [T_pa, T_sig], [T_a32], lambda e, pa=pa: e.tensor_tensor(
                        out=a32[:, 0:N], in0=pa[:, 0:N], in1=sig[:, 0:N], op=ALU.mult))
                    S.op("pool", [T_a32], [T_aT], lambda e, cb=cb: e.tensor_copy(out=aT[:, cb, 30:30 + N], in_=a32[:, 0:N]))
                    if last_group:
                        pm, T_pm = pX[0]
                        S.op("pe", [T_a32, T_identf], [T_pm], lambda e, pm=pm, cb=cb: e.transpose(
                            out=pm[0:30, cb * 128:(cb + 1) * 128], in_=a32[:, N - 30:N], identity=ident_f[:]))
                if last_group:
                    pm, T_pm = pX[0]
                    S.op("act", [T_pm], [T_ctail], lambda e, pm=pm: e.copy(out=ctail[0:30, :], in_=pm[0:30, :]))
                    S.dma("sp", [T_ctail], [], lambda e: e.dma_start(out=cd, in_=ctail[0:30, :]))

                if STOP <= 3:
                    continue
                S.cap = []
                for cb in range(4):
                    pm, T_pm = pM[1]
                    for w in range(31):
                        S.op("pe", [T_aT, T_diag], [T_pm], lambda e, pm=pm, w=w, cb=cb: e.matmul(
                            pm[:, 0:N], lhsT=diag[:, w * 4 + cb, :], rhs=aT[:, cb, w:w + N], start=(w == 0), stop=(w == 30)))
                    S.op("act", [T_pm, T_const], [T_y32], lambda e, pm=pm, cb=cb: e.activation(
                        out=y32[:, cb, 0:N], in_=pm[:, 0:N], func=AF.Identity, bias=cvec_s[:, cb:cb + 1]))
                    S.op("act", [T_pm, T_const], [T_ysq], lambda e, pm=pm, cb=cb: e.activation(
                        out=ysq[:, cb, 0:N], in_=pm[:, 0:N], func=AF.Square, bias=cvec_s[:, cb:cb + 1]))
                    S.op("pool", [T_y32], [T_ybf], lambda e, cb=cb: e.tensor_copy(out=ybf[:, cb, 0:N], in_=y32[:, cb, 0:N]))
                p1, T_p1 = pX[0]
                p2, T_p2 = pX[0]
                for cb in range(4):
                    S.op("pe", [T_ybf, T_ones], [T_p1], lambda e, cb=cb: e.matmul(
                        p1[:, 0:N], lhsT=ones_b[:], rhs=ybf[:, cb, 0:N], start=(cb == 0), stop=(cb == 3)))
                S.op("dve", [T_p1], [T_mu], lambda e: e.tensor_scalar(
                    out=mu[:, 0:N], in0=p1[:, 0:N], scalar1=1.0 / 512, scalar2=None, op0=ALU.mult))
                for cb in range(4):
                    S.op("pe", [T_ysq, T_ones], [T_p2], lambda e, cb=cb: e.matmul(
                        p2[:, 0:N], lhsT=ones_b[:], rhs=ysq[:, cb, 0:N], start=(cb == 0), stop=(cb == 3)))
                S.op("dve", [T_mu], [T_rs], lambda e: e.tensor_tensor(out=rs[:, 0:N], in0=mu[:, 0:N], in1=mu[:, 0:N], op=ALU.mult))
                S.op("dve", [T_p2, T_rs], [T_rs], lambda e: e.scalar_tensor_tensor(
                    out=rs[:, 0:N], in0=p2[:, 0:N], scalar=1.0 / 512, in1=rs[:, 0:N], op0=ALU.mult, op1=ALU.subtract))
                S.op("act", [T_rs, T_eps], [T_rs], lambda e: e.activation(
                    out=rs[:, 0:N], in_=rs[:, 0:N], func=AF.Sqrt, bias=eps_t[:, :]))
                S.op("dve", [T_rs], [T_rs], lambda e: e.reciprocal(out=rs[:, 0:N], in_=rs[:, 0:N]))
                for cb in range(4):
                    S.op("dve", [T_y32, T_mu], [T_y32], lambda e, cb=cb: e.tensor_tensor(
                        out=y32[:, cb, 0:N], in0=y32[:, cb, 0:N], in1=mu[:, 0:N], op=ALU.subtract))
                    S.op("pool", [T_y32, T_rs], [T_y32], lambda e, cb=cb: e.tensor_tensor(
                        out=y32[:, cb, 0:N], in0=y32[:, cb, 0:N], in1=rs[:, 0:N], op=ALU.mult))
                    S.op("act", [T_y32, T_const], [T_cat], lambda e, cb=cb: e.activation(
                        out=catT[:, cb, 0:N], in_=y32[:, cb, 0:N], func=AF.Silu,
                        scale=cvec_s[:, 4 + cb:5 + cb], bias=cvec_s[:, 8 + cb:9 + cb]))
                if not last_group:
                    S.op("pool", [T_aT], [T_aT], lambda e: e.tensor_copy(out=aT[:, :, 0:30], in_=aT[:, :, N:N + 30]))

                capD = S.cap
                S.cap = None
                items = []
                for (c0, n) in tiles:
                    qi = (g0 + c0) // 128
                    if kind == "p":
                        far = list(range(0, max(qi - 1, 0)))
                        near = ([(qi - 1, 128, 1)] if qi >= 1 else []) + [(qi, 128, 0)]
                    else:
                        far = list(range(0, 15))
                        near = [(15, 128, 1), (16, 64, 0)]
                    nblk = len(far) + len(near)
                    for h in range(4):
                        for m in range(2):
                            done = 0
                            for f0 in range(0, len(far), 4):
                                chunk = far[f0:f0 + 4]
                                items.append(dict(kind="far", c0=c0, n=n, h=h, m=m, blks=chunk, done=done, nblk=nblk))
                                done += len(chunk)
                            for (blk, nk, bkind) in near:
                                items.append(dict(kind="near", c0=c0, n=n, h=h, m=m, blk=blk, nk=nk, bkind=bkind,
                                                  done=done, nblk=nblk))
                                done += 1
                        items[-1]["head_end"] = True
                    items[-1]["tile_end"] = True

                def emit_qk(k, it):
                    pS_, T_pS_ = pSS[k % 3]
                    PT_, T_PT_ = PTT[k % 3]
                    c0, n, h, m = it["c0"], it["n"], it["h"], it["m"]
                    mrow = slice(m * 64, (m + 1) * 64)
                    if it["kind"] == "far":
                        for j, blk in enumerate(it["blks"]):
                            S.op("pe", [T_kT, T_qT], [T_pS_], lambda e, j=j, blk=blk: e.matmul(
                                pS_[:, j * n:(j + 1) * n], lhsT=kT[mrow, h, blk * 128:(blk + 1) * 128],
                                rhs=qT[mrow, h, c0:c0 + n], start=True, stop=True))
                        cn = len(it["blks"]) * n
                        S.op("act", [T_pS_, T_const], [T_PT_], lambda e: e.activation(
                            out=PT_[:, 0:cn], in_=pS_[:, 0:cn], func=AF.Exp, bias=relb_s[:, 60 + h:61 + h]))
                    else:
                        blk, nk, bkind = it["blk"], it["nk"], it["bkind"]
                        tb_, T_tb_ = TMP[k % 3]
                        S.op("pe", [T_kT, T_qT], [T_pS_], lambda e: e.matmul(
                            pS_[0:nk, 0:n], lhsT=kT[mrow, h, blk * 128:blk * 128 + nk],
                            rhs=qT[mrow, h, c0:c0 + n], start=True, stop=True))
                        S.op("dve", [T_pS_, T_Tb], [T_tb_], lambda e: e.tensor_tensor(
                            out=tb_[0:nk, 0:n], in0=pS_[0:nk, 0:n], in1=Tb[0:nk, h, bkind, 0:n], op=ALU.add))
                        S.op("act", [T_tb_], [T_PT_], lambda e: e.activation(
                            out=PT_[0:nk, 0:n], in_=tb_[0:nk, 0:n], func=AF.Exp))

                def emit_pv(k, it):
                    PT_, T_PT_ = PTT[k % 3]
                    c0, n, h, m = it["c0"], it["n"], it["h"], it["m"]
                    qi_ = (g0 + c0) // 128
                    pO_, T_pO_ = pOO[(qi_ * 4 + h) % 2]
                    nblk = it["nblk"]
                    if it["kind"] == "far":
                        for j, blk in enumerate(it["blks"]):
                            dn = it["done"] + j
                            S.op("pe", [T_PT_, T_v], [T_pO_], lambda e, j=j, blk=blk, dn=dn: e.matmul(
                                pO_[0:n, m, 0:129], lhsT=PT_[:, j * n:(j + 1) * n], rhs=vaug[:, blk, h, 0:129],
                                start=(dn == 0), stop=(dn == nblk - 1)))
                    else:
                        blk, nk = it["blk"], it["nk"]
                        dn = it["done"]
                        S.op("pe", [T_PT_, T_v], [T_pO_], lambda e: e.matmul(
                            pO_[0:n, m, 0:129], lhsT=PT_[0:nk, 0:n], rhs=vaug[0:nk, blk, h, 0:129],
                            start=(dn == 0), stop=(dn == nblk - 1)))
                    if it.get("head_end"):
                        S.op("dve", [T_pO_], [T_astat], lambda e: e.reciprocal(
                            out=astat[0:n, 0:2], in_=pO_[0:n, :, 128:129].rearrange("p a b -> p (a b)")))
                        S.op("dve", [T_astat, T_small], [T_astat], lambda e: e.tensor_tensor(
                            out=astat[0:n, 2:3], in0=astat[0:n, 1:2], in1=neg_lam[0:n, :], op=ALU.mult))
                        S.op("dve", [T_pO_, T_astat], [T_att], lambda e: e.tensor_scalar(
                            out=att[0:n, :], in0=pO_[0:n, 0, 0:128], scalar1=astat[0:n, 0:1], scalar2=None, op0=ALU.mult))
                        S.op("dve", [T_pO_, T_astat, T_att], [T_att], lambda e: e.scalar_tensor_tensor(
                            out=att[0:n, :], in0=pO_[0:n, 1, 0:128], scalar=astat[0:n, 2:3], in1=att[0:n, :],
                            op0=ALU.mult, op1=ALU.add))
                        S.op("dve", [T_att], [T_sq, T_astat], lambda e: e.scalar_tensor_tensor(
                            out=sqj[0:n, :], in0=att[0:n, :], scalar=1.0, in1=att[0:n, :], op0=ALU.mult, op1=ALU.mult,
                            accum_out=astat[0:n, 3:4]))
                        S.op("act", [T_astat, T_eps], [T_astat], lambda e: e.activation(
                            out=astat[0:n, 5:6], in_=astat[0:n, 3:4], func=AF.Ln, scale=1.0 / 128, bias=eps_t[0:n, :]))
                        S.op("act", [T_astat], [T_astat], lambda e: e.activation(
                            out=astat[0:n, 4:5], in_=astat[0:n, 5:6], func=AF.Exp, scale=-0.5))
                        S.op("dve", [T_att, T_astat, T_const], [T_attb], lambda e: e.scalar_tensor_tensor(
                            out=attb[0:n, h * 128:(h + 1) * 128], in0=att[0:n, :], scalar=astat[0:n, 4:5], in1=gsub_s[0:n, :],
                            op0=ALU.mult, op1=ALU.mult))
                    if it.get("tile_end"):
                        for hh in range(4):
                            S.op("pe", [T_attb, T_identb], [T_pT], lambda e, hh=hh: e.transpose(
                                out=pT[:, hh * 128:hh * 128 + n], in_=attb[0:n, hh * 128:(hh + 1) * 128], identity=ident_b[0:n, 0:n]))
                        S.op("act", [T_pT], [T_cat], lambda e: e.copy(
                            out=catT[:, 4:8, c0:c0 + n], in_=pT[:, 0:512].rearrange("p (k c) -> p k c", k=4)[:, :, 0:n]))

                per_item = -(-len(capD) // max(len(items) - 4, 1))
                cpos = 0
                for k, it in enumerate(items):
                    emit_qk(k, it)
                    if k >= 2:
                        emit_pv(k - 2, items[k - 2])
                    for _ in range(per_item):
                        if cpos < len(capD):
                            S.replay(capD[cpos])
                            cpos += 1
                for k in range(max(len(items) - 2, 0), len(items)):
                    emit_pv(k, items[k])
                while cpos < len(capD):
                    S.replay(capD[cpos])
                    cpos += 1

                if STOP <= 5:
                    continue
                LE1, LE2 = [], []
                for k_, (c0, n) in enumerate(tiles):
                    xr, T_xr = xtl[k_ % 2]
                    S.cap = []
                    S.dma("sp", [], [T_xr], lambda e, c0=c0, n=n, xr=xr: e.dma_start(out=xr[0:n, :], in_=xd[g0 + c0:g0 + c0 + n, :]))
                    for hf in range(2):
                        po, T_po = pX[hf]
                        for kc in range(8):
                            S.op("pe", [T_cat, T_wout], [T_po], lambda e, po=po, kc=kc, hf=hf, c0=c0, n=n: e.matmul(
                                po[0:n, :], lhsT=catT[:, kc, c0:c0 + n], rhs=w_out_b[:, kc, hf * 512:(hf + 1) * 512],
                                start=(kc == 0), stop=(kc == 7)))
                        S.op("dve", [T_po, T_xr], [T_xr], lambda e, po=po, hf=hf, n=n, xr=xr: e.tensor_tensor(
                            out=xr[0:n, hf * 512:(hf + 1) * 512], in0=po[0:n, :], in1=xr[0:n, hf * 512:(hf + 1) * 512], op=ALU.add))
                    S.dma("sp", [T_xr], [], lambda e, c0=c0, n=n, xr=xr: e.dma_start(
                        out=x1d[tok0 + g0 + c0:tok0 + g0 + c0 + n, :], in_=xr[0:n, :]))
                    LE1.append(S.cap)
                    S.cap = []
                    rms_to_bf16(n, xr, T_xr, hb, T_hb, 2)
                    transpose_to_fT(n, hb, T_hb, c0, T_fTk[k_])
                    LE2.append(S.cap)
                    S.cap = None
                S.interleave(LE1[0], [])
                for k_ in range(len(tiles)):
                    S.interleave(LE2[k_], LE1[k_ + 1] if k_ + 1 < len(tiles) else [])
                for kc in range(8):
                    S.dma("sp", T_fTk, [], lambda e, kc=kc: e.dma_start(
                        out=h2Td[kc, :, tok0 + g0:tok0 + g0 + N], in_=fT[:, kc, 0:N]), key=T_fT)

        S.barrier()
        st1.close()
        st2 = ExitStack()
        st.enter_context(st2)
        cur[0] = st2
        TG = 256
        NCH = 64
        if with_peer:
            ijwd = nc.dram_tensor("ijwd", [128, 3, NT], F32, kind="Internal").ap()
            uscr_t = nc.dram_tensor("uscr", [NCH, 128, 8 * 256], BF16, kind="Internal").ap()
            vscr_t = nc.dram_tensor("vscr", [NCH, 128, 2 * DM], BF16, kind="Internal").ap()
            uscr = [uscr_t[ic].rearrange("p (k e) -> p k e", k=8) for ic in range(NCH)]
            vscr = [vscr_t[ic].rearrange("p (b d) -> p b d", b=2) for ic in range(NCH)]
            T_uscr = [S.tile(f"uscr{ic}") for ic in range(NCH)]
            T_vscr = [S.tile(f"vscr{ic}") for ic in range(NCH)]
            T_ijwd = S.tile("ijwd")

            gfin_s, T_gfin = sb("gfin_s", [128, DM], F32)
            iota_b, T_iotab = sb("iota_b", [128, 128], BF16)
            T_c2 = S.tile("consts2")
            S.dma("sp", [], [T_c2], lambda e: e.dma_start(out=gfin_s[:], in_=gfin.to_broadcast([128, DM])))
            S.op("dve", [T_iotar], [T_iotab], lambda e: e.tensor_copy(out=iota_b[:], in_=iota_r[:]))

            st2a = ExitStack()
            cur[0] = st2a
            TA = 512
            wq_b, T_wq = sb("wq_b", [128, 8, 2048], BF16)
            keys_b, T_keys = sb("keys_b", [128, 16, 128], BF16)
            gffn_s, T_gffn = sb("gffn_s", [128, 8], F32)
            sg0, T_sg0 = sb("sg0", [128, 1024], F32)
            sg1, T_sg1 = sb("sg1", [128, 1024], F32)
            h2g, T_h2g = sb("h2g", [128, 8, TA], BF16)
            qryT, T_qry = sb("qryT", [128, 16, TA], BF16)
            s_sb, T_ssb = sb("s_sb", [128, 16, 128], F32)
            wk4l = [sb(f"wk4_{i}", [128, 256], F32) for i in range(4)]
            wk4 = [x[0] for x in wk4l]
            T_wk4 = [x[1] for x in wk4l]
            T_A4 = [S.tile(f"A4_{i}") for i in range(4)]
            T_I4 = [S.tile(f"I4_{i}") for i in range(4)]
            T_C4 = [S.tile(f"C4_{i}") for i in range(4)]
            T_P4 = [S.tile(f"P4t_{i}") for i in range(4)]
            A_, T_A = sb("A_", [128, 16, 16], F32)
            Iu, T_Iu = sb("Iu", [128, 16, 16], U32)
            If, T_If = sb("If", [128, 16, 16], F32)
            cand, T_cand = sb("cand", [128, 8, 256], F32)
            C_, T_C = sb("C_", [128, 8, 16], F32)
            pos, T_pos = sb("pos", [128, 8, 16], U32)
            ku, T_ku = sb("ku", [128, 2, 128], U32)
            kf, T_kf = sb("kf", [128, 2, 128], F32)
            E_, T_E = sb("E_", [128, 8, 16], F32)
            gst, T_gst = sb("gst", [128, 32], F32)
            oh, T_oh = sb("oh", [128, 8, 16, 16], F32)
            ijw, T_ijw = sb("ijw", [128, 3, 128], F32)
            ijT = [sb(f"ijT{i}", [128, 3, 128], F32) for i in range(2)]
            stu = [sb(f"stu{i}", [128, 8, 256], BF16) for i in range(2)]
            stv = [sb(f"stv{i}", [128, 2, DM], BF16) for i in range(2)]
            pGa = [ps(f"pGa{i}", [128, 512], F32) for i in range(4)]
            pga_i = [0]

            def next_pGa():
                pga_i[0] += 1
                return pGa[pga_i[0] % 4]

            S.dma("sp", [], [T_c2], lambda e: e.dma_start(out=gffn_s[:], in_=gffn))
            S.dma("pool", [], [T_keys], lambda e: e.dma_start(out=keys_b[:], in_=keysT.rearrange("r d n -> d r n")))
            sgs = [(sg0, T_sg0), (sg1, T_sg1)]
            ii = 0
            for kc in range(8):
                for hf in range(2):
                    stt, T_s = sgs[ii % 2]
                    ii += 1
                    S.dma("sp", [], [T_s], lambda e, stt=stt, kc=kc, hf=hf: e.dma_start(
                        out=stt[:], in_=wq[kc * 128:(kc + 1) * 128, hf * 1024:(hf + 1) * 1024]))
                    S.op("dve", [T_s, T_c2], [T_wq], lambda e, stt=stt, kc=kc, hf=hf: e.tensor_scalar(
                        out=wq_b[:, kc, hf * 1024:(hf + 1) * 1024], in0=stt[:], scalar1=gffn_s[:, kc:kc + 1],
                        scalar2=None, op0=ALU.mult))

            conv_i = [0]
            T_scw = [S.tile(f"scw_u{i}") for i in range(2)]
            T_scw2 = [S.tile(f"scw_v{i}") for i in range(2)]

            def convert_chunk():
                ic = conv_i[0]
                if ic >= NCH:
                    return
                conv_i[0] += 1
                (su, T_su), (sv, T_sv) = stu[ic % 2], stv[ic % 2]
                for k4 in range(2):
                    S.dma("pool", [], [T_su], lambda e, k4=k4: e.dma_start(
                        out=su[:, k4 * 4:(k4 + 1) * 4, :],
                        in_=uT[k4 * 512:(k4 + 1) * 512, ic * 256:(ic + 1) * 256].rearrange("(k p) e -> p k e", p=128)))
                S.dma("pool", [], [T_sv], lambda e: e.dma_start(
                    out=sv[:], in_=pv[ic * 256:(ic + 1) * 256, :].rearrange("(b p) d -> p b d", p=128)))
                S.dma("sp", [T_su], [T_uscr[ic]], lambda e: e.dma_start(out=uscr[ic], in_=su[:]), key=T_scw[ic % 2])
                S.dma("sp", [T_sv], [T_vscr[ic]], lambda e: e.dma_start(out=vscr[ic], in_=sv[:]), key=T_scw2[ic % 2])

            ssbL = [(s_sb, T_ssb), sb("s_sb1", [128, 16, 128], F32)]
            AL = [(A_, None), sb("A_1", [128, 16, 16], F32)]
            IuL = [(Iu, None), sb("Iu1", [128, 16, 16], U32)]
            IfL = [(If, T_If), sb("If1", [128, 16, 16], F32)]
            candL = [(cand, T_cand), sb("cand1", [128, 8, 256], F32)]
            T_A4L = [T_A4, [S.tile(f"A4b_{i}") for i in range(4)]]
            T_I4L = [T_I4, [S.tile(f"I4b_{i}") for i in range(4)]]
            wkAl = [sb(f"wkA_{i}", [128, 128], F32) for i in range(4)]
            P2A = os.environ.get("KPOOL2A", "pool")
            tile_ctr = [0]

            def gate_tile(t0, c0, n, pb):
                s_sb, T_ssb = ssbL[pb]
                A_ = AL[pb][0]
                Iu = IuL[pb][0]
                If, T_If = IfL[pb]
                cand, T_cand = candL[pb]
                T_A4 = T_A4L[pb]
                T_I4 = T_I4L[pb]
                S.cap = []
                for q4 in range(4):
                    pg, T_pg = next_pGa()
                    for j in range(4):
                        rp = q4 * 4 + j
                        S.op("pe", [T_qry, T_keys], [T_pg], lambda e, pg=pg, j=j, rp=rp: e.matmul(
                            pg[0:n, j * 128:(j + 1) * 128], lhsT=qryT[:, rp, c0:c0 + n], rhs=keys_b[:, rp, :],
                            start=True, stop=True))
                    S.op("act", [T_pg], [T_ssb], lambda e, pg=pg, q4=q4: e.copy(
                        out=s_sb[0:n, q4 * 4:(q4 + 1) * 4, :], in_=pg[0:n, :].rearrange("p (a b) -> p a b", a=4)))
                for rp0 in range(0, 16, 4):
                    rps = list(range(rp0, rp0 + 4))
                    for rp in rps:
                        S.op("dve", [T_ssb], [T_A4[rp % 4]], lambda e, rp=rp: e.max(out=A_[0:n, rp, 0:8], in_=s_sb[0:n, rp, :]))
                    for rp in rps:
                        S.op("dve", [T_ssb, T_A4[rp % 4]], [T_I4[rp % 4]], lambda e, rp=rp: e.max_index(
                            out=Iu[0:n, rp, 0:8], in_max=A_[0:n, rp, 0:8], in_values=s_sb[0:n, rp, :]))
                    for rp in rps:
                        S.op("dve", [T_ssb, T_A4[rp % 4]], [wkAl[rp % 4][1]], lambda e, rp=rp: e.match_replace(
                            out=wkAl[rp % 4][0][0:n, :], in_to_replace=A_[0:n, rp, 0:8], in_values=s_sb[0:n, rp, :], imm_value=-1e30))
                    for rp in rps:
                        S.op("dve", [wkAl[rp % 4][1]], [T_A4[rp % 4]], lambda e, rp=rp: e.max(out=A_[0:n, rp, 8:16], in_=wkAl[rp % 4][0][0:n, :]))
                    for rp in rps:
                        S.op("dve", [wkAl[rp % 4][1], T_A4[rp % 4]], [T_I4[rp % 4]], lambda e, rp=rp: e.max_index(
                            out=Iu[0:n, rp, 8:16], in_max=A_[0:n, rp, 8:16], in_values=wkAl[rp % 4][0][0:n, :]))
                S.op("dve", T_I4, [T_If], lambda e: e.tensor_copy(out=If[0:n], in_=Iu[0:n]))
                A4 = A_[0:n].rearrange("p (r a) k -> p r a k", a=2)
                I4 = If[0:n].rearrange("p (r a) k -> p r a k", a=2)
                S.op(P2A, T_A4, [T_cand], lambda e: e.tensor_tensor(
                    out=cand[0:n].rearrange("p r (a b) -> p r a b", a=16),
                    in0=A4[:, :, 0, :].unsqueeze(3).to_broadcast([n, 8, 16, 16]),
                    in1=A4[:, :, 1, :].unsqueeze(2).to_broadcast([n, 8, 16, 16]), op=ALU.add))
                L1 = S.cap
                S.cap = []
                for r0 in range(0, 8, 4):
                    rs_ = list(range(r0, r0 + 4))
                    for r in rs_:
                        S.op("dve", [T_cand], [T_C4[r % 4]], lambda e, r=r: e.max(out=C_[0:n, r, 0:8], in_=cand[0:n, r, :]))
                    for r in rs_:
                        S.op("dve", [T_cand, T_C4[r % 4]], [T_P4[r % 4]], lambda e, r=r: e.max_index(
                            out=pos[0:n, r, 0:8], in_max=C_[0:n, r, 0:8], in_values=cand[0:n, r, :]))
                    for r in rs_:
                        S.op("dve", [T_cand, T_C4[r % 4]], [T_wk4[r % 4]], lambda e, r=r: e.match_replace(
                            out=wk4[r % 4][0:n, :], in_to_replace=C_[0:n, r, 0:8], in_values=cand[0:n, r, :], imm_value=-1e30))
                    for r in rs_:
                        S.op("dve", [T_wk4[r % 4]], [T_C4[r % 4]], lambda e, r=r: e.max(out=C_[0:n, r, 8:16], in_=wk4[r % 4][0:n, :]))
                    for r in rs_:
                        S.op("dve", [T_wk4[r % 4], T_C4[r % 4]], [T_P4[r % 4]], lambda e, r=r: e.max_index(
                            out=pos[0:n, r, 8:16], in_max=C_[0:n, r, 8:16], in_values=wk4[r % 4][0:n, :]))
                S.op("dve", T_C4, [T_gst], lambda e: e.tensor_scalar(
                    out=gst[0:n, 0:8], in0=C_[0:n, :, 0], scalar1=-1.0, scalar2=None, op0=ALU.mult))
                for r in range(8):
                    S.op("act", T_C4 + [T_gst], [T_E, T_gst], lambda e, r=r: e.activation(
                        out=E_[0:n, r, :], in_=C_[0:n, r, :], func=AF.Exp, bias=gst[0:n, r:r + 1],
                        accum_out=gst[0:n, 8 + r:9 + r]))
                S.op("dve", [T_gst], [T_gst], lambda e: e.reciprocal(out=gst[0:n, 16:24], in_=gst[0:n, 8:16]))
                S.op("dve", [T_E, T_gst], [T_ijw], lambda e: e.tensor_tensor(
                    out=ijw[0:n, 2, :].rearrange("p (r k) -> p r k", r=8), in0=E_[0:n],
                    in1=gst[0:n, 16:24].unsqueeze(2).to_broadcast([n, 8, 16]), op=ALU.mult))
                S.op("dve", T_P4, [T_ku], lambda e: e.tensor_single_scalar(
                    out=ku[0:n, 0, :], in_=pos[0:n].rearrange("p r k -> p (r k)"), scalar=4, op=ALU.logical_shift_right))
                S.op("dve", T_P4, [T_ku], lambda e: e.tensor_single_scalar(
                    out=ku[0:n, 1, :], in_=pos[0:n].rearrange("p r k -> p (r k)"), scalar=15, op=ALU.bitwise_and))
                S.op("dve", [T_ku], [T_kf], lambda e: e.tensor_copy(out=kf[0:n], in_=ku[0:n]))
                for a in range(2):
                    S.op("dve", [T_kf, T_iotar], [T_oh], lambda e, a=a: e.tensor_tensor(
                        out=oh[0:n],
                        in0=kf[0:n, a, :].rearrange("p (r k) -> p r k", r=8).unsqueeze(3).to_broadcast([n, 8, 16, 16]),
                        in1=iota_r[0:n, 0:16].unsqueeze(1).unsqueeze(1).to_broadcast([n, 8, 16, 16]), op=ALU.is_equal))
                    S.op(P2A, [T_oh, T_If], [T_oh], lambda e, a=a: e.tensor_tensor(
                        out=oh[0:n], in0=oh[0:n],
                        in1=I4[:, :, a, :].unsqueeze(2).to_broadcast([n, 8, 16, 16]), op=ALU.mult))
                    S.op("dve", [T_oh], [T_ijw], lambda e, a=a: e.reduce_sum(
                        out=ijw[0:n, a, :].rearrange("p (r k) -> p r k", r=8), in_=oh[0:n], axis=AX.X))
                pg, T_pg = next_pGa()
                for a in range(3):
                    S.op("pe", [T_ijw, T_identf], [T_pg], lambda e, pg=pg, a=a: e.transpose(
                        out=pg[:, a * 128:a * 128 + n], in_=ijw[0:n, a, :], identity=ident_f[0:n, 0:n]))
                (it_, T_it) = ijT[tile_ctr[0] % 2]
                tile_ctr[0] += 1
                S.op("act", [T_pg], [T_it], lambda e, pg=pg, it_=it_: e.copy(
                    out=it_[:, :, 0:n], in_=pg[:, 0:384].rearrange("p (a t) -> p a t", a=3)[:, :, 0:n]))
                S.dma("sp", [T_it], [T_ijwd], lambda e, it_=it_: e.dma_start(
                    out=ijwd[:, :, t0 + c0:t0 + c0 + n], in_=it_[:, :, 0:n]), key=T_it)
                L2 = S.cap
                S.cap = None
                return L1, L2

            def interleave(La, Lb):
                na, nb = len(La), len(Lb)
                ia = ib = 0
                while ia < na or ib < nb:
                    if ib >= nb or (ia < na and ia * nb <= ib * na):
                        S.replay(La[ia])
                        ia += 1
                    else:
                        S.replay(Lb[ib])
                        ib += 1

            prevL2 = []
            xt_i = 0
            for t0 in range(0, NT, TA):
                N = min(TA, NT - t0)
                tiles = [(c0, min(128, N - c0)) for c0 in range(0, N, 128)]
                S.dma("sp", [], [T_h2g], lambda e: e.dma_start(
                    out=h2g[:, :, 0:N], in_=h2Td[:, :, t0:t0 + N].rearrange("k p t -> p k t")))
                for blk in range(16):
                    pg, T_pg = next_pGa()
                    for kc in range(8):
                        S.op("pe", [T_wq, T_h2g], [T_pg], lambda e, pg=pg, kc=kc, blk=blk: e.matmul(
                            pg[:, 0:N], lhsT=wq_b[:, kc, blk * 128:(blk + 1) * 128], rhs=h2g[:, kc, 0:N],
                            start=(kc == 0), stop=(kc == 7)))
                    S.op("act", [T_pg], [T_qry], lambda e, pg=pg, blk=blk: e.copy(out=qryT[:, blk, 0:N], in_=pg[:, 0:N]))
                for (c0, n) in tiles:
                    convert_chunk()
                    convert_chunk()
                    L1, L2 = gate_tile(t0, c0, n, xt_i % 2)
                    xt_i += 1
                    interleave(L1, prevL2)
                    prevL2 = L2
            interleave([], prevL2)
            while conv_i[0] < NCH:
                convert_chunk()
            S.barrier()
            st2a.close()

            st2b = ExitStack()
            st.enter_context(st2b)
            cur[0] = st2b
            Gall = [sb(f"Gall{i}", [128, 128, TG], BF16) for i in range(2)]
            ubuf = [sb(f"ubuf{i}", [128, 8, 256], BF16) for i in range(2)]
            vbuf = [sb(f"vbuf{i}", [128, 2, DM], BF16) for i in range(3)]
            h2g2 = [sb(f"h2g2_{i}", [128, 8, TG], BF16) for i in range(2)]
            ijg = [sb(f"ijg{i}", [128, 3, TG], F32) for i in range(2)]
            P4 = [sb(f"P4_{i}", [128, 4, 128], BF16) for i in range(3)]
            Q4 = [sb(f"Q4_{i}", [128, 4, 128], BF16) for i in range(3)]
            gbuf = [sb(f"gbuf{i}", [128, TG], F32) for i in range(3)]
            cbuf = [sb(f"cbuf{i}", [128, TG], BF16) for i in range(3)]
            x2l = [sb(f"x2_{i}", [128, DM], F32) for i in range(2)]
            junk2, T_junk2 = sb("junk2", [128, DM], BF16)
            st2s, T_st2s = sb("st2s", [128, 8], F32)
            pY = [[ps(f"pY{t}{h}", [128, 512], F32) for h in range(2)] for t in range(2)]
            pA = [ps(f"pA{i}", [128, 512], F32) for i in range(3)]
            pG = [ps(f"pG{i}", [128, 512], F32) for i in range(1)]

            groups = [(t0, min(TG, NT - t0)) for t0 in range(0, NT, TG)]
            MAXG = int(os.environ.get("KGROUPS", "999"))
            groups = groups[:MAXG]
            NG = len(groups)
            MULT_ENG = os.environ.get("KMULT", "pool")

            def gtiles(g):
                N = groups[g][1]
                return [(c0, min(128, N - c0)) for c0 in range(0, N, 128)]

            def load_group(g):
                t0, N = groups[g]
                (hg, T_hg), (ij, T_ij) = h2g2[g % 2], ijg[g % 2]
                S.dma("sp", [], [T_hg], lambda e: e.dma_start(
                    out=hg[:, :, 0:N], in_=h2Td[:, :, t0:t0 + N].rearrange("k p t -> p k t")))
                S.dma("sp", [T_ijwd], [T_ij], lambda e: e.dma_start(out=ij[:, :, 0:N], in_=ijwd[:, :, t0:t0 + N]))

            def p10_dve(g, b):
                (ij, T_ij) = ijg[g % 2]
                (p4, T_p4), (q4_, T_q4) = P4[b % 3], Q4[b % 3]
                for u in range(4):
                    t = b * 4 + u
                    S.op("dve", [T_ij, T_iotab], [T_p4], lambda e, u=u, t=t: e.tensor_scalar(
                        out=p4[:, u, :], in0=iota_b[:], scalar1=ij[:, 0, t:t + 1], scalar2=None, op0=ALU.is_equal))
                    S.op("dve", [T_ij, T_iotab], [T_q4], lambda e, u=u, t=t: e.tensor_scalar(
                        out=q4_[:, u, :], in0=iota_b[:], scalar1=ij[:, 1, t:t + 1], scalar2=ij[:, 2, t:t + 1],
                        op0=ALU.is_equal, op1=ALU.mult))

            def p10_pe(g, b):
                (p4, T_p4), (q4_, T_q4) = P4[b % 3], Q4[b % 3]
                (ga, T_ga) = Gall[g % 2]
                pg, T_pg = pG[0]
                for u in range(4):
                    S.op("pe", [T_p4, T_q4], [T_pg], lambda e, u=u: e.matmul(
                        pg[:, u * 128:(u + 1) * 128], lhsT=q4_[:, u, :], rhs=p4[:, u, :], start=True, stop=True))
                S.op("act", [T_pg], [T_ga], lambda e: e.copy(
                    out=ga[:, :, b * 4:b * 4 + 4], in_=pg[:, :].rearrange("p (t i) -> p i t", t=4)))

            class P10:
                def __init__(self, g):
                    self.g = g
                    self.nb = groups[g][1] // 4
                    self.d = 0
                    self.p = 0

                def step(self):
                    if self.d < self.nb:
                        p10_dve(self.g, self.d)
                        self.d += 1
                        if self.d - self.p >= 3:
                            p10_pe(self.g, self.p)
                            self.p += 1
                    elif self.p < self.nb:
                        p10_pe(self.g, self.p)
                        self.p += 1

                def flush(self):
                    while self.p < self.nb:
                        if self.d < self.nb and self.d - self.p < 3:
                            p10_dve(self.g, self.d)
                            self.d += 1
                        else:
                            p10_pe(self.g, self.p)
                            self.p += 1

            chunk_ctr = [0]

            def load_chunk(ic):
                c = chunk_ctr[0]
                chunk_ctr[0] += 1
                (ub, T_ub), (vb, T_vb) = ubuf[c % 2], vbuf[c % 3]
                S.dma("sp", [T_uscr[ic]], [T_ub], lambda e: e.dma_start(out=ub[:], in_=uscr[ic]))
                S.dma("sp", [T_vscr[ic]], [T_vb], lambda e: e.dma_start(out=vb[:], in_=vscr[ic]))

            def stage_u(g, i):
                N = groups[g][1]
                c = g * NCH + i // 2
                ib = i % 2
                (ub, T_ub) = ubuf[c % 2]
                (hg, T_hg) = h2g2[g % 2]
                (ga, T_ga) = Gall[g % 2]
                gidx = g * 128 + i
                pa, T_pa = pA[gidx % 3]
                gb, T_gb = gbuf[gidx % 3]
                cb_, T_cb = cbuf[gidx % 3]
                for kc in range(8):
                    S.op("pe", [T_ub, T_hg], [T_pa], lambda e, kc=kc: e.matmul(
                        pa[:, 0:N], lhsT=ub[:, kc, ib * 128:(ib + 1) * 128], rhs=hg[:, kc, 0:N],
                        start=(kc == 0), stop=(kc == 7)))
                S.op("act", [T_pa], [T_gb], lambda e: e.activation(out=gb[:, 0:N], in_=pa[:, 0:N], func=AF.Gelu))
                S.op(MULT_ENG, [T_gb, T_ga], [T_cb], lambda e: e.tensor_tensor(
                    out=cb_[:, 0:N], in0=gb[:, 0:N], in1=ga[:, i, 0:N], op=ALU.mult))

            def stage_v(g, i):
                c = g * NCH + i // 2
                ib = i % 2
                (vb, T_vb) = vbuf[c % 3]
                cb_, T_cb = cbuf[(g * 128 + i) % 3]
                for ti, (c0, n) in enumerate(gtiles(g)):
                    for hf in range(2):
                        py, T_py = pY[ti][hf]
                        S.op("pe", [T_cb, T_vb], [T_py], lambda e, py=py, c0=c0, n=n, hf=hf: e.matmul(
                            py[0:n, :], lhsT=cb_[:, c0:c0 + n], rhs=vb[:, ib, hf * 512:(hf + 1) * 512],
                            start=(i == 0), stop=(i == 127)))

            def prefetch_x1(g):
                t0, N = groups[g]
                for ti, (c0, n) in enumerate(gtiles(g)):
                    x2, T_x2 = x2l[ti]
                    tg = t0 + c0
                    S.dma("sp", [], [T_x2], lambda e, tg=tg, n=n, x2=x2: e.dma_start(out=x2[0:n, :], in_=x1d[tg:tg + n, :]))

            def epilogue(g):
                t0, N = groups[g]
                for ti, (c0, n) in enumerate(gtiles(g)):
                    x2, T_x2 = x2l[ti]
                    for hf in range(2):
                        py, T_py = pY[ti][hf]
                        S.op("dve", [T_py, T_x2], [T_x2], lambda e, py=py, hf=hf, n=n, x2=x2: e.tensor_tensor(
                            out=x2[0:n, hf * 512:(hf + 1) * 512], in0=py[0:n, :], in1=x2[0:n, hf * 512:(hf + 1) * 512], op=ALU.add))
                for ti, (c0, n) in enumerate(gtiles(g)):
                    x2, T_x2 = x2l[ti]
                    tg = t0 + c0
                    k0 = ti * 4
                    S.op("act", [T_x2], [T_junk2, T_st2s], lambda e, n=n, x2=x2, k0=k0: e.activation(
                        out=junk2[0:n, :], in_=x2[0:n, :], func=AF.Square, accum_out=st2s[0:n, k0:k0 + 1]))
                    S.op("act", [T_st2s, T_eps], [T_st2s], lambda e, n=n, k0=k0: e.activation(
                        out=st2s[0:n, k0 + 1:k0 + 2], in_=st2s[0:n, k0:k0 + 1], func=AF.Sqrt, scale=1.0 / DM, bias=eps_t[0:n, :]))
                    S.op("dve", [T_st2s], [T_st2s], lambda e, n=n, k0=k0: e.reciprocal(out=st2s[0:n, k0 + 1:k0 + 2], in_=st2s[0:n, k0 + 1:k0 + 2]))
                    S.op("dve", [T_x2, T_st2s, T_c2], [T_x2], lambda e, n=n, x2=x2, k0=k0: e.scalar_tensor_tensor(
                        out=x2[0:n, :], in0=x2[0:n, :], scalar=st2s[0:n, k0 + 1:k0 + 2], in1=gfin_s[0:n, :], op0=ALU.mult, op1=ALU.mult))
                    if tg < n_pseq * SEQ:
                        dst = yp[tg // SEQ][tg % SEQ:tg % SEQ + n, :]
                    else:
                        dst = ys[0:n, :]
                    S.dma("sp", [T_x2], [], lambda e, dst=dst, n=n, x2=x2: e.dma_start(out=dst, in_=x2[0:n, :]))

            load_group(0)
            load_chunk(0)
            pz = P10(0)
            pz.flush()
            seq = [(g, i) for g in range(NG) for i in range(128)]
            nxt = None

            def emit_v(idx):
                g_, i_ = seq[idx]
                stage_v(g_, i_)
                if i_ == 127:
                    epilogue(g_)

            for idx, (g, i) in enumerate(seq):
                if i == 0:
                    nxt = None
                    if g + 1 < NG:
                        load_group(g + 1)
                        nxt = P10(g + 1)
                stage_u(g, i)
                if idx >= 2:
                    emit_v(idx - 2)
                if i % 2 == 0:
                    ic_next = i // 2 + 1
                    if ic_next < NCH:
                        load_chunk(ic_next)
                    elif g + 1 < NG:
                        load_chunk(0)
                if nxt is not None and (i % 2 == 1 or i in (8, 16, 24, 32)):
                    nxt.step()
                if i == 64:
                    prefetch_x1(g)
                if i == 127 and nxt is not None:
                    nxt.flush()
            emit_v(len(seq) - 2)
            emit_v(len(seq) - 1)

        S.barrier()
        S.finish("sp")
        print("ops per engine:", S.nops, "sems:", S.nsem)
    return nc


def _prep_shared(inp):
    f = lambda a: np.ascontiguousarray(np.asarray(a, dtype=np.float32))
    sh = {}
    sh["w_in"] = f(inp["w_in"][0])
    sh["gmix"] = f(inp["g_mix"][0].reshape(8, 128).T)
    sh["convw"] = f(inp["conv_w"][0].reshape(31, 4, 128).transpose(2, 1, 0))
    sh["cvec"] = f(np.concatenate([inp["conv_b"][0].reshape(4, 128).T, inp["conv_ln_g"][0].reshape(4, 128).T,
                                   inp["conv_ln_b"][0].reshape(4, 128).T], axis=1))
    sh["lam"] = f(np.stack([inp["lambda_q1"][0], inp["lambda_k1"][0], inp["lambda_q2"][0], inp["lambda_k2"][0]]).reshape(1, 256))
    sh["subg"] = f(inp["subln_g"][0].reshape(1, 128))
    sh["relb"] = f(inp["rel_bias"].reshape(1, 128))
    sh["w_out"] = f(inp["w_out"][0])
    sh["gffn"] = f(inp["g_ffn"][0].reshape(8, 128).T)
    sh["wq"] = f(inp["w_query"][0])
    sh["keysT"] = f(inp["sub_keys"][0].reshape(16, 128, 128).transpose(0, 2, 1))
    sh["uT"] = f(inp["peer_u"][0].T)
    sh["pv"] = f(inp["peer_v"][0])
    sh["gfin"] = f(inp["g_final"].reshape(1, DM))
    sh["bkc"] = _bucket_tiles()
    return sh


def kernel(**inp):
    f = lambda a: np.ascontiguousarray(np.asarray(a, dtype=np.float32))
    sh = _prep_shared(inp)
    nc = build_program(2)
    in_maps = []
    for c in range(NCORES):
        m = dict(sh)
        m["xp"] = f(inp["x_prompt"][2 * c:2 * c + 2])
        m["xs"] = f(inp["x_sample"][c])
        m["ckT"] = f(np.asarray(inp["cache_k"][0, c]).reshape(SEQ, 4, 128).transpose(1, 2, 0))
        m["cv"] = f(np.asarray(inp["cache_v"][0, c]).reshape(SEQ, 512))
        m["scT"] = f(np.asarray(inp["state_conv"][0, c]).reshape(30, 4, 128).transpose(2, 1, 0))
        in_maps.append(m)
    res = run_bass_kernel_spmd(nc, in_maps, core_ids=list(range(NCORES)))
    R = res.results
    y_prompt = np.concatenate([r["yp"] for r in R], axis=0)
    y_sample = np.stack([r["ys"] for r in R], axis=0)
    k_prompt = np.concatenate([r["kp"] for r in R], axis=0).reshape(1, 16, SEQ, 4, 2, 64)
    v_prompt = np.concatenate([r["vp"] for r in R], axis=0).reshape(1, 16, SEQ, 4, 128)
    c_prompt = np.concatenate([r["cp"] for r in R], axis=0).reshape(1, 16, 30, 512)
    k_sample = np.stack([r["ks"] for r in R], axis=0).reshape(1, 8, 64, 4, 2, 64)
    v_sample = np.stack([r["vs"] for r in R], axis=0).reshape(1, 8, 64, 4, 128)
    c_sample = np.stack([r["cs"] for r in R], axis=0).reshape(1, 8, 30, 512)
    return (y_prompt, y_sample, k_prompt, v_prompt, c_prompt, k_sample, v_sample, c_sample)
```
